# Optimizing a Trainium2 kernel written in Bass

```python
import jax, jax.numpy as jnp
from jax import lax
import numpy as np

D_MODEL = 1024
BATCH = 16
SEQ = 2048
DEPTH = 4

GRID_W = 64
CTX_LEN = 256
N_MIXERS = 4
N_MOD = 6
EPS = 1e-6
NEG_INF = -1e30
D_FF = 4 * D_MODEL
ROPE_BASE = 10000.0
RET_HEADS = 8
RET_DK = D_MODEL // RET_HEADS
RET_DV = 2 * RET_DK
RET_CHUNK = 128
RET_IN = 2 * RET_HEADS * RET_DK + 3 * RET_HEADS * RET_DV
ATT_HEADS = 16
ATT_KV_HEADS = 4
ATT_DH = D_MODEL // ATT_HEADS
ATT_WINDOW = 128
ATT_BLOCK = 128
ATT_IN = (ATT_HEADS + 2 * ATT_KV_HEADS) * ATT_DH
POOL_WINDOWS = (2, 4, 8, 16)
POOL_GROUP = D_MODEL // len(POOL_WINDOWS)
DN_HEADS = 8
DN_DK = D_MODEL // DN_HEADS
DN_DV = DN_DK
DN_CONV_W = 5
DN_CHUNK = 64
DN_QKV = 2 * DN_HEADS * DN_DK + DN_HEADS * DN_DV
DN_IN = DN_QKV + 4 * DN_HEADS + 2 * DN_HEADS * DN_DV

kernel_name = 'hybrid_interleaved_dit_trunk'


def _n_uses(m):
    return len(range(m, DEPTH, N_MIXERS))


def rmsnorm(x, g):
    xf = x.astype(jnp.float32)
    y = xf * lax.rsqrt(jnp.mean(xf * xf, axis=-1, keepdims=True) + EPS)
    return (y * g.astype(jnp.float32)).astype(x.dtype)


def head_norm(o, dtype, g=None):
    of = o.astype(jnp.float32)
    y = of * lax.rsqrt(jnp.mean(of * of, axis=-1, keepdims=True) + EPS)
    if g is not None:
        y = y * g.astype(jnp.float32)
    return y.astype(dtype)


def l2norm(t):
    tf = t.astype(jnp.float32)
    return tf * lax.rsqrt(jnp.sum(tf * tf, axis=-1, keepdims=True) + EPS)


def grid_positions(n):
    rows = n // GRID_W
    row = jnp.broadcast_to(jnp.arange(rows, dtype=jnp.int32)[:, None], (rows, GRID_W)).reshape(-1)
    col = jnp.broadcast_to(jnp.arange(GRID_W, dtype=jnp.int32)[None, :], (rows, GRID_W)).reshape(-1)
    return row, col


def _rotate(x, pos):
    half = x.shape[-1] // 2
    inv = ROPE_BASE ** (-jnp.arange(half, dtype=jnp.float32) / half)
    ang = pos.astype(jnp.float32)[:, None] * inv[None, :]
    cos, sin = jnp.cos(ang).astype(x.dtype), jnp.sin(ang).astype(x.dtype)
    x1, x2 = x[..., :half], x[..., half:]
    return jnp.concatenate([x1 * cos - x2 * sin, x2 * cos + x1 * sin], axis=-1)


def axial_rope(x, row, col):
    h = x.shape[-1] // 2
    return jnp.concatenate([_rotate(x[..., :h], row), _rotate(x[..., h:], col)], axis=-1)


def to_heads(t, n_heads):
    b, n, hd = t.shape
    return t.reshape(b, n, n_heads, hd // n_heads).transpose(0, 2, 1, 3)


def from_heads(t):
    b, h, n, d = t.shape
    return t.transpose(0, 2, 1, 3).reshape(b, n, h * d)


def flip_seq(t):
    return jnp.flip(t, axis=2)


def sink_softmax(scores, sink):
    s = sink[:, :, None, None]
    m = jnp.maximum(jnp.max(scores, axis=-1, keepdims=True), s)
    p = jnp.exp(scores - m)
    return p / (jnp.sum(p, axis=-1, keepdims=True) + jnp.exp(s - m))


def centred_mean(x, w):
    n = x.shape[1]
    lo_off = w // 2
    hi_off = w - 1 - lo_off
    cs = jnp.pad(jnp.cumsum(x.astype(jnp.float32), axis=1), ((0, 0), (1, 0), (0, 0)))
    t = jnp.arange(n)
    lo = jnp.clip(t - lo_off, 0, n)
    hi = jnp.clip(t + hi_off + 1, 0, n)
    cnt = (hi - lo).astype(jnp.float32)[None, :, None]
    return ((cs[:, hi] - cs[:, lo]) / cnt).astype(x.dtype)


def short_conv(x, w):
    k = w.shape[0]
    pad = k // 2
    y = lax.conv_general_dilated(x, w[:, None, :].astype(x.dtype), window_strides=(1,),
                                 padding=[(pad, k - 1 - pad)], dimension_numbers=('NWC', 'WIO', 'NWC'),
                                 feature_group_count=x.shape[-1])
    return jax.nn.silu(y)


def retention_scan(q, k, v, log_gamma, s0):
    b, h, n, _ = q.shape
    dv = v.shape[-1]
    c = RET_CHUNK
    nc = n // c

    def chunks(t):
        return jnp.moveaxis(t.astype(jnp.float32).reshape(b, h, nc, c, t.shape[-1]), 2, 0)

    qc, kc, vc = chunks(q), chunks(k), chunks(v)
    idx = jnp.arange(c, dtype=jnp.float32)
    lg = log_gamma.astype(jnp.float32)[:, None]
    diff = idx[:, None] - idx[None, :]
    decay_in = jnp.where(diff >= 0, jnp.exp(lg[..., None] * jnp.maximum(diff, 0.0)), 0.0)
    q_decay = jnp.exp(lg * (idx + 1.0))
    k_decay = jnp.exp(lg * (c - 1.0 - idx))
    chunk_decay = jnp.exp(lg[:, 0] * c)

    def step(s, inp):
        qi, ki, vi = inp
        scores = jnp.einsum('bhid,bhjd->bhij', qi, ki) * decay_in
        o = jnp.einsum('bhij,bhjv->bhiv', scores, vi) + jnp.einsum('bhid,bhdv->bhiv', qi * q_decay[..., None], s)
        s = s * chunk_decay[:, None, None] + jnp.einsum('bhjd,bhjv->bhdv', ki * k_decay[..., None], vi)
        return s, o

    s_fin, oc = lax.scan(step, s0, (qc, kc, vc))
    return jnp.moveaxis(oc, 0, 2).reshape(b, h, n, dv), s_fin


def gated_delta_scan(q, k, v, beta, log_alpha, s0):
    b, h, n, dk = q.shape
    dv = v.shape[-1]
    c = DN_CHUNK
    nc = n // c

    def chunks(t):
        t = t.astype(jnp.float32)
        return t.reshape(b, h, nc, c, *t.shape[3:])

    q, k, v = chunks(q) * dk ** -0.5, chunks(k), chunks(v)
    beta = chunks(beta)
    g = jnp.cumsum(chunks(log_alpha), axis=-1)
    tri = jnp.tril(jnp.ones((c, c), dtype=bool))
    strict = jnp.tril(jnp.ones((c, c), dtype=bool), -1)
    gd = g[..., :, None] - g[..., None, :]
    decay = jnp.where(tri, jnp.exp(jnp.where(tri, gd, 0.0)), 0.0)
    kb = k * beta[..., None]
    a = jnp.where(strict, jnp.einsum('bhnid,bhnjd->bhnij', kb, k) * decay, 0.0)
    rhs = jnp.concatenate([v * beta[..., None], kb * jnp.exp(g)[..., None]], axis=-1)
    sol = lax.linalg.triangular_solve(a, rhs, left_side=True, lower=True, unit_diagonal=True)
    u, w = sol[..., :dv], sol[..., dv:]
    attn = jnp.einsum('bhnid,bhnjd->bhnij', q, k) * decay
    q_g = q * jnp.exp(g)[..., None]
    k_g = k * jnp.exp(g[..., -1:] - g)[..., None]
    g_last = jnp.exp(g[..., -1])
    xs = tuple(jnp.moveaxis(t, 2, 0) for t in (u, w, attn, q_g, k_g, g_last))

    def step(s, inp):
        u_i, w_i, attn_i, qg_i, kg_i, gl_i = inp
        v_new = u_i - jnp.einsum('bhid,bhdv->bhiv', w_i, s)
        o = jnp.einsum('bhid,bhdv->bhiv', qg_i, s) + jnp.einsum('bhij,bhjv->bhiv', attn_i, v_new)
        s = s * gl_i[..., None, None] + jnp.einsum('bhjd,bhjv->bhdv', kg_i, v_new)
        return s, o

    s_fin, o = lax.scan(step, s0, xs)
    return jnp.moveaxis(o, 0, 2).reshape(b, h, n, dv), s_fin


def retention_mixer(u_lat, u_ctx, w_in, decay_logit, w_out, row, col, with_ctx_out):
    hk, hv = RET_HEADS * RET_DK, RET_HEADS * RET_DV
    splits = [hk, 2 * hk, 2 * hk + hv, 2 * hk + 2 * hv]
    lg_f = jax.nn.log_sigmoid(decay_logit[0].astype(jnp.float32))
    lg_b = jax.nn.log_sigmoid(decay_logit[1].astype(jnp.float32))

    def project(u, rotate):
        q, k, v, g_f, g_b = jnp.split(u @ w_in, splits, axis=-1)
        q = to_heads(q, RET_HEADS) * RET_DK ** -0.5
        k = to_heads(k, RET_HEADS)
        if rotate:
            q, k = axial_rope(q, row, col), axial_rope(k, row, col)
        return q, k, to_heads(v, RET_HEADS), g_f, g_b

    def both_dirs(q, k, v, s0_f, s0_b):
        o_f, s_f = retention_scan(q, k, v, lg_f, s0_f)
        o_b, s_b = retention_scan(flip_seq(q), flip_seq(k), flip_seq(v), lg_b, s0_b)
        return o_f, flip_seq(o_b), s_f, s_b

    def merge(o_f, o_b, g_f, g_b, dtype):
        y = from_heads(head_norm(o_f, dtype)) * jax.nn.silu(g_f) + from_heads(head_norm(o_b, dtype)) * jax.nn.silu(g_b)
        return y @ w_out

    zeros = jnp.zeros((u_lat.shape[0], RET_HEADS, RET_DK, RET_DV), jnp.float32)
    qc, kc, vc, gfc, gbc = project(u_ctx, False)
    oc_f, oc_b, sc_f, sc_b = both_dirs(qc, kc, vc, zeros, zeros)
    ql, kl, vl, gfl, gbl = project(u_lat, True)
    ol_f, ol_b, _, _ = both_dirs(ql, kl, vl, sc_f, sc_b)
    y_lat = merge(ol_f, ol_b, gfl, gbl, u_lat.dtype)
    y_ctx = merge(oc_f, oc_b, gfc, gbc, u_ctx.dtype) if with_ctx_out else None
    return y_lat, y_ctx


def window_attention_mixer(u_lat, u_ctx, w_in, sink, w_out, row, col, with_ctx_out):
    b, n, _ = u_lat.shape
    grp = ATT_HEADS // ATT_KV_HEADS
    nq, nkv = ATT_HEADS * ATT_DH, ATT_KV_HEADS * ATT_DH
    sink_f = sink.astype(jnp.float32).reshape(ATT_KV_HEADS, grp)

    def project(u):
        q, k, v = jnp.split(u @ w_in, [nq, nq + nkv], axis=-1)
        q = q.reshape(b, u.shape[1], ATT_KV_HEADS, grp, ATT_DH).transpose(0, 2, 3, 1, 4) * ATT_DH ** -0.5
        return q, to_heads(k, ATT_KV_HEADS), to_heads(v, ATT_KV_HEADS)

    def merge(o):
        return o.transpose(0, 3, 1, 2, 4).reshape(b, o.shape[3], nq) @ w_out

    qc, kc, vc = project(u_ctx)
    ql, kl, vl = project(u_lat)
    ql, kl = axial_rope(ql, row, col), axial_rope(kl, row, col)

    span = ATT_BLOCK + 2 * ATT_WINDOW
    pad = ((0, 0), (0, 0), (ATT_WINDOW, ATT_WINDOW), (0, 0))
    kp, vp = jnp.pad(kl, pad), jnp.pad(vl, pad)
    qi = jnp.arange(ATT_BLOCK)[:, None]
    kj = jnp.arange(span)[None, :]
    in_window = jnp.abs(kj - ATT_WINDOW - qi) <= ATT_WINDOW

    def block(bi):
        start = bi * ATT_BLOCK
        qb = lax.dynamic_slice_in_dim(ql, start, ATT_BLOCK, axis=3)
        kb = lax.dynamic_slice_in_dim(kp, start, span, axis=2)
        vb = lax.dynamic_slice_in_dim(vp, start, span, axis=2)
        key_pos = start - ATT_WINDOW + kj
        valid = in_window & (key_pos >= 0) & (key_pos < n)
        s_loc = jnp.where(valid, jnp.einsum('bkgqd,bksd->bkgqs', qb, kb).astype(jnp.float32), NEG_INF)
        s_ctx = jnp.einsum('bkgqd,bksd->bkgqs', qb, kc).astype(jnp.float32)
        p = sink_softmax(jnp.concatenate([s_loc, s_ctx], axis=-1), sink_f).astype(vb.dtype)
        return (jnp.einsum('bkgqs,bksd->bkgqd', p[..., :span], vb)
                + jnp.einsum('bkgqs,bksd->bkgqd', p[..., span:], vc))

    o_blocks = lax.map(block, jnp.arange(n // ATT_BLOCK))
    o_lat = jnp.moveaxis(o_blocks, 0, 3).reshape(b, ATT_KV_HEADS, grp, n, ATT_DH)
    y_lat = merge(o_lat)
    if with_ctx_out:
        s_cc = jnp.einsum('bkgqd,bksd->bkgqs', qc, kc).astype(jnp.float32)
        p_cc = sink_softmax(s_cc, sink_f).astype(vc.dtype)
        y_ctx = merge(jnp.einsum('bkgqs,bksd->bkgqd', p_cc, vc))
    else:
        y_ctx = None
    return y_lat, y_ctx


def pool_mixer(u_lat, u_ctx, w_grp, b_grp, scale, with_ctx_out):
    def mix(u):
        b, n, _ = u.shape
        ug = u.reshape(b, n, len(POOL_WINDOWS), POOL_GROUP)
        diffs = jnp.stack([centred_mean(ug[:, :, gi], w) - ug[:, :, gi] for gi, w in enumerate(POOL_WINDOWS)], axis=2)
        y = jnp.einsum('bngc,gcd->bngd', diffs, w_grp).reshape(b, n, D_MODEL) + b_grp
        return y * scale

    return mix(u_lat), (mix(u_ctx) if with_ctx_out else None)


def deltanet_mixer(u_lat, u_ctx, w_in, conv_w, a_log, dt_bias, norm_g, w_out, with_ctx_out):
    nh, hk, hv = DN_HEADS, DN_HEADS * DN_DK, DN_HEADS * DN_DV
    splits = [DN_QKV, DN_QKV + nh, DN_QKV + 2 * nh, DN_QKV + 3 * nh, DN_QKV + 4 * nh, DN_QKV + 4 * nh + hv]

    def gates(bb, aa, d):
        beta = jax.nn.sigmoid(bb.astype(jnp.float32)).transpose(0, 2, 1)
        log_alpha = -jnp.exp(a_log[d].astype(jnp.float32)) * jax.nn.softplus(aa.astype(jnp.float32) + dt_bias[d].astype(jnp.float32))
        return beta, log_alpha.transpose(0, 2, 1)

    def project(u):
        qkv, b_f, b_b, a_f, a_b, g_f, g_b = jnp.split(u @ w_in, splits, axis=-1)
        qkv = short_conv(qkv, conv_w)
        q, k, v = jnp.split(qkv, [hk, 2 * hk], axis=-1)
        q, k, v = l2norm(to_heads(q, nh)), l2norm(to_heads(k, nh)), to_heads(v, nh)
        return q, k, v, gates(b_f, a_f, 0), gates(b_b, a_b, 1), g_f, g_b

    def both_dirs(q, k, v, gf, gb, s0_f, s0_b):
        o_f, s_f = gated_delta_scan(q, k, v, gf[0], gf[1], s0_f)
        o_b, s_b = gated_delta_scan(flip_seq(q), flip_seq(k), flip_seq(v), flip_seq(gb[0]), flip_seq(gb[1]), s0_b)
        return o_f, flip_seq(o_b), s_f, s_b

    def merge(o_f, o_b, g_f, g_b, dtype):
        y = (from_heads(head_norm(o_f, dtype, norm_g)) * jax.nn.silu(g_f)
             + from_heads(head_norm(o_b, dtype, norm_g)) * jax.nn.silu(g_b))
        return y @ w_out

    zeros = jnp.zeros((u_lat.shape[0], nh, DN_DK, DN_DV), jnp.float32)
    qc, kc, vc, gfc, gbc, ofc, obc = project(u_ctx)
    oc_f, oc_b, sc_f, sc_b = both_dirs(qc, kc, vc, gfc, gbc, zeros, zeros)
    ql, kl, vl, gfl, gbl, ofl, obl = project(u_lat)
    ol_f, ol_b, _, _ = both_dirs(ql, kl, vl, gfl, gbl, sc_f, sc_b)
    y_lat = merge(ol_f, ol_b, ofl, obl, u_lat.dtype)
    y_ctx = merge(oc_f, oc_b, ofc, obc, u_ctx.dtype) if with_ctx_out else None
    return y_lat, y_ctx


def ffn_sublayer(h, shift, scale, gate, pre_g, post_g, w1, w2):
    u = rmsnorm(h, pre_g) * (1 + scale) + shift
    y = jnp.square(jax.nn.relu(u @ w1)) @ w2
    return h + gate * rmsnorm(y, post_g)


def setup_inputs(seed: int = 0) -> dict:
    key = jax.random.key(seed)
    ks = iter(jax.random.split(key, 32))

    def nrm(shape, s):
        return jax.random.normal(next(ks), shape, jnp.float32) * s

    def unif(shape, lo, hi):
        return jax.random.uniform(next(ks), shape, jnp.float32, lo, hi)

    D = D_MODEL
    na, nb, nc, nd = (_n_uses(m) for m in range(N_MIXERS))
    gamma = 1.0 - 2.0 ** (-5.0 - jnp.arange(RET_HEADS, dtype=jnp.float32))
    ret_logit0 = jnp.log(gamma) - jnp.log(1.0 - gamma)
    dt = jnp.exp(unif((nd, 2, DN_HEADS), float(np.log(1e-3)), float(np.log(1e-1))))
    return {
        'x': nrm((BATCH, SEQ, D), 1.0),
        'c': nrm((BATCH, D), 1.0),
        'ctx': nrm((BATCH, CTX_LEN, D), 1.0),
        'c_ctx': nrm((D,), 1.0),
        'ada_w': nrm((DEPTH, D, N_MOD * D), 0.5 * D ** -0.5),
        'ada_b': nrm((DEPTH, N_MOD * D), 0.02),
        'mix_pre_g': 1.0 + nrm((DEPTH, D), 0.05),
        'mix_post_g': 1.0 + nrm((DEPTH, D), 0.05),
        'mlp_pre_g': 1.0 + nrm((DEPTH, D), 0.05),
        'mlp_post_g': 1.0 + nrm((DEPTH, D), 0.05),
        'mlp_w1': nrm((DEPTH, D, D_FF), D ** -0.5),
        'mlp_w2': nrm((DEPTH, D_FF, D), D_FF ** -0.5),
        'ret_w_in': nrm((na, D, RET_IN), D ** -0.5),
        'ret_decay_logit': ret_logit0[None, None, :] + nrm((na, 2, RET_HEADS), 0.1),
        'ret_w_out': nrm((na, RET_HEADS * RET_DV, D), (RET_HEADS * RET_DV) ** -0.5),
        'att_w_in': nrm((nb, D, ATT_IN), D ** -0.5),
        'att_sink': nrm((nb, ATT_HEADS), 0.5),
        'att_w_out': nrm((nb, ATT_HEADS * ATT_DH, D), (ATT_HEADS * ATT_DH) ** -0.5),
        'pool_w': nrm((nc, len(POOL_WINDOWS), POOL_GROUP, POOL_GROUP), POOL_GROUP ** -0.5),
        'pool_b': nrm((nc, D), 0.02),
        'pool_scale': 1.0 + nrm((nc, D), 0.1),
        'dn_w_in': nrm((nd, D, DN_IN), D ** -0.5),
        'dn_conv_w': nrm((nd, DN_CONV_W, DN_QKV), DN_CONV_W ** -0.5),
        'dn_a_log': jnp.log(unif((nd, 2, DN_HEADS), 1.0, 16.0)),
        'dn_dt_bias': dt + jnp.log(-jnp.expm1(-dt)),
        'dn_norm_g': 1.0 + nrm((nd, DN_DV), 0.05),
        'dn_w_out': nrm((nd, DN_HEADS * DN_DV, D), (DN_HEADS * DN_DV) ** -0.5),
    }


def reference(x, c, ctx, c_ctx, ada_w, ada_b, mix_pre_g, mix_post_g, mlp_pre_g, mlp_post_g, mlp_w1, mlp_w2,
              ret_w_in, ret_decay_logit, ret_w_out, att_w_in, att_sink, att_w_out,
              pool_w, pool_b, pool_scale, dn_w_in, dn_conv_w, dn_a_log, dn_dt_bias, dn_norm_g, dn_w_out):
    row, col = grid_positions(x.shape[1])
    cond = jax.nn.silu(c)[:, None, :]
    cond_ctx = jax.nn.silu(c_ctx)[None, None, :]
    for i in range(DEPTH):
        kind, inst = i % N_MIXERS, i // N_MIXERS
        need_ctx = i < DEPTH - 1
        m_lat = jnp.split(cond @ ada_w[i] + ada_b[i], N_MOD, axis=-1)
        m_ctx = jnp.split(cond_ctx @ ada_w[i] + ada_b[i], N_MOD, axis=-1)
        u_lat = rmsnorm(x, mix_pre_g[i]) * (1 + m_lat[1]) + m_lat[0]
        u_ctx = rmsnorm(ctx, mix_pre_g[i]) * (1 + m_ctx[1]) + m_ctx[0]
        if kind == 0:
            y_lat, y_ctx = retention_mixer(u_lat, u_ctx, ret_w_in[inst], ret_decay_logit[inst], ret_w_out[inst],
                                           row, col, need_ctx)
        elif kind == 1:
            y_lat, y_ctx = window_attention_mixer(u_lat, u_ctx, att_w_in[inst], att_sink[inst], att_w_out[inst],
                                                  row, col, need_ctx)
        elif kind == 2:
            y_lat, y_ctx = pool_mixer(u_lat, u_ctx, pool_w[inst], pool_b[inst], pool_scale[inst], need_ctx)
        else:
            y_lat, y_ctx = deltanet_mixer(u_lat, u_ctx, dn_w_in[inst], dn_conv_w[inst], dn_a_log[inst],
                                          dn_dt_bias[inst], dn_norm_g[inst], dn_w_out[inst], need_ctx)
        x = x + m_lat[2] * rmsnorm(y_lat, mix_post_g[i])
        x = ffn_sublayer(x, m_lat[3], m_lat[4], m_lat[5], mlp_pre_g[i], mlp_post_g[i], mlp_w1[i], mlp_w2[i])
        if need_ctx:
            ctx = ctx + m_ctx[2] * rmsnorm(y_ctx, mix_post_g[i])
            ctx = ffn_sublayer(ctx, m_ctx[3], m_ctx[4], m_ctx[5], mlp_pre_g[i], mlp_post_g[i], mlp_w1[i], mlp_w2[i])
    return x
```

```python
import numpy as np
from contextlib import ExitStack
import concourse.bass as bass
import concourse.mybir as mybir
from concourse.bass_utils import run_bass_kernel_spmd

F32 = mybir.dt.float32
BF16 = mybir.dt.bfloat16
ALU = mybir.AluOpType
AF = mybir.ActivationFunctionType
AX = mybir.AxisListType

D = 1024
T = 2048
TC = 256
NB = 2
DFF = 4096
EPS = 1e-6
NT = 18
TT = NB * NT
SEQ = TC + T


class Reg:
    __slots__ = ("w", "r")

    def __init__(self):
        self.w = {}
        self.r = {}


class Rot:
    def __init__(self, items):
        self.items = items
        self.i = 0

    def next(self):
        it = self.items[self.i % len(self.items)]
        self.i += 1
        return it


class KB:
    def __init__(self, nc):
        self.nc = nc
        self.eh = {"pe": nc.tensor, "dve": nc.vector, "act": nc.scalar, "pool": nc.gpsimd, "sp": nc.sync}
        self.esem = {k: nc.alloc_semaphore("es_" + k) for k in self.eh}
        self.ecnt = {k: 0 for k in self.eh}
        self.waited = {k: {} for k in self.eh}
        self.dpool = {q: [[nc.alloc_semaphore("ds_%s%d" % (q, i)), 0] for i in range(n)]
                      for q, n in (("sp", 32), ("pool", 16), ("act", 8))}
        self.dnext = {q: 0 for q in self.dpool}
        self.uid = 0

    def name(self, base):
        self.uid += 1
        return "%s_%d" % (base, self.uid)

    def sb(self, es, base, shape, dt):
        return es.enter_context(self.nc.sbuf_tensor(self.name(base), list(shape), dt))

    def rot(self, es, base, shape, dt, n):
        return Rot([(self.sb(es, base, shape, dt), Reg()) for _ in range(n)])

    def _wait(self, e, sem, val):
        w = self.waited[e]
        if w.get(sem.num, 0) < val:
            self.eh[e].wait_ge(sem, val)
            w[sem.num] = val

    def _need(self, e, reads, writes, is_dma):
        own = None if is_dma else self.esem[e].num
        need = {}

        def add(d, skip_same):
            for num, (sem, val) in d.items():
                if num == own and (skip_same or e == "pe"):
                    continue
                if need.get(num, (None, 0))[1] < val:
                    need[num] = (sem, val)
        for r in reads:
            add(r.w, False)
        for r in writes:
            add(r.w, True)
            add(r.r, True)
        return need

    def _mark(self, tok, reads, writes):
        num = tok[0].num
        for r in reads:
            r.r[num] = tok
        for r in writes:
            r.w = {num: tok}
            r.r = {}

    def op(self, e, fn, reads, writes):
        need = self._need(e, reads, writes, False)
        for sem, val in need.values():
            self._wait(e, sem, val)
        inst = fn(self.eh[e])
        self.ecnt[e] += 1
        inst.then_inc(self.esem[e], 1)
        self._mark((self.esem[e], self.ecnt[e]), reads, writes)

    def dma(self, q, out, in_, reads, writes, **kw):
        pool = self.dpool[q]
        i = self.dnext[q]
        self.dnext[q] = (i + 1) % len(pool)
        sem, val = pool[i]
        need = self._need(q, reads, writes, True)
        if val > 0:
            need[sem.num] = (sem, val)
        for s, v in need.values():
            self._wait(q, s, v)
        inst = self.eh[q].dma_start(out=out, in_=in_, **kw)
        inst.then_inc(sem, 16)
        pool[i][1] = val + 16
        self._mark((sem, val + 16), reads, writes)

    def barrier(self):
        toks = [(self.esem[e], self.ecnt[e]) for e in self.eh if self.ecnt[e] > 0]
        toks += [(s, v) for p in self.dpool.values() for (s, v) in p if v > 0]
        for e in self.eh:
            for s, v in toks:
                if s.num != self.esem[e].num:
                    self._wait(e, s, v)

    def mm(self, out, pairs, reads, writes):
        n = len(pairs)

        def fn(h):
            inst = None
            for i, (l, r) in enumerate(pairs):
                inst = h.matmul(out, l, r, start=(i == 0), stop=(i == n - 1))
            return inst
        self.op("pe", fn, reads, writes)

    def mm1(self, out, l, r, start, stop, reads, writes):
        self.op("pe", lambda h: h.matmul(out, l, r, start=start, stop=stop), reads, writes)

    def tr(self, outs_ins, ident, reads, writes):
        def fn(h):
            inst = None
            for o, i in outs_ins:
                inst = h.transpose(o, i, ident)
            return inst
        self.op("pe", fn, reads, writes)

    def act(self, out, in_, func, reads, writes, **kw):
        self.op("act", lambda h: h.activation(out=out, in_=in_, func=func, **kw), reads, writes)

    def ts(self, e, out, in0, s1, s2, op0, op1, reads, writes):
        if s2 is None:
            self.op(e, lambda h: h.tensor_scalar(out=out, in0=in0, scalar1=s1, scalar2=None, op0=op0), reads, writes)
        else:
            self.op(e, lambda h: h.tensor_scalar(out=out, in0=in0, scalar1=s1, scalar2=s2, op0=op0, op1=op1),
                    reads, writes)

    def tt(self, e, out, in0, in1, op, reads, writes):
        self.op(e, lambda h: h.tensor_tensor(out=out, in0=in0, in1=in1, op=op), reads, writes)

    def stt(self, e, out, in0, scalar, in1, op0, op1, reads, writes):
        self.op(e, lambda h: h.scalar_tensor_tensor(out=out, in0=in0, scalar=scalar, in1=in1, op0=op0, op1=op1),
                reads, writes)

    def cp(self, e, out, in_, reads, writes):
        if e == "act":
            self.op(e, lambda h: h.copy(out=out, in_=in_), reads, writes)
        else:
            self.op(e, lambda h: h.tensor_copy(out=out, in_=in_), reads, writes)

    def memset(self, e, ap, val, writes):
        self.op(e, lambda h: h.memset(ap, val), [], writes)

    def recip(self, out, in_, reads, writes):
        self.op("dve", lambda h: h.reciprocal(out=out, in_=in_), reads, writes)


class Stream:
    def __init__(self, xap, cap):
        self.x = xap
        self.c = cap
        self.regs = [Reg() for _ in range(TT)]

    def tile(self, tt):
        b, r = divmod(tt, NT)
        if r < 2:
            return self.c[b, r * 128:(r + 1) * 128, :]
        return self.x[b, (r - 2) * 128:(r - 1) * 128, :]


def tile_slot(tt):
    b, r = divmod(tt, NT)
    return 2 if r < 2 else b


class G:
    pass


def build_program(layers, dbg=False):
    nc = bass.Bass("TRN2", target_bir_lowering=False)
    kb = KB(nc)
    g = G()
    g.nc, g.kb = nc, kb
    L = 4

    def din(name, shape, dt=F32):
        return nc.dram_tensor(name, list(shape), dt, kind="ExternalInput").ap()

    def dscr(name, shape, dt=F32):
        return nc.dram_tensor(name, list(shape), dt, kind="Internal").ap()

    I = {}
    I["x"] = din("x", [NB, T, D])
    I["c"] = din("c", [NB, D])
    I["ctx"] = din("ctx", [NB, TC, D])
    I["c_ctx"] = din("c_ctx", [1, D])
    I["ada_w"] = din("ada_w", [L, D, 6 * D])
    I["ada_b"] = din("ada_b", [L, 6 * D])
    for nm in ("mix_pre_g", "mix_post_g", "mlp_pre_g", "mlp_post_g"):
        I[nm] = din(nm, [L, D])
    I["mlp_w1"] = din("mlp_w1", [L, D, DFF])
    I["mlp_w2"] = din("mlp_w2", [L, DFF, D])
    I["ret_w_in"] = din("ret_w_in", [D, 8192])
    I["ret_decay_logit"] = din("ret_decay_logit", [1, 16])
    I["ret_w_out"] = din("ret_w_out", [2048, D])
    I["att_w_in"] = din("att_w_in", [D, 1536])
    I["att_sink"] = din("att_sink", [1, 16])
    I["att_w_out"] = din("att_w_out", [D, D])
    I["pool_w"] = din("pool_w", [4, 256, 256])
    I["pool_b"] = din("pool_b", [1, D])
    I["pool_scale"] = din("pool_scale", [1, D])
    I["dn_w_in"] = din("dn_w_in", [D, 5152])
    I["dn_conv_w"] = din("dn_conv_w", [5, 3072])
    I["dn_a_log"] = din("dn_a_log", [1, 16])
    I["dn_dt_bias"] = din("dn_dt_bias", [1, 16])
    I["dn_norm_g"] = din("dn_norm_g", [1, 128])
    I["dn_w_out"] = din("dn_w_out", [D, D])
    I["k_ident"] = din("k_ident", [128, 128])
    I["k_pool"] = din("k_pool", [20, 128, 128])
    I["k_rope_att"] = din("k_rope_att", [2, 128, SEQ])
    I["k_rope_ret"] = din("k_rope_ret", [2, 128, SEQ])
    I["k_att_mask"] = din("k_att_mask", [3, 128, 384])
    I["k_ret_tab"] = din("k_ret_tab", [5, 128, 128])
    I["k_dn"] = din("k_dn", [4, 64, 64])
    g.I = I
    yout = nc.dram_tensor("y", [NB, T, D], F32, kind="ExternalOutput").ap()

    g.modD = dscr("modD", [L, 3, 6 * D])
    s_in = Stream(I["x"], I["ctx"])
    s1 = Stream(dscr("s1x", [NB, T, D]), dscr("s1c", [NB, TC, D]))
    s2 = Stream(dscr("s2x", [NB, T, D]), dscr("s2c", [NB, TC, D]))
    s_out = Stream(yout, s2.c)
    g.modD_r = Reg()

    g.ps = [nc.alloc_psum_tensor("psb%d" % i, [128, 512], F32) for i in range(8)]
    g.psr = [Reg() for _ in range(8)]

    with ExitStack() as ges:
        g.ident_f = kb.sb(ges, "identf", [128, 128], F32)
        g.ident_b = kb.sb(ges, "identb", [128, 128], BF16)
        g.eps = kb.sb(ges, "eps", [128, 1], F32)
        g.const_r = Reg()
        kb.dma("sp", g.ident_f[:], I["k_ident"][:, :], [], [g.const_r])
        kb.dma("pool", g.ident_b[:], I["k_ident"][:, :], [], [g.const_r])
        kb.memset("dve", g.eps[:], EPS, [g.const_r])
        kb.barrier()

        prologue_ada(g, layers)
        kb.barrier()

        cur = s_in
        for li, l in enumerate(layers):
            last = (li == len(layers) - 1)
            need_ctx = (l < 3) or dbg
            kind = l % 4
            if kind == 2:
                mixer_pool(g, l, cur, s1, need_ctx)
            elif kind == 1:
                mixer_att(g, l, cur, s1, need_ctx)
            elif kind == 0:
                mixer_ret(g, l, cur, s1, need_ctx)
            elif kind == 3:
                mixer_dn(g, l, cur, s1, need_ctx)
            else:
                raise NotImplementedError
            kb.barrier()
            dst = s_out if last else s2
            ffn(g, l, s1, dst, need_ctx)
            kb.barrier()
            cur = s2
        if dbg:
            cout = nc.dram_tensor("ctx_out", [NB, TC, D], F32, kind="ExternalOutput").ap()
            for b in range(NB):
                kb.dma("sp", cout[b], s2.c[b], [s2.regs[b * NT], s2.regs[b * NT + 1], s_out.regs[b * NT],
                                                s_out.regs[b * NT + 1]], [Reg()])
    kb.barrier()
    return nc


def prologue_ada(g, layers):
    kb, nc, I = g.kb, g.nc, g.I
    with ExitStack() as es:
        condT = kb.sb(es, "condT", [128, 8, 4], F32)
        cr = Reg()
        for s in range(3):
            src = I["c"][s, :] if s < 2 else I["c_ctx"][0, :]
            kb.dma("sp", condT[:, :, s], src.rearrange("(c p) -> p c", p=128), [], [cr],
                   allow_slow_non_contiguous=True)
        kb.memset("dve", condT[:, :, 3], 0.0, [cr])
        kb.act(condT[:, :, 0:3], condT[:, :, 0:3], AF.Silu, [cr], [cr])
        wrot = kb.rot(es, "adaw", [128, 8, 512], F32, 3)
        modrow = kb.sb(es, "modrow", [3, 6 * D], F32)
        mr = Reg()
        bias = kb.sb(es, "adab", [3, 6 * D], F32)
        gains = kb.sb(es, "gains", [3, 4, D], F32)
        br = Reg()
        for l in layers:
            kb.dma("sp", bias[:], I["ada_b"][l, :].partition_broadcast(3), [], [br])
            for gi, nm in enumerate(("mix_pre_g", "mix_post_g", "mlp_pre_g", "mlp_post_g")):
                kb.dma("sp", gains[:, gi, :], I[nm][l, :].partition_broadcast(3), [], [br])
            for j in range(12):
                wt, wr = wrot.next()
                kb.dma("sp", wt[:], I["ada_w"][l, :, j * 512:(j + 1) * 512].rearrange("(c p) n -> p c n", p=128),
                       [], [wr])
                pb = j % 2
                kb.mm(g.ps[pb][0:3, :], [(condT[:, kc, 0:3], wt[:, kc, :]) for kc in range(8)],
                      [cr, wr], [g.psr[pb]])
                kb.tt("dve", modrow[:, j * 512:(j + 1) * 512], g.ps[pb][0:3, :], bias[:, j * 512:(j + 1) * 512],
                      ALU.add, [g.psr[pb], br], [mr])
            for seg, gi, plus1 in ((1, 0, True), (2, 1, False), (4, 2, True), (5, 3, False)):
                sl = modrow[:, seg * D:(seg + 1) * D]
                if plus1:
                    kb.stt("dve", sl, sl, 1.0, gains[:, gi, :], ALU.add, ALU.mult, [mr, br], [mr])
                else:
                    kb.tt("dve", sl, sl, gains[:, gi, :], ALU.mult, [mr, br], [mr])
            kb.dma("sp", g.modD[l], modrow[:], [mr], [g.modD_r])


def load_cols(g, es, l, segA, segB):
    kb = g.kb
    r = Reg()
    outs = []
    for seg in (segA, segB):
        t = kb.sb(es, "mcol", [128, 3, 8], F32)
        for s in range(3):
            kb.dma("sp", t[:, s, :], g.modD[l, s, seg * D:(seg + 1) * D].rearrange("(c p) -> p c", p=128),
                   [g.modD_r], [r], allow_slow_non_contiguous=True)
        outs.append(t)
    return outs[0], outs[1], r


def load_bc(g, es, l, seg):
    kb = g.kb
    out = []
    r = Reg()
    for s in range(3):
        t = kb.sb(es, "mbc", [128, D], F32)
        kb.dma("sp", t[:], g.modD[l, s, seg * D:(seg + 1) * D].partition_broadcast(128), [g.modD_r], [r])
        out.append(t)
    return out, r


class NormWS:
    def __init__(self, g, es, nbuf=3):
        kb = g.kb
        self.st = kb.rot(es, "nst", [128, 4], F32, 4)
        self.junk = kb.sb(es, "njunk", [128, D], BF16)
        self.junk_r = Reg()
        self.xn = kb.rot(es, "nxn", [128, D], BF16, 2)


def rms_stats(g, ws, y_aps, y_regs, isn=1.0 / 32.0):
    kb = g.kb
    st, sr = ws.st.next()
    kb.memset("pool", st[:], 0.0, [sr])
    off = 0
    for i, ya in enumerate(y_aps):
        n = ya.shape[-1]
        kb.act(ws.junk[:, off:off + n], ya, AF.Square, y_regs + [sr], [ws.junk_r, sr], scale=isn,
               accum_out=st[:, i:i + 1])
        off += n
    if len(y_aps) == 2:
        kb.tt("dve", st[:, 0:1], st[:, 0:1], st[:, 1:2], ALU.add, [sr], [sr])
    kb.act(st[:, 1:2], st[:, 0:1], AF.Sqrt, [sr, g.const_r], [sr], bias=g.eps[:, 0:1], scale=1.0)
    kb.recip(st[:, 2:3], st[:, 1:2], [sr], [sr])
    return st, sr


def norm_T(g, ws, xt, xr, Acol, Bcol, colr, slot, dst, dst_r, pbank):
    kb = g.kb
    st, sr = rms_stats(g, ws, [xt], [xr])
    xn, xnr = ws.xn.next()
    kb.act(xn[:], xt, AF.Copy, [xr, sr], [xnr], scale=st[:, 2:3])
    psT = g.ps[pbank][:].bitcast(BF16).rearrange("p (c t) -> p c t", c=8)
    kb.tr([(psT[:, c, :], xn[:, c * 128:(c + 1) * 128]) for c in range(8)], g.ident_b[:],
          [xnr, g.const_r], [g.psr[pbank]])
    for c in range(8):
        kb.ts("dve", dst[:, c, :], psT[:, c, :], Acol[:, slot, c:c + 1], Bcol[:, slot, c:c + 1], ALU.mult, ALU.add,
              [g.psr[pbank], colr], [dst_r])


def post_res(g, ws, y_aps, y_regs, xt, xr, Gbc, gr, tmp, tmpr):
    kb = g.kb
    st, sr = rms_stats(g, ws, y_aps, y_regs)
    off = 0
    for ya in y_aps:
        n = ya.shape[-1]
        kb.stt("dve", tmp[:, off:off + n], ya, st[:, 2:3], Gbc[:, off:off + n], ALU.mult, ALU.mult,
               y_regs + [sr, gr], [tmpr])
        off += n
    kb.tt("pool", xt, xt, tmp[:, :], ALU.add, [xr, tmpr], [xr])


def ffn(g, l, src, dst, need_ctx):
    kb, nc, I = g.kb, g.nc, g.I
    with ExitStack() as es:
        w1 = kb.sb(es, "w1", [128, 8, DFF], BF16)
        w2 = kb.sb(es, "w2", [128, 32, D], BF16)
        w1r = [Reg() for _ in range(8)]
        w2r = [Reg() for _ in range(8)]
        for kc in range(8):
            for hf in range(2):
                kb.dma("pool", w1[:, kc, hf * 2048:(hf + 1) * 2048],
                       I["mlp_w1"][l, kc * 128:(kc + 1) * 128, hf * 2048:(hf + 1) * 2048], [], [w1r[kc]])
        for q in range(8):
            kb.dma("pool", w2[:, q * 4:(q + 1) * 4, :],
                   I["mlp_w2"][l, q * 512:(q + 1) * 512, :].rearrange("(c p) n -> p c n", p=128), [], [w2r[q]])
        Acol, Bcol, acr = load_cols(g, es, l, 4, 3)
        Gbc, gbr = load_bc(g, es, l, 5)
        ws = NormWS(g, es)
        xrot = kb.rot(es, "fx", [128, D], F32, 4)
        uT = kb.rot(es, "fuT", [128, 8, 256], BF16, 2)
        hT = kb.sb(es, "fhT", [128, 32, 256], BF16)
        hTr = [Reg() for _ in range(32)]
        rl = kb.rot(es, "frl", [128, 256], F32, 4)
        tmp = kb.rot(es, "ftmp", [128, D], F32, 2)
        tiles = [tt for tt in range(TT) if need_ctx or (tt % NT) >= 2]
        groups = [tiles[i:i + 2] for i in range(0, len(tiles), 2)]
        hslot = 0
        for grp in groups:
            u, ur = uT.next()
            xs = []
            for j, tt in enumerate(grp):
                xt, xr = xrot.next()
                kb.dma("sp", xt[:], src.tile(tt), [src.regs[tt]], [xr])
                norm_T(g, ws, xt[:], xr, Acol, Bcol, acr, tile_slot(tt), u[:, :, j * 128:(j + 1) * 128], ur, 4)
                xs.append((xt, xr))
            for fc in range(32):
                hb = 5 + (hslot // 2) % 2
                hh = hslot % 2
                hslot += 1
                hp = g.ps[hb][:, hh * 256:(hh + 1) * 256]
                kb.mm(hp, [(w1[:, kc, fc * 128:(fc + 1) * 128], u[:, kc, :]) for kc in range(8)],
                      [ur] + w1r, [g.psr[hb]])
                r_, rr = rl.next()
                kb.act(r_[:], hp, AF.Relu, [g.psr[hb]], [rr])
                kb.tt("dve", hT[:, fc, :], r_[:], r_[:], ALU.mult, [rr], [hTr[fc]])
            for j, tt in enumerate(grp):
                xt, xr = xs[j]
                for half in range(2):
                    pb = j * 2 + half
                    kb.mm(g.ps[pb][:, :], [(hT[:, fc, j * 128:(j + 1) * 128], w2[:, fc, half * 512:(half + 1) * 512])
                                           for fc in range(32)], hTr + w2r, [g.psr[pb]])
                t_, tr_ = tmp.next()
                post_res(g, ws, [g.ps[j * 2][:, :], g.ps[j * 2 + 1][:, :]], [g.psr[j * 2], g.psr[j * 2 + 1]],
                         xt[:], xr, Gbc[tile_slot(tt)], gbr, t_, tr_)
                kb.dma("sp", dst.tile(tt), xt[:], [xr], [dst.regs[tt]])


def mixer_pool(g, l, src, dst, need_ctx):
    kb, nc, I = g.kb, g.nc, g.I
    with ExitStack() as es:
        Acol, Bcol, acr = load_cols(g, es, l, 1, 0)
        Gbc, gbr = load_bc(g, es, l, 2)
        ws = NormWS(g, es)
        wg = kb.sb(es, "pw", [128, 4, 2, 256], BF16)
        wgr = Reg()
        for gi in range(4):
            kb.dma("pool", wg[:, gi, :, :], I["pool_w"][gi].rearrange("(c p) n -> p c n", p=128), [], [wgr])
        pm = kb.sb(es, "pm", [128, 20, 128], BF16)
        kb.dma("pool", pm[:], I["k_pool"].rearrange("m p n -> p m n"), [], [wgr])
        pbb = kb.sb(es, "pbb", [128, D], F32)
        psb = kb.sb(es, "psb", [128, D], F32)
        kb.dma("sp", pbb[:], I["pool_b"][0, :].partition_broadcast(128), [], [wgr])
        kb.dma("sp", psb[:], I["pool_scale"][0, :].partition_broadcast(128), [], [wgr])
        xrot = kb.rot(es, "px", [128, D], F32, 6)
        uT = kb.rot(es, "puT", [128, 8, 128], BF16, 2)
        vrot = kb.rot(es, "pv", [128, D], BF16, 5)
        yb = kb.rot(es, "pyb", [128, D], F32, 2)
        tmp = kb.rot(es, "ptmp", [128, D], F32, 2)
        for b in range(NB):
            for seg_lo, seg_n in ((0, 2), (2, 16)):
                if seg_lo == 0 and not need_ctx:
                    continue
                xs, vs = {}, {}
                for t in range(seg_n + 1):
                    if t < seg_n:
                        tt = b * NT + seg_lo + t
                        xt, xr = xrot.next()
                        kb.dma("sp", xt[:], src.tile(tt), [src.regs[tt]], [xr])
                        u, ur = uT.next()
                        norm_T(g, ws, xt[:], xr, Acol, Bcol, acr, tile_slot(tt), u, ur, 4)
                        for gi in range(4):
                            pb = 5 + gi // 2
                            kb.mm(g.ps[pb][:, (gi % 2) * 256:(gi % 2 + 1) * 256],
                                  [(u[:, gi * 2 + kc, :], wg[:, gi, kc, :]) for kc in range(2)],
                                  [ur, wgr], [g.psr[pb]])
                        v, vr = vrot.next()
                        kb.cp("act", v[:, 0:512], g.ps[5][:, :], [g.psr[5]], [vr])
                        kb.cp("act", v[:, 512:1024], g.ps[6][:, :], [g.psr[6]], [vr])
                        xs[t], vs[t] = (xt, xr), (v, vr)
                    if t >= 1:
                        tq = t - 1
                        tt = b * NT + seg_lo + tq
                        for gi in range(4):
                            pb = gi // 2
                            pairs = []
                            regs = [wgr]
                            cidx = 0 if (0 < tq < seg_n - 1) else (1 if tq == 0 else 2)
                            pairs.append((pm[:, gi * 5 + cidx, :], vs[tq][0][:, gi * 256:(gi + 1) * 256]))
                            regs.append(vs[tq][1])
                            if tq > 0:
                                pairs.append((pm[:, gi * 5 + 3, :], vs[tq - 1][0][:, gi * 256:(gi + 1) * 256]))
                                regs.append(vs[tq - 1][1])
                            if tq < seg_n - 1:
                                pairs.append((pm[:, gi * 5 + 4, :], vs[tq + 1][0][:, gi * 256:(gi + 1) * 256]))
                                regs.append(vs[tq + 1][1])
                            kb.mm(g.ps[pb][:, (gi % 2) * 256:(gi % 2 + 1) * 256], pairs, regs, [g.psr[pb]])
                        y_, yr = yb.next()
                        for h in range(2):
                            kb.tt("dve", y_[:, h * 512:(h + 1) * 512], g.ps[h][:, :], pbb[:, h * 512:(h + 1) * 512],
                                  ALU.add, [g.psr[h], wgr], [yr])
                        kb.tt("pool", y_[:], y_[:], psb[:], ALU.mult, [yr, wgr], [yr])
                        xt, xr = xs[tq]
                        t_, tr_ = tmp.next()
                        post_res(g, ws, [y_[:, :]], [yr], xt[:], xr, Gbc[tile_slot(tt)], gbr, t_, tr_)
                        kb.dma("sp", dst.tile(tt), xt[:], [xr], [dst.regs[tt]])


def phase_uT(g, es, l, src):
    kb = g.kb
    uT = kb.sb(es, "uTall", [128, 8, TT * 128], BF16)
    regs = [Reg() for _ in range(TT)]
    with ExitStack() as es2:
        Acol, Bcol, acr = load_cols(g, es2, l, 1, 0)
        ws = NormWS(g, es2)
        xrot = kb.rot(es2, "ux", [128, D], F32, 3)
        for tt in range(TT):
            xt, xr = xrot.next()
            kb.dma("sp", xt[:], src.tile(tt), [src.regs[tt]], [xr])
            norm_T(g, ws, xt[:], xr, Acol, Bcol, acr, tile_slot(tt), uT[:, :, tt * 128:(tt + 1) * 128], regs[tt],
                   4 + tt % 2)
        kb.barrier()
    return uT, regs


def load_w_bf16(g, rot, wd, c0, ncols):
    kb = g.kb
    wt, wr = rot.next()
    kb.dma("pool", wt[:, :, 0:ncols], wd[:, c0:c0 + ncols].rearrange("(c p) n -> p c n", p=128), [], [wr])
    return wt, wr


def phase_outproj(g, l, a_d, a_regs, Kd, wout_d, src, dst, need_ctx):
    kb, nc = g.kb, g.nc
    nk = Kd // 128
    with ExitStack() as es:
        Gbc, gbr = load_bc(g, es, l, 2)
        ws = NormWS(g, es)
        wo = kb.sb(es, "wo", [128, nk, D], BF16)
        wor = Reg()
        for q in range(nk // 4):
            kb.dma("pool", wo[:, q * 4:(q + 1) * 4, :],
                   wout_d[q * 512:(q + 1) * 512, :].rearrange("(c p) n -> p c n", p=128), [], [wor])
        arot = kb.rot(es, "oa", [128, Kd], BF16, 3)
        aTrot = kb.rot(es, "oaT", [128, nk, 128], BF16, 2)
        xrot = kb.rot(es, "ox", [128, D], F32, 3)
        tmp = kb.rot(es, "otmp", [128, D], F32, 2)
        it = 0
        for tt in range(TT):
            if not need_ctx and (tt % NT) < 2:
                continue
            at, ar = arot.next()
            kb.dma("sp", at[:], a_d[tt * 128:(tt + 1) * 128, :], [a_regs[tt]], [ar])
            xt, xr = xrot.next()
            kb.dma("sp", xt[:], src.tile(tt), [src.regs[tt]], [xr])
            aT, aTr = aTrot.next()
            for q in range(nk // 8):
                pb = 4 + (it % 2)
                it += 1
                psT = g.ps[pb][:].bitcast(BF16).rearrange("p (c t) -> p c t", c=8)
                kb.tr([(psT[:, c, :], at[:, (q * 8 + c) * 128:(q * 8 + c + 1) * 128]) for c in range(8)],
                      g.ident_b[:], [ar, g.const_r], [g.psr[pb]])
                kb.cp("act", aT[:, q * 8:(q + 1) * 8, :], psT, [g.psr[pb]], [aTr])
            yb = (tt % 2) * 2
            for half in range(2):
                kb.mm(g.ps[yb + half][:, :], [(aT[:, kc, :], wo[:, kc, half * 512:(half + 1) * 512])
                                              for kc in range(nk)], [aTr, wor], [g.psr[yb + half]])
            t_, tr_ = tmp.next()
            post_res(g, ws, [g.ps[yb][:, :], g.ps[yb + 1][:, :]], [g.psr[yb], g.psr[yb + 1]], xt[:], xr,
                     Gbc[tile_slot(tt)], gbr, t_, tr_)
            kb.dma("sp", dst.tile(tt), xt[:], [xr], [dst.regs[tt]])


def make_perm(g, wt, wr, wp, wpr, ncols, blk):
    kb = g.kb
    v_in = wt[:, :, 0:ncols].rearrange("p c (q two i) -> p c q two i", two=2, i=blk)
    v_out = wp[:, :, 0:ncols].rearrange("p c (q two i) -> p c q two i", two=2, i=blk)
    for kc in range(8):
        kb.cp("pool", v_out[:, kc, :, 0, :], v_in[:, kc, :, 1, :], [wr], [wpr])
        kb.cp("pool", v_out[:, kc, :, 1, :], v_in[:, kc, :, 0, :], [wr], [wpr])


SEQ_BLOCKS = [(b, o, min(512, SEQ - o)) for b in range(NB) for o in range(0, SEQ, 512)]


def proj_fm_rope(g, uT, uTr, wd, c0, rope, rope_r, scale, blk, out_d, out_r, wrot, wprot, stg, pbase):
    kb = g.kb
    wt, wr = load_w_bf16(g, wrot, wd, c0, 128)
    wp, wpr = wprot.next()
    make_perm(g, wt, wr, wp, wpr, 128, blk)
    for bi, (b, o, n) in enumerate(SEQ_BLOCKS):
        t0 = b * SEQ + o
        tiles = list(range(t0 // 128, (t0 + n) // 128))
        rr = [uTr[t] for t in tiles]
        pa, pb = pbase + (bi % 2) * 2, pbase + (bi % 2) * 2 + 1
        kb.mm(g.ps[pa][:, 0:n], [(wt[:, kc, 0:128], uT[:, kc, t0:t0 + n]) for kc in range(8)], rr + [wr], [g.psr[pa]])
        kb.mm(g.ps[pb][:, 0:n], [(wp[:, kc, 0:128], uT[:, kc, t0:t0 + n]) for kc in range(8)], rr + [wpr], [g.psr[pb]])
        (t1, t1r), (t2, t2r), (t3, t3r) = stg[0].next(), stg[1].next(), stg[2].next()
        kb.stt("dve", t1[:, 0:n], g.ps[pa][:, 0:n], scale, rope[:, 0, o:o + n], ALU.mult, ALU.mult,
               [g.psr[pa], rope_r], [t1r])
        kb.stt("dve", t2[:, 0:n], g.ps[pb][:, 0:n], scale, rope[:, 1, o:o + n], ALU.mult, ALU.mult,
               [g.psr[pb], rope_r], [t2r])
        kb.tt("pool", t3[:, 0:n], t1[:, 0:n], t2[:, 0:n], ALU.add, [t1r, t2r], [t3r])
        kb.dma("sp", out_d[:, t0:t0 + n], t3[:, 0:n], [t3r], [out_r])


def proj_tm(g, uT, uTr, wd, c0, ncols, out_d, out_regs, col0, wrot, stg, pbase, func=None, tiles=None):
    kb = g.kb
    wt, wr = load_w_bf16(g, wrot, wd, c0, ncols)
    for i, tt in enumerate(tiles if tiles is not None else range(TT)):
        pb = pbase + i % 2
        kb.mm(g.ps[pb][:, 0:ncols], [(uT[:, kc, tt * 128:(tt + 1) * 128], wt[:, kc, 0:ncols]) for kc in range(8)],
              [uTr[tt], wr], [g.psr[pb]])
        st_, sr_ = stg.next()
        if func is None:
            kb.cp("act", st_[:, 0:ncols], g.ps[pb][:, 0:ncols], [g.psr[pb]], [sr_])
        else:
            kb.act(st_[:, 0:ncols], g.ps[pb][:, 0:ncols], func, [g.psr[pb]], [sr_])
        kb.dma("sp", out_d[tt * 128:(tt + 1) * 128, col0:col0 + ncols], st_[:, 0:ncols], [sr_], [out_regs[tt]])


def mixer_att(g, l, src, dst, need_ctx):
    kb, nc, I = g.kb, g.nc, g.I
    NTOK = TT * 128
    attq = nc.dram_tensor(kb.name("attq"), [8, 128, NTOK], BF16, kind="Internal").ap()
    attk = nc.dram_tensor(kb.name("attk"), [2, 128, NTOK], BF16, kind="Internal").ap()
    attv = nc.dram_tensor(kb.name("attv"), [NTOK, 256], BF16, kind="Internal").ap()
    atta = nc.dram_tensor(kb.name("atta"), [NTOK, D], BF16, kind="Internal").ap()
    qr_, kr_ = [Reg() for _ in range(8)], [Reg() for _ in range(2)]
    vr_ = [Reg() for _ in range(TT)]
    ar_ = [Reg() for _ in range(TT)]
    with ExitStack() as es:
        uT, uTr = phase_uT(g, es, l, src)
        rope = kb.sb(es, "ropeA", [128, 2, SEQ], F32)
        rope_r = Reg()
        kb.dma("sp", rope[:], I["k_rope_att"].rearrange("t p n -> p t n"), [], [rope_r])
        wrot = kb.rot(es, "aw", [128, 8, 256], BF16, 2)
        wprot = kb.rot(es, "awp", [128, 8, 128], BF16, 2)
        stg = [kb.rot(es, "astg", [128, 512], F32, 2), kb.rot(es, "astg", [128, 512], F32, 2),
               kb.rot(es, "astgb", [128, 512], BF16, 3)]
        for cb in range(8):
            proj_fm_rope(g, uT, uTr, I["att_w_in"], cb * 128, rope, rope_r, 0.125, 16, attq[cb], qr_[cb],
                         wrot, wprot, stg, 0)
        for cb in range(2):
            proj_fm_rope(g, uT, uTr, I["att_w_in"], 1024 + cb * 128, rope, rope_r, 1.0, 16, attk[cb], kr_[cb],
                         wrot, wprot, stg, 0)
        proj_tm(g, uT, uTr, I["att_w_in"], 1280, 256, attv, vr_, 0, wrot, stg[2], 0)
    kb.barrier()
    with ExitStack() as es:
        mb = kb.sb(es, "amask", [128, 3, 384], BF16)
        cr = Reg()
        kb.dma("pool", mb[:], I["k_att_mask"].rearrange("v p n -> p v n"), [], [cr])
        sink = kb.sb(es, "asink", [128, 16], F32)
        kb.dma("sp", sink[:], I["att_sink"][0, :].partition_broadcast(128), [], [cr])
        Krot = kb.rot(es, "aK", [64, 2560], BF16, 2)
        Vrot = kb.rot(es, "aV", [128, 20, 64], BF16, 2)
        for kt, kr in Krot.items:
            kb.memset("pool", kt[:, 0:128], 0.0, [kr])
            kb.memset("pool", kt[:, 2176:2304], 0.0, [kr])
        for vt, vr in Vrot.items:
            kb.memset("pool", vt[:, 0, :], 0.0, [vr])
            kb.memset("pool", vt[:, 17, :], 0.0, [vr])
        Qrot = kb.rot(es, "aQ", [64, SEQ], BF16, 3)
        prot = kb.rot(es, "ap", [128, 640], BF16, 3)
        pTrot = kb.rot(es, "apT", [128, 5, 128], BF16, 3)
        strot = kb.rot(es, "ast", [128, 8], F32, 6)
        atile = kb.rot(es, "aat", [128, 18, 256], BF16, 2)
        it = 0
        for b in range(NB):
            for kv in range(4):
                Kt, Kr = Krot.next()
                Vt, Vr = Vrot.next()
                base = b * SEQ
                ksrc = attk[kv // 2, (kv % 2) * 64:(kv % 2) * 64 + 64, :]
                kb.dma("sp", Kt[:, 128:2176], ksrc[:, base + TC:base + SEQ], [kr_[kv // 2]], [Kr])
                kb.dma("sp", Kt[:, 2304:2560], ksrc[:, base:base + TC], [kr_[kv // 2]], [Kr])
                vsrc = attv[:, kv * 64:(kv + 1) * 64]
                kb.dma("sp", Vt[:, 1:17, :], vsrc[base + TC:base + SEQ, :].rearrange("(t p) d -> p t d", p=128),
                       [vr_[b * NT + t] for t in range(2, 18)], [Vr])
                kb.dma("sp", Vt[:, 18:20, :], vsrc[base:base + TC, :].rearrange("(t p) d -> p t d", p=128),
                       [vr_[b * NT], vr_[b * NT + 1]], [Vr])
                at_, atr = atile.next()
                for hh in range(4):
                    h = kv * 4 + hh
                    Qt, Qr = Qrot.next()
                    kb.dma("sp", Qt[:], attq[h // 2, (h % 2) * 64:(h % 2) * 64 + 64, base:base + SEQ],
                           [qr_[h // 2]], [Qr])
                    for blk in range(18):
                        if blk < 2 and not need_ctx:
                            continue
                        lat = blk >= 2
                        bi = blk - 2
                        qs = Qt[:, blk * 128:(blk + 1) * 128]
                        pa, pbk = (it % 2) * 2, (it % 2) * 2 + 1
                        ptb = 4 + it % 2
                        po = g.ps[6 + (it // 4) % 2][:, (it % 4) * 64:(it % 4) * 64 + 64]
                        por = g.psr[6 + (it // 4) % 2]
                        it += 1
                        st, sr = strot.next()
                        if lat:
                            var = 1 if bi == 0 else (2 if bi == 15 else 0)
                            kb.mm(g.ps[pa][:, 0:384], [(qs, Kt[:, bi * 128:bi * 128 + 384]),
                                                       (g.ident_b[:], mb[:, var, :])], [Qr, Kr, cr, g.const_r],
                                  [g.psr[pa]])
                            kb.op("dve", lambda hd, o=st[:, 0:1], i=g.ps[pa][:, 0:384]: hd.reduce_max(out=o, in_=i, axis=AX.X),
                                  [g.psr[pa]], [sr])
                        kb.mm(g.ps[pbk][:, 0:256], [(qs, Kt[:, 2304:2560])], [Qr, Kr], [g.psr[pbk]])
                        kb.op("dve", lambda hd, o=st[:, 1:2], i=g.ps[pbk][:, 0:256]: hd.reduce_max(out=o, in_=i, axis=AX.X),
                              [g.psr[pbk]], [sr])
                        if lat:
                            kb.tt("dve", st[:, 1:2], st[:, 0:1], st[:, 1:2], ALU.max, [sr], [sr])
                        kb.ts("dve", st[:, 2:3], st[:, 1:2], sink[:, h:h + 1], -1.0, ALU.max, ALU.mult, [sr, cr], [sr])
                        kb.memset("pool", st[:, 4:7], 0.0, [sr])
                        p_, pr = prot.next()
                        if lat:
                            kb.act(p_[:, 0:384], g.ps[pa][:, 0:384], AF.Exp, [g.psr[pa], sr], [pr, sr],
                                   bias=st[:, 2:3], scale=1.0, accum_out=st[:, 4:5])
                        kb.act(p_[:, 384:640], g.ps[pbk][:, 0:256], AF.Exp, [g.psr[pbk], sr], [pr, sr],
                               bias=st[:, 2:3], scale=1.0, accum_out=st[:, 5:6])
                        kb.act(st[:, 6:7], st[:, 2:3], AF.Exp, [sr, cr], [sr], bias=sink[:, h:h + 1], scale=1.0)
                        js = list(range(5)) if lat else [3, 4]
                        psT = g.ps[ptb][:].bitcast(BF16).rearrange("p (c t) -> p c t", c=8)
                        kb.tr([(psT[:, j, :], p_[:, j * 128:(j + 1) * 128]) for j in js], g.ident_b[:],
                              [pr, g.const_r], [g.psr[ptb]])
                        pT, pTr = pTrot.next()
                        kb.cp("dve", pT[:, js[0]:5, :], psT[:, js[0]:5, :], [g.psr[ptb]], [pTr])
                        pairs = []
                        if lat:
                            pairs += [(pT[:, j, :], Vt[:, bi + j, :]) for j in range(3)]
                        pairs += [(pT[:, 3, :], Vt[:, 18, :]), (pT[:, 4, :], Vt[:, 19, :])]
                        kb.mm(po, pairs, [pTr, Vr], [por])
                        kb.stt("dve", st[:, 7:8], st[:, 4:5], st[:, 5:6], st[:, 6:7], ALU.add, ALU.add, [sr], [sr])
                        kb.recip(st[:, 3:4], st[:, 7:8], [sr], [sr])
                        kb.ts("dve", at_[:, blk, hh * 64:(hh + 1) * 64], po, st[:, 3:4], None, ALU.mult, None,
                              [por, sr], [atr])
                for blk in range(18):
                    if blk < 2 and not need_ctx:
                        continue
                    tt = b * NT + blk
                    kb.dma("sp", atta[tt * 128:(tt + 1) * 128, kv * 256:(kv + 1) * 256], at_[:, blk, :], [atr],
                           [ar_[tt]])
    kb.barrier()
    phase_outproj(g, l, atta, ar_, D, I["att_w_out"], src, dst, need_ctx)


def mixer_ret(g, l, src, dst, need_ctx):
    kb, nc, I = g.kb, g.nc, g.I
    NTOK = TT * 128
    retq = nc.dram_tensor(kb.name("retq"), [8, 128, NTOK], BF16, kind="Internal").ap()
    retk = nc.dram_tensor(kb.name("retk"), [8, 128, NTOK], BF16, kind="Internal").ap()
    retv = nc.dram_tensor(kb.name("retv"), [NTOK, 2048], BF16, kind="Internal").ap()
    retg = nc.dram_tensor(kb.name("retg"), [NTOK, 4096], BF16, kind="Internal").ap()
    reta = nc.dram_tensor(kb.name("reta"), [NTOK, 2048], BF16, kind="Internal").ap()
    ar_ = [Reg() for _ in range(TT)]

    class Fresh(list):
        def __getitem__(self, i):
            return Reg()
    with ExitStack() as es:
        uT, uTr = phase_uT(g, es, l, src)
        rope = kb.sb(es, "ropeR", [128, 2, SEQ], F32)
        rope_r = Reg()
        kb.dma("sp", rope[:], I["k_rope_ret"].rearrange("t p n -> p t n"), [], [rope_r])
        wrot = kb.rot(es, "rw", [128, 8, 512], BF16, 2)
        wprot = kb.rot(es, "rwp", [128, 8, 128], BF16, 2)
        stg = [kb.rot(es, "rstg", [128, 512], F32, 2), kb.rot(es, "rstg", [128, 512], F32, 2),
               kb.rot(es, "rstgb", [128, 512], BF16, 4)]
        for h in range(8):
            proj_fm_rope(g, uT, uTr, I["ret_w_in"], h * 128, rope, rope_r, 128.0 ** -0.5, 32, retq[h], Reg(),
                         wrot, wprot, stg, 0)
            proj_fm_rope(g, uT, uTr, I["ret_w_in"], 1024 + h * 128, rope, rope_r, 1.0, 32, retk[h], Reg(),
                         wrot, wprot, stg, 0)
        for ch in range(4):
            proj_tm(g, uT, uTr, I["ret_w_in"], 2048 + ch * 512, 512, retv, Fresh(), ch * 512, wrot, stg[2], 4)
        for ch in range(8):
            proj_tm(g, uT, uTr, I["ret_w_in"], 4096 + ch * 512, 512, retg, Fresh(), ch * 512, wrot, stg[2], 4,
                    func=AF.Silu)
    kb.barrier()
    with ExitStack() as es:
        cr = Reg()
        tab = kb.sb(es, "rtab", [128, 5, 128], F32)
        kb.dma("sp", tab[:], I["k_ret_tab"].rearrange("t p n -> p t n"), [], [cr])
        lg = kb.sb(es, "rlg", [128, 16], F32)
        kb.dma("sp", lg[:], I["ret_decay_logit"][0, :].partition_broadcast(128), [], [cr])
        kb.act(lg[:], lg[:], AF.Exp, [cr], [cr], scale=-1.0)
        kb.ts("dve", lg[:], lg[:], 1.0, None, ALU.add, None, [cr], [cr])
        kb.act(lg[:], lg[:], AF.Ln, [cr], [cr])
        kb.ts("dve", lg[:], lg[:], -1.0, None, ALU.mult, None, [cr], [cr])
        htab = kb.sb(es, "rhtab", [128, 8, 4, 128], F32)
        hcol = kb.sb(es, "rhcol", [128, 8, 4], F32)
        for h in range(8):
            for ti, (src_i, lcol) in enumerate(((0, h), (1, 8 + h), (2, h), (3, 8 + h))):
                kb.act(htab[:, h, ti, :], tab[:, src_i, :], AF.Exp, [cr], [cr], scale=lg[:, lcol:lcol + 1])
            for ci, (src_c, lcol) in enumerate(((0, h), (1, 8 + h), (2, h), (2, 8 + h))):
                kb.act(hcol[:, h, ci:ci + 1], tab[:, 4, src_c:src_c + 1], AF.Exp, [cr], [cr],
                       scale=lg[:, lcol:lcol + 1])
        ws = NormWS(g, es)
        Qrot = kb.rot(es, "rQ", [128, SEQ], BF16, 2)
        Krot = kb.rot(es, "rK", [128, SEQ], BF16, 2)
        Vrot = kb.rot(es, "rV", [128, 18, 256], BF16, 2)
        GFrot = kb.rot(es, "rGF", [128, 18, 256], BF16, 2)
        GBrot = kb.rot(es, "rGB", [128, 18, 256], BF16, 2)
        pre = [kb.sb(es, "rpre", [128, 18, 128], BF16) for _ in range(6)]
        prer = [[Reg() for _ in range(18)] for _ in range(6)]
        og = [kb.sb(es, "rog", [128, 18, 256], F32) for _ in range(2)]
        ogr = [[Reg() for _ in range(18)] for _ in range(2)]
        S = [kb.sb(es, "rS", [128, 256], F32) for _ in range(2)]
        Sb = [kb.sb(es, "rSb", [128, 256], BF16) for _ in range(2)]
        Sr = [Reg(), Reg()]
        Sbr = [Reg(), Reg()]
        arot = kb.rot(es, "ra", [128, 256], BF16, 3)
        it = 0
        for b in range(NB):
            base = b * SEQ
            for h in range(8):
                Qt, Qr = Qrot.next()
                Kt, Kr = Krot.next()
                Vt, Vr = Vrot.next()
                GF, GFr = GFrot.next()
                GB, GBr = GBrot.next()
                kb.dma("sp", Qt[:], retq[h][:, base:base + SEQ], [], [Qr])
                kb.dma("sp", Kt[:], retk[h][:, base:base + SEQ], [], [Kr])
                kb.dma("sp", Vt[:], retv[base:base + SEQ, h * 256:(h + 1) * 256].rearrange("(t p) d -> p t d", p=128),
                       [], [Vr])
                kb.dma("sp", GF[:], retg[base:base + SEQ, h * 256:(h + 1) * 256].rearrange("(t p) d -> p t d", p=128),
                       [], [GFr])
                kb.dma("sp", GB[:], retg[base:base + SEQ, 2048 + h * 256:2048 + (h + 1) * 256]
                       .rearrange("(t p) d -> p t d", p=128), [], [GBr])
                for c in range(18):
                    cs = slice(c * 128, (c + 1) * 128)
                    pb = it % 2
                    pkb = 2 + it % 2
                    it += 1
                    pss = g.ps[pb][:, 0:128]
                    kb.mm(pss, [(Kt[:, cs], Qt[:, cs])], [Kr, Qr], [g.psr[pb]])
                    kb.tt("dve", pre[0][:, c, :], pss, htab[:, h, 0, :], ALU.mult, [g.psr[pb], cr], [prer[0][c]])
                    kb.tt("dve", pre[1][:, c, :], pss, htab[:, h, 1, :], ALU.mult, [g.psr[pb], cr], [prer[1][c]])
                    kb.tt("pool", pre[2][:, c, :], Qt[:, cs], htab[:, h, 2, :], ALU.mult, [Qr, cr], [prer[2][c]])
                    kb.tt("pool", pre[3][:, c, :], Qt[:, cs], htab[:, h, 3, :], ALU.mult, [Qr, cr], [prer[3][c]])
                    psk = g.ps[pkb][:].bitcast(BF16)[:, 0:128]
                    kb.tr([(psk, Kt[:, cs])], g.ident_b[:], [Kr, g.const_r], [g.psr[pkb]])
                    kb.act(pre[4][:, c, :], psk, AF.Copy, [g.psr[pkb], cr], [prer[4][c]], scale=hcol[:, h, 0:1])
                    kb.act(pre[5][:, c, :], psk, AF.Copy, [g.psr[pkb], cr], [prer[5][c]], scale=hcol[:, h, 1:2])
                for d_ in range(2):
                    kb.memset("pool", S[d_][:], 0.0, [Sr[d_]])
                    kb.memset("pool", Sb[d_][:], 0.0, [Sbr[d_]])
                orders = [list(range(18)), [1, 0] + list(range(17, 1, -1))]
                for step in range(18):
                    for d_ in range(2):
                        c = orders[d_][step]
                        hf = step % 2
                        po = g.ps[4 + d_][:, hf * 256:(hf + 1) * 256]
                        pS = g.ps[6 + d_][:, hf * 256:(hf + 1) * 256]
                        kb.mm(po, [(pre[d_][:, c, :], Vt[:, c, :]), (pre[2 + d_][:, c, :], Sb[d_][:])],
                              [prer[d_][c], prer[2 + d_][c], Vr, Sbr[d_]], [g.psr[4 + d_]])
                        kb.mm(pS, [(pre[4 + d_][:, c, :], Vt[:, c, :])], [prer[4 + d_][c], Vr], [g.psr[6 + d_]])
                        kb.stt("dve", S[d_][:], S[d_][:], hcol[:, h, 2 + d_:3 + d_], pS, ALU.mult, ALU.add,
                               [Sr[d_], g.psr[6 + d_], cr], [Sr[d_]])
                        kb.cp("act", Sb[d_][:], S[d_][:], [Sr[d_]], [Sbr[d_]])
                        st, sr = rms_stats(g, ws, [po], [g.psr[4 + d_]], isn=1.0 / 16.0)
                        gate = GF if d_ == 0 else GB
                        kb.stt("dve", og[d_][:, c, :], po, st[:, 2:3], gate[:, c, :], ALU.mult, ALU.mult,
                               [g.psr[4 + d_], sr, GFr if d_ == 0 else GBr], [ogr[d_][c]])
                for c in range(18):
                    a_, a_r = arot.next()
                    kb.tt("pool", a_[:], og[0][:, c, :], og[1][:, c, :], ALU.add, [ogr[0][c], ogr[1][c]], [a_r])
                    tt = b * NT + c
                    kb.dma("sp", reta[tt * 128:(tt + 1) * 128, h * 256:(h + 1) * 256], a_[:], [a_r], [Reg()])
    kb.barrier()
    phase_outproj(g, l, reta, ar_, 2048, I["ret_w_out"], src, dst, need_ctx)


def mixer_dn(g, l, src, dst, need_ctx):
    kb, nc, I = g.kb, g.nc, g.I
    NTOK = TT * 128
    dnq = nc.dram_tensor(kb.name("dnq"), [8, 128, NTOK], BF16, kind="Internal").ap()
    dnk = nc.dram_tensor(kb.name("dnk"), [8, 128, NTOK], BF16, kind="Internal").ap()
    dnkt = nc.dram_tensor(kb.name("dnkt"), [NTOK, D], BF16, kind="Internal").ap()
    dnvt = nc.dram_tensor(kb.name("dnvt"), [NTOK, D], BF16, kind="Internal").ap()
    dnba = nc.dram_tensor(kb.name("dnba"), [NTOK, 32], F32, kind="Internal").ap()
    dng = nc.dram_tensor(kb.name("dng"), [NTOK, 2048], BF16, kind="Internal").ap()
    dnof = nc.dram_tensor(kb.name("dnof"), [NTOK, D], F32, kind="Internal").ap()
    dna = nc.dram_tensor(kb.name("dna"), [NTOK, D], BF16, kind="Internal").ap()
    ar_ = [Reg() for _ in range(TT)]

    class Fresh(list):
        def __getitem__(self, i):
            return Reg()
    NX = 2308
    with ExitStack() as es:
        uT, uTr = phase_uT(g, es, l, src)
        cr = Reg()
        cw = kb.sb(es, "dcw", [128, 24, 5], F32)
        for k in range(5):
            kb.dma("sp", cw[:, :, k], I["dn_conv_w"][k, :].rearrange("(c p) -> p c", p=128), [], [cr],
                   allow_slow_non_contiguous=True)
        ones_b = kb.sb(es, "donesb", [128, 128], BF16)
        kb.memset("dve", ones_b[:], 1.0, [cr])
        wrot = kb.rot(es, "dw", [128, 8, 512], BF16, 2)
        Xrot = kb.rot(es, "dX", [128, 2320], F32, 2)
        for xt_, xr_ in Xrot.items:
            kb.memset("pool", xt_[:, 0:2], 0.0, [xr_])
            kb.memset("pool", xt_[:, 258:262], 0.0, [xr_])
            kb.memset("pool", xt_[:, 2310:2320], 0.0, [xr_])
        accA = kb.sb(es, "daccA", [128, NX], F32)
        accB = kb.sb(es, "daccB", [128, NX], F32)
        accAr, accBr = Reg(), Reg()
        Ssil = kb.sb(es, "dsil", [128, NX], F32)
        Ssr = Reg()
        sqb = kb.sb(es, "dsq", [128, NX], BF16)
        sqr = Reg()
        sdrot = kb.rot(es, "dsd", [128, 512], F32, 2)
        QNrot = kb.rot(es, "dQN", [128, NX], BF16, 2)
        tstg = kb.rot(es, "dts", [128, 8, 128], BF16, 2)
        it = 0
        for cb in range(24):
            kind, h = cb // 8, cb % 8
            wt, wr = load_w_bf16(g, wrot, I["dn_w_in"], cb * 128, 128)
            for b in range(NB):
                base = b * SEQ
                X, Xr = Xrot.next()
                for bi, (o, n) in enumerate(((0, 512), (512, 512), (1024, 512), (1536, 512), (2048, 256))):
                    t0 = base + o
                    tiles = list(range(t0 // 128, (t0 + n) // 128))
                    pb = it % 2
                    it += 1
                    kb.mm(g.ps[pb][:, 0:n], [(wt[:, kc, 0:128], uT[:, kc, t0:t0 + n]) for kc in range(8)],
                          [uTr[t] for t in tiles] + [wr], [g.psr[pb]])
                    if o == 0:
                        kb.cp("act", X[:, 2:258], g.ps[pb][:, 0:256], [g.psr[pb]], [Xr])
                        kb.cp("act", X[:, 262:518], g.ps[pb][:, 256:512], [g.psr[pb]], [Xr])
                    else:
                        kb.cp("act", X[:, 6 + o:6 + o + n], g.ps[pb][:, 0:n], [g.psr[pb]], [Xr])
                kb.ts("dve", accA[:], X[:, 0:NX], cw[:, cb, 0:1], None, ALU.mult, None, [Xr, cr], [accAr])
                for k in (1, 2, 3):
                    kb.stt("dve", accA[:], X[:, k:k + NX], cw[:, cb, k:k + 1], accA[:], ALU.mult, ALU.add,
                           [Xr, cr, accAr], [accAr])
                kb.ts("pool", accB[:], X[:, 4:4 + NX], cw[:, cb, 4:5], None, ALU.mult, None, [Xr, cr], [accBr])
                kb.tt("pool", accB[:], accB[:], accA[:], ALU.add, [accAr, accBr], [accBr])
                kb.act(Ssil[:], accB[:], AF.Silu, [accBr], [Ssr])
                QN, QNr = QNrot.next()
                if kind < 2:
                    kb.act(sqb[:], Ssil[:], AF.Square, [Ssr], [sqr])
                    for o in range(0, NX, 512):
                        n = min(512, NX - o)
                        pb = 2 + it % 2
                        it += 1
                        kb.mm(g.ps[pb][:, 0:n], [(ones_b[:], sqb[:, o:o + n])], [sqr, cr], [g.psr[pb]])
                        sd, sdr = sdrot.next()
                        kb.act(sd[:, 0:n], g.ps[pb][:, 0:n], AF.Sqrt, [g.psr[pb], g.const_r], [sdr],
                               bias=g.eps[:, 0:1], scale=1.0)
                        kb.recip(sd[:, 0:n], sd[:, 0:n], [sdr], [sdr])
                        kb.stt("dve", QN[:, o:o + n], Ssil[:, o:o + n], (128.0 ** -0.5) if kind == 0 else 1.0,
                               sd[:, 0:n], ALU.mult, ALU.mult, [Ssr, sdr], [QNr])
                    dstq = dnq if kind == 0 else dnk
                    kb.dma("sp", dstq[h][:, base:base + TC], QN[:, 0:TC], [QNr], [Reg()])
                    kb.dma("sp", dstq[h][:, base + TC:base + SEQ], QN[:, 260:NX], [QNr], [Reg()])
                else:
                    kb.cp("dve", QN[:], Ssil[:], [Ssr], [QNr])
                if kind >= 1:
                    dstt = dnkt if kind == 1 else dnvt
                    for t0 in range(0, NT, 8):
                        ts_ = list(range(t0, min(NT, t0 + 8)))
                        pb = 4 + it % 2
                        it += 1
                        psT = g.ps[pb][:].bitcast(BF16).rearrange("p (c t) -> p c t", c=8)
                        kb.tr([(psT[:, j, :], QN[:, (t * 128 if t < 2 else t * 128 + 4):(t * 128 if t < 2 else t * 128 + 4) + 128])
                               for j, t in enumerate(ts_)], g.ident_b[:], [QNr, g.const_r], [g.psr[pb]])
                        stt_, str_ = tstg.next()
                        kb.cp("act", stt_[:, 0:len(ts_), :], psT[:, 0:len(ts_), :], [g.psr[pb]], [str_])
                        r0 = base + t0 * 128
                        kb.dma("sp", dstt[r0:r0 + len(ts_) * 128, h * 128:(h + 1) * 128]
                               .rearrange("(t p) d -> p t d", p=128), stt_[:, 0:len(ts_), :], [str_], [Reg()])
        abr = Reg()
        dtb = kb.sb(es, "ddtb", [128, 16], F32)
        nea = kb.sb(es, "dnea", [128, 16], F32)
        kb.dma("sp", dtb[:], I["dn_dt_bias"][0, :].partition_broadcast(128), [], [abr])
        kb.dma("sp", nea[:], I["dn_a_log"][0, :].partition_broadcast(128), [], [abr])
        kb.act(nea[:], nea[:], AF.Exp, [abr], [abr])
        kb.ts("dve", nea[:], nea[:], -1.0, None, ALU.mult, None, [abr], [abr])
        wt, wr = load_w_bf16(g, wrot, I["dn_w_in"], 3072, 32)
        bstg = kb.rot(es, "dbs", [128, 32], F32, 3)
        btmp = kb.rot(es, "dbt", [128, 16], F32, 3)
        for tt in range(TT):
            pb = 6 + tt % 2
            kb.mm(g.ps[pb][:, 0:32], [(uT[:, kc, tt * 128:(tt + 1) * 128], wt[:, kc, 0:32]) for kc in range(8)],
                  [uTr[tt], wr], [g.psr[pb]])
            bs, bsr = bstg.next()
            bt, btr = btmp.next()
            kb.act(bs[:, 0:16], g.ps[pb][:, 0:16], AF.Sigmoid, [g.psr[pb]], [bsr])
            kb.tt("dve", bt[:], g.ps[pb][:, 16:32], dtb[:], ALU.add, [g.psr[pb], abr], [btr])
            kb.act(bt[:], bt[:], AF.Exp, [btr], [btr])
            kb.ts("dve", bt[:], bt[:], 1.0, None, ALU.add, None, [btr], [btr])
            kb.act(bt[:], bt[:], AF.Ln, [btr], [btr])
            kb.tt("dve", bs[:, 16:32], bt[:], nea[:], ALU.mult, [btr, abr], [bsr])
            kb.dma("sp", dnba[tt * 128:(tt + 1) * 128, :], bs[:], [bsr], [Reg()])
        stgb = kb.rot(es, "dgs", [128, 512], BF16, 4)
        tiles = [tt for tt in range(TT) if need_ctx or (tt % NT) >= 2]
        for ch in range(4):
            proj_tm(g, uT, uTr, I["dn_w_in"], 3104 + ch * 512, 512, dng, Fresh(), ch * 512, wrot, stgb, 2,
                    func=AF.Silu, tiles=tiles)
    kb.barrier()
    with ExitStack() as es:
        cr = Reg()
        ktab = kb.sb(es, "dktab", [64, 4, 64], F32)
        kb.dma("sp", ktab[:], I["k_dn"].rearrange("t p n -> p t n"), [], [cr])
        ones3 = kb.sb(es, "dones3", [64, 128], F32)
        kb.memset("dve", ones3[:], 1.0, [cr])
        ng = kb.sb(es, "dng_", [64, 128], F32)
        kb.dma("sp", ng[:], I["dn_norm_g"][0, :].partition_broadcast(64), [], [cr])
        identf = g.ident_f

        def R1(shape, dt, name, n=2):
            return kb.rot(es, name, shape, dt, n)
        kTr_, qTr_ = R1([128, 8, 64], BF16, "dkT"), R1([128, 8, 64], BF16, "dqT")
        ktr_, vtr_ = R1([64, 8, 128], BF16, "dkt"), R1([64, 8, 128], BF16, "dvt")
        bar_ = R1([64, 32], F32, "dba")
        gtr_ = R1([64, 8, 128], BF16, "dgt")
        ofr_ = R1([64, 8, 128], F32, "dofl")
        LBn_ = R1([64, 8, 128], F32, "dLBn")
        sm_ = R1([128, 64], F32, "dsm", 3)
        gdm_ = R1([64, 8, 64], F32, "dgdm")
        t12_ = R1([64, 2, 8, 64], F32, "dt12")
        Dms_, DTi_ = R1([64, 8, 64], F32, "dDms"), R1([64, 8, 64], F32, "dDTi")
        Egr_ = R1([128, 8, 64], F32, "dEgr")
        qgT_ = R1([128, 8, 64], BF16, "dqgT")
        A_ = R1([64, 8, 64], F32, "dA", 3)
        B_ = R1([64, 8, 64], F32, "dB", 3)
        Tt_ = R1([64, 8, 64], F32, "dTt")
        Ttb_ = R1([64, 8, 64], BF16, "dTtb")
        attnT_ = R1([64, 8, 64], BF16, "dattn")
        rv_, kbg_, kg_ = R1([64, 8, 128], BF16, "drv"), R1([64, 8, 128], BF16, "dkbg"), R1([64, 8, 128], BF16, "dkg")
        u_ = R1([64, 8, 128], F32, "du")
        wT_ = R1([128, 8, 64], BF16, "dwT")
        vn_ = R1([64, 8, 128], BF16, "dvn")
        sq_ = R1([64, 8, 128], F32, "dsq2")
        on_ = R1([64, 8, 128], F32, "don")
        gn_ = R1([64, 8, 128], F32, "dgn")
        oa_ = R1([64, 8, 128], BF16, "doa")
        S = kb.sb(es, "dS", [128, 8, 128], F32)
        Sb = kb.sb(es, "dSb", [128, 8, 128], BF16)
        Sr, Sbr = Reg(), Reg()
        P = g.ps
        PR = g.psr

        for b in range(NB):
            base = b * SEQ
            for d_ in range(2):
                order = list(range(36)) if d_ == 0 else [3, 2, 1, 0] + list(range(35, 3, -1))
                Tri = ktab[:, d_, :]
                strict = ktab[:, 2 + d_, :]
                kb.memset("pool", S[:], 0.0, [Sr])
                kb.memset("pool", Sb[:], 0.0, [Sbr])
                for c in order:
                    tok0 = base + c * 64
                    emit = need_ctx or c >= 4
                    kT, kTr = kTr_.next()
                    qT, qTr = qTr_.next()
                    kt, ktr = ktr_.next()
                    vt, vtr = vtr_.next()
                    ba, bar = bar_.next()
                    kb.dma("sp", kT[:], dnk[:, :, tok0:tok0 + 64].rearrange("h p n -> p h n"), [], [kTr])
                    kb.dma("sp", qT[:], dnq[:, :, tok0:tok0 + 64].rearrange("h p n -> p h n"), [], [qTr])
                    kb.dma("sp", kt[:], dnkt[tok0:tok0 + 64, :].rearrange("p (h d) -> p h d", h=8), [], [ktr])
                    kb.dma("sp", vt[:], dnvt[tok0:tok0 + 64, :].rearrange("p (h d) -> p h d", h=8), [], [vtr])
                    kb.dma("sp", ba[:], dnba[tok0:tok0 + 64, :], [], [bar])
                    beta = ba[:, 8 * d_:8 * d_ + 8]
                    la = ba[:, 16 + 8 * d_:24 + 8 * d_]
                    bc64 = lambda ap: ap.unsqueeze(2).broadcast_to([64, 8, 64])
                    bc128 = lambda ap: ap.unsqueeze(2).broadcast_to([64, 8, 128])
                    LBn, LBr = LBn_.next()
                    kb.ts("dve", LBn[:], bc128(la), -1.0, None, ALU.mult, None, [bar], [LBr])
                    kb.mm(P[3][0:64, 0:8], [(Tri, la)], [cr, bar], [PR[3]])
                    kb.mm(P[3][:, 8:16], [(ones3[:, :], la)], [cr, bar], [PR[3]])
                    for h in range(8):
                        kb.mm(P[2][:, h * 64:(h + 1) * 64], [(LBn[:, h, :], Tri)], [LBr, cr], [PR[2]])
                    sm, smr = sm_.next()
                    kb.cp("dve", sm[0:64, 0:8], P[3][0:64, 0:8], [PR[3]], [smr])
                    gc = sm[0:64, 0:8]
                    gdm, gdmr = gdm_.next()
                    GR = P[2][:].rearrange("p (h n) -> p h n", h=8)
                    kb.tt("dve", gdm[:], GR[0:64], bc64(gc), ALU.add, [PR[2], smr], [gdmr])
                    t12, t12r = t12_.next()
                    kb.ts("dve", t12[:, 0], gdm[:], 0.0, None, ALU.min, None, [gdmr], [t12r])
                    kb.ts("dve", t12[:, 1], gdm[:], -1.0, 0.0, ALU.mult, ALU.min, [gdmr], [t12r])
                    kb.act(t12[:], t12[:], AF.Exp, [t12r], [t12r])
                    Dms, Dmsr = Dms_.next()
                    DTi, DTir = DTi_.next()
                    kb.tt("pool", Dms[:], t12[:, 0], strict.unsqueeze(1).broadcast_to([64, 8, 64]), ALU.mult,
                          [t12r, cr], [Dmsr])
                    kb.tt("pool", DTi[:], t12[:, 1], Tri.unsqueeze(1).broadcast_to([64, 8, 64]), ALU.mult,
                          [t12r, cr], [DTir])
                    Egr, Egrr = Egr_.next()
                    kb.act(Egr[:], GR, AF.Exp, [PR[2]], [Egrr], scale=-1.0)
                    qgT, qgTr = qgT_.next()
                    kb.tt("pool", qgT[:], qT[:], Egr[:], ALU.mult, [qTr, Egrr], [qgTr])
                    kb.act(sm[0:64, 8:16], gc, AF.Exp, [smr], [smr])
                    kb.tt("dve", sm[0:64, 8:16], sm[0:64, 8:16], beta, ALU.mult, [smr, bar], [smr])
                    kb.tt("dve", sm[0:64, 16:24], P[3][0:64, 8:16], gc, ALU.subtract, [PR[3], smr], [smr])
                    kb.act(sm[0:64, 16:24], sm[0:64, 16:24], AF.Exp, [smr], [smr])
                    kb.act(sm[:, 24:32], P[3][:, 8:16], AF.Exp, [PR[3]], [smr])
                    for h in range(8):
                        kb.mm(P[0][0:64, h * 64:(h + 1) * 64], [(kT[:, h, :], kT[:, h, :])], [kTr], [PR[0]])
                    for h in range(8):
                        kb.mm(P[1][0:64, h * 64:(h + 1) * 64], [(kT[:, h, :], qT[:, h, :])], [kTr, qTr], [PR[1]])
                    KK = P[0][0:64, :].rearrange("p (h n) -> p h n", h=8)
                    QK = P[1][0:64, :].rearrange("p (h n) -> p h n", h=8)
                    A0, A0r = A_.next()
                    kb.tt("dve", A0[:], KK, bc64(beta), ALU.mult, [PR[0], bar], [A0r])
                    kb.tt("pool", A0[:], A0[:], Dms[:], ALU.mult, [A0r, Dmsr], [A0r])
                    attnT, attnr = attnT_.next()
                    kb.tt("dve", attnT[:], QK, DTi[:], ALU.mult, [PR[1], DTir], [attnr])
                    for h in range(8):
                        kb.tr([(P[0][0:64, h * 64:(h + 1) * 64], A0[:, h, :])], identf[0:64, 0:64], [A0r, g.const_r],
                              [PR[0]])
                    B0, B0r = B_.next()
                    kb.cp("act", B0[:], KK, [PR[0]], [B0r])
                    Tt, Ttr = Tt_.next()
                    kb.tt("dve", Tt[:], identf[0:64, 0:64].unsqueeze(1).broadcast_to([64, 8, 64]), B0[:],
                          ALU.subtract, [g.const_r, B0r], [Ttr])
                    Ak, Akr, Bk, Bkr = A0, A0r, B0, B0r
                    for lev in range(5):
                        An, Anr = A_.next()
                        for h in range(8):
                            kb.mm(P[0][0:64, h * 64:(h + 1) * 64], [(Bk[:, h, :], Ak[:, h, :])], [Akr, Bkr], [PR[0]])
                        kb.cp("act", An[:], KK, [PR[0]], [Anr])
                        if lev < 4:
                            Bn, Bnr = B_.next()
                            for h in range(8):
                                kb.mm(P[1][0:64, h * 64:(h + 1) * 64], [(Ak[:, h, :], Bk[:, h, :])], [Akr, Bkr],
                                      [PR[1]])
                            kb.cp("dve", Bn[:], QK, [PR[1]], [Bnr])
                        for h in range(8):
                            kb.mm(P[3][0:64, h * 64:(h + 1) * 64], [(An[:, h, :], Tt[:, h, :])], [Anr, Ttr], [PR[3]])
                        kb.tt("dve", Tt[:], Tt[:], P[3][0:64, :].rearrange("p (h n) -> p h n", h=8), ALU.add,
                              [Ttr, PR[3]], [Ttr])
                        Ak, Akr = An, Anr
                        if lev < 4:
                            Bk, Bkr = Bn, Bnr
                    Ttb, Ttbr = Ttb_.next()
                    kb.cp("act", Ttb[:], Tt[:], [Ttr], [Ttbr])
                    rv, rvr = rv_.next()
                    kbg, kbgr = kbg_.next()
                    kg, kgr = kg_.next()
                    kb.tt("pool", rv[:], vt[:], bc128(beta), ALU.mult, [vtr, bar], [rvr])
                    kb.tt("pool", kbg[:], kt[:], bc128(sm[0:64, 8:16]), ALU.mult, [ktr, smr], [kbgr])
                    kb.tt("pool", kg[:], kt[:], bc128(sm[0:64, 16:24]), ALU.mult, [ktr, smr], [kgr])
                    u, ur = u_.next()
                    wT, wTr = wT_.next()
                    for hb in range(2):
                        for hh in range(4):
                            h = hb * 4 + hh
                            kb.mm(P[4 + hb][0:64, hh * 128:(hh + 1) * 128], [(Ttb[:, h, :], rv[:, h, :])],
                                  [Ttbr, rvr], [PR[4 + hb]])
                        kb.cp("act", u[:, hb * 4:(hb + 1) * 4, :],
                              P[4 + hb][0:64, :].rearrange("p (h n) -> p h n", h=4), [PR[4 + hb]], [ur])
                    for h in range(8):
                        kb.mm(P[2][:, h * 64:(h + 1) * 64], [(kbg[:, h, :], Ttb[:, h, :])], [kbgr, Ttbr], [PR[2]])
                    kb.cp("act", wT[:], GR, [PR[2]], [wTr])
                    vn, vnr = vn_.next()
                    for hb in range(2):
                        for hh in range(4):
                            h = hb * 4 + hh
                            kb.mm(P[4 + hb][0:64, hh * 128:(hh + 1) * 128], [(wT[:, h, :], Sb[:, h, :])],
                                  [wTr, Sbr], [PR[4 + hb]])
                        kb.tt("dve", vn[:, hb * 4:(hb + 1) * 4, :], u[:, hb * 4:(hb + 1) * 4, :],
                              P[4 + hb][0:64, :].rearrange("p (h n) -> p h n", h=4), ALU.subtract,
                              [ur, PR[4 + hb]], [vnr])
                    for hb in range(2):
                        for hh in range(4):
                            h = hb * 4 + hh
                            if emit:
                                kb.mm(P[6 + hb][0:64, hh * 128:(hh + 1) * 128],
                                      [(qgT[:, h, :], Sb[:, h, :]), (attnT[:, h, :], vn[:, h, :])],
                                      [qgTr, Sbr, attnr, vnr], [PR[6 + hb]])
                            kb.mm(P[hb][:, hh * 128:(hh + 1) * 128], [(kg[:, h, :], vn[:, h, :])], [kgr, vnr],
                                  [PR[hb]])
                    kb.tt("dve", S[:], S[:], sm[:, 24:32].unsqueeze(2).broadcast_to([128, 8, 128]), ALU.mult,
                          [Sr, smr], [Sr])
                    for hb in range(2):
                        kb.tt("dve", S[:, hb * 4:(hb + 1) * 4, :], S[:, hb * 4:(hb + 1) * 4, :],
                              P[hb][:, :].rearrange("p (h n) -> p h n", h=4), ALU.add, [Sr, PR[hb]], [Sr])
                    kb.cp("act", Sb[:], S[:], [Sr], [Sbr])
                    if not emit:
                        continue
                    gt, gtr = gtr_.next()
                    kb.dma("sp", gt[:], dng[tok0:tok0 + 64, d_ * 1024:(d_ + 1) * 1024].rearrange("p (h d) -> p h d", h=8),
                           [], [gtr])
                    gn, gnr = gn_.next()
                    kb.tt("pool", gn[:], gt[:], ng[:, :].unsqueeze(1).broadcast_to([64, 8, 128]), ALU.mult,
                          [gtr, cr], [gnr])
                    sq, sqr = sq_.next()
                    for hb in range(2):
                        kb.act(sq[:, hb * 4:(hb + 1) * 4, :], P[6 + hb][0:64, :].rearrange("p (h n) -> p h n", h=4),
                               AF.Square, [PR[6 + hb]], [sqr])
                    kb.op("dve", lambda hd, o=sm[0:64, 32:40], i=sq[:]: hd.reduce_sum(out=o, in_=i, axis=AX.X),
                          [sqr], [smr])
                    kb.act(sm[0:64, 40:48], sm[0:64, 32:40], AF.Sqrt, [smr, g.const_r], [smr], bias=g.eps[0:64, 0:1],
                           scale=1.0 / 128.0)
                    kb.recip(sm[0:64, 48:56], sm[0:64, 40:48], [smr], [smr])
                    on, onr = on_.next()
                    for hb in range(2):
                        kb.tt("dve", on[:, hb * 4:(hb + 1) * 4, :], P[6 + hb][0:64, :].rearrange("p (h n) -> p h n", h=4),
                              sm[0:64, 48 + hb * 4:52 + hb * 4].unsqueeze(2).broadcast_to([64, 4, 128]), ALU.mult,
                              [PR[6 + hb], smr], [onr])
                    if d_ == 0:
                        kb.tt("pool", on[:], on[:], gn[:], ALU.mult, [onr, gnr], [onr])
                        kb.dma("sp", dnof[tok0:tok0 + 64, :].rearrange("p (h d) -> p h d", h=8), on[:], [onr], [Reg()])
                    else:
                        of, ofr = ofr_.next()
                        kb.dma("sp", of[:], dnof[tok0:tok0 + 64, :].rearrange("p (h d) -> p h d", h=8), [], [ofr])
                        kb.tt("pool", on[:], on[:], gn[:], ALU.mult, [onr, gnr], [onr])
                        oa, oar = oa_.next()
                        kb.tt("pool", oa[:], on[:], of[:], ALU.add, [onr, ofr], [oar])
                        kb.dma("sp", dna[tok0:tok0 + 64, :].rearrange("p (h d) -> p h d", h=8), oa[:], [oar], [Reg()])
                if d_ == 0:
                    kb.barrier()
    kb.barrier()
    phase_outproj(g, l, dna, ar_, D, I["dn_w_out"], src, dst, need_ctx)


def host_consts():
    c = {}
    c["k_ident"] = np.eye(128, dtype=np.float32)
    mats = np.zeros((20, 128, 128), np.float32)
    for wi, w in enumerate((2, 4, 8, 16)):
        lo = w // 2
        hi = w - 1 - lo
        n = 384
        for var, (t0, nseq_lo, nseq_hi) in enumerate(((128, 0, 384), (0, 0, 384), (256, 0, 384))):
            pass
        A = np.zeros((n, n), np.float32)
        for t in range(n):
            a, b_ = max(0, t - lo), min(n, t + hi + 1)
            A[t, a:b_] = 1.0 / (b_ - a)
        M = A - np.eye(n, dtype=np.float32)
        MT = M.T
        mats[wi * 5 + 0] = MT[128:256, 128:256]
        mats[wi * 5 + 1] = MT[0:128, 0:128]
        mats[wi * 5 + 2] = MT[256:384, 256:384]
        mats[wi * 5 + 3] = MT[0:128, 128:256]
        mats[wi * 5 + 4] = MT[256:384, 128:256]
    c["k_pool"] = mats
    pos = np.arange(T)
    row, col = pos // 64, pos % 64

    def rope_tab(dh, nrep):
        q = dh // 4
        tab = np.zeros((2, dh, SEQ), np.float64)
        tab[0, :, :TC] = 1.0
        for d in range(dh):
            blk, i = d // q, d % q
            inv = 10000.0 ** (-i / q)
            p_ = row if blk < 2 else col
            ang = (p_.astype(np.float32) * np.float32(inv)).astype(np.float64)
            tab[0, d, TC:] = np.cos(ang)
            tab[1, d, TC:] = np.sin(ang) * (-1.0 if blk % 2 == 0 else 1.0)
        return np.tile(tab, (1, nrep, 1)).astype(np.float32)
    c["k_rope_att"] = rope_tab(64, 2)
    c["k_rope_ret"] = rope_tab(128, 1)
    qi = np.arange(128)[:, None]
    kj = np.arange(384)[None, :]
    keep = np.abs(kj - 128 - qi) <= 128
    m = np.zeros((3, 128, 384), np.float32)
    m[0] = np.where(keep, 0.0, -30000.0)
    m[1] = np.where(keep & (kj >= 128), 0.0, -30000.0)
    m[2] = np.where(keep & (kj < 256), 0.0, -30000.0)
    c["k_att_mask"] = m
    jj = np.arange(128)[:, None].astype(np.float64)
    ii = np.arange(128)[None, :].astype(np.float64)
    rt = np.zeros((5, 128, 128), np.float64)
    rt[0] = np.where(ii >= jj, ii - jj, 1e9)
    rt[1] = np.where(jj >= ii, jj - ii, 1e9)
    rt[2] = np.broadcast_to(ii + 1.0, (128, 128))
    rt[3] = np.broadcast_to(128.0 - ii, (128, 128))
    rt[4, :, 0] = 127.0 - np.arange(128)
    rt[4, :, 1] = np.arange(128)
    rt[4, :, 2] = 128.0
    c["k_ret_tab"] = rt.astype(np.float32)
    p_ = np.arange(64)[:, None]
    f_ = np.arange(64)[None, :]
    c["k_dn"] = np.stack([p_ <= f_, p_ >= f_, p_ > f_, p_ < f_]).astype(np.float32)
    return c


LAYERS = [0, 1, 2, 3]
_cache = {}


def _prep_inputs(inputs):
    f = lambda a: np.ascontiguousarray(np.asarray(a, dtype=np.float32))
    shared = {}
    for k in ("ada_w", "ada_b", "mix_pre_g", "mix_post_g", "mlp_pre_g", "mlp_post_g", "mlp_w1", "mlp_w2"):
        shared[k] = f(inputs[k])
    shared["c_ctx"] = f(inputs["c_ctx"]).reshape(1, D)
    shared["ret_w_in"] = f(inputs["ret_w_in"][0])
    shared["ret_decay_logit"] = f(inputs["ret_decay_logit"][0]).reshape(1, 16)
    shared["ret_w_out"] = f(inputs["ret_w_out"][0])
    shared["att_w_in"] = f(inputs["att_w_in"][0])
    shared["att_sink"] = f(inputs["att_sink"][0]).reshape(1, 16)
    shared["att_w_out"] = f(inputs["att_w_out"][0])
    shared["pool_w"] = f(inputs["pool_w"][0])
    shared["pool_b"] = f(inputs["pool_b"][0]).reshape(1, D)
    shared["pool_scale"] = f(inputs["pool_scale"][0]).reshape(1, D)
    shared["dn_w_in"] = f(inputs["dn_w_in"][0])
    shared["dn_conv_w"] = f(inputs["dn_conv_w"][0])
    shared["dn_a_log"] = f(inputs["dn_a_log"][0]).reshape(1, 16)
    shared["dn_dt_bias"] = f(inputs["dn_dt_bias"][0]).reshape(1, 16)
    shared["dn_norm_g"] = f(inputs["dn_norm_g"][0]).reshape(1, 128)
    shared["dn_w_out"] = f(inputs["dn_w_out"][0])
    shared.update(host_consts())
    return shared


def run(inputs, layers, n_cores, dbg=False):
    key = (tuple(layers), dbg)
    if key not in _cache:
        _cache[key] = build_program(layers, dbg)
    nc = _cache[key]
    shared = _prep_inputs(inputs)
    x = np.asarray(inputs["x"], dtype=np.float32)
    c = np.asarray(inputs["c"], dtype=np.float32)
    ctx = np.asarray(inputs["ctx"], dtype=np.float32)
    in_maps = []
    for i in range(n_cores):
        m = dict(shared)
        m["x"] = np.ascontiguousarray(x[i * NB:(i + 1) * NB])
        m["c"] = np.ascontiguousarray(c[i * NB:(i + 1) * NB])
        m["ctx"] = np.ascontiguousarray(ctx[i * NB:(i + 1) * NB])
        in_maps.append(m)
    res = run_bass_kernel_spmd(nc, in_maps, core_ids=list(range(n_cores)))
    y = np.concatenate([r["y"] for r in res.results], axis=0)
    if dbg:
        return y, np.concatenate([r["ctx_out"] for r in res.results], axis=0)
    return y


def kernel(**inputs):
    return run(inputs, LAYERS, 8).astype(np.float32)
```

```python
import os
import numpy as np
from contextlib import ExitStack
import concourse.bass as bass
import concourse.mybir as mybir
from concourse.bass_utils import run_bass_kernel_spmd

F32 = mybir.dt.float32
BF16 = mybir.dt.bfloat16
ALU = mybir.AluOpType
AF = mybir.ActivationFunctionType
AX = mybir.AxisListType

D = 1024
T = 2048
TC = 256
NB = 2
DFF = 4096
EPS = 1e-6
NT = 18
TT = NB * NT
SEQ = TC + T


class Reg:
    __slots__ = ("w", "r", "excl")

    def __init__(self, excl=False):
        self.w = {}
        self.r = {}
        self.excl = excl


class Rot:
    def __init__(self, items):
        self.items = items
        self.i = 0

    def next(self):
        it = self.items[self.i % len(self.items)]
        self.i += 1
        return it


class KB:
    def __init__(self, nc):
        self.nc = nc
        self.eh = {"pe": nc.tensor, "dve": nc.vector, "act": nc.scalar, "pool": nc.gpsimd, "sp": nc.sync}
        self.esem = {k: nc.alloc_semaphore("es_" + k) for k in self.eh}
        self.ecnt = {k: 0 for k in self.eh}
        self.waited = {k: {} for k in self.eh}
        self.dpool = {q: [[nc.alloc_semaphore("ds_%s%d" % (q, i)), 0] for i in range(n)]
                      for q, n in (("sp", 32), ("pool", 16), ("act", 8))}
        self.dnext = {q: 0 for q in self.dpool}
        self.uid = 0

    def name(self, base):
        self.uid += 1
        return "%s_%d" % (base, self.uid)

    def sb(self, es, base, shape, dt):
        return es.enter_context(self.nc.sbuf_tensor(self.name(base), list(shape), dt))

    def rot(self, es, base, shape, dt, n):
        return Rot([(self.sb(es, base, shape, dt), Reg()) for _ in range(n)])

    def _wait(self, e, sem, val):
        w = self.waited[e]
        if w.get(sem.num, 0) < val:
            self.eh[e].wait_ge(sem, val)
            w[sem.num] = val

    def _need(self, e, reads, writes, is_dma):
        own = None if is_dma else self.esem[e].num
        need = {}

        def add(d, skip_same):
            for num, (sem, val) in d.items():
                if num == own and (skip_same or e == "pe"):
                    continue
                if need.get(num, (None, 0))[1] < val:
                    need[num] = (sem, val)
        for r in reads:
            add(r.w, False)
            if r.excl:
                add(r.r, True)
        for r in writes:
            add(r.w, True)
            add(r.r, True)
        return need

    def _mark(self, tok, reads, writes):
        num = tok[0].num
        for r in reads:
            r.r[num] = tok
        for r in writes:
            r.w = {num: tok}
            r.r = {}

    def op(self, e, fn, reads, writes):
        need = self._need(e, reads, writes, False)
        for sem, val in need.values():
            self._wait(e, sem, val)
        inst = fn(self.eh[e])
        self.ecnt[e] += 1
        inst.then_inc(self.esem[e], 1)
        self._mark((self.esem[e], self.ecnt[e]), reads, writes)

    def dma(self, q, out, in_, reads, writes, **kw):
        pool = self.dpool[q]
        i = self.dnext[q]
        self.dnext[q] = (i + 1) % len(pool)
        sem, val = pool[i]
        need = self._need(q, reads, writes, True)
        if val > 0:
            need[sem.num] = (sem, val)
        for s, v in need.values():
            self._wait(q, s, v)
        inst = self.eh[q].dma_start(out=out, in_=in_, **kw)
        inst.then_inc(sem, 16)
        pool[i][1] = val + 16
        self._mark((sem, val + 16), reads, writes)

    def barrier(self):
        toks = [(self.esem[e], self.ecnt[e]) for e in self.eh if self.ecnt[e] > 0]
        toks += [(s, v) for p in self.dpool.values() for (s, v) in p if v > 0]
        for e in self.eh:
            for s, v in toks:
                if s.num != self.esem[e].num:
                    self._wait(e, s, v)

    def mm(self, out, pairs, reads, writes):
        n = len(pairs)

        def fn(h):
            inst = None
            for i, (l, r) in enumerate(pairs):
                inst = h.matmul(out, l, r, start=(i == 0), stop=(i == n - 1))
            return inst
        self.op("pe", fn, reads, writes)

    def mm1(self, out, l, r, start, stop, reads, writes):
        self.op("pe", lambda h: h.matmul(out, l, r, start=start, stop=stop), reads, writes)

    def tr(self, outs_ins, ident, reads, writes):
        def fn(h):
            inst = None
            for o, i in outs_ins:
                inst = h.transpose(o, i, ident)
            return inst
        self.op("pe", fn, reads, writes)

    def act(self, out, in_, func, reads, writes, **kw):
        self.op("act", lambda h: h.activation(out=out, in_=in_, func=func, **kw), reads, writes)

    def ts(self, e, out, in0, s1, s2, op0, op1, reads, writes):
        if s2 is None:
            self.op(e, lambda h: h.tensor_scalar(out=out, in0=in0, scalar1=s1, scalar2=None, op0=op0), reads, writes)
        else:
            self.op(e, lambda h: h.tensor_scalar(out=out, in0=in0, scalar1=s1, scalar2=s2, op0=op0, op1=op1),
                    reads, writes)

    def tt(self, e, out, in0, in1, op, reads, writes):
        self.op(e, lambda h: h.tensor_tensor(out=out, in0=in0, in1=in1, op=op), reads, writes)

    def stt(self, e, out, in0, scalar, in1, op0, op1, reads, writes):
        self.op(e, lambda h: h.scalar_tensor_tensor(out=out, in0=in0, scalar=scalar, in1=in1, op0=op0, op1=op1),
                reads, writes)

    def cp(self, e, out, in_, reads, writes):
        if e == "act":
            self.op(e, lambda h: h.copy(out=out, in_=in_), reads, writes)
        else:
            self.op(e, lambda h: h.tensor_copy(out=out, in_=in_), reads, writes)

    def memset(self, e, ap, val, writes):
        self.op(e, lambda h: h.memset(ap, val), [], writes)

    def recip(self, out, in_, reads, writes):
        self.op("dve", lambda h: h.reciprocal(out=out, in_=in_), reads, writes)


class Stream:
    def __init__(self, xap, cap):
        self.x = xap
        self.c = cap
        self.regs = [Reg() for _ in range(TT)]

    def tile(self, tt):
        b, r = divmod(tt, NT)
        if r < 2:
            return self.c[b, r * 128:(r + 1) * 128, :]
        return self.x[b, (r - 2) * 128:(r - 1) * 128, :]


def tile_slot(tt):
    b, r = divmod(tt, NT)
    return 2 if r < 2 else b


class G:
    pass


def build_program(layers, dbg=False):
    nc = bass.Bass("TRN2", target_bir_lowering=False)
    kb = KB(nc)
    g = G()
    g.nc, g.kb = nc, kb
    g.dbg = dbg
    L = 4

    def din(name, shape, dt=F32):
        return nc.dram_tensor(name, list(shape), dt, kind="ExternalInput").ap()

    def dscr(name, shape, dt=F32):
        return nc.dram_tensor(name, list(shape), dt, kind="Internal").ap()

    I = {}
    I["x"] = din("x", [NB, T, D])
    I["c"] = din("c", [NB, D])
    I["ctx"] = din("ctx", [NB, TC, D])
    I["c_ctx"] = din("c_ctx", [1, D])
    I["ada_w"] = din("ada_w", [L, D, 6 * D])
    I["ada_b"] = din("ada_b", [L, 6 * D])
    for nm in ("mix_pre_g", "mix_post_g", "mlp_pre_g", "mlp_post_g"):
        I[nm] = din(nm, [L, D])
    I["mlp_w1"] = din("mlp_w1", [L, D, DFF])
    I["mlp_w2"] = din("mlp_w2", [L, DFF, D])
    I["ret_w_in"] = din("ret_w_in", [D, 8192])
    I["ret_decay_logit"] = din("ret_decay_logit", [1, 16])
    I["ret_w_out"] = din("ret_w_out", [2048, D])
    I["att_w_in"] = din("att_w_in", [D, 1536])
    I["att_sink"] = din("att_sink", [1, 16])
    I["att_w_out"] = din("att_w_out", [D, D])
    I["pool_w"] = din("pool_w", [4, 256, 256])
    I["pool_b"] = din("pool_b", [1, D])
    I["pool_scale"] = din("pool_scale", [1, D])
    I["dn_w_in"] = din("dn_w_in", [D, 5152])
    I["dn_conv_w"] = din("dn_conv_w", [5, 3072])
    I["dn_a_log"] = din("dn_a_log", [1, 16])
    I["dn_dt_bias"] = din("dn_dt_bias", [1, 16])
    I["dn_norm_g"] = din("dn_norm_g", [1, 128])
    I["dn_w_out"] = din("dn_w_out", [D, D])
    I["k_ident"] = din("k_ident", [128, 128])
    I["k_pool"] = din("k_pool", [20, 128, 128])
    I["k_rope_att"] = din("k_rope_att", [2, 128, SEQ])
    I["k_rope_ret"] = din("k_rope_ret", [2, 128, SEQ])
    I["k_att_mask"] = din("k_att_mask", [3, 128, 384])
    I["k_ret_tab"] = din("k_ret_tab", [5, 128, 128])
    I["k_dn"] = din("k_dn", [4, 64, 64])
    g.I = I
    yout = nc.dram_tensor("y", [NB, T, D], F32, kind="ExternalOutput").ap()

    g.modD = dscr("modD", [L, 3, 6 * D])
    s_in = Stream(I["x"], I["ctx"])
    s1 = Stream(dscr("s1x", [NB, T, D]), dscr("s1c", [NB, TC, D]))
    s2 = Stream(dscr("s2x", [NB, T, D]), dscr("s2c", [NB, TC, D]))
    s_out = Stream(yout, s2.c)
    g.modD_r = Reg()

    g.ps = [nc.alloc_psum_tensor("psb%d" % i, [128, 512], F32) for i in range(8)]
    g.psr = [Reg(excl=True) for _ in range(8)]

    with ExitStack() as ges:
        g.ident_f = kb.sb(ges, "identf", [128, 128], F32)
        g.ident_b = kb.sb(ges, "identb", [128, 128], BF16)
        g.eps = kb.sb(ges, "eps", [128, 1], F32)
        g.const_r = Reg()
        kb.dma("sp", g.ident_f[:], I["k_ident"][:, :], [], [g.const_r])
        kb.dma("pool", g.ident_b[:], I["k_ident"][:, :], [], [g.const_r])
        kb.memset("dve", g.eps[:], EPS, [g.const_r])
        g.eps128 = kb.sb(ges, "eps128", [128, 1], F32)
        kb.memset("dve", g.eps128[:], EPS * 128.0, [g.const_r])
        kb.barrier()

        prologue_ada(g, layers)
        kb.barrier()

        cur = s_in
        for li, l in enumerate(layers):
            last = (li == len(layers) - 1)
            need_ctx = (l < 3) or dbg
            kind = l % 4
            if kind == 2:
                mixer_pool(g, l, cur, s1, need_ctx)
            elif kind == 1:
                mixer_att(g, l, cur, s1, need_ctx)
            elif kind == 0:
                mixer_ret(g, l, cur, s1, need_ctx)
            elif kind == 3:
                mixer_dn(g, l, cur, s1, need_ctx)
            else:
                raise NotImplementedError
            kb.barrier()
            dst = s_out if last else s2
            ffn(g, l, s1, dst, need_ctx)
            kb.barrier()
            cur = s2
        if dbg:
            cout = nc.dram_tensor("ctx_out", [NB, TC, D], F32, kind="ExternalOutput").ap()
            for b in range(NB):
                kb.dma("sp", cout[b], s2.c[b], [s2.regs[b * NT], s2.regs[b * NT + 1], s_out.regs[b * NT],
                                                s_out.regs[b * NT + 1]], [Reg()])
    kb.barrier()
    return nc


def prologue_ada(g, layers):
    kb, nc, I = g.kb, g.nc, g.I
    with ExitStack() as es:
        condT = kb.sb(es, "condT", [128, 8, 4], F32)
        cr = Reg()
        for s in range(3):
            src = I["c"][s, :] if s < 2 else I["c_ctx"][0, :]
            kb.dma("sp", condT[:, :, s], src.rearrange("(c p) -> p c", p=128), [], [cr],
                   allow_slow_non_contiguous=True)
        kb.memset("dve", condT[:, :, 3], 0.0, [cr])
        kb.act(condT[:, :, 0:3], condT[:, :, 0:3], AF.Silu, [cr], [cr])
        wrot = kb.rot(es, "adaw", [128, 8, 512], F32, 3)
        modrow = kb.sb(es, "modrow", [3, 6 * D], F32)
        mr = Reg()
        bias = kb.sb(es, "adab", [3, 6 * D], F32)
        gains = kb.sb(es, "gains", [3, 4, D], F32)
        br = Reg()
        for l in layers:
            kb.dma("sp", bias[:], I["ada_b"][l, :].partition_broadcast(3), [], [br])
            for gi, nm in enumerate(("mix_pre_g", "mix_post_g", "mlp_pre_g", "mlp_post_g")):
                kb.dma("sp", gains[:, gi, :], I[nm][l, :].partition_broadcast(3), [], [br])
            for j in range(12):
                wt, wr = wrot.next()
                kb.dma("sp", wt[:], I["ada_w"][l, :, j * 512:(j + 1) * 512].rearrange("(c p) n -> p c n", p=128),
                       [], [wr])
                pb = j % 2
                kb.mm(g.ps[pb][0:3, :], [(condT[:, kc, 0:3], wt[:, kc, :]) for kc in range(8)],
                      [cr, wr], [g.psr[pb]])
                kb.tt("dve", modrow[:, j * 512:(j + 1) * 512], g.ps[pb][0:3, :], bias[:, j * 512:(j + 1) * 512],
                      ALU.add, [g.psr[pb], br], [mr])
            for seg, gi, plus1 in ((1, 0, True), (2, 1, False), (4, 2, True), (5, 3, False)):
                sl = modrow[:, seg * D:(seg + 1) * D]
                if plus1:
                    kb.stt("dve", sl, sl, 1.0, gains[:, gi, :], ALU.add, ALU.mult, [mr, br], [mr])
                else:
                    kb.tt("dve", sl, sl, gains[:, gi, :], ALU.mult, [mr, br], [mr])
            kb.dma("sp", g.modD[l], modrow[:], [mr], [g.modD_r])


def load_cols(g, es, l, segA, segB):
    kb = g.kb
    r = Reg()
    outs = []
    for seg in (segA, segB):
        t = kb.sb(es, "mcol", [128, 3, 8], F32)
        for s in range(3):
            kb.dma("sp", t[:, s, :], g.modD[l, s, seg * D:(seg + 1) * D].rearrange("(c p) -> p c", p=128),
                   [g.modD_r], [r], allow_slow_non_contiguous=True)
        outs.append(t)
    return outs[0], outs[1], r


def load_bc(g, es, l, seg):
    kb = g.kb
    out = []
    r = Reg()
    for s in range(3):
        t = kb.sb(es, "mbc", [128, D], F32)
        kb.dma("sp", t[:], g.modD[l, s, seg * D:(seg + 1) * D].partition_broadcast(128), [g.modD_r], [r])
        out.append(t)
    return out, r


class NormWS:
    def __init__(self, g, es, nbuf=3):
        kb = g.kb
        self.st = kb.rot(es, "nst", [128, 4], F32, 4)
        self.junk = kb.sb(es, "njunk", [128, D], BF16)
        self.junk_r = Reg()
        self.xn = kb.rot(es, "nxn", [128, D], BF16, 2)


def rms_stats(g, ws, y_aps, y_regs, isn=1.0 / 32.0):
    kb = g.kb
    st, sr = ws.st.next()
    kb.memset("pool", st[:], 0.0, [sr])
    off = 0
    for i, ya in enumerate(y_aps):
        n = ya.shape[-1]
        kb.act(ws.junk[:, off:off + n], ya, AF.Square, y_regs + [sr], [ws.junk_r, sr], scale=isn,
               accum_out=st[:, i:i + 1])
        off += n
    if len(y_aps) == 2:
        kb.tt("dve", st[:, 0:1], st[:, 0:1], st[:, 1:2], ALU.add, [sr], [sr])
    kb.act(st[:, 1:2], st[:, 0:1], AF.Sqrt, [sr, g.const_r], [sr], bias=g.eps[:, 0:1], scale=1.0)
    kb.recip(st[:, 2:3], st[:, 1:2], [sr], [sr])
    return st, sr


def norm_T(g, ws, xt, xr, Acol, Bcol, colr, slot, dst, dst_r, pbank):
    kb = g.kb
    st, sr = rms_stats(g, ws, [xt], [xr])
    xn, xnr = ws.xn.next()
    kb.act(xn[:], xt, AF.Copy, [xr, sr], [xnr], scale=st[:, 2:3])
    psT = g.ps[pbank][:].bitcast(BF16).rearrange("p (c t) -> p c t", c=8)
    kb.tr([(psT[:, c, :], xn[:, c * 128:(c + 1) * 128]) for c in range(8)], g.ident_b[:],
          [xnr, g.const_r], [g.psr[pbank]])
    for c in range(8):
        kb.ts("dve", dst[:, c, :], psT[:, c, :], Acol[:, slot, c:c + 1], Bcol[:, slot, c:c + 1], ALU.mult, ALU.add,
              [g.psr[pbank], colr], [dst_r])


def post_res(g, ws, y_aps, y_regs, xt, xr, Gbc, gr, tmp, tmpr):
    kb = g.kb
    st, sr = rms_stats(g, ws, y_aps, y_regs)
    off = 0
    for ya in y_aps:
        n = ya.shape[-1]
        kb.stt("dve", tmp[:, off:off + n], ya, st[:, 2:3], Gbc[:, off:off + n], ALU.mult, ALU.mult,
               y_regs + [sr, gr], [tmpr])
        off += n
    kb.tt("pool", xt, xt, tmp[:, :], ALU.add, [xr, tmpr], [xr])


def ffn(g, l, src, dst, need_ctx):
    kb, nc, I = g.kb, g.nc, g.I
    with ExitStack() as es:
        w1 = kb.sb(es, "w1", [128, 8, DFF], BF16)
        w2 = kb.sb(es, "w2", [128, 32, D], BF16)
        w1r = [Reg() for _ in range(8)]
        w2r = [Reg() for _ in range(8)]
        for kc in range(8):
            for hf in range(2):
                kb.dma("pool", w1[:, kc, hf * 2048:(hf + 1) * 2048],
                       I["mlp_w1"][l, kc * 128:(kc + 1) * 128, hf * 2048:(hf + 1) * 2048], [], [w1r[kc]])
        for q in range(8):
            kb.dma("pool", w2[:, q * 4:(q + 1) * 4, :],
                   I["mlp_w2"][l, q * 512:(q + 1) * 512, :].rearrange("(c p) n -> p c n", p=128), [], [w2r[q]])
        Acol, Bcol, acr = load_cols(g, es, l, 4, 3)
        Gbc, gbr = load_bc(g, es, l, 5)
        ws = NormWS(g, es)
        xrot = kb.rot(es, "fx", [128, D], F32, 6)
        uT = kb.rot(es, "fuT", [128, 8, 256], BF16, 2)
        hT = kb.sb(es, "fhT", [128, 32, 256], BF16)
        hTr = [Reg() for _ in range(32)]
        rl = kb.rot(es, "frl", [128, 256], F32, 4)
        tmp = kb.rot(es, "ftmp", [128, D], F32, 2)
        tiles = [tt for tt in range(TT) if need_ctx or (tt % NT) >= 2]
        groups = [tiles[i:i + 2] for i in range(0, len(tiles), 2)]
        hslot = 0

        def prep(grp):
            u, ur = uT.next()
            xs = []
            for j, tt in enumerate(grp):
                xt, xr = xrot.next()
                kb.dma("sp", xt[:], src.tile(tt), [src.regs[tt]], [xr])
                norm_T(g, ws, xt[:], xr, Acol, Bcol, acr, tile_slot(tt), u[:, :, j * 128:(j + 1) * 128], ur, 4)
                xs.append((xt, xr))
            return u, ur, xs
        nxt = prep(groups[0])
        for gi_, grp in enumerate(groups):
            u, ur, xs = nxt
            for fc in range(32):
                hb = 5 + (hslot // 2) % 2
                hh = hslot % 2
                hslot += 1
                hp = g.ps[hb][:, hh * 256:(hh + 1) * 256]
                kb.mm(hp, [(w1[:, kc, fc * 128:(fc + 1) * 128], u[:, kc, :]) for kc in range(8)],
                      [ur] + w1r, [g.psr[hb]])
                r_, rr = rl.next()
                kb.act(r_[:], hp, AF.Relu, [g.psr[hb]], [rr])
                kb.tt("dve", hT[:, fc, :], r_[:], r_[:], ALU.mult, [rr], [hTr[fc]])
            if gi_ + 1 < len(groups):
                nxt = prep(groups[gi_ + 1])
            for j, tt in enumerate(grp):
                xt, xr = xs[j]
                for half in range(2):
                    pb = j * 2 + half
                    kb.mm(g.ps[pb][:, :], [(hT[:, fc, j * 128:(j + 1) * 128], w2[:, fc, half * 512:(half + 1) * 512])
                                           for fc in range(32)], hTr + w2r, [g.psr[pb]])
                t_, tr_ = tmp.next()
                post_res(g, ws, [g.ps[j * 2][:, :], g.ps[j * 2 + 1][:, :]], [g.psr[j * 2], g.psr[j * 2 + 1]],
                         xt[:], xr, Gbc[tile_slot(tt)], gbr, t_, tr_)
                kb.dma("sp", dst.tile(tt), xt[:], [xr], [dst.regs[tt]])


def mixer_pool(g, l, src, dst, need_ctx):
    kb, nc, I = g.kb, g.nc, g.I
    with ExitStack() as es:
        Acol, Bcol, acr = load_cols(g, es, l, 1, 0)
        Gbc, gbr = load_bc(g, es, l, 2)
        ws = NormWS(g, es)
        wg = kb.sb(es, "pw", [128, 4, 2, 256], BF16)
        wgr = Reg()
        for gi in range(4):
            kb.dma("pool", wg[:, gi, :, :], I["pool_w"][gi].rearrange("(c p) n -> p c n", p=128), [], [wgr])
        pm = kb.sb(es, "pm", [128, 20, 128], BF16)
        kb.dma("pool", pm[:], I["k_pool"].rearrange("m p n -> p m n"), [], [wgr])
        pbb = kb.sb(es, "pbb", [128, D], F32)
        psb = kb.sb(es, "psb", [128, D], F32)
        kb.dma("sp", pbb[:], I["pool_b"][0, :].partition_broadcast(128), [], [wgr])
        kb.dma("sp", psb[:], I["pool_scale"][0, :].partition_broadcast(128), [], [wgr])
        xrot = kb.rot(es, "px", [128, D], F32, 6)
        uT = kb.rot(es, "puT", [128, 8, 128], BF16, 2)
        vrot = kb.rot(es, "pv", [128, D], BF16, 5)
        yb = kb.rot(es, "pyb", [128, D], F32, 2)
        tmp = kb.rot(es, "ptmp", [128, D], F32, 2)
        for b in range(NB):
            for seg_lo, seg_n in ((0, 2), (2, 16)):
                if seg_lo == 0 and not need_ctx:
                    continue
                xs, vs = {}, {}
                for t in range(seg_n + 1):
                    if t < seg_n:
                        tt = b * NT + seg_lo + t
                        xt, xr = xrot.next()
                        kb.dma("sp", xt[:], src.tile(tt), [src.regs[tt]], [xr])
                        u, ur = uT.next()
                        norm_T(g, ws, xt[:], xr, Acol, Bcol, acr, tile_slot(tt), u, ur, 4)
                        for gi in range(4):
                            pb = 5 + gi // 2
                            kb.mm(g.ps[pb][:, (gi % 2) * 256:(gi % 2 + 1) * 256],
                                  [(u[:, gi * 2 + kc, :], wg[:, gi, kc, :]) for kc in range(2)],
                                  [ur, wgr], [g.psr[pb]])
                        v, vr = vrot.next()
                        kb.cp("act", v[:, 0:512], g.ps[5][:, :], [g.psr[5]], [vr])
                        kb.cp("act", v[:, 512:1024], g.ps[6][:, :], [g.psr[6]], [vr])
                        xs[t], vs[t] = (xt, xr), (v, vr)
                    if t >= 1:
                        tq = t - 1
                        tt = b * NT + seg_lo + tq
                        for gi in range(4):
                            pb = gi // 2
                            pairs = []
                            regs = [wgr]
                            cidx = 0 if (0 < tq < seg_n - 1) else (1 if tq == 0 else 2)
                            pairs.append((pm[:, gi * 5 + cidx, :], vs[tq][0][:, gi * 256:(gi + 1) * 256]))
                            regs.append(vs[tq][1])
                            if tq > 0:
                                pairs.append((pm[:, gi * 5 + 3, :], vs[tq - 1][0][:, gi * 256:(gi + 1) * 256]))
                                regs.append(vs[tq - 1][1])
                            if tq < seg_n - 1:
                                pairs.append((pm[:, gi * 5 + 4, :], vs[tq + 1][0][:, gi * 256:(gi + 1) * 256]))
                                regs.append(vs[tq + 1][1])
                            kb.mm(g.ps[pb][:, (gi % 2) * 256:(gi % 2 + 1) * 256], pairs, regs, [g.psr[pb]])
                        y_, yr = yb.next()
                        for h in range(2):
                            kb.tt("dve", y_[:, h * 512:(h + 1) * 512], g.ps[h][:, :], pbb[:, h * 512:(h + 1) * 512],
                                  ALU.add, [g.psr[h], wgr], [yr])
                        kb.tt("pool", y_[:], y_[:], psb[:], ALU.mult, [yr, wgr], [yr])
                        xt, xr = xs[tq]
                        t_, tr_ = tmp.next()
                        post_res(g, ws, [y_[:, :]], [yr], xt[:], xr, Gbc[tile_slot(tt)], gbr, t_, tr_)
                        kb.dma("sp", dst.tile(tt), xt[:], [xr], [dst.regs[tt]])


def phase_uT(g, es, l, src):
    kb = g.kb
    uT = kb.sb(es, "uTall", [128, 8, TT * 128], BF16)
    regs = [Reg() for _ in range(TT)]
    with ExitStack() as es2:
        Acol, Bcol, acr = load_cols(g, es2, l, 1, 0)
        ws = NormWS(g, es2)
        xrot = kb.rot(es2, "ux", [128, D], F32, 3)
        for tt in range(TT):
            xt, xr = xrot.next()
            kb.dma("sp", xt[:], src.tile(tt), [src.regs[tt]], [xr])
            norm_T(g, ws, xt[:], xr, Acol, Bcol, acr, tile_slot(tt), uT[:, :, tt * 128:(tt + 1) * 128], regs[tt],
                   4 + tt % 2)
        kb.barrier()
    return uT, regs


def load_w_bf16(g, rot, wd, c0, ncols):
    kb = g.kb
    wt, wr = rot.next()
    kb.dma("pool", wt[:, :, 0:ncols], wd[:, c0:c0 + ncols].rearrange("(c p) n -> p c n", p=128), [], [wr])
    return wt, wr


def phase_outproj(g, l, a_d, a_regs, Kd, wout_d, src, dst, need_ctx):
    kb, nc = g.kb, g.nc
    nk = Kd // 128
    with ExitStack() as es:
        Gbc, gbr = load_bc(g, es, l, 2)
        ws = NormWS(g, es)
        wo = kb.sb(es, "wo", [128, nk, D], BF16)
        wor = Reg()
        for q in range(nk // 4):
            kb.dma("pool", wo[:, q * 4:(q + 1) * 4, :],
                   wout_d[q * 512:(q + 1) * 512, :].rearrange("(c p) n -> p c n", p=128), [], [wor])
        a_list = a_d if isinstance(a_d, list) else [a_d]
        arot = kb.rot(es, "oa", [128, Kd], BF16, 3 * len(a_list))
        aTrot = kb.rot(es, "oaT", [128, nk, 128], BF16, 2)
        xrot = kb.rot(es, "ox", [128, D], F32, 3)
        tmp = kb.rot(es, "otmp", [128, D], F32, 2)
        it = 0
        for tt in range(TT):
            if not need_ctx and (tt % NT) < 2:
                continue
            at, ar = arot.next()
            kb.dma("sp", at[:], a_list[0][tt * 128:(tt + 1) * 128, :], [a_regs[tt]], [ar])
            for extra in a_list[1:]:
                at2, ar2 = arot.next()
                kb.dma("sp", at2[:], extra[tt * 128:(tt + 1) * 128, :], [a_regs[tt]], [ar2])
                kb.tt("pool", at[:], at[:], at2[:], ALU.add, [ar, ar2], [ar])
            xt, xr = xrot.next()
            kb.dma("sp", xt[:], src.tile(tt), [src.regs[tt]], [xr])
            aT, aTr = aTrot.next()
            for q in range(nk // 8):
                pb = 4 + (it % 2)
                it += 1
                psT = g.ps[pb][:].bitcast(BF16).rearrange("p (c t) -> p c t", c=8)
                kb.tr([(psT[:, c, :], at[:, (q * 8 + c) * 128:(q * 8 + c + 1) * 128]) for c in range(8)],
                      g.ident_b[:], [ar, g.const_r], [g.psr[pb]])
                kb.cp("act", aT[:, q * 8:(q + 1) * 8, :], psT, [g.psr[pb]], [aTr])
            yb = (tt % 2) * 2
            for half in range(2):
                kb.mm(g.ps[yb + half][:, :], [(aT[:, kc, :], wo[:, kc, half * 512:(half + 1) * 512])
                                              for kc in range(nk)], [aTr, wor], [g.psr[yb + half]])
            t_, tr_ = tmp.next()
            post_res(g, ws, [g.ps[yb][:, :], g.ps[yb + 1][:, :]], [g.psr[yb], g.psr[yb + 1]], xt[:], xr,
                     Gbc[tile_slot(tt)], gbr, t_, tr_)
            kb.dma("sp", dst.tile(tt), xt[:], [xr], [dst.regs[tt]])


def make_perm(g, wt, wr, wp, wpr, ncols, blk):
    kb = g.kb
    v_in = wt[:, :, 0:ncols].rearrange("p c (q two i) -> p c q two i", two=2, i=blk)
    v_out = wp[:, :, 0:ncols].rearrange("p c (q two i) -> p c q two i", two=2, i=blk)
    for kc in range(8):
        kb.cp("pool", v_out[:, kc, :, 0, :], v_in[:, kc, :, 1, :], [wr], [wpr])
        kb.cp("pool", v_out[:, kc, :, 1, :], v_in[:, kc, :, 0, :], [wr], [wpr])


SEQ_BLOCKS = [(b, o, min(512, SEQ - o)) for b in range(NB) for o in range(0, SEQ, 512)]


def proj_fm_rope(g, uT, uTr, wd, c0, rope, rope_r, scale, blk, out_d, out_r, wrot, wprot, stg, pbase):
    kb = g.kb
    wt, wr = load_w_bf16(g, wrot, wd, c0, 128)
    wp, wpr = wprot.next()
    make_perm(g, wt, wr, wp, wpr, 128, blk)
    for bi, (b, o, n) in enumerate(SEQ_BLOCKS):
        t0 = b * SEQ + o
        tiles = list(range(t0 // 128, (t0 + n) // 128))
        rr = [uTr[t] for t in tiles]
        pa, pb = pbase + (bi % 2) * 2, pbase + (bi % 2) * 2 + 1
        kb.mm(g.ps[pa][:, 0:n], [(wt[:, kc, 0:128], uT[:, kc, t0:t0 + n]) for kc in range(8)], rr + [wr], [g.psr[pa]])
        kb.mm(g.ps[pb][:, 0:n], [(wp[:, kc, 0:128], uT[:, kc, t0:t0 + n]) for kc in range(8)], rr + [wpr], [g.psr[pb]])
        (t1, t1r), (t2, t2r), (t3, t3r) = stg[0].next(), stg[1].next(), stg[2].next()
        kb.stt("dve", t1[:, 0:n], g.ps[pa][:, 0:n], scale, rope[:, 0, o:o + n], ALU.mult, ALU.mult,
               [g.psr[pa], rope_r], [t1r])
        kb.stt("dve", t2[:, 0:n], g.ps[pb][:, 0:n], scale, rope[:, 1, o:o + n], ALU.mult, ALU.mult,
               [g.psr[pb], rope_r], [t2r])
        kb.tt("pool", t3[:, 0:n], t1[:, 0:n], t2[:, 0:n], ALU.add, [t1r, t2r], [t3r])
        kb.dma("sp", out_d[:, t0:t0 + n], t3[:, 0:n], [t3r], [out_r])


def proj_tm(g, uT, uTr, wd, c0, ncols, out_d, out_regs, col0, wrot, stg, pbase, func=None, tiles=None):
    kb = g.kb
    wt, wr = load_w_bf16(g, wrot, wd, c0, ncols)
    for i, tt in enumerate(tiles if tiles is not None else range(TT)):
        pb = pbase + i % 2
        kb.mm(g.ps[pb][:, 0:ncols], [(uT[:, kc, tt * 128:(tt + 1) * 128], wt[:, kc, 0:ncols]) for kc in range(8)],
              [uTr[tt], wr], [g.psr[pb]])
        st_, sr_ = stg.next()
        if func is None:
            kb.cp("act", st_[:, 0:ncols], g.ps[pb][:, 0:ncols], [g.psr[pb]], [sr_])
        else:
            kb.act(st_[:, 0:ncols], g.ps[pb][:, 0:ncols], func, [g.psr[pb]], [sr_])
        kb.dma("sp", out_d[tt * 128:(tt + 1) * 128, col0:col0 + ncols], st_[:, 0:ncols], [sr_], [out_regs[tt]])


def mixer_att(g, l, src, dst, need_ctx):
    kb, nc, I = g.kb, g.nc, g.I
    NTOK = TT * 128
    attq = nc.dram_tensor(kb.name("attq"), [8, 128, NTOK], BF16, kind="Internal").ap()
    attk = nc.dram_tensor(kb.name("attk"), [2, 128, NTOK], BF16, kind="Internal").ap()
    attv = nc.dram_tensor(kb.name("attv"), [NTOK, 256], BF16, kind="Internal").ap()
    atta = nc.dram_tensor(kb.name("atta"), [NTOK, D], BF16, kind="Internal").ap()
    qr_, kr_ = [Reg() for _ in range(8)], [Reg() for _ in range(2)]
    vr_ = [Reg() for _ in range(TT)]
    ar_ = [Reg() for _ in range(TT)]
    with ExitStack() as es:
        uT, uTr = phase_uT(g, es, l, src)
        rope = kb.sb(es, "ropeA", [128, 2, SEQ], F32)
        rope_r = Reg()
        kb.dma("sp", rope[:], I["k_rope_att"].rearrange("t p n -> p t n"), [], [rope_r])
        wrot = kb.rot(es, "aw", [128, 8, 256], BF16, 2)
        wprot = kb.rot(es, "awp", [128, 8, 128], BF16, 2)
        stg = [kb.rot(es, "astg", [128, 512], F32, 2), kb.rot(es, "astg", [128, 512], F32, 2),
               kb.rot(es, "astgb", [128, 512], BF16, 3)]
        for cb in range(8):
            proj_fm_rope(g, uT, uTr, I["att_w_in"], cb * 128, rope, rope_r, 0.125, 16, attq[cb], qr_[cb],
                         wrot, wprot, stg, 0)
        for cb in range(2):
            proj_fm_rope(g, uT, uTr, I["att_w_in"], 1024 + cb * 128, rope, rope_r, 1.0, 16, attk[cb], kr_[cb],
                         wrot, wprot, stg, 0)
        proj_tm(g, uT, uTr, I["att_w_in"], 1280, 256, attv, vr_, 0, wrot, stg[2], 0)
    kb.barrier()
    with ExitStack() as es:
        mb = kb.sb(es, "amask", [128, 3, 384], BF16)
        cr = Reg()
        kb.dma("pool", mb[:], I["k_att_mask"].rearrange("v p n -> p v n"), [], [cr])
        sink = kb.sb(es, "asink", [128, 16], F32)
        kb.dma("sp", sink[:], I["att_sink"][0, :].partition_broadcast(128), [], [cr])
        Krot = kb.rot(es, "aK", [64, 2560], BF16, 2)
        Vrot = kb.rot(es, "aV", [128, 20, 64], BF16, 2)
        for kt, kr in Krot.items:
            kb.memset("pool", kt[:, 0:128], 0.0, [kr])
            kb.memset("pool", kt[:, 2176:2304], 0.0, [kr])
        for vt, vr in Vrot.items:
            kb.memset("pool", vt[:, 0, :], 0.0, [vr])
            kb.memset("pool", vt[:, 17, :], 0.0, [vr])
        Qrot = kb.rot(es, "aQ", [64, SEQ], BF16, 3)
        prot = kb.rot(es, "ap", [128, 640], BF16, 3)
        pTrot = kb.rot(es, "apT", [128, 5, 128], BF16, 3)
        strot = kb.rot(es, "ast", [128, 8], F32, 6)
        atile = kb.rot(es, "aat", [128, 18, 256], BF16, 2)
        it = 0
        for b in range(NB):
            for kv in range(4):
                Kt, Kr = Krot.next()
                Vt, Vr = Vrot.next()
                base = b * SEQ
                ksrc = attk[kv // 2, (kv % 2) * 64:(kv % 2) * 64 + 64, :]
                kb.dma("sp", Kt[:, 128:2176], ksrc[:, base + TC:base + SEQ], [kr_[kv // 2]], [Kr])
                kb.dma("sp", Kt[:, 2304:2560], ksrc[:, base:base + TC], [kr_[kv // 2]], [Kr])
                vsrc = attv[:, kv * 64:(kv + 1) * 64]
                kb.dma("sp", Vt[:, 1:17, :], vsrc[base + TC:base + SEQ, :].rearrange("(t p) d -> p t d", p=128),
                       [vr_[b * NT + t] for t in range(2, 18)], [Vr])
                kb.dma("sp", Vt[:, 18:20, :], vsrc[base:base + TC, :].rearrange("(t p) d -> p t d", p=128),
                       [vr_[b * NT], vr_[b * NT + 1]], [Vr])
                at_, atr = atile.next()
                for hh in range(4):
                    h = kv * 4 + hh
                    Qt, Qr = Qrot.next()
                    kb.dma("sp", Qt[:], attq[h // 2, (h % 2) * 64:(h % 2) * 64 + 64, base:base + SEQ],
                           [qr_[h // 2]], [Qr])
                    for blk in range(18):
                        if blk < 2 and not need_ctx:
                            continue
                        lat = blk >= 2
                        bi = blk - 2
                        qs = Qt[:, blk * 128:(blk + 1) * 128]
                        pa, pbk = (it % 2) * 2, (it % 2) * 2 + 1
                        ptb = 4 + it % 2
                        po = g.ps[6 + (it // 4) % 2][:, (it % 4) * 64:(it % 4) * 64 + 64]
                        por = g.psr[6 + (it // 4) % 2]
                        it += 1
                        st, sr = strot.next()
                        if lat:
                            var = 1 if bi == 0 else (2 if bi == 15 else 0)
                            kb.mm(g.ps[pa][:, 0:384], [(qs, Kt[:, bi * 128:bi * 128 + 384]),
                                                       (g.ident_b[:], mb[:, var, :])], [Qr, Kr, cr, g.const_r],
                                  [g.psr[pa]])
                            kb.op("dve", lambda hd, o=st[:, 0:1], i=g.ps[pa][:, 0:384]: hd.reduce_max(out=o, in_=i, axis=AX.X),
                                  [g.psr[pa]], [sr])
                        kb.mm(g.ps[pbk][:, 0:256], [(qs, Kt[:, 2304:2560])], [Qr, Kr], [g.psr[pbk]])
                        kb.op("dve", lambda hd, o=st[:, 1:2], i=g.ps[pbk][:, 0:256]: hd.reduce_max(out=o, in_=i, axis=AX.X),
                              [g.psr[pbk]], [sr])
                        if lat:
                            kb.tt("dve", st[:, 1:2], st[:, 0:1], st[:, 1:2], ALU.max, [sr], [sr])
                        kb.ts("dve", st[:, 2:3], st[:, 1:2], sink[:, h:h + 1], -1.0, ALU.max, ALU.mult, [sr, cr], [sr])
                        kb.memset("pool", st[:, 4:7], 0.0, [sr])
                        p_, pr = prot.next()
                        if lat:
                            kb.act(p_[:, 0:384], g.ps[pa][:, 0:384], AF.Exp, [g.psr[pa], sr], [pr, sr],
                                   bias=st[:, 2:3], scale=1.0, accum_out=st[:, 4:5])
                        kb.act(p_[:, 384:640], g.ps[pbk][:, 0:256], AF.Exp, [g.psr[pbk], sr], [pr, sr],
                               bias=st[:, 2:3], scale=1.0, accum_out=st[:, 5:6])
                        kb.act(st[:, 6:7], st[:, 2:3], AF.Exp, [sr, cr], [sr], bias=sink[:, h:h + 1], scale=1.0)
                        js = list(range(5)) if lat else [3, 4]
                        psT = g.ps[ptb][:].bitcast(BF16).rearrange("p (c t) -> p c t", c=8)
                        kb.tr([(psT[:, j, :], p_[:, j * 128:(j + 1) * 128]) for j in js], g.ident_b[:],
                              [pr, g.const_r], [g.psr[ptb]])
                        pT, pTr = pTrot.next()
                        kb.cp("dve", pT[:, js[0]:5, :], psT[:, js[0]:5, :], [g.psr[ptb]], [pTr])
                        pairs = []
                        if lat:
                            pairs += [(pT[:, j, :], Vt[:, bi + j, :]) for j in range(3)]
                        pairs += [(pT[:, 3, :], Vt[:, 18, :]), (pT[:, 4, :], Vt[:, 19, :])]
                        kb.mm(po, pairs, [pTr, Vr], [por])
                        kb.stt("dve", st[:, 7:8], st[:, 4:5], st[:, 5:6], st[:, 6:7], ALU.add, ALU.add, [sr], [sr])
                        kb.recip(st[:, 3:4], st[:, 7:8], [sr], [sr])
                        kb.ts("dve", at_[:, blk, hh * 64:(hh + 1) * 64], po, st[:, 3:4], None, ALU.mult, None,
                              [por, sr], [atr])
                for blk in range(18):
                    if blk < 2 and not need_ctx:
                        continue
                    tt = b * NT + blk
                    kb.dma("sp", atta[tt * 128:(tt + 1) * 128, kv * 256:(kv + 1) * 256], at_[:, blk, :], [atr],
                           [ar_[tt]])
    kb.barrier()
    phase_outproj(g, l, atta, ar_, D, I["att_w_out"], src, dst, need_ctx)


def mixer_ret(g, l, src, dst, need_ctx):
    kb, nc, I = g.kb, g.nc, g.I
    NTOK = TT * 128
    retq = nc.dram_tensor(kb.name("retq"), [8, 128, NTOK], BF16, kind="Internal").ap()
    retk = nc.dram_tensor(kb.name("retk"), [8, 128, NTOK], BF16, kind="Internal").ap()
    retv = nc.dram_tensor(kb.name("retv"), [NTOK, 2048], BF16, kind="Internal").ap()
    retg = nc.dram_tensor(kb.name("retg"), [NTOK, 4096], BF16, kind="Internal").ap()
    reta = nc.dram_tensor(kb.name("reta"), [NTOK, 2048], BF16, kind="Internal").ap()
    ar_ = [Reg() for _ in range(TT)]

    class Fresh(list):
        def __getitem__(self, i):
            return Reg()
    with ExitStack() as es:
        uT, uTr = phase_uT(g, es, l, src)
        rope = kb.sb(es, "ropeR", [128, 2, SEQ], F32)
        rope_r = Reg()
        kb.dma("sp", rope[:], I["k_rope_ret"].rearrange("t p n -> p t n"), [], [rope_r])
        wrot = kb.rot(es, "rw", [128, 8, 512], BF16, 2)
        wprot = kb.rot(es, "rwp", [128, 8, 128], BF16, 2)
        stg = [kb.rot(es, "rstg", [128, 512], F32, 2), kb.rot(es, "rstg", [128, 512], F32, 2),
               kb.rot(es, "rstgb", [128, 512], BF16, 4)]
        for h in range(8):
            proj_fm_rope(g, uT, uTr, I["ret_w_in"], h * 128, rope, rope_r, 128.0 ** -0.5, 32, retq[h], Reg(),
                         wrot, wprot, stg, 0)
            proj_fm_rope(g, uT, uTr, I["ret_w_in"], 1024 + h * 128, rope, rope_r, 1.0, 32, retk[h], Reg(),
                         wrot, wprot, stg, 0)
        for ch in range(4):
            proj_tm(g, uT, uTr, I["ret_w_in"], 2048 + ch * 512, 512, retv, Fresh(), ch * 512, wrot, stg[2], 4)
        for ch in range(8):
            proj_tm(g, uT, uTr, I["ret_w_in"], 4096 + ch * 512, 512, retg, Fresh(), ch * 512, wrot, stg[2], 4,
                    func=AF.Silu)
    kb.barrier()
    with ExitStack() as es:
        cr = Reg()
        tab = kb.sb(es, "rtab", [128, 5, 128], F32)
        kb.dma("sp", tab[:], I["k_ret_tab"].rearrange("t p n -> p t n"), [], [cr])
        lg = kb.sb(es, "rlg", [128, 16], F32)
        kb.dma("sp", lg[:], I["ret_decay_logit"][0, :].partition_broadcast(128), [], [cr])
        kb.act(lg[:], lg[:], AF.Exp, [cr], [cr], scale=-1.0)
        kb.ts("dve", lg[:], lg[:], 1.0, None, ALU.add, None, [cr], [cr])
        kb.act(lg[:], lg[:], AF.Ln, [cr], [cr])
        kb.ts("dve", lg[:], lg[:], -1.0, None, ALU.mult, None, [cr], [cr])
        htab = kb.sb(es, "rhtab", [128, 8, 4, 128], F32)
        hcol = kb.sb(es, "rhcol", [128, 8, 4], F32)
        for h in range(8):
            for ti, (src_i, lcol) in enumerate(((0, h), (1, 8 + h), (2, h), (3, 8 + h))):
                kb.act(htab[:, h, ti, :], tab[:, src_i, :], AF.Exp, [cr], [cr], scale=lg[:, lcol:lcol + 1])
            for ci, (src_c, lcol) in enumerate(((0, h), (1, 8 + h), (2, h), (2, 8 + h))):
                kb.act(hcol[:, h, ci:ci + 1], tab[:, 4, src_c:src_c + 1], AF.Exp, [cr], [cr],
                       scale=lg[:, lcol:lcol + 1])
        ws = NormWS(g, es)
        Qrot = kb.rot(es, "rQ", [128, SEQ], BF16, 2)
        Krot = kb.rot(es, "rK", [128, SEQ], BF16, 2)
        Vrot = kb.rot(es, "rV", [128, 18, 256], BF16, 2)
        GFrot = kb.rot(es, "rGF", [128, 18, 256], BF16, 2)
        GBrot = kb.rot(es, "rGB", [128, 18, 256], BF16, 2)
        pre = [kb.sb(es, "rpre", [128, 18, 128], BF16) for _ in range(6)]
        prer = [[Reg() for _ in range(18)] for _ in range(6)]
        og = [kb.sb(es, "rog", [128, 18, 256], F32) for _ in range(2)]
        ogr = [[Reg() for _ in range(18)] for _ in range(2)]
        S = [kb.sb(es, "rS", [128, 256], F32) for _ in range(2)]
        Sb = [kb.sb(es, "rSb", [128, 256], BF16) for _ in range(2)]
        Sr = [Reg(), Reg()]
        Sbr = [Reg(), Reg()]
        arot = kb.rot(es, "ra", [128, 256], BF16, 3)
        it = 0
        for b in range(NB):
            base = b * SEQ
            for h in range(8):
                Qt, Qr = Qrot.next()
                Kt, Kr = Krot.next()
                Vt, Vr = Vrot.next()
                GF, GFr = GFrot.next()
                GB, GBr = GBrot.next()
                kb.dma("sp", Qt[:], retq[h][:, base:base + SEQ], [], [Qr])
                kb.dma("sp", Kt[:], retk[h][:, base:base + SEQ], [], [Kr])
                kb.dma("sp", Vt[:], retv[base:base + SEQ, h * 256:(h + 1) * 256].rearrange("(t p) d -> p t d", p=128),
                       [], [Vr])
                kb.dma("sp", GF[:], retg[base:base + SEQ, h * 256:(h + 1) * 256].rearrange("(t p) d -> p t d", p=128),
                       [], [GFr])
                kb.dma("sp", GB[:], retg[base:base + SEQ, 2048 + h * 256:2048 + (h + 1) * 256]
                       .rearrange("(t p) d -> p t d", p=128), [], [GBr])
                for c in range(18):
                    cs = slice(c * 128, (c + 1) * 128)
                    pb = it % 2
                    pkb = 2 + it % 2
                    it += 1
                    pss = g.ps[pb][:, 0:128]
                    kb.mm(pss, [(Kt[:, cs], Qt[:, cs])], [Kr, Qr], [g.psr[pb]])
                    kb.tt("dve", pre[0][:, c, :], pss, htab[:, h, 0, :], ALU.mult, [g.psr[pb], cr], [prer[0][c]])
                    kb.tt("dve", pre[1][:, c, :], pss, htab[:, h, 1, :], ALU.mult, [g.psr[pb], cr], [prer[1][c]])
                    kb.tt("pool", pre[2][:, c, :], Qt[:, cs], htab[:, h, 2, :], ALU.mult, [Qr, cr], [prer[2][c]])
                    kb.tt("pool", pre[3][:, c, :], Qt[:, cs], htab[:, h, 3, :], ALU.mult, [Qr, cr], [prer[3][c]])
                    psk = g.ps[pkb][:].bitcast(BF16)[:, 0:128]
                    kb.tr([(psk, Kt[:, cs])], g.ident_b[:], [Kr, g.const_r], [g.psr[pkb]])
                    kb.act(pre[4][:, c, :], psk, AF.Copy, [g.psr[pkb], cr], [prer[4][c]], scale=hcol[:, h, 0:1])
                    kb.act(pre[5][:, c, :], psk, AF.Copy, [g.psr[pkb], cr], [prer[5][c]], scale=hcol[:, h, 1:2])
                for d_ in range(2):
                    kb.memset("pool", S[d_][:], 0.0, [Sr[d_]])
                    kb.memset("pool", Sb[d_][:], 0.0, [Sbr[d_]])
                orders = [list(range(18)), [1, 0] + list(range(17, 1, -1))]
                for step in range(18):
                    for d_ in range(2):
                        c = orders[d_][step]
                        hf = step % 2
                        po = g.ps[4 + d_][:, hf * 256:(hf + 1) * 256]
                        pS = g.ps[6 + d_][:, hf * 256:(hf + 1) * 256]
                        kb.mm(po, [(pre[d_][:, c, :], Vt[:, c, :]), (pre[2 + d_][:, c, :], Sb[d_][:])],
                              [prer[d_][c], prer[2 + d_][c], Vr, Sbr[d_]], [g.psr[4 + d_]])
                        kb.mm(pS, [(pre[4 + d_][:, c, :], Vt[:, c, :])], [prer[4 + d_][c], Vr], [g.psr[6 + d_]])
                        kb.stt("dve", S[d_][:], S[d_][:], hcol[:, h, 2 + d_:3 + d_], pS, ALU.mult, ALU.add,
                               [Sr[d_], g.psr[6 + d_], cr], [Sr[d_]])
                        kb.cp("act", Sb[d_][:], S[d_][:], [Sr[d_]], [Sbr[d_]])
                        st, sr = rms_stats(g, ws, [po], [g.psr[4 + d_]], isn=1.0 / 16.0)
                        gate = GF if d_ == 0 else GB
                        kb.stt("dve", og[d_][:, c, :], po, st[:, 2:3], gate[:, c, :], ALU.mult, ALU.mult,
                               [g.psr[4 + d_], sr, GFr if d_ == 0 else GBr], [ogr[d_][c]])
                for c in range(18):
                    a_, a_r = arot.next()
                    kb.tt("pool", a_[:], og[0][:, c, :], og[1][:, c, :], ALU.add, [ogr[0][c], ogr[1][c]], [a_r])
                    tt = b * NT + c
                    kb.dma("sp", reta[tt * 128:(tt + 1) * 128, h * 256:(h + 1) * 256], a_[:], [a_r], [Reg()])
    kb.barrier()
    phase_outproj(g, l, reta, ar_, 2048, I["ret_w_out"], src, dst, need_ctx)


def interleave(gens, width):
    gens = iter(gens)
    active = []
    done = False
    while True:
        while not done and len(active) < width:
            try:
                active.append(next(gens))
            except StopIteration:
                done = True
        if not active:
            break
        nxt = []
        for gen in active:
            try:
                next(gen)
                nxt.append(gen)
            except StopIteration:
                pass
        active = nxt


def mixer_dn(g, l, src, dst, need_ctx):
    kb, nc, I = g.kb, g.nc, g.I
    NTOK = TT * 128
    NCH = NB * 2 * 36

    def dt_(name, shape, dt):
        if g.dbg and name in ("dnq", "dnk", "dnvt", "dnba", "dnkt"):
            return nc.dram_tensor("dbg_" + name, list(shape), dt, kind="ExternalOutput").ap()
        return nc.dram_tensor(kb.name(name), list(shape), dt, kind="Internal").ap()
    dnq = dt_("dnq", [8, 128, NTOK], BF16)
    dnk = dt_("dnk", [8, 128, NTOK], BF16)
    dnkt = dt_("dnkt", [NTOK, D], BF16)
    dnvt = dt_("dnvt", [NTOK, D], BF16)
    dnba = dt_("dnba", [NTOK, 32], F32)
    dng = dt_("dng", [NTOK, 2048], BF16)
    dna = [dt_("dna", [NTOK, D], BF16) for _ in range(2)]
    pu = dt_("dpu", [NCH, 64, 1024], BF16)
    pw = dt_("dpw", [NCH, 128, 512], BF16)
    pat = dt_("dpat", [NCH, 64, 512], BF16)
    pqg = dt_("dpqg", [NCH, 128, 512], BF16)
    pkg = dt_("dpkg", [NCH, 64, 1024], BF16)
    pgl = dt_("dpgl", [NCH, 128, 8], F32)
    ar_ = [Reg() for _ in range(TT)]

    class Fresh(list):
        def __getitem__(self, i):
            return Reg()
    NX = 2308
    BLK = ((0, 512), (512, 512), (1024, 512), (1536, 512), (2048, 260))
    with ExitStack() as es:
        uT, uTr = phase_uT(g, es, l, src)
        cr = Reg()
        cw = kb.sb(es, "dcw", [128, 24, 5], F32)
        for k in range(5):
            kb.dma("sp", cw[:, :, k], I["dn_conv_w"][k, :].rearrange("(c p) -> p c", p=128), [], [cr],
                   allow_slow_non_contiguous=True)
        ones_b = kb.sb(es, "donesb", [128, 128], BF16)
        kb.memset("dve", ones_b[:], 1.0, [cr])
        wrot = kb.rot(es, "dw", [128, 8, 512], BF16, 3)
        dgrot = kb.rot(es, "ddg", [128, 5, 128], BF16, 3)
        Xrot = kb.rot(es, "dX", [128, 2320], BF16, 3)
        for xt_, xr_ in Xrot.items:
            kb.memset("pool", xt_[:, 0:2], 0.0, [xr_])
            kb.memset("pool", xt_[:, 258:262], 0.0, [xr_])
            kb.memset("pool", xt_[:, 2310:2320], 0.0, [xr_])
        Srot = kb.rot(es, "dsil", [128, NX], F32, 3)
        sqrot = kb.rot(es, "dsq", [128, 512], BF16, 4)
        sdrot = kb.rot(es, "dsd", [128, 512], F32, 4)
        QNrot = kb.rot(es, "dQN", [128, NX], BF16, 3)
        tstg = kb.rot(es, "dts", [128, 8, 128], BF16, 3)

        def inproj_unit(cb, b, slot, wt, wr, dg, dgr):
            kind, h = cb // 8, cb % 8
            base = b * SEQ
            pbs = [slot * 3, slot * 3 + 1]
            ptb = slot * 3 + 2
            X, Xr = Xrot.next()
            for bi, (o, n) in enumerate(((0, 512), (512, 512), (1024, 512), (1536, 512), (2048, 256))):
                t0 = base + o
                tiles = list(range(t0 // 128, (t0 + n) // 128))
                pb = pbs[bi % 2]
                kb.mm(g.ps[pb][:, 0:n], [(wt[:, kc, 0:128], uT[:, kc, t0:t0 + n]) for kc in range(8)],
                      [uTr[t] for t in tiles] + [wr], [g.psr[pb]])
                if o == 0:
                    kb.cp("act", X[:, 2:258], g.ps[pb][:, 0:256], [g.psr[pb]], [Xr])
                    kb.cp("act", X[:, 262:518], g.ps[pb][:, 256:512], [g.psr[pb]], [Xr])
                else:
                    kb.cp("act", X[:, 6 + o:6 + o + n], g.ps[pb][:, 0:n], [g.psr[pb]], [Xr])
                yield
            Ssil, Ssr = Srot.next()
            for bi, (o, n) in enumerate(BLK):
                pb = pbs[(bi + 1) % 2]
                kb.mm(g.ps[pb][:, 0:n], [(dg[:, k, :], X[:, o + k:o + k + n]) for k in range(5)], [Xr, dgr],
                      [g.psr[pb]])
                kb.act(Ssil[:, o:o + n], g.ps[pb][:, 0:n], AF.Silu, [g.psr[pb]], [Ssr])
                yield
            QN, QNr = QNrot.next()
            if kind < 2:
                for bi, (o, n) in enumerate(BLK):
                    pb = pbs[bi % 2]
                    sq, sqr = sqrot.next()
                    kb.act(sq[:, 0:n], Ssil[:, o:o + n], AF.Square, [Ssr], [sqr])
                    kb.mm(g.ps[pb][:, 0:n], [(ones_b[:], sq[:, 0:n])], [sqr, cr], [g.psr[pb]])
                    sd, sdr = sdrot.next()
                    kb.act(sd[:, 0:n], g.ps[pb][:, 0:n], AF.Sqrt, [g.psr[pb], g.const_r], [sdr],
                           bias=(g.eps128 if kind == 0 else g.eps)[:, 0:1], scale=128.0 if kind == 0 else 1.0)
                    kb.recip(sd[:, 0:n], sd[:, 0:n], [sdr], [sdr])
                    kb.tt("pool", QN[:, o:o + n], Ssil[:, o:o + n], sd[:, 0:n], ALU.mult, [Ssr, sdr], [QNr])
                    yield
                dstq = dnq if kind == 0 else dnk
                kb.dma("sp", dstq[h][:, base:base + TC], QN[:, 0:TC], [QNr], [Reg()])
                kb.dma("sp", dstq[h][:, base + TC:base + SEQ], QN[:, 260:NX], [QNr], [Reg()])
            else:
                kb.cp("pool", QN[:], Ssil[:], [Ssr], [QNr])
                yield
            if kind >= 1:
                dstt = dnkt if kind == 1 else dnvt
                for t0 in range(0, NT, 8):
                    ts_ = list(range(t0, min(NT, t0 + 8)))
                    psT = g.ps[ptb][:].bitcast(BF16).rearrange("p (c t) -> p c t", c=8)
                    kb.tr([(psT[:, j, :], QN[:, (t * 128 if t < 2 else t * 128 + 4):(t * 128 if t < 2 else t * 128 + 4) + 128])
                           for j, t in enumerate(ts_)], g.ident_b[:], [QNr, g.const_r], [g.psr[ptb]])
                    stt_, str_ = tstg.next()
                    kb.cp("act", stt_[:, 0:len(ts_), :], psT[:, 0:len(ts_), :], [g.psr[ptb]], [str_])
                    r0 = base + t0 * 128
                    kb.dma("sp", dstt[r0:r0 + len(ts_) * 128, h * 128:(h + 1) * 128]
                           .rearrange("(t p) d -> p t d", p=128), stt_[:, 0:len(ts_), :], [str_], [Reg()])
                    yield

        def inproj_units():
            i = 0
            for cb in range(24):
                wt, wr = load_w_bf16(g, wrot, I["dn_w_in"], cb * 128, 128)
                dg, dgr = dgrot.next()
                for k in range(5):
                    kb.ts("pool", dg[:, k, :], g.ident_f[:], cw[:, cb, k:k + 1], None, ALU.mult, None,
                          [g.const_r, cr], [dgr])
                for b in range(NB):
                    yield inproj_unit(cb, b, i % 2, wt, wr, dg, dgr)
                    i += 1
        interleave(inproj_units(), 2)
        abr = Reg()
        dtb = kb.sb(es, "ddtb", [128, 16], F32)
        nea = kb.sb(es, "dnea", [128, 16], F32)
        kb.dma("sp", dtb[:], I["dn_dt_bias"][0, :].partition_broadcast(128), [], [abr])
        kb.dma("sp", nea[:], I["dn_a_log"][0, :].partition_broadcast(128), [], [abr])
        kb.act(nea[:], nea[:], AF.Exp, [abr], [abr])
        kb.ts("dve", nea[:], nea[:], -1.0, None, ALU.mult, None, [abr], [abr])
        wt, wr = load_w_bf16(g, wrot, I["dn_w_in"], 3072, 32)
        bstg = kb.rot(es, "dbs", [128, 32], F32, 4)
        btmp = kb.rot(es, "dbt", [128, 16], F32, 4)

        def ba_unit(tt):
            pb = 6 + tt % 2
            off = (tt // 2) % 8 * 32
            pp = g.ps[pb][:, off:off + 32]
            kb.mm(pp, [(uT[:, kc, tt * 128:(tt + 1) * 128], wt[:, kc, 0:32]) for kc in range(8)],
                  [uTr[tt], wr], [g.psr[pb]])
            bs, bsr = bstg.next()
            bt, btr = btmp.next()
            yield
            kb.act(bs[:, 0:16], pp[:, 0:16], AF.Sigmoid, [g.psr[pb]], [bsr])
            kb.tt("dve", bt[:], pp[:, 16:32], dtb[:], ALU.add, [g.psr[pb], abr], [btr])
            yield
            kb.act(bt[:], bt[:], AF.Exp, [btr], [btr])
            yield
            kb.ts("dve", bt[:], bt[:], 1.0, None, ALU.add, None, [btr], [btr])
            yield
            kb.act(bt[:], bt[:], AF.Ln, [btr], [btr])
            yield
            kb.tt("dve", bs[:, 16:32], bt[:], nea[:], ALU.mult, [btr, abr], [bsr])
            kb.dma("sp", dnba[tt * 128:(tt + 1) * 128, :], bs[:], [bsr], [Reg()])
        interleave((ba_unit(tt) for tt in range(TT)), 4)
        stgb = kb.rot(es, "dgs", [128, 512], BF16, 4)
        tiles = [tt for tt in range(TT) if need_ctx or (tt % NT) >= 2]
        for ch in range(4):
            proj_tm(g, uT, uTr, I["dn_w_in"], 3104 + ch * 512, 512, dng, Fresh(), ch * 512, wrot, stgb, 0,
                    func=AF.Silu, tiles=tiles)
    kb.barrier()
    bc64 = lambda ap: ap.unsqueeze(2).broadcast_to([64, 8, 64])
    bc128 = lambda ap: ap.unsqueeze(2).broadcast_to([64, 8, 128])

    def chunk_id(b, d_, c):
        return (b * 2 + d_) * 36 + c
    with ExitStack() as es:
        cr = Reg()
        ktab = kb.sb(es, "dktab", [64, 4, 64], F32)
        kb.dma("sp", ktab[:], I["k_dn"].rearrange("t p n -> p t n"), [], [cr])
        ones3 = kb.sb(es, "dones3", [64, 128], F32)
        kb.memset("dve", ones3[:], 1.0, [cr])
        identf = g.ident_f
        identb = g.ident_b
        NW = 2

        def R1(shape, dt, name, n=NW + 1):
            return kb.rot(es, name, shape, dt, n)
        kTr_, qTr_ = R1([128, 8, 64], BF16, "dkT"), R1([128, 8, 64], BF16, "dqT")
        ktr_, vtr_ = R1([64, 8, 128], BF16, "dkt"), R1([64, 8, 128], BF16, "dvt")
        bar_ = R1([64, 32], F32, "dba")
        LBn_ = R1([64, 8, 128], F32, "dLBn")
        sm_ = R1([128, 32], F32, "dsm")
        gdm_ = R1([64, 8, 64], F32, "dgdm")
        t12_ = R1([64, 2, 8, 64], F32, "dt12")
        Dms_, DTi_ = R1([64, 8, 64], F32, "dDms"), R1([64, 8, 64], F32, "dDTi")
        Egr_ = R1([128, 8, 64], F32, "dEgr")
        qgT_ = R1([128, 8, 64], BF16, "dqgT")
        INV_F32 = os.environ.get("DN_INV_F32", "1") == "1"
        IDT = F32 if INV_F32 else BF16
        A_ = R1([64, 8, 64], IDT, "dA", 2 * NW + 2)
        B_ = R1([64, 8, 64], IDT, "dB", 2 * NW + 2)
        Af_ = R1([64, 8, 64], F32, "dAf")
        Tt_ = R1([64, 8, 64], F32, "dTt")
        Ttb_ = R1([64, 8, 64], IDT, "dTtb", 2 * NW + 2)
        Tt16_ = R1([64, 8, 64], BF16, "dTt16")
        attnT_ = R1([64, 8, 64], BF16, "dattn")
        rv_, kbg_, kg_ = R1([64, 8, 128], BF16, "drv"), R1([64, 8, 128], BF16, "dkbg"), R1([64, 8, 128], BF16, "dkg")
        u_ = R1([64, 8, 128], BF16, "du")
        wT_ = R1([128, 8, 64], BF16, "dwT")
        gls_ = R1([128, 8], F32, "dgls")
        P, PR = g.ps, g.psr

        def pre_unit(b, d_, c, slot):
            cid = chunk_id(b, d_, c)
            tok0 = b * SEQ + c * 64
            Pa, Pb, Pc, Pd = (slot * 4 + i for i in range(4))
            Tri = ktab[:, d_, :]
            strict = ktab[:, 2 + d_, :]
            kT, kTr = kTr_.next()
            qT, qTr = qTr_.next()
            kt, ktr = ktr_.next()
            vt, vtr = vtr_.next()
            ba, bar = bar_.next()
            kb.dma("sp", kT[:], dnk[:, :, tok0:tok0 + 64].rearrange("h p n -> p h n"), [], [kTr])
            kb.dma("sp", qT[:], dnq[:, :, tok0:tok0 + 64].rearrange("h p n -> p h n"), [], [qTr])
            kb.dma("sp", kt[:], dnkt[tok0:tok0 + 64, :].rearrange("p (h d) -> p h d", h=8), [], [ktr])
            kb.dma("sp", vt[:], dnvt[tok0:tok0 + 64, :].rearrange("p (h d) -> p h d", h=8), [], [vtr])
            kb.dma("sp", ba[:], dnba[tok0:tok0 + 64, :], [], [bar])
            beta = ba[:, 8 * d_:8 * d_ + 8]
            la = ba[:, 16 + 8 * d_:24 + 8 * d_]
            yield
            LBn, LBr = LBn_.next()
            kb.ts("dve", LBn[:], bc128(la), -1.0, None, ALU.mult, None, [bar], [LBr])
            kb.mm(P[Pd][0:64, 0:8], [(Tri, la)], [cr, bar], [PR[Pd]])
            kb.mm(P[Pd][:, 8:16], [(ones3[:, :], la)], [cr, bar], [PR[Pd]])
            for h in range(8):
                kb.mm(P[Pa][0:64, h * 64:(h + 1) * 64], [(kT[:, h, :], kT[:, h, :])], [kTr], [PR[Pa]])
            for h in range(8):
                kb.mm(P[Pb][0:64, h * 64:(h + 1) * 64], [(kT[:, h, :], qT[:, h, :])], [kTr, qTr], [PR[Pb]])
            yield
            for h in range(8):
                kb.mm(P[Pc][:, h * 64:(h + 1) * 64], [(LBn[:, h, :], Tri)], [LBr, cr], [PR[Pc]])
            sm, smr = sm_.next()
            kb.cp("dve", sm[0:64, 0:8], P[Pd][0:64, 0:8], [PR[Pd]], [smr])
            gc = sm[0:64, 0:8]
            gls, glsr = gls_.next()
            kb.act(gls[:], P[Pd][:, 8:16], AF.Exp, [PR[Pd]], [glsr])
            kb.dma("sp", pgl[cid], gls[:], [glsr], [Reg()])
            yield
            gdm, gdmr = gdm_.next()
            GR = P[Pc][:].rearrange("p (h n) -> p h n", h=8)
            kb.tt("dve", gdm[:], GR[0:64], bc64(gc), ALU.add, [PR[Pc], smr], [gdmr])
            Egr, Egrr = Egr_.next()
            kb.act(Egr[:], GR, AF.Exp, [PR[Pc]], [Egrr], scale=-1.0)
            kb.act(sm[0:64, 8:16], gc, AF.Exp, [smr], [smr])
            kb.tt("dve", sm[0:64, 16:24], P[Pd][0:64, 8:16], gc, ALU.subtract, [PR[Pd], smr], [smr])
            yield
            t12, t12r = t12_.next()
            kb.ts("dve", t12[:, 0], gdm[:], 0.0, None, ALU.min, None, [gdmr], [t12r])
            kb.ts("dve", t12[:, 1], gdm[:], -1.0, 0.0, ALU.mult, ALU.min, [gdmr], [t12r])
            qgT, qgTr = qgT_.next()
            kb.tt("pool", qgT[:], qT[:], Egr[:], ALU.mult, [qTr, Egrr], [qgTr])
            kb.dma("sp", pqg[cid].rearrange("p (h n) -> p h n", h=8), qgT[:], [qgTr], [Reg()])
            kb.tt("dve", sm[0:64, 8:16], sm[0:64, 8:16], beta, ALU.mult, [smr, bar], [smr])
            kb.act(sm[0:64, 16:24], sm[0:64, 16:24], AF.Exp, [smr], [smr])
            yield
            kb.act(t12[:], t12[:], AF.Exp, [t12r], [t12r])
            rv, rvr = rv_.next()
            kbg, kbgr = kbg_.next()
            kg, kgr = kg_.next()
            kb.tt("pool", rv[:], vt[:], bc128(beta), ALU.mult, [vtr, bar], [rvr])
            kb.tt("pool", kbg[:], kt[:], bc128(sm[0:64, 8:16]), ALU.mult, [ktr, smr], [kbgr])
            kb.tt("pool", kg[:], kt[:], bc128(sm[0:64, 16:24]), ALU.mult, [ktr, smr], [kgr])
            kb.dma("sp", pkg[cid].rearrange("p (h n) -> p h n", h=8), kg[:], [kgr], [Reg()])
            yield
            Dms, Dmsr = Dms_.next()
            DTi, DTir = DTi_.next()
            kb.tt("pool", Dms[:], t12[:, 0], strict.unsqueeze(1).broadcast_to([64, 8, 64]), ALU.mult,
                  [t12r, cr], [Dmsr])
            kb.tt("pool", DTi[:], t12[:, 1], Tri.unsqueeze(1).broadcast_to([64, 8, 64]), ALU.mult,
                  [t12r, cr], [DTir])
            KK = P[Pa][0:64, :].rearrange("p (h n) -> p h n", h=8)
            QK = P[Pb][0:64, :].rearrange("p (h n) -> p h n", h=8)
            Af, Afr = Af_.next()
            kb.tt("dve", Af[:], KK, bc64(beta), ALU.mult, [PR[Pa], bar], [Afr])
            yield
            A0, A0r = A_.next()
            kb.tt("dve", A0[:], Af[:], Dms[:], ALU.mult, [Afr, Dmsr], [A0r])
            attnT, attnr = attnT_.next()
            kb.tt("dve", attnT[:], QK, DTi[:], ALU.mult, [PR[Pb], DTir], [attnr])
            kb.dma("sp", pat[cid].rearrange("p (h n) -> p h n", h=8), attnT[:], [attnr], [Reg()])
            yield
            if INV_F32:
                KKb = P[Pa][0:64, :].rearrange("p (h n) -> p h n", h=8)
            else:
                KKb = P[Pa][0:64, :].bitcast(BF16)[:, 0:512].rearrange("p (h n) -> p h n", h=8)
            kb.tr([(KKb[:, h, :], A0[:, h, :]) for h in range(8)], (identf if INV_F32 else identb)[0:64, 0:64],
                  [A0r, g.const_r], [PR[Pa]])
            yield
            B0, B0r = B_.next()
            kb.cp("act", B0[:], KKb, [PR[Pa]], [B0r])
            Tt, Ttr = Tt_.next()
            kb.tt("dve", Tt[:], identf[0:64, 0:64].unsqueeze(1).broadcast_to([64, 8, 64]), KKb,
                  ALU.subtract, [g.const_r, PR[Pa]], [Ttr])
            Ttb, Ttbr = Ttb_.next()
            kb.cp("pool", Ttb[:], Tt[:], [Ttr], [Ttbr])
            yield
            Ak, Akr, Bk, Bkr = A0, A0r, B0, B0r
            for lev in range(5):
                An, Anr = A_.next()
                for h in range(8):
                    kb.mm(P[Pa][0:64, h * 64:(h + 1) * 64], [(Bk[:, h, :], Ak[:, h, :])], [Akr, Bkr], [PR[Pa]])
                if lev < 4:
                    Bn, Bnr = B_.next()
                    for h in range(8):
                        kb.mm(P[Pb][0:64, h * 64:(h + 1) * 64], [(Ak[:, h, :], Bk[:, h, :])], [Akr, Bkr],
                              [PR[Pb]])
                yield
                kb.cp("act", An[:], KK, [PR[Pa]], [Anr])
                if lev < 4:
                    kb.cp("dve", Bn[:], QK, [PR[Pb]], [Bnr])
                yield
                for h in range(8):
                    kb.mm(P[Pd][0:64, h * 64:(h + 1) * 64], [(An[:, h, :], Ttb[:, h, :])], [Anr, Ttbr], [PR[Pd]])
                yield
                kb.tt("dve", Tt[:], Tt[:], P[Pd][0:64, :].rearrange("p (h n) -> p h n", h=8), ALU.add,
                      [Ttr, PR[Pd]], [Ttr])
                yield
                Ttb, Ttbr = Ttb_.next()
                kb.cp("act", Ttb[:], Tt[:], [Ttr], [Ttbr])
                yield
                Ak, Akr = An, Anr
                if lev < 4:
                    Bk, Bkr = Bn, Bnr
            u, ur = u_.next()
            wT, wTr = wT_.next()
            if INV_F32:
                Ttb, Ttbr = Tt16_.next()
                kb.cp("act", Ttb[:], Tt[:], [Ttr], [Ttbr])
                yield
            for hb in range(2):
                pp = (Pa, Pb)[hb]
                for hh in range(4):
                    h = hb * 4 + hh
                    kb.mm(P[pp][0:64, hh * 128:(hh + 1) * 128], [(Ttb[:, h, :], rv[:, h, :])],
                          [Ttbr, rvr], [PR[pp]])
            for h in range(8):
                kb.mm(P[Pc][:, h * 64:(h + 1) * 64], [(kbg[:, h, :], Ttb[:, h, :])], [kbgr, Ttbr], [PR[Pc]])
            yield
            for hb in range(2):
                pp = (Pa, Pb)[hb]
                kb.cp("act" if hb == 0 else "dve", u[:, hb * 4:(hb + 1) * 4, :],
                      P[pp][0:64, :].rearrange("p (h n) -> p h n", h=4), [PR[pp]], [ur])
            kb.cp("act", wT[:], GR, [PR[Pc]], [wTr])
            yield
            kb.dma("sp", pu[cid].rearrange("p (h n) -> p h n", h=8), u[:], [ur], [Reg()])
            kb.dma("sp", pw[cid].rearrange("p (h n) -> p h n", h=8), wT[:], [wTr], [Reg()])

        units = []
        for b in range(NB):
            for d_ in range(2):
                for c in range(36):
                    units.append((b, d_, c))
        if int(os.environ.get("DNSTOP", "9")) >= 2:
            interleave((pre_unit(b, d_, c, i % NW) for i, (b, d_, c) in enumerate(units)), int(os.environ.get("DN_NW", NW)))
    kb.barrier()
    with ExitStack() as es:
        cr = Reg()
        ng = kb.sb(es, "dng_", [64, 128], F32)
        kb.dma("sp", ng[:], I["dn_norm_g"][0, :].partition_broadcast(64), [], [cr])
        P, PR = g.ps, g.psr
        NC_ = 4

        def R2(shape, dt, name, n=2):
            return [kb.rot(es, name, shape, dt, n) for _ in range(NC_)]
        u_, wT_ = R2([64, 8, 128], BF16, "eu"), R2([128, 8, 64], BF16, "ewT")
        at_, qg_ = R2([64, 8, 64], BF16, "eat"), R2([128, 8, 64], BF16, "eqg")
        kg_, gl_ = R2([64, 8, 128], BF16, "ekg"), R2([128, 8], F32, "egl")
        gt_ = R2([64, 8, 128], BF16, "egt")
        vn_ = R2([64, 8, 128], BF16, "evn", 1)
        osb_ = R2([64, 8, 128], F32, "eosb", 1)
        sq_ = R2([64, 8, 128], F32, "esq", 1)
        gn_ = R2([64, 8, 128], F32, "egn", 1)
        oa_ = R2([64, 8, 128], BF16, "eoa", 2)
        sm_ = R2([64, 32], F32, "esm", 2)
        print("sbuf remaining (dn rec)", nc.sbuf_bytes_remaining)
        S = [kb.sb(es, "dS", [128, 8, 128], F32) for _ in range(NC_)]
        Sb = [kb.sb(es, "dSb", [128, 8, 128], BF16) for _ in range(NC_)]
        Sr = [Reg() for _ in range(NC_)]
        Sbr = [Reg() for _ in range(NC_)]

        def rec_chain(b, d_, ch):
            order = list(range(36)) if d_ == 0 else [3, 2, 1, 0] + list(range(35, 3, -1))
            Ra, Rb = ch * 2, ch * 2 + 1
            kb.memset("pool", S[ch][:], 0.0, [Sr[ch]])
            kb.memset("pool", Sb[ch][:], 0.0, [Sbr[ch]])
            loaded = {}

            def load(c):
                cid = chunk_id(b, d_, c)
                tok0 = b * SEQ + c * 64
                emit = need_ctx or c >= 4
                r = {}
                for nm, rot_, srcap in (("u", u_[ch], pu[cid]), ("wT", wT_[ch], pw[cid]), ("at", at_[ch], pat[cid]),
                                        ("qg", qg_[ch], pqg[cid]), ("kg", kg_[ch], pkg[cid])):
                    if nm in ("at", "qg") and not emit:
                        continue
                    t_, tr_ = rot_.next()
                    kb.dma("sp", t_[:], srcap.rearrange("p (h n) -> p h n", h=8), [], [tr_])
                    r[nm] = (t_, tr_)
                t_, tr_ = gl_[ch].next()
                kb.dma("sp", t_[:], pgl[cid], [], [tr_])
                r["gl"] = (t_, tr_)
                if emit:
                    t_, tr_ = gt_[ch].next()
                    kb.dma("sp", t_[:], dng[tok0:tok0 + 64, d_ * 1024:(d_ + 1) * 1024]
                           .rearrange("p (h d) -> p h d", h=8), [], [tr_])
                    r["gt"] = (t_, tr_)
                return r
            loaded[0] = load(order[0])
            for step, c in enumerate(order):
                if step + 1 < 36:
                    loaded[step + 1] = load(order[step + 1])
                L_ = loaded.pop(step)
                tok0 = b * SEQ + c * 64
                emit = need_ctx or c >= 4
                (u, ur), (wT, wTr), (kg, kgr), (gl, glr) = L_["u"], L_["wT"], L_["kg"], L_["gl"]
                for hb in range(2):
                    pp = (Ra, Rb)[hb]
                    for hh in range(4):
                        h = hb * 4 + hh
                        kb.mm(P[pp][0:64, hh * 128:(hh + 1) * 128], [(wT[:, h, :], Sb[ch][:, h, :])],
                              [wTr, Sbr[ch]], [PR[pp]])
                yield
                vn, vnr = vn_[ch].next()
                for hb in range(2):
                    pp = (Ra, Rb)[hb]
                    kb.tt("dve", vn[:, hb * 4:(hb + 1) * 4, :], u[:, hb * 4:(hb + 1) * 4, :],
                          P[pp][0:64, :].rearrange("p (h n) -> p h n", h=4), ALU.subtract, [ur, PR[pp]], [vnr])
                if emit:
                    gn, gnr = gn_[ch].next()
                    kb.tt("pool", gn[:], L_["gt"][0][:], ng[:, :].unsqueeze(1).broadcast_to([64, 8, 128]), ALU.mult,
                          [L_["gt"][1], cr], [gnr])
                yield
                if emit:
                    (at, atr), (qg, qgr) = L_["at"], L_["qg"]
                    for hb in range(2):
                        pp = (Ra, Rb)[hb]
                        for hh in range(4):
                            h = hb * 4 + hh
                            kb.mm(P[pp][0:64, hh * 128:(hh + 1) * 128],
                                  [(qg[:, h, :], Sb[ch][:, h, :]), (at[:, h, :], vn[:, h, :])],
                                  [qgr, Sbr[ch], atr, vnr], [PR[pp]])
                    yield
                    osb, osr = osb_[ch].next()
                    kb.cp("act", osb[:, 0:4, :], P[Ra][0:64, :].rearrange("p (h n) -> p h n", h=4), [PR[Ra]], [osr])
                    kb.cp("dve", osb[:, 4:8, :], P[Rb][0:64, :].rearrange("p (h n) -> p h n", h=4), [PR[Rb]], [osr])
                    yield
                for hb in range(2):
                    pp = (Ra, Rb)[hb]
                    for hh in range(4):
                        h = hb * 4 + hh
                        kb.mm(P[pp][:, hh * 128:(hh + 1) * 128], [(kg[:, h, :], vn[:, h, :])], [kgr, vnr], [PR[pp]])
                kb.tt("pool", S[ch][:], S[ch][:], gl[:, :].unsqueeze(2).broadcast_to([128, 8, 128]), ALU.mult,
                      [Sr[ch], glr], [Sr[ch]])
                yield
                for hb in range(2):
                    pp = (Ra, Rb)[hb]
                    kb.tt("dve", S[ch][:, hb * 4:(hb + 1) * 4, :], S[ch][:, hb * 4:(hb + 1) * 4, :],
                          P[pp][:, :].rearrange("p (h n) -> p h n", h=4), ALU.add, [Sr[ch], PR[pp]], [Sr[ch]])
                yield
                kb.cp("act", Sb[ch][:], S[ch][:], [Sr[ch]], [Sbr[ch]])
                if emit:
                    sq, sqr = sq_[ch].next()
                    kb.act(sq[:], osb[:], AF.Square, [osr], [sqr])
                    yield
                    sm, smr = sm_[ch].next()
                    kb.op("dve", lambda hd, o=sm[:, 0:8], i=sq[:]: hd.reduce_sum(out=o, in_=i, axis=AX.X),
                          [sqr], [smr])
                    yield
                    kb.act(sm[:, 8:16], sm[:, 0:8], AF.Sqrt, [smr, g.const_r], [smr], bias=g.eps[0:64, 0:1],
                           scale=1.0 / 128.0)
                    yield
                    kb.recip(sm[:, 16:24], sm[:, 8:16], [smr], [smr])
                    yield
                    kb.tt("dve", osb[:], osb[:], sm[:, 16:24].unsqueeze(2).broadcast_to([64, 8, 128]), ALU.mult,
                          [osr, smr], [osr])
                    yield
                    oa, oar = oa_[ch].next()
                    kb.tt("pool", oa[:], osb[:], gn[:], ALU.mult, [osr, gnr], [oar])
                    kb.dma("sp", dna[d_][tok0:tok0 + 64, :].rearrange("p (h d) -> p h d", h=8), oa[:], [oar], [Reg()])
                yield
        if int(os.environ.get("DNSTOP", "9")) >= 3:
            interleave((rec_chain(b, d_, b * 2 + d_) for b in range(NB) for d_ in range(2)), int(os.environ.get("DN_NC", NC_)))
    kb.barrier()
    phase_outproj(g, l, dna, ar_, D, I["dn_w_out"], src, dst, need_ctx)


def host_consts():
    c = {}
    c["k_ident"] = np.eye(128, dtype=np.float32)
    mats = np.zeros((20, 128, 128), np.float32)
    for wi, w in enumerate((2, 4, 8, 16)):
        lo = w // 2
        hi = w - 1 - lo
        n = 384
        for var, (t0, nseq_lo, nseq_hi) in enumerate(((128, 0, 384), (0, 0, 384), (256, 0, 384))):
            pass
        A = np.zeros((n, n), np.float32)
        for t in range(n):
            a, b_ = max(0, t - lo), min(n, t + hi + 1)
            A[t, a:b_] = 1.0 / (b_ - a)
        M = A - np.eye(n, dtype=np.float32)
        MT = M.T
        mats[wi * 5 + 0] = MT[128:256, 128:256]
        mats[wi * 5 + 1] = MT[0:128, 0:128]
        mats[wi * 5 + 2] = MT[256:384, 256:384]
        mats[wi * 5 + 3] = MT[0:128, 128:256]
        mats[wi * 5 + 4] = MT[256:384, 128:256]
    c["k_pool"] = mats
    pos = np.arange(T)
    row, col = pos // 64, pos % 64

    def rope_tab(dh, nrep):
        q = dh // 4
        tab = np.zeros((2, dh, SEQ), np.float64)
        tab[0, :, :TC] = 1.0
        for d in range(dh):
            blk, i = d // q, d % q
            inv = 10000.0 ** (-i / q)
            p_ = row if blk < 2 else col
            ang = (p_.astype(np.float32) * np.float32(inv)).astype(np.float64)
            tab[0, d, TC:] = np.cos(ang)
            tab[1, d, TC:] = np.sin(ang) * (-1.0 if blk % 2 == 0 else 1.0)
        return np.tile(tab, (1, nrep, 1)).astype(np.float32)
    c["k_rope_att"] = rope_tab(64, 2)
    c["k_rope_ret"] = rope_tab(128, 1)
    qi = np.arange(128)[:, None]
    kj = np.arange(384)[None, :]
    keep = np.abs(kj - 128 - qi) <= 128
    m = np.zeros((3, 128, 384), np.float32)
    m[0] = np.where(keep, 0.0, -30000.0)
    m[1] = np.where(keep & (kj >= 128), 0.0, -30000.0)
    m[2] = np.where(keep & (kj < 256), 0.0, -30000.0)
    c["k_att_mask"] = m
    jj = np.arange(128)[:, None].astype(np.float64)
    ii = np.arange(128)[None, :].astype(np.float64)
    rt = np.zeros((5, 128, 128), np.float64)
    rt[0] = np.where(ii >= jj, ii - jj, 1e9)
    rt[1] = np.where(jj >= ii, jj - ii, 1e9)
    rt[2] = np.broadcast_to(ii + 1.0, (128, 128))
    rt[3] = np.broadcast_to(128.0 - ii, (128, 128))
    rt[4, :, 0] = 127.0 - np.arange(128)
    rt[4, :, 1] = np.arange(128)
    rt[4, :, 2] = 128.0
    c["k_ret_tab"] = rt.astype(np.float32)
    p_ = np.arange(64)[:, None]
    f_ = np.arange(64)[None, :]
    c["k_dn"] = np.stack([p_ <= f_, p_ >= f_, p_ > f_, p_ < f_]).astype(np.float32)
    return c


LAYERS = [0, 1, 2, 3]
_cache = {}


def _prep_inputs(inputs):
    f = lambda a: np.ascontiguousarray(np.asarray(a, dtype=np.float32))
    shared = {}
    for k in ("ada_w", "ada_b", "mix_pre_g", "mix_post_g", "mlp_pre_g", "mlp_post_g", "mlp_w1", "mlp_w2"):
        shared[k] = f(inputs[k])
    shared["c_ctx"] = f(inputs["c_ctx"]).reshape(1, D)
    shared["ret_w_in"] = f(inputs["ret_w_in"][0])
    shared["ret_decay_logit"] = f(inputs["ret_decay_logit"][0]).reshape(1, 16)
    shared["ret_w_out"] = f(inputs["ret_w_out"][0])
    shared["att_w_in"] = f(inputs["att_w_in"][0])
    shared["att_sink"] = f(inputs["att_sink"][0]).reshape(1, 16)
    shared["att_w_out"] = f(inputs["att_w_out"][0])
    shared["pool_w"] = f(inputs["pool_w"][0])
    shared["pool_b"] = f(inputs["pool_b"][0]).reshape(1, D)
    shared["pool_scale"] = f(inputs["pool_scale"][0]).reshape(1, D)
    shared["dn_w_in"] = f(inputs["dn_w_in"][0])
    shared["dn_conv_w"] = f(inputs["dn_conv_w"][0])
    shared["dn_a_log"] = f(inputs["dn_a_log"][0]).reshape(1, 16)
    shared["dn_dt_bias"] = f(inputs["dn_dt_bias"][0]).reshape(1, 16)
    shared["dn_norm_g"] = f(inputs["dn_norm_g"][0]).reshape(1, 128)
    shared["dn_w_out"] = f(inputs["dn_w_out"][0])
    shared.update(host_consts())
    return shared


def run(inputs, layers, n_cores, dbg=False):
    key = (tuple(layers), dbg)
    if key not in _cache:
        _cache[key] = build_program(layers, dbg)
    nc = _cache[key]
    shared = _prep_inputs(inputs)
    x = np.asarray(inputs["x"], dtype=np.float32)
    c = np.asarray(inputs["c"], dtype=np.float32)
    ctx = np.asarray(inputs["ctx"], dtype=np.float32)
    in_maps = []
    for i in range(n_cores):
        m = dict(shared)
        m["x"] = np.ascontiguousarray(x[i * NB:(i + 1) * NB])
        m["c"] = np.ascontiguousarray(c[i * NB:(i + 1) * NB])
        m["ctx"] = np.ascontiguousarray(ctx[i * NB:(i + 1) * NB])
        in_maps.append(m)
    res = run_bass_kernel_spmd(nc, in_maps, core_ids=list(range(n_cores)))
    y = np.concatenate([r["y"] for r in res.results], axis=0)
    if dbg:
        global DBG_OUT
        DBG_OUT = {k: np.asarray(v) for k, v in res.results[0].items() if k.startswith("dbg_")}
        return y, np.concatenate([r["ctx_out"] for r in res.results], axis=0)
    return y


def kernel(**inputs):
    return run(inputs, LAYERS, 8).astype(np.float32)
```

```python
import os
import numpy as np
from contextlib import ExitStack
import concourse.bass as bass
import concourse.mybir as mybir
from concourse.bass_utils import run_bass_kernel_spmd

F32 = mybir.dt.float32
BF16 = mybir.dt.bfloat16
ALU = mybir.AluOpType
AF = mybir.ActivationFunctionType
AX = mybir.AxisListType

D = 1024
T = 2048
TC = 256
NB = 2
DFF = 4096
EPS = 1e-6
NT = 18
TT = NB * NT
SEQ = TC + T


class Reg:
    __slots__ = ("w", "r", "excl")

    def __init__(self, excl=False):
        self.w = {}
        self.r = {}
        self.excl = excl


class Rot:
    def __init__(self, items):
        self.items = items
        self.i = 0

    def next(self):
        it = self.items[self.i % len(self.items)]
        self.i += 1
        return it


class KB:
    def __init__(self, nc):
        self.nc = nc
        self.eh = {"pe": nc.tensor, "dve": nc.vector, "act": nc.scalar, "pool": nc.gpsimd, "sp": nc.sync}
        self.esem = {k: nc.alloc_semaphore("es_" + k) for k in self.eh}
        self.ecnt = {k: 0 for k in self.eh}
        self.waited = {k: {} for k in self.eh}
        self.dpool = {q: [[nc.alloc_semaphore("ds_%s%d" % (q, i)), 0] for i in range(n)]
                      for q, n in (("sp", 32), ("pool", 16), ("act", 8))}
        self.dnext = {q: 0 for q in self.dpool}
        self.uid = 0

    def name(self, base):
        self.uid += 1
        return "%s_%d" % (base, self.uid)

    def sb(self, es, base, shape, dt):
        return es.enter_context(self.nc.sbuf_tensor(self.name(base), list(shape), dt))

    def rot(self, es, base, shape, dt, n):
        return Rot([(self.sb(es, base, shape, dt), Reg()) for _ in range(n)])

    def _wait(self, e, sem, val):
        w = self.waited[e]
        if w.get(sem.num, 0) < val:
            self.eh[e].wait_ge(sem, val)
            w[sem.num] = val

    def _need(self, e, reads, writes, is_dma):
        own = None if is_dma else self.esem[e].num
        need = {}

        def add(d, skip_same):
            for num, (sem, val) in d.items():
                if num == own and (skip_same or e == "pe"):
                    continue
                if need.get(num, (None, 0))[1] < val:
                    need[num] = (sem, val)
        for r in reads:
            add(r.w, False)
            if r.excl:
                add(r.r, True)
        for r in writes:
            add(r.w, True)
            add(r.r, True)
        return need

    def _mark(self, tok, reads, writes):
        num = tok[0].num
        for r in reads:
            r.r[num] = tok
        for r in writes:
            r.w = {num: tok}
            r.r = {}

    def op(self, e, fn, reads, writes):
        need = self._need(e, reads, writes, False)
        for sem, val in need.values():
            self._wait(e, sem, val)
        inst = fn(self.eh[e])
        self.ecnt[e] += 1
        inst.then_inc(self.esem[e], 1)
        self._mark((self.esem[e], self.ecnt[e]), reads, writes)

    def dma(self, q, out, in_, reads, writes, **kw):
        pool = self.dpool[q]
        i = self.dnext[q]
        self.dnext[q] = (i + 1) % len(pool)
        sem, val = pool[i]
        need = self._need(q, reads, writes, True)
        if val > 0:
            need[sem.num] = (sem, val)
        for s, v in need.values():
            self._wait(q, s, v)
        inst = self.eh[q].dma_start(out=out, in_=in_, **kw)
        inst.then_inc(sem, 16)
        pool[i][1] = val + 16
        self._mark((sem, val + 16), reads, writes)

    def barrier(self):
        toks = [(self.esem[e], self.ecnt[e]) for e in self.eh if self.ecnt[e] > 0]
        toks += [(s, v) for p in self.dpool.values() for (s, v) in p if v > 0]
        for e in self.eh:
            for s, v in toks:
                if s.num != self.esem[e].num:
                    self._wait(e, s, v)

    def mm(self, out, pairs, reads, writes):
        n = len(pairs)

        def fn(h):
            inst = None
            for i, (l, r) in enumerate(pairs):
                inst = h.matmul(out, l, r, start=(i == 0), stop=(i == n - 1))
            return inst
        self.op("pe", fn, reads, writes)

    def mm1(self, out, l, r, start, stop, reads, writes):
        self.op("pe", lambda h: h.matmul(out, l, r, start=start, stop=stop), reads, writes)

    def tr(self, outs_ins, ident, reads, writes):
        def fn(h):
            inst = None
            for o, i in outs_ins:
                inst = h.transpose(o, i, ident)
            return inst
        self.op("pe", fn, reads, writes)

    def act(self, out, in_, func, reads, writes, **kw):
        self.op("act", lambda h: h.activation(out=out, in_=in_, func=func, **kw), reads, writes)

    def ts(self, e, out, in0, s1, s2, op0, op1, reads, writes):
        if s2 is None:
            self.op(e, lambda h: h.tensor_scalar(out=out, in0=in0, scalar1=s1, scalar2=None, op0=op0), reads, writes)
        else:
            self.op(e, lambda h: h.tensor_scalar(out=out, in0=in0, scalar1=s1, scalar2=s2, op0=op0, op1=op1),
                    reads, writes)

    def tt(self, e, out, in0, in1, op, reads, writes):
        self.op(e, lambda h: h.tensor_tensor(out=out, in0=in0, in1=in1, op=op), reads, writes)

    def stt(self, e, out, in0, scalar, in1, op0, op1, reads, writes):
        self.op(e, lambda h: h.scalar_tensor_tensor(out=out, in0=in0, scalar=scalar, in1=in1, op0=op0, op1=op1),
                reads, writes)

    def cp(self, e, out, in_, reads, writes):
        if e == "act":
            self.op(e, lambda h: h.copy(out=out, in_=in_), reads, writes)
        else:
            self.op(e, lambda h: h.tensor_copy(out=out, in_=in_), reads, writes)

    def memset(self, e, ap, val, writes):
        self.op(e, lambda h: h.memset(ap, val), [], writes)

    def recip(self, out, in_, reads, writes):
        self.op("dve", lambda h: h.reciprocal(out=out, in_=in_), reads, writes)


class Stream:
    def __init__(self, xap, cap):
        self.x = xap
        self.c = cap
        self.regs = [Reg() for _ in range(TT)]

    def tile(self, tt):
        b, r = divmod(tt, NT)
        if r < 2:
            return self.c[b, r * 128:(r + 1) * 128, :]
        return self.x[b, (r - 2) * 128:(r - 1) * 128, :]


def tile_slot(tt):
    b, r = divmod(tt, NT)
    return 2 if r < 2 else b


class G:
    pass


def build_program(layers, dbg=False):
    nc = bass.Bass("TRN2", target_bir_lowering=False)
    kb = KB(nc)
    g = G()
    g.nc, g.kb = nc, kb
    g.dbg = dbg
    L = 4

    def din(name, shape, dt=F32):
        return nc.dram_tensor(name, list(shape), dt, kind="ExternalInput").ap()

    def dscr(name, shape, dt=F32):
        return nc.dram_tensor(name, list(shape), dt, kind="Internal").ap()

    I = {}
    I["x"] = din("x", [NB, T, D])
    I["c"] = din("c", [NB, D])
    I["ctx"] = din("ctx", [NB, TC, D])
    I["c_ctx"] = din("c_ctx", [1, D])
    I["ada_w"] = din("ada_w", [L, D, 6 * D])
    I["ada_b"] = din("ada_b", [L, 6 * D])
    for nm in ("mix_pre_g", "mix_post_g", "mlp_pre_g", "mlp_post_g"):
        I[nm] = din(nm, [L, D])
    I["mlp_w1"] = din("mlp_w1", [L, D, DFF])
    I["mlp_w2"] = din("mlp_w2", [L, DFF, D])
    I["ret_w_in"] = din("ret_w_in", [D, 8192])
    I["ret_decay_logit"] = din("ret_decay_logit", [1, 16])
    I["ret_w_out"] = din("ret_w_out", [2048, D])
    I["att_w_in"] = din("att_w_in", [D, 1536])
    I["att_sink"] = din("att_sink", [1, 16])
    I["att_w_out"] = din("att_w_out", [D, D])
    I["pool_w"] = din("pool_w", [4, 256, 256])
    I["pool_b"] = din("pool_b", [1, D])
    I["pool_scale"] = din("pool_scale", [1, D])
    I["dn_w_in"] = din("dn_w_in", [D, 5152])
    I["dn_conv_w"] = din("dn_conv_w", [5, 3072])
    I["dn_a_log"] = din("dn_a_log", [1, 16])
    I["dn_dt_bias"] = din("dn_dt_bias", [1, 16])
    I["dn_norm_g"] = din("dn_norm_g", [1, 128])
    I["dn_w_out"] = din("dn_w_out", [D, D])
    I["k_ident"] = din("k_ident", [128, 128])
    I["k_pool"] = din("k_pool", [20, 128, 128])
    I["k_rope_att"] = din("k_rope_att", [2, 128, SEQ])
    I["k_rope_ret"] = din("k_rope_ret", [2, 128, SEQ])
    I["k_att_mask"] = din("k_att_mask", [3, 128, 384])
    I["k_ret_tab"] = din("k_ret_tab", [5, 128, 128])
    I["k_dn"] = din("k_dn", [4, 64, 64])
    g.I = I
    yout = nc.dram_tensor("y", [NB, T, D], F32, kind="ExternalOutput").ap()

    g.modD = dscr("modD", [L, 3, 6 * D])
    s_in = Stream(I["x"], I["ctx"])
    s1 = Stream(dscr("s1x", [NB, T, D]), dscr("s1c", [NB, TC, D]))
    s2 = Stream(dscr("s2x", [NB, T, D]), dscr("s2c", [NB, TC, D]))
    s_out = Stream(yout, s2.c)
    g.modD_r = Reg()

    g.ps = [nc.alloc_psum_tensor("psb%d" % i, [128, 512], F32) for i in range(8)]
    g.psr = [Reg(excl=True) for _ in range(8)]

    with ExitStack() as ges:
        g.ident_f = kb.sb(ges, "identf", [128, 128], F32)
        g.ident_b = kb.sb(ges, "identb", [128, 128], BF16)
        g.eps = kb.sb(ges, "eps", [128, 1], F32)
        g.const_r = Reg()
        kb.dma("sp", g.ident_f[:], I["k_ident"][:, :], [], [g.const_r])
        kb.dma("pool", g.ident_b[:], I["k_ident"][:, :], [], [g.const_r])
        kb.memset("dve", g.eps[:], EPS, [g.const_r])
        g.eps128 = kb.sb(ges, "eps128", [128, 1], F32)
        kb.memset("dve", g.eps128[:], EPS * 128.0, [g.const_r])
        kb.barrier()

        prologue_ada(g, layers)
        kb.barrier()

        cur = s_in
        for li, l in enumerate(layers):
            last = (li == len(layers) - 1)
            need_ctx = (l < 3) or dbg
            kind = l % 4
            if kind == 2:
                mixer_pool(g, l, cur, s1, need_ctx)
            elif kind == 1:
                mixer_att(g, l, cur, s1, need_ctx)
            elif kind == 0:
                mixer_ret(g, l, cur, s1, need_ctx)
            elif kind == 3:
                mixer_dn(g, l, cur, s1, need_ctx)
            else:
                raise NotImplementedError
            kb.barrier()
            dst = s_out if last else s2
            ffn(g, l, s1, dst, need_ctx)
            kb.barrier()
            cur = s2
        if dbg:
            cout = nc.dram_tensor("ctx_out", [NB, TC, D], F32, kind="ExternalOutput").ap()
            for b in range(NB):
                kb.dma("sp", cout[b], s2.c[b], [s2.regs[b * NT], s2.regs[b * NT + 1], s_out.regs[b * NT],
                                                s_out.regs[b * NT + 1]], [Reg()])
    kb.barrier()
    return nc


def prologue_ada(g, layers):
    kb, nc, I = g.kb, g.nc, g.I
    with ExitStack() as es:
        condT = kb.sb(es, "condT", [128, 8, 4], F32)
        cr = Reg()
        for s in range(3):
            src = I["c"][s, :] if s < 2 else I["c_ctx"][0, :]
            kb.dma("sp", condT[:, :, s], src.rearrange("(c p) -> p c", p=128), [], [cr],
                   allow_slow_non_contiguous=True)
        kb.memset("dve", condT[:, :, 3], 0.0, [cr])
        kb.act(condT[:, :, 0:3], condT[:, :, 0:3], AF.Silu, [cr], [cr])
        wrot = kb.rot(es, "adaw", [128, 8, 512], F32, 3)
        modrow = kb.sb(es, "modrow", [3, 6 * D], F32)
        mr = Reg()
        bias = kb.sb(es, "adab", [3, 6 * D], F32)
        gains = kb.sb(es, "gains", [3, 4, D], F32)
        br = Reg()
        for l in layers:
            kb.dma("sp", bias[:], I["ada_b"][l, :].partition_broadcast(3), [], [br])
            for gi, nm in enumerate(("mix_pre_g", "mix_post_g", "mlp_pre_g", "mlp_post_g")):
                kb.dma("sp", gains[:, gi, :], I[nm][l, :].partition_broadcast(3), [], [br])
            for j in range(12):
                wt, wr = wrot.next()
                kb.dma("sp", wt[:], I["ada_w"][l, :, j * 512:(j + 1) * 512].rearrange("(c p) n -> p c n", p=128),
                       [], [wr])
                pb = j % 2
                kb.mm(g.ps[pb][0:3, :], [(condT[:, kc, 0:3], wt[:, kc, :]) for kc in range(8)],
                      [cr, wr], [g.psr[pb]])
                kb.tt("dve", modrow[:, j * 512:(j + 1) * 512], g.ps[pb][0:3, :], bias[:, j * 512:(j + 1) * 512],
                      ALU.add, [g.psr[pb], br], [mr])
            for seg, gi, plus1 in ((1, 0, True), (2, 1, False), (4, 2, True), (5, 3, False)):
                sl = modrow[:, seg * D:(seg + 1) * D]
                if plus1:
                    kb.stt("dve", sl, sl, 1.0, gains[:, gi, :], ALU.add, ALU.mult, [mr, br], [mr])
                else:
                    kb.tt("dve", sl, sl, gains[:, gi, :], ALU.mult, [mr, br], [mr])
            kb.dma("sp", g.modD[l], modrow[:], [mr], [g.modD_r])


def load_cols(g, es, l, segA, segB):
    kb = g.kb
    r = Reg()
    outs = []
    for seg in (segA, segB):
        t = kb.sb(es, "mcol", [128, 3, 8], F32)
        for s in range(3):
            kb.dma("sp", t[:, s, :], g.modD[l, s, seg * D:(seg + 1) * D].rearrange("(c p) -> p c", p=128),
                   [g.modD_r], [r], allow_slow_non_contiguous=True)
        outs.append(t)
    return outs[0], outs[1], r


def load_bc(g, es, l, seg):
    kb = g.kb
    out = []
    r = Reg()
    for s in range(3):
        t = kb.sb(es, "mbc", [128, D], F32)
        kb.dma("sp", t[:], g.modD[l, s, seg * D:(seg + 1) * D].partition_broadcast(128), [g.modD_r], [r])
        out.append(t)
    return out, r


class NormWS:
    def __init__(self, g, es, nbuf=2):
        kb = g.kb
        self.st = kb.rot(es, "nst", [128, 4], F32, 8)
        self.junk = kb.sb(es, "njunk", [128, D], BF16)
        self.junk_r = Reg()
        self.xn = kb.rot(es, "nxn", [128, D], BF16, nbuf)


def rms_stats_gen(g, ws, y_aps, y_regs, out, isn=1.0 / 32.0):
    kb = g.kb
    st, sr = ws.st.next()
    out.append((st, sr))
    kb.memset("pool", st[:], 0.0, [sr])
    yield
    off = 0
    for i, ya in enumerate(y_aps):
        n = ya.shape[-1]
        kb.act(ws.junk[:, off:off + n], ya, AF.Square, y_regs + [sr], [ws.junk_r, sr], scale=isn,
               accum_out=st[:, i:i + 1])
        off += n
    yield
    if len(y_aps) == 2:
        kb.tt("dve", st[:, 0:1], st[:, 0:1], st[:, 1:2], ALU.add, [sr], [sr])
        yield
    kb.act(st[:, 1:2], st[:, 0:1], AF.Sqrt, [sr, g.const_r], [sr], bias=g.eps[:, 0:1], scale=1.0)
    yield
    kb.recip(st[:, 2:3], st[:, 1:2], [sr], [sr])
    yield


def rms_stats(g, ws, y_aps, y_regs, isn=1.0 / 32.0):
    out = []
    for _ in rms_stats_gen(g, ws, y_aps, y_regs, out, isn):
        pass
    return out[0]


def norm_T_gen(g, ws, xt, xr, Acol, Bcol, colr, slot, dst, dst_r, pbank):
    kb = g.kb
    out = []
    yield from rms_stats_gen(g, ws, [xt], [xr], out)
    st, sr = out[0]
    xn, xnr = ws.xn.next()
    kb.act(xn[:], xt, AF.Copy, [xr, sr], [xnr], scale=st[:, 2:3])
    yield
    psT = g.ps[pbank][:].bitcast(BF16).rearrange("p (c t) -> p c t", c=8)
    kb.tr([(psT[:, c, :], xn[:, c * 128:(c + 1) * 128]) for c in range(8)], g.ident_b[:],
          [xnr, g.const_r], [g.psr[pbank]])
    yield
    for c in range(8):
        kb.ts("dve", dst[:, c, :], psT[:, c, :], Acol[:, slot, c:c + 1], Bcol[:, slot, c:c + 1], ALU.mult, ALU.add,
              [g.psr[pbank], colr], [dst_r])
    yield


def norm_T(*args):
    for _ in norm_T_gen(*args):
        pass


def post_res_gen(g, ws, y_aps, y_regs, xt, xr, Gbc, gr, tmp, tmpr):
    kb = g.kb
    out = []
    yield from rms_stats_gen(g, ws, y_aps, y_regs, out)
    st, sr = out[0]
    off = 0
    for ya in y_aps:
        n = ya.shape[-1]
        kb.stt("dve", tmp[:, off:off + n], ya, st[:, 2:3], Gbc[:, off:off + n], ALU.mult, ALU.mult,
               y_regs + [sr, gr], [tmpr])
        off += n
    yield
    kb.tt("pool", xt, xt, tmp[:, :], ALU.add, [xr, tmpr], [xr])
    yield


def post_res(*args):
    for _ in post_res_gen(*args):
        pass


def ffn(g, l, src, dst, need_ctx):
    kb, nc, I = g.kb, g.nc, g.I
    with ExitStack() as es:
        w1 = kb.sb(es, "w1", [128, 8, DFF], BF16)
        w2 = kb.sb(es, "w2", [128, 32, D], BF16)
        w1r = [Reg() for _ in range(8)]
        w2r = [Reg() for _ in range(8)]
        for kc in range(8):
            for hf in range(2):
                kb.dma("pool", w1[:, kc, hf * 2048:(hf + 1) * 2048],
                       I["mlp_w1"][l, kc * 128:(kc + 1) * 128, hf * 2048:(hf + 1) * 2048], [], [w1r[kc]])
        for q in range(8):
            kb.dma("pool", w2[:, q * 4:(q + 1) * 4, :],
                   I["mlp_w2"][l, q * 512:(q + 1) * 512, :].rearrange("(c p) n -> p c n", p=128), [], [w2r[q]])
        Acol, Bcol, acr = load_cols(g, es, l, 4, 3)
        Gbc, gbr = load_bc(g, es, l, 5)
        ws = NormWS(g, es)
        xrot = kb.rot(es, "fx", [128, D], F32, 6)
        uT = kb.rot(es, "fuT", [128, 8, 256], BF16, 2)
        hT = kb.sb(es, "fhT", [128, 32, 256], BF16)
        hTr = [Reg() for _ in range(32)]
        rl = kb.rot(es, "frl", [128, 256], F32, 4)
        tmp = kb.rot(es, "ftmp", [128, D], F32, 2)
        tiles = [tt for tt in range(TT) if need_ctx or (tt % NT) >= 2]
        groups = [tiles[i:i + 2] for i in range(0, len(tiles), 2)]
        hslot = 0

        def prep(grp):
            u, ur = uT.next()
            xs = []
            for j, tt in enumerate(grp):
                xt, xr = xrot.next()
                kb.dma("sp", xt[:], src.tile(tt), [src.regs[tt]], [xr])
                norm_T(g, ws, xt[:], xr, Acol, Bcol, acr, tile_slot(tt), u[:, :, j * 128:(j + 1) * 128], ur, 4)
                xs.append((xt, xr))
            return u, ur, xs
        nxt = prep(groups[0])
        for gi_, grp in enumerate(groups):
            u, ur, xs = nxt
            for fc in range(32):
                hb = 5 + (hslot // 2) % 2
                hh = hslot % 2
                hslot += 1
                hp = g.ps[hb][:, hh * 256:(hh + 1) * 256]
                kb.mm(hp, [(w1[:, kc, fc * 128:(fc + 1) * 128], u[:, kc, :]) for kc in range(8)],
                      [ur] + w1r, [g.psr[hb]])
                r_, rr = rl.next()
                kb.act(r_[:], hp, AF.Relu, [g.psr[hb]], [rr])
                kb.tt("dve", hT[:, fc, :], r_[:], r_[:], ALU.mult, [rr], [hTr[fc]])
            if gi_ + 1 < len(groups):
                nxt = prep(groups[gi_ + 1])
            for j, tt in enumerate(grp):
                xt, xr = xs[j]
                for half in range(2):
                    pb = j * 2 + half
                    kb.mm(g.ps[pb][:, :], [(hT[:, fc, j * 128:(j + 1) * 128], w2[:, fc, half * 512:(half + 1) * 512])
                                           for fc in range(32)], hTr + w2r, [g.psr[pb]])
                t_, tr_ = tmp.next()
                post_res(g, ws, [g.ps[j * 2][:, :], g.ps[j * 2 + 1][:, :]], [g.psr[j * 2], g.psr[j * 2 + 1]],
                         xt[:], xr, Gbc[tile_slot(tt)], gbr, t_, tr_)
                kb.dma("sp", dst.tile(tt), xt[:], [xr], [dst.regs[tt]])


def mixer_pool(g, l, src, dst, need_ctx):
    kb, nc, I = g.kb, g.nc, g.I
    with ExitStack() as es:
        Acol, Bcol, acr = load_cols(g, es, l, 1, 0)
        Gbc, gbr = load_bc(g, es, l, 2)
        ws = NormWS(g, es)
        wg = kb.sb(es, "pw", [128, 4, 2, 256], BF16)
        wgr = Reg()
        for gi in range(4):
            kb.dma("pool", wg[:, gi, :, :], I["pool_w"][gi].rearrange("(c p) n -> p c n", p=128), [], [wgr])
        pm = kb.sb(es, "pm", [128, 20, 128], BF16)
        kb.dma("pool", pm[:], I["k_pool"].rearrange("m p n -> p m n"), [], [wgr])
        pbb = kb.sb(es, "pbb", [128, D], F32)
        psb = kb.sb(es, "psb", [128, D], F32)
        kb.dma("sp", pbb[:], I["pool_b"][0, :].partition_broadcast(128), [], [wgr])
        kb.dma("sp", psb[:], I["pool_scale"][0, :].partition_broadcast(128), [], [wgr])
        xrot = kb.rot(es, "px", [128, D], F32, 6)
        uT = kb.rot(es, "puT", [128, 8, 128], BF16, 2)
        vrot = kb.rot(es, "pv", [128, D], BF16, 5)
        yb = kb.rot(es, "pyb", [128, D], F32, 2)
        tmp = kb.rot(es, "ptmp", [128, D], F32, 2)
        for b in range(NB):
            for seg_lo, seg_n in ((0, 2), (2, 16)):
                if seg_lo == 0 and not need_ctx:
                    continue
                xs, vs = {}, {}
                for t in range(seg_n + 1):
                    if t < seg_n:
                        tt = b * NT + seg_lo + t
                        xt, xr = xrot.next()
                        kb.dma("sp", xt[:], src.tile(tt), [src.regs[tt]], [xr])
                        u, ur = uT.next()
                        norm_T(g, ws, xt[:], xr, Acol, Bcol, acr, tile_slot(tt), u, ur, 4)
                        for gi in range(4):
                            pb = 5 + gi // 2
                            kb.mm(g.ps[pb][:, (gi % 2) * 256:(gi % 2 + 1) * 256],
                                  [(u[:, gi * 2 + kc, :], wg[:, gi, kc, :]) for kc in range(2)],
                                  [ur, wgr], [g.psr[pb]])
                        v, vr = vrot.next()
                        kb.cp("act", v[:, 0:512], g.ps[5][:, :], [g.psr[5]], [vr])
                        kb.cp("act", v[:, 512:1024], g.ps[6][:, :], [g.psr[6]], [vr])
                        xs[t], vs[t] = (xt, xr), (v, vr)
                    if t >= 1:
                        tq = t - 1
                        tt = b * NT + seg_lo + tq
                        for gi in range(4):
                            pb = gi // 2
                            pairs = []
                            regs = [wgr]
                            cidx = 0 if (0 < tq < seg_n - 1) else (1 if tq == 0 else 2)
                            pairs.append((pm[:, gi * 5 + cidx, :], vs[tq][0][:, gi * 256:(gi + 1) * 256]))
                            regs.append(vs[tq][1])
                            if tq > 0:
                                pairs.append((pm[:, gi * 5 + 3, :], vs[tq - 1][0][:, gi * 256:(gi + 1) * 256]))
                                regs.append(vs[tq - 1][1])
                            if tq < seg_n - 1:
                                pairs.append((pm[:, gi * 5 + 4, :], vs[tq + 1][0][:, gi * 256:(gi + 1) * 256]))
                                regs.append(vs[tq + 1][1])
                            kb.mm(g.ps[pb][:, (gi % 2) * 256:(gi % 2 + 1) * 256], pairs, regs, [g.psr[pb]])
                        y_, yr = yb.next()
                        for h in range(2):
                            kb.tt("dve", y_[:, h * 512:(h + 1) * 512], g.ps[h][:, :], pbb[:, h * 512:(h + 1) * 512],
                                  ALU.add, [g.psr[h], wgr], [yr])
                        kb.tt("pool", y_[:], y_[:], psb[:], ALU.mult, [yr, wgr], [yr])
                        xt, xr = xs[tq]
                        t_, tr_ = tmp.next()
                        post_res(g, ws, [y_[:, :]], [yr], xt[:], xr, Gbc[tile_slot(tt)], gbr, t_, tr_)
                        kb.dma("sp", dst.tile(tt), xt[:], [xr], [dst.regs[tt]])


def phase_uT(g, es, l, src):
    kb = g.kb
    uT = kb.sb(es, "uTall", [128, 8, TT * 128], BF16)
    regs = [Reg() for _ in range(TT)]
    with ExitStack() as es2:
        Acol, Bcol, acr = load_cols(g, es2, l, 1, 0)
        ws = NormWS(g, es2, 4)
        xrot = kb.rot(es2, "ux", [128, D], F32, 4)

        def unit(tt, slot):
            xt, xr = xrot.next()
            kb.dma("sp", xt[:], src.tile(tt), [src.regs[tt]], [xr])
            yield
            yield from norm_T_gen(g, ws, xt[:], xr, Acol, Bcol, acr, tile_slot(tt), uT[:, :, tt * 128:(tt + 1) * 128],
                                  regs[tt], 4 + slot)
        interleave((unit(tt, tt % 3) for tt in range(TT)), 3)
        kb.barrier()
    return uT, regs


def load_w_bf16(g, rot, wd, c0, ncols):
    kb = g.kb
    wt, wr = rot.next()
    kb.dma("pool", wt[:, :, 0:ncols], wd[:, c0:c0 + ncols].rearrange("(c p) n -> p c n", p=128), [], [wr])
    return wt, wr


def phase_outproj(g, l, a_d, a_regs, Kd, wout_d, src, dst, need_ctx):
    kb, nc = g.kb, g.nc
    nk = Kd // 128
    with ExitStack() as es:
        Gbc, gbr = load_bc(g, es, l, 2)
        ws = NormWS(g, es)
        wo = kb.sb(es, "wo", [128, nk, D], BF16)
        wor = Reg()
        for q in range(nk // 4):
            kb.dma("pool", wo[:, q * 4:(q + 1) * 4, :],
                   wout_d[q * 512:(q + 1) * 512, :].rearrange("(c p) n -> p c n", p=128), [], [wor])
        a_list = a_d if isinstance(a_d, list) else [a_d]
        arot = kb.rot(es, "oa", [128, Kd], BF16, 3 * len(a_list))
        aTrot = kb.rot(es, "oaT", [128, nk, 128], BF16, 3)
        xrot = kb.rot(es, "ox", [128, D], F32, 3)
        tmp = kb.rot(es, "otmp", [128, D], F32, 3)
        def unit(tt, slot):
            at, ar = arot.next()
            kb.dma("sp", at[:], a_list[0][tt * 128:(tt + 1) * 128, :], [a_regs[tt]], [ar])
            xt, xr = xrot.next()
            kb.dma("sp", xt[:], src.tile(tt), [src.regs[tt]], [xr])
            for extra in a_list[1:]:
                at2, ar2 = arot.next()
                kb.dma("sp", at2[:], extra[tt * 128:(tt + 1) * 128, :], [a_regs[tt]], [ar2])
                yield
                kb.tt("pool", at[:], at[:], at2[:], ALU.add, [ar, ar2], [ar])
            yield
            aT, aTr = aTrot.next()
            pb = slot * 3 + 2
            for q in range(nk // 8):
                psT = g.ps[pb][:].bitcast(BF16).rearrange("p (c t) -> p c t", c=8)
                kb.tr([(psT[:, c, :], at[:, (q * 8 + c) * 128:(q * 8 + c + 1) * 128]) for c in range(8)],
                      g.ident_b[:], [ar, g.const_r], [g.psr[pb]])
                yield
                kb.cp("act", aT[:, q * 8:(q + 1) * 8, :], psT, [g.psr[pb]], [aTr])
                yield
            yb = slot * 3
            for half in range(2):
                kb.mm(g.ps[yb + half][:, :], [(aT[:, kc, :], wo[:, kc, half * 512:(half + 1) * 512])
                                              for kc in range(nk)], [aTr, wor], [g.psr[yb + half]])
            yield
            t_, tr_ = tmp.next()
            yield from post_res_gen(g, ws, [g.ps[yb][:, :], g.ps[yb + 1][:, :]], [g.psr[yb], g.psr[yb + 1]], xt[:], xr,
                                    Gbc[tile_slot(tt)], gbr, t_, tr_)
            kb.dma("sp", dst.tile(tt), xt[:], [xr], [dst.regs[tt]])
        tts = [tt for tt in range(TT) if need_ctx or (tt % NT) >= 2]
        interleave((unit(tt, i % 2) for i, tt in enumerate(tts)), 2)


def make_perm(g, wt, wr, wp, wpr, ncols, blk):
    kb = g.kb
    v_in = wt[:, :, 0:ncols].rearrange("p c (q two i) -> p c q two i", two=2, i=blk)
    v_out = wp[:, :, 0:ncols].rearrange("p c (q two i) -> p c q two i", two=2, i=blk)
    for kc in range(8):
        kb.cp("pool", v_out[:, kc, :, 0, :], v_in[:, kc, :, 1, :], [wr], [wpr])
        kb.cp("pool", v_out[:, kc, :, 1, :], v_in[:, kc, :, 0, :], [wr], [wpr])


SEQ_BLOCKS = [(b, o, min(512, SEQ - o)) for b in range(NB) for o in range(0, SEQ, 512)]


def proj_fm_rope(g, uT, uTr, wd, c0, rope, rope_r, scale, blk, out_d, out_r, wrot, wprot, stg, pbase):
    kb = g.kb
    wt, wr = load_w_bf16(g, wrot, wd, c0, 128)
    wp, wpr = wprot.next()
    make_perm(g, wt, wr, wp, wpr, 128, blk)
    for bi, (b, o, n) in enumerate(SEQ_BLOCKS):
        t0 = b * SEQ + o
        tiles = list(range(t0 // 128, (t0 + n) // 128))
        rr = [uTr[t] for t in tiles]
        pa, pb = pbase + (bi % 2) * 2, pbase + (bi % 2) * 2 + 1
        kb.mm(g.ps[pa][:, 0:n], [(wt[:, kc, 0:128], uT[:, kc, t0:t0 + n]) for kc in range(8)], rr + [wr], [g.psr[pa]])
        kb.mm(g.ps[pb][:, 0:n], [(wp[:, kc, 0:128], uT[:, kc, t0:t0 + n]) for kc in range(8)], rr + [wpr], [g.psr[pb]])
        (t1, t1r), (t2, t2r), (t3, t3r) = stg[0].next(), stg[1].next(), stg[2].next()
        kb.stt("dve", t1[:, 0:n], g.ps[pa][:, 0:n], scale, rope[:, 0, o:o + n], ALU.mult, ALU.mult,
               [g.psr[pa], rope_r], [t1r])
        kb.stt("dve", t2[:, 0:n], g.ps[pb][:, 0:n], scale, rope[:, 1, o:o + n], ALU.mult, ALU.mult,
               [g.psr[pb], rope_r], [t2r])
        kb.tt("pool", t3[:, 0:n], t1[:, 0:n], t2[:, 0:n], ALU.add, [t1r, t2r], [t3r])
        kb.dma("sp", out_d[:, t0:t0 + n], t3[:, 0:n], [t3r], [out_r])


def proj_tm(g, uT, uTr, wd, c0, ncols, out_d, out_regs, col0, wrot, stg, pbase, func=None, tiles=None):
    kb = g.kb
    wt, wr = load_w_bf16(g, wrot, wd, c0, ncols)
    for i, tt in enumerate(tiles if tiles is not None else range(TT)):
        pb = pbase + i % 2
        kb.mm(g.ps[pb][:, 0:ncols], [(uT[:, kc, tt * 128:(tt + 1) * 128], wt[:, kc, 0:ncols]) for kc in range(8)],
              [uTr[tt], wr], [g.psr[pb]])
        st_, sr_ = stg.next()
        if func is None:
            kb.cp("act", st_[:, 0:ncols], g.ps[pb][:, 0:ncols], [g.psr[pb]], [sr_])
        else:
            kb.act(st_[:, 0:ncols], g.ps[pb][:, 0:ncols], func, [g.psr[pb]], [sr_])
        kb.dma("sp", out_d[tt * 128:(tt + 1) * 128, col0:col0 + ncols], st_[:, 0:ncols], [sr_], [out_regs[tt]])


def mixer_att(g, l, src, dst, need_ctx):
    kb, nc, I = g.kb, g.nc, g.I
    NTOK = TT * 128
    attq = nc.dram_tensor(kb.name("attq"), [8, 128, NTOK], BF16, kind="Internal").ap()
    attk = nc.dram_tensor(kb.name("attk"), [2, 128, NTOK], BF16, kind="Internal").ap()
    attv = nc.dram_tensor(kb.name("attv"), [NTOK, 256], BF16, kind="Internal").ap()
    atta = nc.dram_tensor(kb.name("atta"), [NTOK, D], BF16, kind="Internal").ap()
    qr_, kr_ = [Reg() for _ in range(8)], [Reg() for _ in range(2)]
    vr_ = [Reg() for _ in range(TT)]
    ar_ = [Reg() for _ in range(TT)]
    with ExitStack() as es:
        uT, uTr = phase_uT(g, es, l, src)
        rope = kb.sb(es, "ropeA", [128, 2, SEQ], F32)
        rope_r = Reg()
        kb.dma("sp", rope[:], I["k_rope_att"].rearrange("t p n -> p t n"), [], [rope_r])
        wrot = kb.rot(es, "aw", [128, 8, 256], BF16, 2)
        wprot = kb.rot(es, "awp", [128, 8, 128], BF16, 2)
        stg = [kb.rot(es, "astg", [128, 512], F32, 2), kb.rot(es, "astg", [128, 512], F32, 2),
               kb.rot(es, "astgb", [128, 512], BF16, 3)]
        for cb in range(8):
            proj_fm_rope(g, uT, uTr, I["att_w_in"], cb * 128, rope, rope_r, 0.125, 16, attq[cb], qr_[cb],
                         wrot, wprot, stg, 0)
        for cb in range(2):
            proj_fm_rope(g, uT, uTr, I["att_w_in"], 1024 + cb * 128, rope, rope_r, 1.0, 16, attk[cb], kr_[cb],
                         wrot, wprot, stg, 0)
        proj_tm(g, uT, uTr, I["att_w_in"], 1280, 256, attv, vr_, 0, wrot, stg[2], 0)
    kb.barrier()
    with ExitStack() as es:
        mb = kb.sb(es, "amask", [128, 3, 384], BF16)
        cr = Reg()
        kb.dma("pool", mb[:], I["k_att_mask"].rearrange("v p n -> p v n"), [], [cr])
        sink = kb.sb(es, "asink", [128, 16], F32)
        kb.dma("sp", sink[:], I["att_sink"][0, :].partition_broadcast(128), [], [cr])
        Krot = kb.rot(es, "aK", [64, 2560], BF16, 3)
        Vrot = kb.rot(es, "aV", [128, 20, 64], BF16, 3)
        for kt, kr in Krot.items:
            kb.memset("pool", kt[:, 0:128], 0.0, [kr])
            kb.memset("pool", kt[:, 2176:2304], 0.0, [kr])
        for vt, vr in Vrot.items:
            kb.memset("pool", vt[:, 0, :], 0.0, [vr])
            kb.memset("pool", vt[:, 17, :], 0.0, [vr])
        Qrot = kb.rot(es, "aQ", [64, SEQ], BF16, 4)
        prot = kb.rot(es, "ap", [128, 640], BF16, 4)
        pTrot = kb.rot(es, "apT", [128, 5, 128], BF16, 4)
        strot = kb.rot(es, "ast", [128, 8], F32, 8)
        ostg = kb.rot(es, "aos", [128, 64], BF16, 6)

        def att_unit(b, h, blk, slot, cnt, Qt, Qr, Kt, Kr, Vt, Vr):
            lat = blk >= 2
            bi = blk - 2
            qs = Qt[:, blk * 128:(blk + 1) * 128]
            pa, pbk = slot * 2, slot * 2 + 1
            ptb = 4 + slot
            po = g.ps[6 + slot][:, (cnt % 8) * 64:(cnt % 8) * 64 + 64]
            por = g.psr[6 + slot]
            st, sr = strot.next()
            if lat:
                var = 1 if bi == 0 else (2 if bi == 15 else 0)
                kb.mm(g.ps[pa][:, 0:384], [(qs, Kt[:, bi * 128:bi * 128 + 384]),
                                           (g.ident_b[:], mb[:, var, :])], [Qr, Kr, cr, g.const_r], [g.psr[pa]])
            kb.mm(g.ps[pbk][:, 0:256], [(qs, Kt[:, 2304:2560])], [Qr, Kr], [g.psr[pbk]])
            kb.memset("pool", st[:, 4:7], 0.0, [sr])
            yield
            if lat:
                kb.op("dve", lambda hd, o=st[:, 0:1], i=g.ps[pa][:, 0:384]: hd.reduce_max(out=o, in_=i, axis=AX.X),
                      [g.psr[pa]], [sr])
            kb.op("dve", lambda hd, o=st[:, 1:2], i=g.ps[pbk][:, 0:256]: hd.reduce_max(out=o, in_=i, axis=AX.X),
                  [g.psr[pbk]], [sr])
            if lat:
                kb.tt("dve", st[:, 1:2], st[:, 0:1], st[:, 1:2], ALU.max, [sr], [sr])
            kb.ts("dve", st[:, 2:3], st[:, 1:2], sink[:, h:h + 1], -1.0, ALU.max, ALU.mult, [sr, cr], [sr])
            yield
            p_, pr = prot.next()
            if lat:
                kb.act(p_[:, 0:384], g.ps[pa][:, 0:384], AF.Exp, [g.psr[pa], sr], [pr, sr],
                       bias=st[:, 2:3], scale=1.0, accum_out=st[:, 4:5])
            kb.act(p_[:, 384:640], g.ps[pbk][:, 0:256], AF.Exp, [g.psr[pbk], sr], [pr, sr],
                   bias=st[:, 2:3], scale=1.0, accum_out=st[:, 5:6])
            kb.act(st[:, 6:7], st[:, 2:3], AF.Exp, [sr, cr], [sr], bias=sink[:, h:h + 1], scale=1.0)
            yield
            js = list(range(5)) if lat else [3, 4]
            psT = g.ps[ptb][:].bitcast(BF16).rearrange("p (c t) -> p c t", c=8)
            kb.tr([(psT[:, j, :], p_[:, j * 128:(j + 1) * 128]) for j in js], g.ident_b[:],
                  [pr, g.const_r], [g.psr[ptb]])
            kb.stt("dve", st[:, 7:8], st[:, 4:5], st[:, 5:6], st[:, 6:7], ALU.add, ALU.add, [sr], [sr])
            yield
            pT, pTr = pTrot.next()
            kb.cp("dve", pT[:, js[0]:5, :], psT[:, js[0]:5, :], [g.psr[ptb]], [pTr])
            kb.recip(st[:, 3:4], st[:, 7:8], [sr], [sr])
            yield
            pairs = []
            if lat:
                pairs += [(pT[:, j, :], Vt[:, bi + j, :]) for j in range(3)]
            pairs += [(pT[:, 3, :], Vt[:, 18, :]), (pT[:, 4, :], Vt[:, 19, :])]
            kb.mm(po, pairs, [pTr, Vr], [por])
            yield
            os_, osr = ostg.next()
            kb.ts("dve", os_[:], po, st[:, 3:4], None, ALU.mult, None, [por, sr], [osr])
            tt = b * NT + blk
            kb.dma("sp", atta[tt * 128:(tt + 1) * 128, h * 64:(h + 1) * 64], os_[:], [osr], [Reg()])

        def att_units():
            cnt = 0
            for b in range(NB):
                for kv in range(4):
                    Kt, Kr = Krot.next()
                    Vt, Vr = Vrot.next()
                    base = b * SEQ
                    ksrc = attk[kv // 2, (kv % 2) * 64:(kv % 2) * 64 + 64, :]
                    kb.dma("sp", Kt[:, 128:2176], ksrc[:, base + TC:base + SEQ], [kr_[kv // 2]], [Kr])
                    kb.dma("sp", Kt[:, 2304:2560], ksrc[:, base:base + TC], [kr_[kv // 2]], [Kr])
                    vsrc = attv[:, kv * 64:(kv + 1) * 64]
                    kb.dma("sp", Vt[:, 1:17, :], vsrc[base + TC:base + SEQ, :].rearrange("(t p) d -> p t d", p=128),
                           [vr_[b * NT + t] for t in range(2, 18)], [Vr])
                    kb.dma("sp", Vt[:, 18:20, :], vsrc[base:base + TC, :].rearrange("(t p) d -> p t d", p=128),
                           [vr_[b * NT], vr_[b * NT + 1]], [Vr])
                    for hh in range(4):
                        h = kv * 4 + hh
                        Qt, Qr = Qrot.next()
                        kb.dma("sp", Qt[:], attq[h // 2, (h % 2) * 64:(h % 2) * 64 + 64, base:base + SEQ],
                               [qr_[h // 2]], [Qr])
                        for blk in range(18):
                            if blk < 2 and not need_ctx:
                                continue
                            yield att_unit(b, h, blk, cnt % 2, cnt // 2, Qt, Qr, Kt, Kr, Vt, Vr)
                            cnt += 1
        interleave(att_units(), 2)
    kb.barrier()
    phase_outproj(g, l, atta, ar_, D, I["att_w_out"], src, dst, need_ctx)


def mixer_ret(g, l, src, dst, need_ctx):
    kb, nc, I = g.kb, g.nc, g.I
    NTOK = TT * 128
    retq = nc.dram_tensor(kb.name("retq"), [8, 128, NTOK], BF16, kind="Internal").ap()
    retk = nc.dram_tensor(kb.name("retk"), [8, 128, NTOK], BF16, kind="Internal").ap()
    retv = nc.dram_tensor(kb.name("retv"), [NTOK, 2048], BF16, kind="Internal").ap()
    retg = nc.dram_tensor(kb.name("retg"), [NTOK, 4096], BF16, kind="Internal").ap()
    reta = [nc.dram_tensor(kb.name("reta"), [NTOK, 2048], BF16, kind="Internal").ap() for _ in range(2)]
    ar_ = [Reg() for _ in range(TT)]

    class Fresh(list):
        def __getitem__(self, i):
            return Reg()
    with ExitStack() as es:
        uT, uTr = phase_uT(g, es, l, src)
        rope = kb.sb(es, "ropeR", [128, 2, SEQ], F32)
        rope_r = Reg()
        kb.dma("sp", rope[:], I["k_rope_ret"].rearrange("t p n -> p t n"), [], [rope_r])
        wrot = kb.rot(es, "rw", [128, 8, 512], BF16, 2)
        wprot = kb.rot(es, "rwp", [128, 8, 128], BF16, 2)
        stg = [kb.rot(es, "rstg", [128, 512], F32, 2), kb.rot(es, "rstg", [128, 512], F32, 2),
               kb.rot(es, "rstgb", [128, 512], BF16, 4)]
        for h in range(8):
            proj_fm_rope(g, uT, uTr, I["ret_w_in"], h * 128, rope, rope_r, 128.0 ** -0.5, 32, retq[h], Reg(),
                         wrot, wprot, stg, 0)
            proj_fm_rope(g, uT, uTr, I["ret_w_in"], 1024 + h * 128, rope, rope_r, 1.0, 32, retk[h], Reg(),
                         wrot, wprot, stg, 0)
        for ch in range(4):
            proj_tm(g, uT, uTr, I["ret_w_in"], 2048 + ch * 512, 512, retv, Fresh(), ch * 512, wrot, stg[2], 4)
        for ch in range(8):
            proj_tm(g, uT, uTr, I["ret_w_in"], 4096 + ch * 512, 512, retg, Fresh(), ch * 512, wrot, stg[2], 4,
                    func=AF.Silu)
    kb.barrier()
    with ExitStack() as es:
        cr = Reg()
        tab = kb.sb(es, "rtab", [128, 5, 128], F32)
        kb.dma("sp", tab[:], I["k_ret_tab"].rearrange("t p n -> p t n"), [], [cr])
        lg = kb.sb(es, "rlg", [128, 16], F32)
        kb.dma("sp", lg[:], I["ret_decay_logit"][0, :].partition_broadcast(128), [], [cr])
        kb.act(lg[:], lg[:], AF.Exp, [cr], [cr], scale=-1.0)
        kb.ts("dve", lg[:], lg[:], 1.0, None, ALU.add, None, [cr], [cr])
        kb.act(lg[:], lg[:], AF.Ln, [cr], [cr])
        kb.ts("dve", lg[:], lg[:], -1.0, None, ALU.mult, None, [cr], [cr])
        htab = kb.sb(es, "rhtab", [128, 8, 4, 128], F32)
        hcol = kb.sb(es, "rhcol", [128, 8, 4], F32)
        for h in range(8):
            for ti, (src_i, lcol) in enumerate(((0, h), (1, 8 + h), (2, h), (3, 8 + h))):
                kb.act(htab[:, h, ti, :], tab[:, src_i, :], AF.Exp, [cr], [cr], scale=lg[:, lcol:lcol + 1])
            for ci, (src_c, lcol) in enumerate(((0, h), (1, 8 + h), (2, h), (2, 8 + h))):
                kb.act(hcol[:, h, ci:ci + 1], tab[:, 4, src_c:src_c + 1], AF.Exp, [cr], [cr],
                       scale=lg[:, lcol:lcol + 1])
        ws = NormWS(g, es)
        NU = 2
        Qrot = kb.rot(es, "rQ", [128, SEQ], BF16, NU)
        Krot = kb.rot(es, "rK", [128, SEQ], BF16, NU)
        Vrot = kb.rot(es, "rV", [128, 18, 256], BF16, NU)
        GFrot = kb.rot(es, "rGF", [128, 18, 256], BF16, NU)
        GBrot = kb.rot(es, "rGB", [128, 18, 256], BF16, NU)
        pre_all = [[kb.sb(es, "rpre", [128, 18, 128], BF16) for _ in range(6)] for _ in range(NU)]
        prer_all = [[[Reg() for _ in range(18)] for _ in range(6)] for _ in range(NU)]
        S_all = [[kb.sb(es, "rS", [128, 256], F32) for _ in range(2)] for _ in range(NU)]
        Sb_all = [[kb.sb(es, "rSb", [128, 256], BF16) for _ in range(2)] for _ in range(NU)]
        Sr_all = [[Reg(), Reg()] for _ in range(NU)]
        Sbr_all = [[Reg(), Reg()] for _ in range(NU)]
        arot = kb.rot(es, "ra", [128, 256], BF16, 8)
        P, PR = g.ps, g.psr

        def head_unit(b, h, slot):
            base = b * SEQ
            pre, prer = pre_all[slot], prer_all[slot]
            S, Sb, Sr, Sbr = S_all[slot], Sb_all[slot], Sr_all[slot], Sbr_all[slot]
            B0 = slot * 4
            Qt, Qr = Qrot.next()
            Kt, Kr = Krot.next()
            Vt, Vr = Vrot.next()
            GF, GFr = GFrot.next()
            GB, GBr = GBrot.next()
            kb.dma("sp", Qt[:], retq[h][:, base:base + SEQ], [], [Qr])
            kb.dma("sp", Kt[:], retk[h][:, base:base + SEQ], [], [Kr])
            kb.dma("sp", Vt[:], retv[base:base + SEQ, h * 256:(h + 1) * 256].rearrange("(t p) d -> p t d", p=128),
                   [], [Vr])
            kb.dma("sp", GF[:], retg[base:base + SEQ, h * 256:(h + 1) * 256].rearrange("(t p) d -> p t d", p=128),
                   [], [GFr])
            kb.dma("sp", GB[:], retg[base:base + SEQ, 2048 + h * 256:2048 + (h + 1) * 256]
                   .rearrange("(t p) d -> p t d", p=128), [], [GBr])
            for d_ in range(2):
                kb.memset("pool", S[d_][:], 0.0, [Sr[d_]])
                kb.memset("pool", Sb[d_][:], 0.0, [Sbr[d_]])
            yield
            for c0 in range(0, 18, 2):
                cs_ = [c0, c0 + 1]
                for j, c in enumerate(cs_):
                    cs = slice(c * 128, (c + 1) * 128)
                    kb.mm(P[B0][:, j * 128:(j + 1) * 128], [(Kt[:, cs], Qt[:, cs])], [Kr, Qr], [PR[B0]])
                    psk = P[B0 + 1][:].bitcast(BF16)[:, j * 128:(j + 1) * 128]
                    kb.tr([(psk, Kt[:, cs])], g.ident_b[:], [Kr, g.const_r], [PR[B0 + 1]])
                    kb.tt("pool", pre[2][:, c, :], Qt[:, cs], htab[:, h, 2, :], ALU.mult, [Qr, cr], [prer[2][c]])
                    kb.tt("pool", pre[3][:, c, :], Qt[:, cs], htab[:, h, 3, :], ALU.mult, [Qr, cr], [prer[3][c]])
                yield
                for j, c in enumerate(cs_):
                    pss = P[B0][:, j * 128:(j + 1) * 128]
                    psk = P[B0 + 1][:].bitcast(BF16)[:, j * 128:(j + 1) * 128]
                    kb.tt("dve", pre[0][:, c, :], pss, htab[:, h, 0, :], ALU.mult, [PR[B0], cr], [prer[0][c]])
                    kb.tt("dve", pre[1][:, c, :], pss, htab[:, h, 1, :], ALU.mult, [PR[B0], cr], [prer[1][c]])
                    kb.act(pre[4][:, c, :], psk, AF.Copy, [PR[B0 + 1], cr], [prer[4][c]], scale=hcol[:, h, 0:1])
                    kb.act(pre[5][:, c, :], psk, AF.Copy, [PR[B0 + 1], cr], [prer[5][c]], scale=hcol[:, h, 1:2])
                yield
            orders = [list(range(18)), [1, 0] + list(range(17, 1, -1))]
            for step in range(18):
                sts = [None, None]
                for d_ in range(2):
                    c = orders[d_][step]
                    po = P[B0 + 2 + d_][:, 0:256]
                    pS = P[B0 + d_][:, 0:256]
                    kb.mm(po, [(pre[d_][:, c, :], Vt[:, c, :]), (pre[2 + d_][:, c, :], Sb[d_][:])],
                          [prer[d_][c], prer[2 + d_][c], Vr, Sbr[d_]], [PR[B0 + 2 + d_]])
                    kb.mm(pS, [(pre[4 + d_][:, c, :], Vt[:, c, :])], [prer[4 + d_][c], Vr], [PR[B0 + d_]])
                yield
                outs = [[], []]
                gens = []
                for d_ in range(2):
                    pS = P[B0 + d_][:, 0:256]
                    po = P[B0 + 2 + d_][:, 0:256]
                    kb.stt("dve", S[d_][:], S[d_][:], hcol[:, h, 2 + d_:3 + d_], pS, ALU.mult, ALU.add,
                           [Sr[d_], PR[B0 + d_], cr], [Sr[d_]])
                    gens.append(rms_stats_gen(g, ws, [po], [PR[B0 + 2 + d_]], outs[d_], 1.0 / 16.0))
                yield
                for d_ in range(2):
                    kb.cp("act", Sb[d_][:], S[d_][:], [Sr[d_]], [Sbr[d_]])
                alive = True
                while alive:
                    alive = False
                    for gn_ in gens:
                        try:
                            next(gn_)
                            alive = True
                        except StopIteration:
                            pass
                    if alive:
                        yield
                for d_ in range(2):
                    c = orders[d_][step]
                    po = P[B0 + 2 + d_][:, 0:256]
                    st, sr = outs[d_][0]
                    gate = GF if d_ == 0 else GB
                    a_, a_r = arot.next()
                    kb.stt("dve", a_[:], po, st[:, 2:3], gate[:, c, :], ALU.mult, ALU.mult,
                           [PR[B0 + 2 + d_], sr, GFr if d_ == 0 else GBr], [a_r])
                    tt = b * NT + c
                    kb.dma("sp", reta[d_][tt * 128:(tt + 1) * 128, h * 256:(h + 1) * 256], a_[:], [a_r], [Reg()])
                yield
        interleave((head_unit(b, h, i % NU) for i, (b, h) in enumerate((b, h) for b in range(NB) for h in range(8))), NU)
    kb.barrier()
    phase_outproj(g, l, reta, ar_, 2048, I["ret_w_out"], src, dst, need_ctx)


def interleave(gens, width):
    gens = iter(gens)
    active = []
    done = False
    while True:
        while not done and len(active) < width:
            try:
                active.append(next(gens))
            except StopIteration:
                done = True
        if not active:
            break
        nxt = []
        for gen in active:
            try:
                next(gen)
                nxt.append(gen)
            except StopIteration:
                pass
        active = nxt


def mixer_dn(g, l, src, dst, need_ctx):
    kb, nc, I = g.kb, g.nc, g.I
    NTOK = TT * 128
    NCH = NB * 2 * 36

    def dt_(name, shape, dt):
        if g.dbg and name in ("dnq", "dnk", "dnvt", "dnba", "dnkt"):
            return nc.dram_tensor("dbg_" + name, list(shape), dt, kind="ExternalOutput").ap()
        return nc.dram_tensor(kb.name(name), list(shape), dt, kind="Internal").ap()
    dnq = dt_("dnq", [8, 128, NTOK], BF16)
    dnk = dt_("dnk", [8, 128, NTOK], BF16)
    dnkt = dt_("dnkt", [NTOK, D], BF16)
    dnvt = dt_("dnvt", [NTOK, D], BF16)
    dnba = dt_("dnba", [NTOK, 32], F32)
    dng = dt_("dng", [NTOK, 2048], BF16)
    dna = [dt_("dna", [NTOK, D], BF16) for _ in range(2)]
    pu = dt_("dpu", [NCH, 64, 1024], F32)
    pw = dt_("dpw", [NCH, 128, 512], BF16)
    pat = dt_("dpat", [NCH, 64, 512], BF16)
    pqg = dt_("dpqg", [NCH, 128, 512], BF16)
    pkg = dt_("dpkg", [NCH, 64, 1024], BF16)
    pgl = dt_("dpgl", [NCH, 128, 8], F32)
    ar_ = [Reg() for _ in range(TT)]

    class Fresh(list):
        def __getitem__(self, i):
            return Reg()
    NX = 2308
    BLK = ((0, 512), (512, 512), (1024, 512), (1536, 512), (2048, 260))
    with ExitStack() as es:
        uT, uTr = phase_uT(g, es, l, src)
        cr = Reg()
        cw = kb.sb(es, "dcw", [128, 24, 5], F32)
        for k in range(5):
            kb.dma("sp", cw[:, :, k], I["dn_conv_w"][k, :].rearrange("(c p) -> p c", p=128), [], [cr],
                   allow_slow_non_contiguous=True)
        ones_b = kb.sb(es, "donesb", [128, 128], BF16)
        kb.memset("dve", ones_b[:], 1.0, [cr])
        wrot = kb.rot(es, "dw", [128, 8, 512], BF16, 3)
        XDT = F32 if os.environ.get("DN_XF32", "1") == "1" else BF16
        dgrot = kb.rot(es, "ddg", [128, 5, 128], XDT, 3)
        Xrot = kb.rot(es, "dX", [128, 2320], XDT, 3 if XDT == BF16 else 2)
        for xt_, xr_ in Xrot.items:
            kb.memset("pool", xt_[:, 0:2], 0.0, [xr_])
            kb.memset("pool", xt_[:, 258:262], 0.0, [xr_])
            kb.memset("pool", xt_[:, 2310:2320], 0.0, [xr_])
        Srot = kb.rot(es, "dsil", [128, NX], F32, 3)
        sqrot = kb.rot(es, "dsq", [128, 512], BF16, 4)
        sdrot = kb.rot(es, "dsd", [128, 512], F32, 4)
        QNrot = kb.rot(es, "dQN", [128, NX], BF16, 3)
        tstg = kb.rot(es, "dts", [128, 8, 128], BF16, 3)

        def inproj_unit(cb, b, slot, wt, wr, dg, dgr):
            kind, h = cb // 8, cb % 8
            base = b * SEQ
            pbs = [slot * 3, slot * 3 + 1]
            ptb = slot * 3 + 2
            X, Xr = Xrot.next()
            for bi, (o, n) in enumerate(((0, 512), (512, 512), (1024, 512), (1536, 512), (2048, 256))):
                t0 = base + o
                tiles = list(range(t0 // 128, (t0 + n) // 128))
                pb = pbs[bi % 2]
                kb.mm(g.ps[pb][:, 0:n], [(wt[:, kc, 0:128], uT[:, kc, t0:t0 + n]) for kc in range(8)],
                      [uTr[t] for t in tiles] + [wr], [g.psr[pb]])
                if o == 0:
                    kb.cp("act", X[:, 2:258], g.ps[pb][:, 0:256], [g.psr[pb]], [Xr])
                    kb.cp("act", X[:, 262:518], g.ps[pb][:, 256:512], [g.psr[pb]], [Xr])
                else:
                    kb.cp("act", X[:, 6 + o:6 + o + n], g.ps[pb][:, 0:n], [g.psr[pb]], [Xr])
                yield
            Ssil, Ssr = Srot.next()
            for bi, (o, n) in enumerate(BLK):
                pb = pbs[(bi + 1) % 2]
                kb.mm(g.ps[pb][:, 0:n], [(dg[:, k, :], X[:, o + k:o + k + n]) for k in range(5)], [Xr, dgr],
                      [g.psr[pb]])
                kb.act(Ssil[:, o:o + n], g.ps[pb][:, 0:n], AF.Silu, [g.psr[pb]], [Ssr])
                yield
            QN, QNr = QNrot.next()
            if kind < 2:
                for bi, (o, n) in enumerate(BLK):
                    pb = pbs[bi % 2]
                    sq, sqr = sqrot.next()
                    kb.act(sq[:, 0:n], Ssil[:, o:o + n], AF.Square, [Ssr], [sqr])
                    kb.mm(g.ps[pb][:, 0:n], [(ones_b[:], sq[:, 0:n])], [sqr, cr], [g.psr[pb]])
                    sd, sdr = sdrot.next()
                    kb.act(sd[:, 0:n], g.ps[pb][:, 0:n], AF.Sqrt, [g.psr[pb], g.const_r], [sdr],
                           bias=(g.eps128 if kind == 0 else g.eps)[:, 0:1], scale=128.0 if kind == 0 else 1.0)
                    kb.recip(sd[:, 0:n], sd[:, 0:n], [sdr], [sdr])
                    kb.tt("pool", QN[:, o:o + n], Ssil[:, o:o + n], sd[:, 0:n], ALU.mult, [Ssr, sdr], [QNr])
                    yield
                dstq = dnq if kind == 0 else dnk
                kb.dma("sp", dstq[h][:, base:base + TC], QN[:, 0:TC], [QNr], [Reg()])
                kb.dma("sp", dstq[h][:, base + TC:base + SEQ], QN[:, 260:NX], [QNr], [Reg()])
            else:
                kb.cp("pool", QN[:], Ssil[:], [Ssr], [QNr])
                yield
            if kind >= 1:
                dstt = dnkt if kind == 1 else dnvt
                for t0 in range(0, NT, 8):
                    ts_ = list(range(t0, min(NT, t0 + 8)))
                    psT = g.ps[ptb][:].bitcast(BF16).rearrange("p (c t) -> p c t", c=8)
                    kb.tr([(psT[:, j, :], QN[:, (t * 128 if t < 2 else t * 128 + 4):(t * 128 if t < 2 else t * 128 + 4) + 128])
                           for j, t in enumerate(ts_)], g.ident_b[:], [QNr, g.const_r], [g.psr[ptb]])
                    stt_, str_ = tstg.next()
                    kb.cp("act", stt_[:, 0:len(ts_), :], psT[:, 0:len(ts_), :], [g.psr[ptb]], [str_])
                    r0 = base + t0 * 128
                    kb.dma("sp", dstt[r0:r0 + len(ts_) * 128, h * 128:(h + 1) * 128]
                           .rearrange("(t p) d -> p t d", p=128), stt_[:, 0:len(ts_), :], [str_], [Reg()])
                    yield

        def inproj_units():
            i = 0
            for cb in range(24):
                wt, wr = load_w_bf16(g, wrot, I["dn_w_in"], cb * 128, 128)
                dg, dgr = dgrot.next()
                for k in range(5):
                    kb.ts("pool", dg[:, k, :], g.ident_f[:], cw[:, cb, k:k + 1], None, ALU.mult, None,
                          [g.const_r, cr], [dgr])
                for b in range(NB):
                    yield inproj_unit(cb, b, i % 2, wt, wr, dg, dgr)
                    i += 1
        interleave(inproj_units(), 2)
        abr = Reg()
        dtb = kb.sb(es, "ddtb", [128, 16], F32)
        nea = kb.sb(es, "dnea", [128, 16], F32)
        kb.dma("sp", dtb[:], I["dn_dt_bias"][0, :].partition_broadcast(128), [], [abr])
        kb.dma("sp", nea[:], I["dn_a_log"][0, :].partition_broadcast(128), [], [abr])
        kb.act(nea[:], nea[:], AF.Exp, [abr], [abr])
        kb.ts("dve", nea[:], nea[:], -1.0, None, ALU.mult, None, [abr], [abr])
        wt, wr = load_w_bf16(g, wrot, I["dn_w_in"], 3072, 32)
        bstg = kb.rot(es, "dbs", [128, 32], F32, 4)
        btmp = kb.rot(es, "dbt", [128, 16], F32, 4)

        def ba_unit(tt):
            pb = 6 + tt % 2
            off = (tt // 2) % 8 * 32
            pp = g.ps[pb][:, off:off + 32]
            kb.mm(pp, [(uT[:, kc, tt * 128:(tt + 1) * 128], wt[:, kc, 0:32]) for kc in range(8)],
                  [uTr[tt], wr], [g.psr[pb]])
            bs, bsr = bstg.next()
            bt, btr = btmp.next()
            yield
            kb.act(bs[:, 0:16], pp[:, 0:16], AF.Sigmoid, [g.psr[pb]], [bsr])
            kb.tt("dve", bt[:], pp[:, 16:32], dtb[:], ALU.add, [g.psr[pb], abr], [btr])
            yield
            kb.act(bt[:], bt[:], AF.Exp, [btr], [btr])
            yield
            kb.ts("dve", bt[:], bt[:], 1.0, None, ALU.add, None, [btr], [btr])
            yield
            kb.act(bt[:], bt[:], AF.Ln, [btr], [btr])
            yield
            kb.tt("dve", bs[:, 16:32], bt[:], nea[:], ALU.mult, [btr, abr], [bsr])
            kb.dma("sp", dnba[tt * 128:(tt + 1) * 128, :], bs[:], [bsr], [Reg()])
        interleave((ba_unit(tt) for tt in range(TT)), 4)
        stgb = kb.rot(es, "dgs", [128, 512], BF16, 4)
        tiles = [tt for tt in range(TT) if need_ctx or (tt % NT) >= 2]
        for ch in range(4):
            proj_tm(g, uT, uTr, I["dn_w_in"], 3104 + ch * 512, 512, dng, Fresh(), ch * 512, wrot, stgb, 0,
                    func=AF.Silu, tiles=tiles)
    kb.barrier()
    bc64 = lambda ap: ap.unsqueeze(2).broadcast_to([64, 8, 64])
    bc128 = lambda ap: ap.unsqueeze(2).broadcast_to([64, 8, 128])

    def chunk_id(b, d_, c):
        return (b * 2 + d_) * 36 + c
    with ExitStack() as es:
        cr = Reg()
        ktab = kb.sb(es, "dktab", [64, 4, 64], F32)
        kb.dma("sp", ktab[:], I["k_dn"].rearrange("t p n -> p t n"), [], [cr])
        ones3 = kb.sb(es, "dones3", [64, 128], F32)
        kb.memset("dve", ones3[:], 1.0, [cr])
        identf = g.ident_f
        identb = g.ident_b
        NW = 2

        def R1(shape, dt, name, n=NW + 1):
            return kb.rot(es, name, shape, dt, n)
        kTr_, qTr_ = R1([128, 8, 64], BF16, "dkT"), R1([128, 8, 64], BF16, "dqT")
        ktr_, vtr_ = R1([64, 8, 128], BF16, "dkt"), R1([64, 8, 128], BF16, "dvt")
        bar_ = R1([64, 32], F32, "dba")
        LBn_ = R1([64, 8, 128], F32, "dLBn")
        sm_ = R1([128, 32], F32, "dsm")
        gdm_ = R1([64, 8, 64], F32, "dgdm")
        t12_ = R1([64, 2, 8, 64], F32, "dt12")
        Dms_, DTi_ = R1([64, 8, 64], F32, "dDms"), R1([64, 8, 64], F32, "dDTi")
        Egr_ = R1([128, 8, 64], F32, "dEgr")
        qgT_ = R1([128, 8, 64], BF16, "dqgT")
        INV_F32 = os.environ.get("DN_INV_F32", "1") == "1"
        IDT = F32 if INV_F32 else BF16
        A_ = R1([64, 8, 64], IDT, "dA", 2 * NW + 2)
        B_ = R1([64, 8, 64], IDT, "dB", 2 * NW + 2)
        Af_ = R1([64, 8, 64], F32, "dAf")
        Tt_ = R1([64, 8, 64], F32, "dTt")
        Ttb_ = R1([64, 8, 64], IDT, "dTtb", 2 * NW + 2)
        Tt16_ = R1([64, 8, 64], BF16, "dTt16")
        attnT_ = R1([64, 8, 64], BF16, "dattn")
        rv_, kbg_, kg_ = R1([64, 8, 128], BF16, "drv"), R1([64, 8, 128], BF16, "dkbg"), R1([64, 8, 128], BF16, "dkg")
        u_ = R1([64, 8, 128], F32, "du")
        wT_ = R1([128, 8, 64], BF16, "dwT")
        gls_ = R1([128, 8], F32, "dgls")
        P, PR = g.ps, g.psr

        def pre_unit(b, d_, c, slot):
            cid = chunk_id(b, d_, c)
            tok0 = b * SEQ + c * 64
            Pa, Pb, Pc, Pd = (slot * 4 + i for i in range(4))
            Tri = ktab[:, d_, :]
            strict = ktab[:, 2 + d_, :]
            kT, kTr = kTr_.next()
            qT, qTr = qTr_.next()
            kt, ktr = ktr_.next()
            vt, vtr = vtr_.next()
            ba, bar = bar_.next()
            kb.dma("sp", kT[:], dnk[:, :, tok0:tok0 + 64].rearrange("h p n -> p h n"), [], [kTr])
            kb.dma("sp", qT[:], dnq[:, :, tok0:tok0 + 64].rearrange("h p n -> p h n"), [], [qTr])
            kb.dma("sp", kt[:], dnkt[tok0:tok0 + 64, :].rearrange("p (h d) -> p h d", h=8), [], [ktr])
            kb.dma("sp", vt[:], dnvt[tok0:tok0 + 64, :].rearrange("p (h d) -> p h d", h=8), [], [vtr])
            kb.dma("sp", ba[:], dnba[tok0:tok0 + 64, :], [], [bar])
            beta = ba[:, 8 * d_:8 * d_ + 8]
            la = ba[:, 16 + 8 * d_:24 + 8 * d_]
            yield
            LBn, LBr = LBn_.next()
            kb.ts("dve", LBn[:], bc128(la), -1.0, None, ALU.mult, None, [bar], [LBr])
            kb.mm(P[Pd][0:64, 0:8], [(Tri, la)], [cr, bar], [PR[Pd]])
            kb.mm(P[Pd][:, 8:16], [(ones3[:, :], la)], [cr, bar], [PR[Pd]])
            for h in range(8):
                kb.mm(P[Pa][0:64, h * 64:(h + 1) * 64], [(kT[:, h, :], kT[:, h, :])], [kTr], [PR[Pa]])
            for h in range(8):
                kb.mm(P[Pb][0:64, h * 64:(h + 1) * 64], [(kT[:, h, :], qT[:, h, :])], [kTr, qTr], [PR[Pb]])
            yield
            for h in range(8):
                kb.mm(P[Pc][:, h * 64:(h + 1) * 64], [(LBn[:, h, :], Tri)], [LBr, cr], [PR[Pc]])
            sm, smr = sm_.next()
            kb.cp("dve", sm[0:64, 0:8], P[Pd][0:64, 0:8], [PR[Pd]], [smr])
            gc = sm[0:64, 0:8]
            gls, glsr = gls_.next()
            kb.act(gls[:], P[Pd][:, 8:16], AF.Exp, [PR[Pd]], [glsr])
            kb.dma("sp", pgl[cid], gls[:], [glsr], [Reg()])
            yield
            gdm, gdmr = gdm_.next()
            GR = P[Pc][:].rearrange("p (h n) -> p h n", h=8)
            kb.tt("dve", gdm[:], GR[0:64], bc64(gc), ALU.add, [PR[Pc], smr], [gdmr])
            Egr, Egrr = Egr_.next()
            kb.act(Egr[:], GR, AF.Exp, [PR[Pc]], [Egrr], scale=-1.0)
            kb.act(sm[0:64, 8:16], gc, AF.Exp, [smr], [smr])
            kb.tt("dve", sm[0:64, 16:24], P[Pd][0:64, 8:16], gc, ALU.subtract, [PR[Pd], smr], [smr])
            yield
            t12, t12r = t12_.next()
            kb.ts("dve", t12[:, 0], gdm[:], 0.0, None, ALU.min, None, [gdmr], [t12r])
            kb.ts("dve", t12[:, 1], gdm[:], -1.0, 0.0, ALU.mult, ALU.min, [gdmr], [t12r])
            qgT, qgTr = qgT_.next()
            kb.tt("pool", qgT[:], qT[:], Egr[:], ALU.mult, [qTr, Egrr], [qgTr])
            kb.dma("sp", pqg[cid].rearrange("p (h n) -> p h n", h=8), qgT[:], [qgTr], [Reg()])
            kb.tt("dve", sm[0:64, 8:16], sm[0:64, 8:16], beta, ALU.mult, [smr, bar], [smr])
            kb.act(sm[0:64, 16:24], sm[0:64, 16:24], AF.Exp, [smr], [smr])
            yield
            kb.act(t12[:], t12[:], AF.Exp, [t12r], [t12r])
            rv, rvr = rv_.next()
            kbg, kbgr = kbg_.next()
            kg, kgr = kg_.next()
            kb.tt("pool", rv[:], vt[:], bc128(beta), ALU.mult, [vtr, bar], [rvr])
            kb.tt("pool", kbg[:], kt[:], bc128(sm[0:64, 8:16]), ALU.mult, [ktr, smr], [kbgr])
            kb.tt("pool", kg[:], kt[:], bc128(sm[0:64, 16:24]), ALU.mult, [ktr, smr], [kgr])
            kb.dma("sp", pkg[cid].rearrange("p (h n) -> p h n", h=8), kg[:], [kgr], [Reg()])
            yield
            Dms, Dmsr = Dms_.next()
            DTi, DTir = DTi_.next()
            kb.tt("pool", Dms[:], t12[:, 0], strict.unsqueeze(1).broadcast_to([64, 8, 64]), ALU.mult,
                  [t12r, cr], [Dmsr])
            kb.tt("pool", DTi[:], t12[:, 1], Tri.unsqueeze(1).broadcast_to([64, 8, 64]), ALU.mult,
                  [t12r, cr], [DTir])
            KK = P[Pa][0:64, :].rearrange("p (h n) -> p h n", h=8)
            QK = P[Pb][0:64, :].rearrange("p (h n) -> p h n", h=8)
            Af, Afr = Af_.next()
            kb.tt("dve", Af[:], KK, bc64(beta), ALU.mult, [PR[Pa], bar], [Afr])
            yield
            A0, A0r = A_.next()
            kb.tt("dve", A0[:], Af[:], Dms[:], ALU.mult, [Afr, Dmsr], [A0r])
            attnT, attnr = attnT_.next()
            kb.tt("dve", attnT[:], QK, DTi[:], ALU.mult, [PR[Pb], DTir], [attnr])
            kb.dma("sp", pat[cid].rearrange("p (h n) -> p h n", h=8), attnT[:], [attnr], [Reg()])
            yield
            if INV_F32:
                KKb = P[Pa][0:64, :].rearrange("p (h n) -> p h n", h=8)
            else:
                KKb = P[Pa][0:64, :].bitcast(BF16)[:, 0:512].rearrange("p (h n) -> p h n", h=8)
            kb.tr([(KKb[:, h, :], A0[:, h, :]) for h in range(8)], (identf if INV_F32 else identb)[0:64, 0:64],
                  [A0r, g.const_r], [PR[Pa]])
            yield
            B0, B0r = B_.next()
            kb.cp("act", B0[:], KKb, [PR[Pa]], [B0r])
            Tt, Ttr = Tt_.next()
            kb.tt("dve", Tt[:], identf[0:64, 0:64].unsqueeze(1).broadcast_to([64, 8, 64]), KKb,
                  ALU.subtract, [g.const_r, PR[Pa]], [Ttr])
            Ttb, Ttbr = Ttb_.next()
            kb.cp("pool", Ttb[:], Tt[:], [Ttr], [Ttbr])
            yield
            Ak, Akr, Bk, Bkr = A0, A0r, B0, B0r

            def sq_mm(Ak, Akr, Bk, Bkr, need_b):
                for h in range(8):
                    kb.mm(P[Pa][0:64, h * 64:(h + 1) * 64], [(Bk[:, h, :], Ak[:, h, :])], [Akr, Bkr], [PR[Pa]])
                if need_b:
                    for h in range(8):
                        kb.mm(P[Pb][0:64, h * 64:(h + 1) * 64], [(Ak[:, h, :], Bk[:, h, :])], [Akr, Bkr], [PR[Pb]])
            sq_mm(Ak, Akr, Bk, Bkr, True)
            yield
            An, Anr = A_.next()
            Bn, Bnr = B_.next()
            kb.cp("act", An[:], KK, [PR[Pa]], [Anr])
            kb.cp("pool" if False else "dve", Bn[:], QK, [PR[Pb]], [Bnr])
            yield
            for lev in range(5):
                for h in range(8):
                    kb.mm(P[Pd][0:64, h * 64:(h + 1) * 64], [(An[:, h, :], Ttb[:, h, :])], [Anr, Ttbr], [PR[Pd]])
                if lev < 4:
                    sq_mm(An, Anr, Bn, Bnr, lev < 3)
                yield
                kb.tt("dve", Tt[:], Tt[:], P[Pd][0:64, :].rearrange("p (h n) -> p h n", h=8), ALU.add,
                      [Ttr, PR[Pd]], [Ttr])
                if lev < 4:
                    An2, An2r = A_.next()
                    kb.cp("act", An2[:], KK, [PR[Pa]], [An2r])
                    if lev < 3:
                        Bn2, Bn2r = B_.next()
                        kb.cp("pool" if False else "dve", Bn2[:], QK, [PR[Pb]], [Bn2r])
                        Bn, Bnr = Bn2, Bn2r
                    An, Anr = An2, An2r
                yield
                if lev < 4:
                    Ttb, Ttbr = Ttb_.next()
                    kb.cp("act", Ttb[:], Tt[:], [Ttr], [Ttbr])
                    yield
            u, ur = u_.next()
            wT, wTr = wT_.next()
            if INV_F32:
                Ttb, Ttbr = Tt16_.next()
                kb.cp("act", Ttb[:], Tt[:], [Ttr], [Ttbr])
                yield
            for hb in range(2):
                pp = (Pa, Pb)[hb]
                for hh in range(4):
                    h = hb * 4 + hh
                    kb.mm(P[pp][0:64, hh * 128:(hh + 1) * 128], [(Ttb[:, h, :], rv[:, h, :])],
                          [Ttbr, rvr], [PR[pp]])
            for h in range(8):
                kb.mm(P[Pc][:, h * 64:(h + 1) * 64], [(kbg[:, h, :], Ttb[:, h, :])], [kbgr, Ttbr], [PR[Pc]])
            yield
            for hb in range(2):
                pp = (Pa, Pb)[hb]
                kb.cp("act" if hb == 0 else "dve", u[:, hb * 4:(hb + 1) * 4, :],
                      P[pp][0:64, :].rearrange("p (h n) -> p h n", h=4), [PR[pp]], [ur])
            kb.cp("act", wT[:], GR, [PR[Pc]], [wTr])
            yield
            kb.dma("sp", pu[cid].rearrange("p (h n) -> p h n", h=8), u[:], [ur], [Reg()])
            kb.dma("sp", pw[cid].rearrange("p (h n) -> p h n", h=8), wT[:], [wTr], [Reg()])

        units = []
        for b in range(NB):
            for d_ in range(2):
                for c in range(36):
                    units.append((b, d_, c))
        if int(os.environ.get("DNSTOP", "9")) >= 2:
            interleave((pre_unit(b, d_, c, i % NW) for i, (b, d_, c) in enumerate(units)), int(os.environ.get("DN_NW", NW)))
    kb.barrier()
    with ExitStack() as es:
        cr = Reg()
        ng = kb.sb(es, "dng_", [64, 128], F32)
        kb.dma("sp", ng[:], I["dn_norm_g"][0, :].partition_broadcast(64), [], [cr])
        P, PR = g.ps, g.psr
        NC_ = 4

        def R2(shape, dt, name, n=2):
            return [kb.rot(es, name, shape, dt, n) for _ in range(NC_)]
        u_, wT_ = R2([64, 8, 128], F32, "eu"), R2([128, 8, 64], BF16, "ewT")
        at_, qg_ = R2([64, 8, 64], BF16, "eat"), R2([128, 8, 64], BF16, "eqg")
        kg_, gl_ = R2([64, 8, 128], BF16, "ekg"), R2([128, 8], F32, "egl")
        gt_ = R2([64, 8, 128], BF16, "egt")
        vn_ = R2([64, 8, 128], BF16, "evn", 1)
        osb_ = R2([64, 8, 128], F32, "eosb", 1)
        sq_ = R2([64, 8, 128], F32, "esq", 1)
        gn_ = R2([64, 8, 128], F32, "egn", 1)
        oa_ = R2([64, 8, 128], BF16, "eoa", 2)
        sm_ = R2([64, 32], F32, "esm", 2)
        print("sbuf remaining (dn rec)", nc.sbuf_bytes_remaining)
        S = [kb.sb(es, "dS", [128, 8, 128], F32) for _ in range(NC_)]
        Sb = [kb.sb(es, "dSb", [128, 8, 128], BF16) for _ in range(NC_)]
        Sr = [Reg() for _ in range(NC_)]
        Sbr = [Reg() for _ in range(NC_)]

        def rec_chain(b, d_, ch):
            order = list(range(36)) if d_ == 0 else [3, 2, 1, 0] + list(range(35, 3, -1))
            Ra, Rb = ch * 2, ch * 2 + 1
            kb.memset("pool", S[ch][:], 0.0, [Sr[ch]])
            kb.memset("pool", Sb[ch][:], 0.0, [Sbr[ch]])
            loaded = {}

            def load(c):
                cid = chunk_id(b, d_, c)
                tok0 = b * SEQ + c * 64
                emit = need_ctx or c >= 4
                r = {}
                for nm, rot_, srcap in (("u", u_[ch], pu[cid]), ("wT", wT_[ch], pw[cid]), ("at", at_[ch], pat[cid]),
                                        ("qg", qg_[ch], pqg[cid]), ("kg", kg_[ch], pkg[cid])):
                    if nm in ("at", "qg") and not emit:
                        continue
                    t_, tr_ = rot_.next()
                    kb.dma("sp", t_[:], srcap.rearrange("p (h n) -> p h n", h=8), [], [tr_])
                    r[nm] = (t_, tr_)
                t_, tr_ = gl_[ch].next()
                kb.dma("sp", t_[:], pgl[cid], [], [tr_])
                r["gl"] = (t_, tr_)
                if emit:
                    t_, tr_ = gt_[ch].next()
                    kb.dma("sp", t_[:], dng[tok0:tok0 + 64, d_ * 1024:(d_ + 1) * 1024]
                           .rearrange("p (h d) -> p h d", h=8), [], [tr_])
                    r["gt"] = (t_, tr_)
                return r
            loaded[0] = load(order[0])
            for step, c in enumerate(order):
                if step + 1 < 36:
                    loaded[step + 1] = load(order[step + 1])
                L_ = loaded.pop(step)
                tok0 = b * SEQ + c * 64
                emit = need_ctx or c >= 4
                (u, ur), (wT, wTr), (kg, kgr), (gl, glr) = L_["u"], L_["wT"], L_["kg"], L_["gl"]
                for hb in range(2):
                    pp = (Ra, Rb)[hb]
                    for hh in range(4):
                        h = hb * 4 + hh
                        kb.mm(P[pp][0:64, hh * 128:(hh + 1) * 128], [(wT[:, h, :], Sb[ch][:, h, :])],
                              [wTr, Sbr[ch]], [PR[pp]])
                yield
                vn, vnr = vn_[ch].next()
                for hb in range(2):
                    pp = (Ra, Rb)[hb]
                    kb.tt("dve", vn[:, hb * 4:(hb + 1) * 4, :], u[:, hb * 4:(hb + 1) * 4, :],
                          P[pp][0:64, :].rearrange("p (h n) -> p h n", h=4), ALU.subtract, [ur, PR[pp]], [vnr])
                if emit:
                    gn, gnr = gn_[ch].next()
                    kb.tt("pool", gn[:], L_["gt"][0][:], ng[:, :].unsqueeze(1).broadcast_to([64, 8, 128]), ALU.mult,
                          [L_["gt"][1], cr], [gnr])
                yield
                if emit:
                    (at, atr), (qg, qgr) = L_["at"], L_["qg"]
                    for hb in range(2):
                        pp = (Ra, Rb)[hb]
                        for hh in range(4):
                            h = hb * 4 + hh
                            kb.mm(P[pp][0:64, hh * 128:(hh + 1) * 128],
                                  [(qg[:, h, :], Sb[ch][:, h, :]), (at[:, h, :], vn[:, h, :])],
                                  [qgr, Sbr[ch], atr, vnr], [PR[pp]])
                    yield
                    osb, osr = osb_[ch].next()
                    kb.cp("act", osb[:, 0:4, :], P[Ra][0:64, :].rearrange("p (h n) -> p h n", h=4), [PR[Ra]], [osr])
                    kb.cp("dve", osb[:, 4:8, :], P[Rb][0:64, :].rearrange("p (h n) -> p h n", h=4), [PR[Rb]], [osr])
                    yield
                for hb in range(2):
                    pp = (Ra, Rb)[hb]
                    for hh in range(4):
                        h = hb * 4 + hh
                        kb.mm(P[pp][:, hh * 128:(hh + 1) * 128], [(kg[:, h, :], vn[:, h, :])], [kgr, vnr], [PR[pp]])
                kb.tt("pool", S[ch][:], S[ch][:], gl[:, :].unsqueeze(2).broadcast_to([128, 8, 128]), ALU.mult,
                      [Sr[ch], glr], [Sr[ch]])
                yield
                for hb in range(2):
                    pp = (Ra, Rb)[hb]
                    kb.tt("dve", S[ch][:, hb * 4:(hb + 1) * 4, :], S[ch][:, hb * 4:(hb + 1) * 4, :],
                          P[pp][:, :].rearrange("p (h n) -> p h n", h=4), ALU.add, [Sr[ch], PR[pp]], [Sr[ch]])
                yield
                kb.cp("act", Sb[ch][:], S[ch][:], [Sr[ch]], [Sbr[ch]])
                if emit:
                    sq, sqr = sq_[ch].next()
                    kb.act(sq[:], osb[:], AF.Square, [osr], [sqr])
                    yield
                    sm, smr = sm_[ch].next()
                    kb.op("dve", lambda hd, o=sm[:, 0:8], i=sq[:]: hd.reduce_sum(out=o, in_=i, axis=AX.X),
                          [sqr], [smr])
                    yield
                    kb.act(sm[:, 8:16], sm[:, 0:8], AF.Sqrt, [smr, g.const_r], [smr], bias=g.eps[0:64, 0:1],
                           scale=1.0 / 128.0)
                    yield
                    kb.recip(sm[:, 16:24], sm[:, 8:16], [smr], [smr])
                    yield
                    kb.tt("dve", osb[:], osb[:], sm[:, 16:24].unsqueeze(2).broadcast_to([64, 8, 128]), ALU.mult,
                          [osr, smr], [osr])
                    yield
                    oa, oar = oa_[ch].next()
                    kb.tt("pool", oa[:], osb[:], gn[:], ALU.mult, [osr, gnr], [oar])
                    kb.dma("sp", dna[d_][tok0:tok0 + 64, :].rearrange("p (h d) -> p h d", h=8), oa[:], [oar], [Reg()])
                yield
        if int(os.environ.get("DNSTOP", "9")) >= 3:
            interleave((rec_chain(b, d_, b * 2 + d_) for b in range(NB) for d_ in range(2)), int(os.environ.get("DN_NC", NC_)))
    kb.barrier()
    phase_outproj(g, l, dna, ar_, D, I["dn_w_out"], src, dst, need_ctx)


def host_consts():
    c = {}
    c["k_ident"] = np.eye(128, dtype=np.float32)
    mats = np.zeros((20, 128, 128), np.float32)
    for wi, w in enumerate((2, 4, 8, 16)):
        lo = w // 2
        hi = w - 1 - lo
        n = 384
        for var, (t0, nseq_lo, nseq_hi) in enumerate(((128, 0, 384), (0, 0, 384), (256, 0, 384))):
            pass
        A = np.zeros((n, n), np.float32)
        for t in range(n):
            a, b_ = max(0, t - lo), min(n, t + hi + 1)
            A[t, a:b_] = 1.0 / (b_ - a)
        M = A - np.eye(n, dtype=np.float32)
        MT = M.T
        mats[wi * 5 + 0] = MT[128:256, 128:256]
        mats[wi * 5 + 1] = MT[0:128, 0:128]
        mats[wi * 5 + 2] = MT[256:384, 256:384]
        mats[wi * 5 + 3] = MT[0:128, 128:256]
        mats[wi * 5 + 4] = MT[256:384, 128:256]
    c["k_pool"] = mats
    pos = np.arange(T)
    row, col = pos // 64, pos % 64

    def rope_tab(dh, nrep):
        q = dh // 4
        tab = np.zeros((2, dh, SEQ), np.float64)
        tab[0, :, :TC] = 1.0
        for d in range(dh):
            blk, i = d // q, d % q
            inv = 10000.0 ** (-i / q)
            p_ = row if blk < 2 else col
            ang = (p_.astype(np.float32) * np.float32(inv)).astype(np.float64)
            tab[0, d, TC:] = np.cos(ang)
            tab[1, d, TC:] = np.sin(ang) * (-1.0 if blk % 2 == 0 else 1.0)
        return np.tile(tab, (1, nrep, 1)).astype(np.float32)
    c["k_rope_att"] = rope_tab(64, 2)
    c["k_rope_ret"] = rope_tab(128, 1)
    qi = np.arange(128)[:, None]
    kj = np.arange(384)[None, :]
    keep = np.abs(kj - 128 - qi) <= 128
    m = np.zeros((3, 128, 384), np.float32)
    m[0] = np.where(keep, 0.0, -30000.0)
    m[1] = np.where(keep & (kj >= 128), 0.0, -30000.0)
    m[2] = np.where(keep & (kj < 256), 0.0, -30000.0)
    c["k_att_mask"] = m
    jj = np.arange(128)[:, None].astype(np.float64)
    ii = np.arange(128)[None, :].astype(np.float64)
    rt = np.zeros((5, 128, 128), np.float64)
    rt[0] = np.where(ii >= jj, ii - jj, 1e9)
    rt[1] = np.where(jj >= ii, jj - ii, 1e9)
    rt[2] = np.broadcast_to(ii + 1.0, (128, 128))
    rt[3] = np.broadcast_to(128.0 - ii, (128, 128))
    rt[4, :, 0] = 127.0 - np.arange(128)
    rt[4, :, 1] = np.arange(128)
    rt[4, :, 2] = 128.0
    c["k_ret_tab"] = rt.astype(np.float32)
    p_ = np.arange(64)[:, None]
    f_ = np.arange(64)[None, :]
    c["k_dn"] = np.stack([p_ <= f_, p_ >= f_, p_ > f_, p_ < f_]).astype(np.float32)
    return c


LAYERS = [0, 1, 2, 3]
_cache = {}


def _prep_inputs(inputs):
    f = lambda a: np.ascontiguousarray(np.asarray(a, dtype=np.float32))
    shared = {}
    for k in ("ada_w", "ada_b", "mix_pre_g", "mix_post_g", "mlp_pre_g", "mlp_post_g", "mlp_w1", "mlp_w2"):
        shared[k] = f(inputs[k])
    shared["c_ctx"] = f(inputs["c_ctx"]).reshape(1, D)
    shared["ret_w_in"] = f(inputs["ret_w_in"][0])
    shared["ret_decay_logit"] = f(inputs["ret_decay_logit"][0]).reshape(1, 16)
    shared["ret_w_out"] = f(inputs["ret_w_out"][0])
    shared["att_w_in"] = f(inputs["att_w_in"][0])
    shared["att_sink"] = f(inputs["att_sink"][0]).reshape(1, 16)
    shared["att_w_out"] = f(inputs["att_w_out"][0])
    shared["pool_w"] = f(inputs["pool_w"][0])
    shared["pool_b"] = f(inputs["pool_b"][0]).reshape(1, D)
    shared["pool_scale"] = f(inputs["pool_scale"][0]).reshape(1, D)
    shared["dn_w_in"] = f(inputs["dn_w_in"][0])
    shared["dn_conv_w"] = f(inputs["dn_conv_w"][0])
    shared["dn_a_log"] = f(inputs["dn_a_log"][0]).reshape(1, 16)
    shared["dn_dt_bias"] = f(inputs["dn_dt_bias"][0]).reshape(1, 16)
    shared["dn_norm_g"] = f(inputs["dn_norm_g"][0]).reshape(1, 128)
    shared["dn_w_out"] = f(inputs["dn_w_out"][0])
    shared.update(host_consts())
    return shared


def run(inputs, layers, n_cores, dbg=False):
    key = (tuple(layers), dbg)
    if key not in _cache:
        _cache[key] = build_program(layers, dbg)
    nc = _cache[key]
    shared = _prep_inputs(inputs)
    x = np.asarray(inputs["x"], dtype=np.float32)
    c = np.asarray(inputs["c"], dtype=np.float32)
    ctx = np.asarray(inputs["ctx"], dtype=np.float32)
    in_maps = []
    for i in range(n_cores):
        m = dict(shared)
        m["x"] = np.ascontiguousarray(x[i * NB:(i + 1) * NB])
        m["c"] = np.ascontiguousarray(c[i * NB:(i + 1) * NB])
        m["ctx"] = np.ascontiguousarray(ctx[i * NB:(i + 1) * NB])
        in_maps.append(m)
    res = run_bass_kernel_spmd(nc, in_maps, core_ids=list(range(n_cores)))
    y = np.concatenate([r["y"] for r in res.results], axis=0)
    if dbg:
        global DBG_OUT
        DBG_OUT = {k: np.asarray(v) for k, v in res.results[0].items() if k.startswith("dbg_")}
        return y, np.concatenate([r["ctx_out"] for r in res.results], axis=0)
    return y


def kernel(**inputs):
    return run(inputs, LAYERS, 8).astype(np.float32)
```

```python
import os
import numpy as np
from contextlib import ExitStack
import concourse.bass as bass
import concourse.mybir as mybir
from concourse.bass_utils import run_bass_kernel_spmd

F32 = mybir.dt.float32
BF16 = mybir.dt.bfloat16
ALU = mybir.AluOpType
AF = mybir.ActivationFunctionType
AX = mybir.AxisListType

D = 1024
T = 2048
TC = 256
NB = 2
DFF = 4096
EPS = 1e-6
NT = 18
TT = NB * NT
SEQ = TC + T


class Reg:
    __slots__ = ("w", "r", "excl")

    def __init__(self, excl=False):
        self.w = {}
        self.r = {}
        self.excl = excl


class Rot:
    def __init__(self, items):
        self.items = items
        self.i = 0

    def next(self):
        it = self.items[self.i % len(self.items)]
        self.i += 1
        return it


class KB:
    def __init__(self, nc):
        self.nc = nc
        self.eh = {"pe": nc.tensor, "dve": nc.vector, "act": nc.scalar, "pool": nc.gpsimd, "sp": nc.sync}
        self.esem = {k: nc.alloc_semaphore("es_" + k) for k in self.eh}
        self.ecnt = {k: 0 for k in self.eh}
        self.waited = {k: {} for k in self.eh}
        self.dpool = {q: [[nc.alloc_semaphore("ds_%s%d" % (q, i)), 0] for i in range(n)]
                      for q, n in (("sp", 32), ("pool", 16), ("act", 8))}
        self.dnext = {q: 0 for q in self.dpool}
        self.uid = 0

    def name(self, base):
        self.uid += 1
        return "%s_%d" % (base, self.uid)

    def sb(self, es, base, shape, dt):
        return es.enter_context(self.nc.sbuf_tensor(self.name(base), list(shape), dt))

    def rot(self, es, base, shape, dt, n):
        return Rot([(self.sb(es, base, shape, dt), Reg()) for _ in range(n)])

    def _wait(self, e, sem, val):
        w = self.waited[e]
        if w.get(sem.num, 0) < val:
            self.eh[e].wait_ge(sem, val)
            w[sem.num] = val

    def _need(self, e, reads, writes, is_dma):
        own = None if is_dma else self.esem[e].num
        need = {}

        def add(d, skip_same):
            for num, (sem, val) in d.items():
                if num == own and (skip_same or e == "pe"):
                    continue
                if need.get(num, (None, 0))[1] < val:
                    need[num] = (sem, val)
        for r in reads:
            add(r.w, False)
            if r.excl:
                add(r.r, True)
        for r in writes:
            add(r.w, True)
            add(r.r, True)
        return need

    def _mark(self, tok, reads, writes):
        num = tok[0].num
        for r in reads:
            r.r[num] = tok
        for r in writes:
            r.w = {num: tok}
            r.r = {}

    def op(self, e, fn, reads, writes):
        need = self._need(e, reads, writes, False)
        for sem, val in need.values():
            self._wait(e, sem, val)
        inst = fn(self.eh[e])
        self.ecnt[e] += 1
        inst.then_inc(self.esem[e], 1)
        self._mark((self.esem[e], self.ecnt[e]), reads, writes)

    def dma(self, q, out, in_, reads, writes, **kw):
        pool = self.dpool[q]
        i = self.dnext[q]
        self.dnext[q] = (i + 1) % len(pool)
        sem, val = pool[i]
        need = self._need(q, reads, writes, True)
        if val > 0:
            need[sem.num] = (sem, val)
        for s, v in need.values():
            self._wait(q, s, v)
        inst = self.eh[q].dma_start(out=out, in_=in_, **kw)
        inst.then_inc(sem, 16)
        pool[i][1] = val + 16
        self._mark((sem, val + 16), reads, writes)

    def barrier(self):
        toks = [(self.esem[e], self.ecnt[e]) for e in self.eh if self.ecnt[e] > 0]
        toks += [(s, v) for p in self.dpool.values() for (s, v) in p if v > 0]
        for e in self.eh:
            for s, v in toks:
                if s.num != self.esem[e].num:
                    self._wait(e, s, v)

    def mm(self, out, pairs, reads, writes):
        n = len(pairs)

        def fn(h):
            inst = None
            for i, (l, r) in enumerate(pairs):
                inst = h.matmul(out, l, r, start=(i == 0), stop=(i == n - 1))
            return inst
        self.op("pe", fn, reads, writes)

    def mm1(self, out, l, r, start, stop, reads, writes):
        self.op("pe", lambda h: h.matmul(out, l, r, start=start, stop=stop), reads, writes)

    def tr(self, outs_ins, ident, reads, writes):
        def fn(h):
            inst = None
            for o, i in outs_ins:
                inst = h.transpose(o, i, ident)
            return inst
        self.op("pe", fn, reads, writes)

    def act(self, out, in_, func, reads, writes, **kw):
        self.op("act", lambda h: h.activation(out=out, in_=in_, func=func, **kw), reads, writes)

    def ts(self, e, out, in0, s1, s2, op0, op1, reads, writes):
        if s2 is None:
            self.op(e, lambda h: h.tensor_scalar(out=out, in0=in0, scalar1=s1, scalar2=None, op0=op0), reads, writes)
        else:
            self.op(e, lambda h: h.tensor_scalar(out=out, in0=in0, scalar1=s1, scalar2=s2, op0=op0, op1=op1),
                    reads, writes)

    def tt(self, e, out, in0, in1, op, reads, writes):
        self.op(e, lambda h: h.tensor_tensor(out=out, in0=in0, in1=in1, op=op), reads, writes)

    def stt(self, e, out, in0, scalar, in1, op0, op1, reads, writes):
        self.op(e, lambda h: h.scalar_tensor_tensor(out=out, in0=in0, scalar=scalar, in1=in1, op0=op0, op1=op1),
                reads, writes)

    def cp(self, e, out, in_, reads, writes):
        if e == "act":
            self.op(e, lambda h: h.copy(out=out, in_=in_), reads, writes)
        else:
            self.op(e, lambda h: h.tensor_copy(out=out, in_=in_), reads, writes)

    def memset(self, e, ap, val, writes):
        self.op(e, lambda h: h.memset(ap, val), [], writes)

    def recip(self, out, in_, reads, writes):
        self.op("dve", lambda h: h.reciprocal(out=out, in_=in_), reads, writes)


class Stream:
    def __init__(self, xap, cap):
        self.x = xap
        self.c = cap
        self.regs = [Reg() for _ in range(TT)]

    def tile(self, tt):
        b, r = divmod(tt, NT)
        if r < 2:
            return self.c[b, r * 128:(r + 1) * 128, :]
        return self.x[b, (r - 2) * 128:(r - 1) * 128, :]


def tile_slot(tt):
    b, r = divmod(tt, NT)
    return 2 if r < 2 else b


class G:
    pass


def build_program(layers, dbg=False):
    nc = bass.Bass("TRN2", target_bir_lowering=False)
    kb = KB(nc)
    g = G()
    g.nc, g.kb = nc, kb
    g.dbg = dbg
    L = 4

    def din(name, shape, dt=F32):
        return nc.dram_tensor(name, list(shape), dt, kind="ExternalInput").ap()

    def dscr(name, shape, dt=F32):
        return nc.dram_tensor(name, list(shape), dt, kind="Internal").ap()

    I = {}
    I["x"] = din("x", [NB, T, D])
    I["c"] = din("c", [NB, D])
    I["ctx"] = din("ctx", [NB, TC, D])
    I["c_ctx"] = din("c_ctx", [1, D])
    I["ada_w"] = din("ada_w", [L, D, 6 * D])
    I["ada_b"] = din("ada_b", [L, 6 * D])
    for nm in ("mix_pre_g", "mix_post_g", "mlp_pre_g", "mlp_post_g"):
        I[nm] = din(nm, [L, D])
    I["mlp_w1"] = din("mlp_w1", [L, D, DFF])
    I["mlp_w2"] = din("mlp_w2", [L, DFF, D])
    I["ret_w_in"] = din("ret_w_in", [D, 8192])
    I["ret_decay_logit"] = din("ret_decay_logit", [1, 16])
    I["ret_w_out"] = din("ret_w_out", [2048, D])
    I["att_w_in"] = din("att_w_in", [D, 1536])
    I["att_sink"] = din("att_sink", [1, 16])
    I["att_w_out"] = din("att_w_out", [D, D])
    I["pool_w"] = din("pool_w", [4, 256, 256])
    I["pool_b"] = din("pool_b", [1, D])
    I["pool_scale"] = din("pool_scale", [1, D])
    I["dn_w_in"] = din("dn_w_in", [D, 5152])
    I["dn_conv_w"] = din("dn_conv_w", [5, 3072])
    I["dn_a_log"] = din("dn_a_log", [1, 16])
    I["dn_dt_bias"] = din("dn_dt_bias", [1, 16])
    I["dn_norm_g"] = din("dn_norm_g", [1, 128])
    I["dn_w_out"] = din("dn_w_out", [D, D])
    I["k_ident"] = din("k_ident", [128, 128])
    I["k_pool"] = din("k_pool", [20, 128, 128])
    I["k_rope_att"] = din("k_rope_att", [2, 128, SEQ])
    I["k_rope_ret"] = din("k_rope_ret", [2, 128, SEQ])
    I["k_att_mask"] = din("k_att_mask", [3, 128, 384])
    I["k_ret_tab"] = din("k_ret_tab", [5, 128, 128])
    I["k_dn"] = din("k_dn", [4, 64, 64])
    g.I = I
    yout = nc.dram_tensor("y", [NB, T, D], F32, kind="ExternalOutput").ap()

    g.modD = dscr("modD", [L, 3, 6 * D])
    s_in = Stream(I["x"], I["ctx"])
    s1 = Stream(dscr("s1x", [NB, T, D]), dscr("s1c", [NB, TC, D]))
    s2 = Stream(dscr("s2x", [NB, T, D]), dscr("s2c", [NB, TC, D]))
    s_out = Stream(yout, s2.c)
    g.modD_r = Reg()

    g.ps = [nc.alloc_psum_tensor("psb%d" % i, [128, 512], F32) for i in range(8)]
    g.psr = [Reg(excl=True) for _ in range(8)]

    with ExitStack() as ges:
        g.ident_f = kb.sb(ges, "identf", [128, 128], F32)
        g.ident_b = kb.sb(ges, "identb", [128, 128], BF16)
        g.eps = kb.sb(ges, "eps", [128, 1], F32)
        g.const_r = Reg()
        kb.dma("sp", g.ident_f[:], I["k_ident"][:, :], [], [g.const_r])
        kb.dma("pool", g.ident_b[:], I["k_ident"][:, :], [], [g.const_r])
        kb.memset("dve", g.eps[:], EPS, [g.const_r])
        g.eps128 = kb.sb(ges, "eps128", [128, 1], F32)
        kb.memset("dve", g.eps128[:], EPS * 128.0, [g.const_r])
        kb.barrier()

        prologue_ada(g, layers)
        kb.barrier()

        cur = s_in
        for li, l in enumerate(layers):
            last = (li == len(layers) - 1)
            need_ctx = (l < 3) or dbg
            kind = l % 4
            if kind == 2:
                mixer_pool(g, l, cur, s1, need_ctx)
            elif kind == 1:
                mixer_att(g, l, cur, s1, need_ctx)
            elif kind == 0:
                mixer_ret(g, l, cur, s1, need_ctx)
            elif kind == 3:
                mixer_dn(g, l, cur, s1, need_ctx)
            else:
                raise NotImplementedError
            kb.barrier()
            dst = s_out if last else s2
            ffn(g, l, s1, dst, need_ctx)
            kb.barrier()
            cur = s2
        if dbg:
            cout = nc.dram_tensor("ctx_out", [NB, TC, D], F32, kind="ExternalOutput").ap()
            for b in range(NB):
                kb.dma("sp", cout[b], s2.c[b], [s2.regs[b * NT], s2.regs[b * NT + 1], s_out.regs[b * NT],
                                                s_out.regs[b * NT + 1]], [Reg()])
    kb.barrier()
    return nc


def prologue_ada(g, layers):
    kb, nc, I = g.kb, g.nc, g.I
    with ExitStack() as es:
        condT = kb.sb(es, "condT", [128, 8, 4], F32)
        cr = Reg()
        for s in range(3):
            src = I["c"][s, :] if s < 2 else I["c_ctx"][0, :]
            kb.dma("sp", condT[:, :, s], src.rearrange("(c p) -> p c", p=128), [], [cr],
                   allow_slow_non_contiguous=True)
        kb.memset("dve", condT[:, :, 3], 0.0, [cr])
        kb.act(condT[:, :, 0:3], condT[:, :, 0:3], AF.Silu, [cr], [cr])
        wrot = kb.rot(es, "adaw", [128, 8, 512], F32, 3)
        modrow = kb.sb(es, "modrow", [3, 6 * D], F32)
        mr = Reg()
        bias = kb.sb(es, "adab", [3, 6 * D], F32)
        gains = kb.sb(es, "gains", [3, 4, D], F32)
        br = Reg()
        for l in layers:
            kb.dma("sp", bias[:], I["ada_b"][l, :].partition_broadcast(3), [], [br])
            for gi, nm in enumerate(("mix_pre_g", "mix_post_g", "mlp_pre_g", "mlp_post_g")):
                kb.dma("sp", gains[:, gi, :], I[nm][l, :].partition_broadcast(3), [], [br])
            for j in range(12):
                wt, wr = wrot.next()
                kb.dma("sp", wt[:], I["ada_w"][l, :, j * 512:(j + 1) * 512].rearrange("(c p) n -> p c n", p=128),
                       [], [wr])
                pb = j % 2
                kb.mm(g.ps[pb][0:3, :], [(condT[:, kc, 0:3], wt[:, kc, :]) for kc in range(8)],
                      [cr, wr], [g.psr[pb]])
                kb.tt("dve", modrow[:, j * 512:(j + 1) * 512], g.ps[pb][0:3, :], bias[:, j * 512:(j + 1) * 512],
                      ALU.add, [g.psr[pb], br], [mr])
            for seg, gi, plus1 in ((1, 0, True), (2, 1, False), (4, 2, True), (5, 3, False)):
                sl = modrow[:, seg * D:(seg + 1) * D]
                if plus1:
                    kb.stt("dve", sl, sl, 1.0, gains[:, gi, :], ALU.add, ALU.mult, [mr, br], [mr])
                else:
                    kb.tt("dve", sl, sl, gains[:, gi, :], ALU.mult, [mr, br], [mr])
            kb.dma("sp", g.modD[l], modrow[:], [mr], [g.modD_r])


def load_cols(g, es, l, segA, segB):
    kb = g.kb
    r = Reg()
    outs = []
    for seg in (segA, segB):
        t = kb.sb(es, "mcol", [128, 3, 8], F32)
        for s in range(3):
            kb.dma("sp", t[:, s, :], g.modD[l, s, seg * D:(seg + 1) * D].rearrange("(c p) -> p c", p=128),
                   [g.modD_r], [r], allow_slow_non_contiguous=True)
        outs.append(t)
    return outs[0], outs[1], r


def load_bc(g, es, l, seg):
    kb = g.kb
    out = []
    r = Reg()
    for s in range(3):
        t = kb.sb(es, "mbc", [128, D], F32)
        kb.dma("sp", t[:], g.modD[l, s, seg * D:(seg + 1) * D].partition_broadcast(128), [g.modD_r], [r])
        out.append(t)
    return out, r


class NormWS:
    def __init__(self, g, es, nbuf=2):
        kb = g.kb
        self.st = kb.rot(es, "nst", [128, 4], F32, 8)
        self.junk = kb.sb(es, "njunk", [128, D], BF16)
        self.junk_r = Reg()
        self.xn = kb.rot(es, "nxn", [128, D], BF16, nbuf)


def rms_stats_gen(g, ws, y_aps, y_regs, out, isn=1.0 / 32.0):
    kb = g.kb
    st, sr = ws.st.next()
    out.append((st, sr))
    kb.memset("pool", st[:], 0.0, [sr])
    yield
    off = 0
    for i, ya in enumerate(y_aps):
        n = ya.shape[-1]
        kb.act(ws.junk[:, off:off + n], ya, AF.Square, y_regs + [sr], [ws.junk_r, sr], scale=isn,
               accum_out=st[:, i:i + 1])
        off += n
    yield
    if len(y_aps) == 2:
        kb.tt("dve", st[:, 0:1], st[:, 0:1], st[:, 1:2], ALU.add, [sr], [sr])
        yield
    kb.act(st[:, 1:2], st[:, 0:1], AF.Sqrt, [sr, g.const_r], [sr], bias=g.eps[:, 0:1], scale=1.0)
    yield
    kb.recip(st[:, 2:3], st[:, 1:2], [sr], [sr])
    yield


def rms_stats(g, ws, y_aps, y_regs, isn=1.0 / 32.0):
    out = []
    for _ in rms_stats_gen(g, ws, y_aps, y_regs, out, isn):
        pass
    return out[0]


def norm_T_gen(g, ws, xt, xr, Acol, Bcol, colr, slot, dst, dst_r, pbank):
    kb = g.kb
    out = []
    yield from rms_stats_gen(g, ws, [xt], [xr], out)
    st, sr = out[0]
    xn, xnr = ws.xn.next()
    kb.act(xn[:], xt, AF.Copy, [xr, sr], [xnr], scale=st[:, 2:3])
    yield
    psT = g.ps[pbank][:].bitcast(BF16).rearrange("p (c t) -> p c t", c=8)
    kb.tr([(psT[:, c, :], xn[:, c * 128:(c + 1) * 128]) for c in range(8)], g.ident_b[:],
          [xnr, g.const_r], [g.psr[pbank]])
    yield
    for c in range(8):
        kb.ts("dve", dst[:, c, :], psT[:, c, :], Acol[:, slot, c:c + 1], Bcol[:, slot, c:c + 1], ALU.mult, ALU.add,
              [g.psr[pbank], colr], [dst_r])
    yield


def norm_T(*args):
    for _ in norm_T_gen(*args):
        pass


def post_res_gen(g, ws, y_aps, y_regs, xt, xr, Gbc, gr, tmp, tmpr):
    kb = g.kb
    out = []
    yield from rms_stats_gen(g, ws, y_aps, y_regs, out)
    st, sr = out[0]
    off = 0
    for ya in y_aps:
        n = ya.shape[-1]
        kb.stt("dve", tmp[:, off:off + n], ya, st[:, 2:3], Gbc[:, off:off + n], ALU.mult, ALU.mult,
               y_regs + [sr, gr], [tmpr])
        off += n
    yield
    kb.tt("pool", xt, xt, tmp[:, :], ALU.add, [xr, tmpr], [xr])
    yield


def post_res(*args):
    for _ in post_res_gen(*args):
        pass


def ffn_weights(g, es, l):
    kb, I = g.kb, g.I
    w1 = kb.sb(es, "w1", [128, 8, DFF], BF16)
    w2 = kb.sb(es, "w2", [128, 32, D], BF16)
    w1r = [Reg() for _ in range(8)]
    w2r = [Reg() for _ in range(8)]
    for kc in range(8):
        for hf in range(2):
            kb.dma("pool", w1[:, kc, hf * 2048:(hf + 1) * 2048],
                   I["mlp_w1"][l, kc * 128:(kc + 1) * 128, hf * 2048:(hf + 1) * 2048], [], [w1r[kc]])
    for q in range(8):
        kb.dma("pool", w2[:, q * 4:(q + 1) * 4, :],
               I["mlp_w2"][l, q * 512:(q + 1) * 512, :].rearrange("(c p) n -> p c n", p=128), [], [w2r[q]])
    return w1, w2, w1r, w2r


def ffn(g, l, src, dst, need_ctx, weights=None):
    kb, nc, I = g.kb, g.nc, g.I
    with ExitStack() as es:
        w1, w2, w1r, w2r = weights if weights is not None else ffn_weights(g, es, l)
        Acol, Bcol, acr = load_cols(g, es, l, 4, 3)
        Gbc, gbr = load_bc(g, es, l, 5)
        ws = NormWS(g, es)
        xrot = kb.rot(es, "fx", [128, D], F32, 6)
        uT = kb.rot(es, "fuT", [128, 8, 256], BF16, 2)
        hT = kb.sb(es, "fhT", [128, 32, 256], BF16)
        hTr = [Reg() for _ in range(32)]
        rl = kb.rot(es, "frl", [128, 256], F32, 4)
        tmp = kb.rot(es, "ftmp", [128, D], F32, 2)
        tiles = [tt for tt in range(TT) if need_ctx or (tt % NT) >= 2]
        groups = [tiles[i:i + 2] for i in range(0, len(tiles), 2)]
        hslot = 0

        def prep(grp):
            u, ur = uT.next()
            xs = []
            for j, tt in enumerate(grp):
                xt, xr = xrot.next()
                kb.dma("sp", xt[:], src.tile(tt), [src.regs[tt]], [xr])
                norm_T(g, ws, xt[:], xr, Acol, Bcol, acr, tile_slot(tt), u[:, :, j * 128:(j + 1) * 128], ur, 4)
                xs.append((xt, xr))
            return u, ur, xs
        nxt = prep(groups[0])
        for gi_, grp in enumerate(groups):
            u, ur, xs = nxt
            for fc in range(32):
                hb = 5 + (hslot // 2) % 2
                hh = hslot % 2
                hslot += 1
                hp = g.ps[hb][:, hh * 256:(hh + 1) * 256]
                kb.mm(hp, [(w1[:, kc, fc * 128:(fc + 1) * 128], u[:, kc, :]) for kc in range(8)],
                      [ur] + w1r, [g.psr[hb]])
                r_, rr = rl.next()
                kb.act(r_[:], hp, AF.Relu, [g.psr[hb]], [rr])
                kb.tt("dve", hT[:, fc, :], r_[:], r_[:], ALU.mult, [rr], [hTr[fc]])
            if gi_ + 1 < len(groups):
                nxt = prep(groups[gi_ + 1])
            for j, tt in enumerate(grp):
                xt, xr = xs[j]
                for half in range(2):
                    pb = j * 2 + half
                    kb.mm(g.ps[pb][:, :], [(hT[:, fc, j * 128:(j + 1) * 128], w2[:, fc, half * 512:(half + 1) * 512])
                                           for fc in range(32)], hTr + w2r, [g.psr[pb]])
                t_, tr_ = tmp.next()
                post_res(g, ws, [g.ps[j * 2][:, :], g.ps[j * 2 + 1][:, :]], [g.psr[j * 2], g.psr[j * 2 + 1]],
                         xt[:], xr, Gbc[tile_slot(tt)], gbr, t_, tr_)
                kb.dma("sp", dst.tile(tt), xt[:], [xr], [dst.regs[tt]])


def mixer_pool(g, l, src, dst, need_ctx):
    kb, nc, I = g.kb, g.nc, g.I
    with ExitStack() as es:
        Acol, Bcol, acr = load_cols(g, es, l, 1, 0)
        Gbc, gbr = load_bc(g, es, l, 2)
        ws = NormWS(g, es, 4)
        wg = kb.sb(es, "pw", [128, 4, 2, 256], BF16)
        wgr = Reg()
        for gi in range(4):
            kb.dma("pool", wg[:, gi, :, :], I["pool_w"][gi].rearrange("(c p) n -> p c n", p=128), [], [wgr])
        pm = kb.sb(es, "pm", [128, 20, 128], BF16)
        kb.dma("pool", pm[:], I["k_pool"].rearrange("m p n -> p m n"), [], [wgr])
        pbb = kb.sb(es, "pbb", [128, D], F32)
        psb = kb.sb(es, "psb", [128, D], F32)
        kb.dma("sp", pbb[:], I["pool_b"][0, :].partition_broadcast(128), [], [wgr])
        kb.dma("sp", psb[:], I["pool_scale"][0, :].partition_broadcast(128), [], [wgr])
        xrot = kb.rot(es, "px", [128, D], F32, 4)
        uT = kb.rot(es, "puT", [128, 8, 128], BF16, 3)
        vall = kb.sb(es, "pvall", [128, TT, D], BF16)
        vreg = [Reg() for _ in range(TT)]
        yb = kb.rot(es, "pyb", [128, D], F32, 4)
        tmp = kb.rot(es, "ptmp", [128, D], F32, 4)
        tiles = [tt for tt in range(TT) if need_ctx or (tt % NT) >= 2]

        def unit1(tt, slot):
            xt, xr = xrot.next()
            kb.dma("sp", xt[:], src.tile(tt), [src.regs[tt]], [xr])
            u, ur = uT.next()
            yield
            yield from norm_T_gen(g, ws, xt[:], xr, Acol, Bcol, acr, tile_slot(tt), u, ur, 4 + slot)
            for gi in range(4):
                pb = slot * 2 + gi // 2
                kb.mm(g.ps[pb][:, (gi % 2) * 256:(gi % 2 + 1) * 256],
                      [(u[:, gi * 2 + kc, :], wg[:, gi, kc, :]) for kc in range(2)], [ur, wgr], [g.psr[pb]])
            yield
            kb.cp("act", vall[:, tt, 0:512], g.ps[slot * 2][:, :], [g.psr[slot * 2]], [vreg[tt]])
            kb.cp("act", vall[:, tt, 512:1024], g.ps[slot * 2 + 1][:, :], [g.psr[slot * 2 + 1]], [vreg[tt]])
            yield
        interleave((unit1(tt, i % 2) for i, tt in enumerate(tiles)), 2)

        def unit2(tt, slot):
            r = tt % NT
            seg_lo, seg_n = (0, 2) if r < 2 else (2, 16)
            tq = r - seg_lo
            xt, xr = xrot.next()
            kb.dma("sp", xt[:], src.tile(tt), [src.regs[tt]], [xr])
            for gi in range(4):
                pb = slot * 2 + gi // 2
                cidx = 0 if (0 < tq < seg_n - 1) else (1 if tq == 0 else 2)
                pairs = [(pm[:, gi * 5 + cidx, :], vall[:, tt, gi * 256:(gi + 1) * 256])]
                regs = [wgr, vreg[tt]]
                if tq > 0:
                    pairs.append((pm[:, gi * 5 + 3, :], vall[:, tt - 1, gi * 256:(gi + 1) * 256]))
                    regs.append(vreg[tt - 1])
                if tq < seg_n - 1:
                    pairs.append((pm[:, gi * 5 + 4, :], vall[:, tt + 1, gi * 256:(gi + 1) * 256]))
                    regs.append(vreg[tt + 1])
                kb.mm(g.ps[pb][:, (gi % 2) * 256:(gi % 2 + 1) * 256], pairs, regs, [g.psr[pb]])
            yield
            y_, yr = yb.next()
            for h in range(2):
                kb.tt("dve", y_[:, h * 512:(h + 1) * 512], g.ps[slot * 2 + h][:, :], pbb[:, h * 512:(h + 1) * 512],
                      ALU.add, [g.psr[slot * 2 + h], wgr], [yr])
            yield
            kb.tt("pool", y_[:], y_[:], psb[:], ALU.mult, [yr, wgr], [yr])
            yield
            t_, tr_ = tmp.next()
            yield from post_res_gen(g, ws, [y_[:, :]], [yr], xt[:], xr, Gbc[tile_slot(tt)], gbr, t_, tr_)
            kb.dma("sp", dst.tile(tt), xt[:], [xr], [dst.regs[tt]])
        interleave((unit2(tt, i % 3) for i, tt in enumerate(tiles)), 3)


def phase_uT(g, es, l, src):
    kb = g.kb
    uT = kb.sb(es, "uTall", [128, 8, TT * 128], BF16)
    regs = [Reg() for _ in range(TT)]
    with ExitStack() as es2:
        Acol, Bcol, acr = load_cols(g, es2, l, 1, 0)
        ws = NormWS(g, es2, 4)
        xrot = kb.rot(es2, "ux", [128, D], F32, 4)

        def unit(tt, slot):
            xt, xr = xrot.next()
            kb.dma("sp", xt[:], src.tile(tt), [src.regs[tt]], [xr])
            yield
            yield from norm_T_gen(g, ws, xt[:], xr, Acol, Bcol, acr, tile_slot(tt), uT[:, :, tt * 128:(tt + 1) * 128],
                                  regs[tt], 4 + slot)
        interleave((unit(tt, tt % 3) for tt in range(TT)), 3)
        kb.barrier()
    return uT, regs


def load_w_bf16(g, rot, wd, c0, ncols):
    kb = g.kb
    wt, wr = rot.next()
    kb.dma("pool", wt[:, :, 0:ncols], wd[:, c0:c0 + ncols].rearrange("(c p) n -> p c n", p=128), [], [wr])
    return wt, wr


def phase_outproj(g, l, a_d, a_regs, Kd, wout_d, src, dst, need_ctx):
    kb, nc = g.kb, g.nc
    nk = Kd // 128
    with ExitStack() as es:
        Gbc, gbr = load_bc(g, es, l, 2)
        ws = NormWS(g, es)
        wo = kb.sb(es, "wo", [128, nk, D], BF16)
        wor = Reg()
        for q in range(nk // 4):
            kb.dma("pool", wo[:, q * 4:(q + 1) * 4, :],
                   wout_d[q * 512:(q + 1) * 512, :].rearrange("(c p) n -> p c n", p=128), [], [wor])
        a_list = a_d if isinstance(a_d, list) else [a_d]
        arot = kb.rot(es, "oa", [128, Kd], BF16, 3 * len(a_list))
        aTrot = kb.rot(es, "oaT", [128, nk, 128], BF16, 3)
        xrot = kb.rot(es, "ox", [128, D], F32, 3)
        tmp = kb.rot(es, "otmp", [128, D], F32, 3)
        def unit(tt, slot):
            at, ar = arot.next()
            kb.dma("sp", at[:], a_list[0][tt * 128:(tt + 1) * 128, :], [a_regs[tt]], [ar])
            xt, xr = xrot.next()
            kb.dma("sp", xt[:], src.tile(tt), [src.regs[tt]], [xr])
            for extra in a_list[1:]:
                at2, ar2 = arot.next()
                kb.dma("sp", at2[:], extra[tt * 128:(tt + 1) * 128, :], [a_regs[tt]], [ar2])
                yield
                kb.tt("pool", at[:], at[:], at2[:], ALU.add, [ar, ar2], [ar])
            yield
            aT, aTr = aTrot.next()
            pb = slot * 3 + 2
            for q in range(nk // 8):
                psT = g.ps[pb][:].bitcast(BF16).rearrange("p (c t) -> p c t", c=8)
                kb.tr([(psT[:, c, :], at[:, (q * 8 + c) * 128:(q * 8 + c + 1) * 128]) for c in range(8)],
                      g.ident_b[:], [ar, g.const_r], [g.psr[pb]])
                yield
                kb.cp("act", aT[:, q * 8:(q + 1) * 8, :], psT, [g.psr[pb]], [aTr])
                yield
            yb = slot * 3
            for half in range(2):
                kb.mm(g.ps[yb + half][:, :], [(aT[:, kc, :], wo[:, kc, half * 512:(half + 1) * 512])
                                              for kc in range(nk)], [aTr, wor], [g.psr[yb + half]])
            yield
            t_, tr_ = tmp.next()
            yield from post_res_gen(g, ws, [g.ps[yb][:, :], g.ps[yb + 1][:, :]], [g.psr[yb], g.psr[yb + 1]], xt[:], xr,
                                    Gbc[tile_slot(tt)], gbr, t_, tr_)
            kb.dma("sp", dst.tile(tt), xt[:], [xr], [dst.regs[tt]])
        tts = [tt for tt in range(TT) if need_ctx or (tt % NT) >= 2]
        interleave((unit(tt, i % 2) for i, tt in enumerate(tts)), 2)


def make_perm(g, wt, wr, wp, wpr, ncols, blk):
    kb = g.kb
    v_in = wt[:, :, 0:ncols].rearrange("p c (q two i) -> p c q two i", two=2, i=blk)
    v_out = wp[:, :, 0:ncols].rearrange("p c (q two i) -> p c q two i", two=2, i=blk)
    for kc in range(8):
        kb.cp("pool", v_out[:, kc, :, 0, :], v_in[:, kc, :, 1, :], [wr], [wpr])
        kb.cp("pool", v_out[:, kc, :, 1, :], v_in[:, kc, :, 0, :], [wr], [wpr])


SEQ_BLOCKS = [(b, o, min(512, SEQ - o)) for b in range(NB) for o in range(0, SEQ, 512)]


def proj_fm_rope(g, uT, uTr, wd, c0, rope, rope_r, scale, blk, out_d, out_r, wrot, wprot, stg, pbase):
    kb = g.kb
    wt, wr = load_w_bf16(g, wrot, wd, c0, 128)
    wp, wpr = wprot.next()
    make_perm(g, wt, wr, wp, wpr, 128, blk)
    for bi, (b, o, n) in enumerate(SEQ_BLOCKS):
        t0 = b * SEQ + o
        tiles = list(range(t0 // 128, (t0 + n) // 128))
        rr = [uTr[t] for t in tiles]
        pa, pb = pbase + (bi % 2) * 2, pbase + (bi % 2) * 2 + 1
        kb.mm(g.ps[pa][:, 0:n], [(wt[:, kc, 0:128], uT[:, kc, t0:t0 + n]) for kc in range(8)], rr + [wr], [g.psr[pa]])
        kb.mm(g.ps[pb][:, 0:n], [(wp[:, kc, 0:128], uT[:, kc, t0:t0 + n]) for kc in range(8)], rr + [wpr], [g.psr[pb]])
        (t1, t1r), (t2, t2r), (t3, t3r) = stg[0].next(), stg[1].next(), stg[2].next()
        kb.stt("dve", t1[:, 0:n], g.ps[pa][:, 0:n], scale, rope[:, 0, o:o + n], ALU.mult, ALU.mult,
               [g.psr[pa], rope_r], [t1r])
        kb.stt("dve", t2[:, 0:n], g.ps[pb][:, 0:n], scale, rope[:, 1, o:o + n], ALU.mult, ALU.mult,
               [g.psr[pb], rope_r], [t2r])
        kb.tt("pool", t3[:, 0:n], t1[:, 0:n], t2[:, 0:n], ALU.add, [t1r, t2r], [t3r])
        kb.dma("sp", out_d[:, t0:t0 + n], t3[:, 0:n], [t3r], [out_r])


def proj_tm(g, uT, uTr, wd, c0, ncols, out_d, out_regs, col0, wrot, stg, pbase, func=None, tiles=None):
    kb = g.kb
    wt, wr = load_w_bf16(g, wrot, wd, c0, ncols)
    for i, tt in enumerate(tiles if tiles is not None else range(TT)):
        pb = pbase + i % 2
        kb.mm(g.ps[pb][:, 0:ncols], [(uT[:, kc, tt * 128:(tt + 1) * 128], wt[:, kc, 0:ncols]) for kc in range(8)],
              [uTr[tt], wr], [g.psr[pb]])
        st_, sr_ = stg.next()
        if func is None:
            kb.cp("act", st_[:, 0:ncols], g.ps[pb][:, 0:ncols], [g.psr[pb]], [sr_])
        else:
            kb.act(st_[:, 0:ncols], g.ps[pb][:, 0:ncols], func, [g.psr[pb]], [sr_])
        kb.dma("sp", out_d[tt * 128:(tt + 1) * 128, col0:col0 + ncols], st_[:, 0:ncols], [sr_], [out_regs[tt]])


def mixer_att(g, l, src, dst, need_ctx, hook=None):
    kb, nc, I = g.kb, g.nc, g.I
    NTOK = TT * 128
    attq = nc.dram_tensor(kb.name("attq"), [8, 128, NTOK], BF16, kind="Internal").ap()
    attk = nc.dram_tensor(kb.name("attk"), [2, 128, NTOK], BF16, kind="Internal").ap()
    attv = nc.dram_tensor(kb.name("attv"), [NTOK, 256], BF16, kind="Internal").ap()
    atta = nc.dram_tensor(kb.name("atta"), [NTOK, D], BF16, kind="Internal").ap()
    qr_, kr_ = [Reg() for _ in range(8)], [Reg() for _ in range(2)]
    vr_ = [Reg() for _ in range(TT)]
    ar_ = [Reg() for _ in range(TT)]
    with ExitStack() as es:
        uT, uTr = phase_uT(g, es, l, src)
        rope = kb.sb(es, "ropeA", [128, 2, SEQ], F32)
        rope_r = Reg()
        kb.dma("sp", rope[:], I["k_rope_att"].rearrange("t p n -> p t n"), [], [rope_r])
        wrot = kb.rot(es, "aw", [128, 8, 256], BF16, 2)
        wprot = kb.rot(es, "awp", [128, 8, 128], BF16, 2)
        stg = [kb.rot(es, "astg", [128, 512], F32, 2), kb.rot(es, "astg", [128, 512], F32, 2),
               kb.rot(es, "astgb", [128, 512], BF16, 3)]
        for cb in range(8):
            proj_fm_rope(g, uT, uTr, I["att_w_in"], cb * 128, rope, rope_r, 0.125, 16, attq[cb], qr_[cb],
                         wrot, wprot, stg, 0)
        for cb in range(2):
            proj_fm_rope(g, uT, uTr, I["att_w_in"], 1024 + cb * 128, rope, rope_r, 1.0, 16, attk[cb], kr_[cb],
                         wrot, wprot, stg, 0)
        proj_tm(g, uT, uTr, I["att_w_in"], 1280, 256, attv, vr_, 0, wrot, stg[2], 0)
    kb.barrier()
    with ExitStack() as es:
        mb = kb.sb(es, "amask", [128, 3, 384], BF16)
        cr = Reg()
        kb.dma("pool", mb[:], I["k_att_mask"].rearrange("v p n -> p v n"), [], [cr])
        sink = kb.sb(es, "asink", [128, 16], F32)
        kb.dma("sp", sink[:], I["att_sink"][0, :].partition_broadcast(128), [], [cr])
        Krot = kb.rot(es, "aK", [64, 2560], BF16, 3)
        Vrot = kb.rot(es, "aV", [128, 20, 64], BF16, 3)
        for kt, kr in Krot.items:
            kb.memset("pool", kt[:, 0:128], 0.0, [kr])
            kb.memset("pool", kt[:, 2176:2304], 0.0, [kr])
        for vt, vr in Vrot.items:
            kb.memset("pool", vt[:, 0, :], 0.0, [vr])
            kb.memset("pool", vt[:, 17, :], 0.0, [vr])
        Qrot = kb.rot(es, "aQ", [64, SEQ], BF16, 4)
        prot = kb.rot(es, "ap", [128, 640], BF16, 4)
        pTrot = kb.rot(es, "apT", [128, 5, 128], BF16, 4)
        strot = kb.rot(es, "ast", [128, 8], F32, 8)
        ostg = kb.rot(es, "aos", [128, 64], BF16, 6)

        def att_unit(b, h, blk, slot, cnt, Qt, Qr, Kt, Kr, Vt, Vr):
            lat = blk >= 2
            bi = blk - 2
            qs = Qt[:, blk * 128:(blk + 1) * 128]
            pa, pbk = slot * 2, slot * 2 + 1
            ptb = 4 + slot
            po = g.ps[6 + slot][:, (cnt % 8) * 64:(cnt % 8) * 64 + 64]
            por = g.psr[6 + slot]
            st, sr = strot.next()
            if lat:
                var = 1 if bi == 0 else (2 if bi == 15 else 0)
                kb.mm(g.ps[pa][:, 0:384], [(qs, Kt[:, bi * 128:bi * 128 + 384]),
                                           (g.ident_b[:], mb[:, var, :])], [Qr, Kr, cr, g.const_r], [g.psr[pa]])
            kb.mm(g.ps[pbk][:, 0:256], [(qs, Kt[:, 2304:2560])], [Qr, Kr], [g.psr[pbk]])
            kb.memset("pool", st[:, 4:7], 0.0, [sr])
            yield
            if lat:
                kb.op("dve", lambda hd, o=st[:, 0:1], i=g.ps[pa][:, 0:384]: hd.reduce_max(out=o, in_=i, axis=AX.X),
                      [g.psr[pa]], [sr])
            kb.op("dve", lambda hd, o=st[:, 1:2], i=g.ps[pbk][:, 0:256]: hd.reduce_max(out=o, in_=i, axis=AX.X),
                  [g.psr[pbk]], [sr])
            if lat:
                kb.tt("dve", st[:, 1:2], st[:, 0:1], st[:, 1:2], ALU.max, [sr], [sr])
            kb.ts("dve", st[:, 2:3], st[:, 1:2], sink[:, h:h + 1], -1.0, ALU.max, ALU.mult, [sr, cr], [sr])
            yield
            p_, pr = prot.next()
            if lat:
                kb.act(p_[:, 0:384], g.ps[pa][:, 0:384], AF.Exp, [g.psr[pa], sr], [pr, sr],
                       bias=st[:, 2:3], scale=1.0, accum_out=st[:, 4:5])
            kb.act(p_[:, 384:640], g.ps[pbk][:, 0:256], AF.Exp, [g.psr[pbk], sr], [pr, sr],
                   bias=st[:, 2:3], scale=1.0, accum_out=st[:, 5:6])
            kb.act(st[:, 6:7], st[:, 2:3], AF.Exp, [sr, cr], [sr], bias=sink[:, h:h + 1], scale=1.0)
            yield
            js = list(range(5)) if lat else [3, 4]
            psT = g.ps[ptb][:].bitcast(BF16).rearrange("p (c t) -> p c t", c=8)
            kb.tr([(psT[:, j, :], p_[:, j * 128:(j + 1) * 128]) for j in js], g.ident_b[:],
                  [pr, g.const_r], [g.psr[ptb]])
            kb.stt("dve", st[:, 7:8], st[:, 4:5], st[:, 5:6], st[:, 6:7], ALU.add, ALU.add, [sr], [sr])
            yield
            pT, pTr = pTrot.next()
            kb.cp("dve", pT[:, js[0]:5, :], psT[:, js[0]:5, :], [g.psr[ptb]], [pTr])
            kb.recip(st[:, 3:4], st[:, 7:8], [sr], [sr])
            yield
            pairs = []
            if lat:
                pairs += [(pT[:, j, :], Vt[:, bi + j, :]) for j in range(3)]
            pairs += [(pT[:, 3, :], Vt[:, 18, :]), (pT[:, 4, :], Vt[:, 19, :])]
            kb.mm(po, pairs, [pTr, Vr], [por])
            yield
            os_, osr = ostg.next()
            kb.ts("dve", os_[:], po, st[:, 3:4], None, ALU.mult, None, [por, sr], [osr])
            tt = b * NT + blk
            kb.dma("sp", atta[tt * 128:(tt + 1) * 128, h * 64:(h + 1) * 64], os_[:], [osr], [Reg()])

        def att_units():
            cnt = 0
            for b in range(NB):
                for kv in range(4):
                    Kt, Kr = Krot.next()
                    Vt, Vr = Vrot.next()
                    base = b * SEQ
                    ksrc = attk[kv // 2, (kv % 2) * 64:(kv % 2) * 64 + 64, :]
                    kb.dma("sp", Kt[:, 128:2176], ksrc[:, base + TC:base + SEQ], [kr_[kv // 2]], [Kr])
                    kb.dma("sp", Kt[:, 2304:2560], ksrc[:, base:base + TC], [kr_[kv // 2]], [Kr])
                    vsrc = attv[:, kv * 64:(kv + 1) * 64]
                    kb.dma("sp", Vt[:, 1:17, :], vsrc[base + TC:base + SEQ, :].rearrange("(t p) d -> p t d", p=128),
                           [vr_[b * NT + t] for t in range(2, 18)], [Vr])
                    kb.dma("sp", Vt[:, 18:20, :], vsrc[base:base + TC, :].rearrange("(t p) d -> p t d", p=128),
                           [vr_[b * NT], vr_[b * NT + 1]], [Vr])
                    for hh in range(4):
                        h = kv * 4 + hh
                        Qt, Qr = Qrot.next()
                        kb.dma("sp", Qt[:], attq[h // 2, (h % 2) * 64:(h % 2) * 64 + 64, base:base + SEQ],
                               [qr_[h // 2]], [Qr])
                        for blk in range(18):
                            if blk < 2 and not need_ctx:
                                continue
                            yield att_unit(b, h, blk, cnt % 2, cnt // 2, Qt, Qr, Kt, Kr, Vt, Vr)
                            cnt += 1
        interleave(att_units(), 2)
    kb.barrier()
    if hook is not None:
        hook()
    phase_outproj(g, l, atta, ar_, D, I["att_w_out"], src, dst, need_ctx)


def mixer_ret(g, l, src, dst, need_ctx):
    kb, nc, I = g.kb, g.nc, g.I
    NTOK = TT * 128
    retq = nc.dram_tensor(kb.name("retq"), [8, 128, NTOK], BF16, kind="Internal").ap()
    retk = nc.dram_tensor(kb.name("retk"), [8, 128, NTOK], BF16, kind="Internal").ap()
    retv = nc.dram_tensor(kb.name("retv"), [NTOK, 2048], BF16, kind="Internal").ap()
    retg = nc.dram_tensor(kb.name("retg"), [NTOK, 4096], BF16, kind="Internal").ap()
    reta = [nc.dram_tensor(kb.name("reta"), [NTOK, 2048], BF16, kind="Internal").ap() for _ in range(2)]
    ar_ = [Reg() for _ in range(TT)]

    class Fresh(list):
        def __getitem__(self, i):
            return Reg()
    with ExitStack() as es:
        uT, uTr = phase_uT(g, es, l, src)
        rope = kb.sb(es, "ropeR", [128, 2, SEQ], F32)
        rope_r = Reg()
        kb.dma("sp", rope[:], I["k_rope_ret"].rearrange("t p n -> p t n"), [], [rope_r])
        wrot = kb.rot(es, "rw", [128, 8, 512], BF16, 2)
        wprot = kb.rot(es, "rwp", [128, 8, 128], BF16, 2)
        stg = [kb.rot(es, "rstg", [128, 512], F32, 2), kb.rot(es, "rstg", [128, 512], F32, 2),
               kb.rot(es, "rstgb", [128, 512], BF16, 4)]
        for h in range(8):
            proj_fm_rope(g, uT, uTr, I["ret_w_in"], h * 128, rope, rope_r, 128.0 ** -0.5, 32, retq[h], Reg(),
                         wrot, wprot, stg, 0)
            proj_fm_rope(g, uT, uTr, I["ret_w_in"], 1024 + h * 128, rope, rope_r, 1.0, 32, retk[h], Reg(),
                         wrot, wprot, stg, 0)
        for ch in range(4):
            proj_tm(g, uT, uTr, I["ret_w_in"], 2048 + ch * 512, 512, retv, Fresh(), ch * 512, wrot, stg[2], 4)
        for ch in range(8):
            proj_tm(g, uT, uTr, I["ret_w_in"], 4096 + ch * 512, 512, retg, Fresh(), ch * 512, wrot, stg[2], 4,
                    func=AF.Silu)
    kb.barrier()
    with ExitStack() as es:
        cr = Reg()
        tab = kb.sb(es, "rtab", [128, 5, 128], F32)
        kb.dma("sp", tab[:], I["k_ret_tab"].rearrange("t p n -> p t n"), [], [cr])
        lg = kb.sb(es, "rlg", [128, 16], F32)
        kb.dma("sp", lg[:], I["ret_decay_logit"][0, :].partition_broadcast(128), [], [cr])
        kb.act(lg[:], lg[:], AF.Exp, [cr], [cr], scale=-1.0)
        kb.ts("dve", lg[:], lg[:], 1.0, None, ALU.add, None, [cr], [cr])
        kb.act(lg[:], lg[:], AF.Ln, [cr], [cr])
        kb.ts("dve", lg[:], lg[:], -1.0, None, ALU.mult, None, [cr], [cr])
        htab = kb.sb(es, "rhtab", [128, 8, 4, 128], F32)
        hcol = kb.sb(es, "rhcol", [128, 8, 4], F32)
        for h in range(8):
            for ti, (src_i, lcol) in enumerate(((0, h), (1, 8 + h), (2, h), (3, 8 + h))):
                kb.act(htab[:, h, ti, :], tab[:, src_i, :], AF.Exp, [cr], [cr], scale=lg[:, lcol:lcol + 1])
            for ci, (src_c, lcol) in enumerate(((0, h), (1, 8 + h), (2, h), (2, 8 + h))):
                kb.act(hcol[:, h, ci:ci + 1], tab[:, 4, src_c:src_c + 1], AF.Exp, [cr], [cr],
                       scale=lg[:, lcol:lcol + 1])
        ws = NormWS(g, es)
        NU = 2
        Qrot = kb.rot(es, "rQ", [128, SEQ], BF16, NU)
        Krot = kb.rot(es, "rK", [128, SEQ], BF16, NU)
        Vrot = kb.rot(es, "rV", [128, 18, 256], BF16, NU)
        GFrot = kb.rot(es, "rGF", [128, 18, 256], BF16, NU)
        GBrot = kb.rot(es, "rGB", [128, 18, 256], BF16, NU)
        pre_all = [[kb.sb(es, "rpre", [128, 18, 128], BF16) for _ in range(6)] for _ in range(NU)]
        prer_all = [[[Reg() for _ in range(18)] for _ in range(6)] for _ in range(NU)]
        S_all = [[kb.sb(es, "rS", [128, 256], F32) for _ in range(2)] for _ in range(NU)]
        Sb_all = [[kb.sb(es, "rSb", [128, 256], BF16) for _ in range(2)] for _ in range(NU)]
        Sr_all = [[Reg(), Reg()] for _ in range(NU)]
        Sbr_all = [[Reg(), Reg()] for _ in range(NU)]
        arot = kb.rot(es, "ra", [128, 256], BF16, 8)
        P, PR = g.ps, g.psr

        def head_unit(b, h, slot):
            base = b * SEQ
            pre, prer = pre_all[slot], prer_all[slot]
            S, Sb, Sr, Sbr = S_all[slot], Sb_all[slot], Sr_all[slot], Sbr_all[slot]
            B0 = slot * 4
            Qt, Qr = Qrot.next()
            Kt, Kr = Krot.next()
            Vt, Vr = Vrot.next()
            GF, GFr = GFrot.next()
            GB, GBr = GBrot.next()
            kb.dma("sp", Qt[:], retq[h][:, base:base + SEQ], [], [Qr])
            kb.dma("sp", Kt[:], retk[h][:, base:base + SEQ], [], [Kr])
            kb.dma("sp", Vt[:], retv[base:base + SEQ, h * 256:(h + 1) * 256].rearrange("(t p) d -> p t d", p=128),
                   [], [Vr])
            kb.dma("sp", GF[:], retg[base:base + SEQ, h * 256:(h + 1) * 256].rearrange("(t p) d -> p t d", p=128),
                   [], [GFr])
            kb.dma("sp", GB[:], retg[base:base + SEQ, 2048 + h * 256:2048 + (h + 1) * 256]
                   .rearrange("(t p) d -> p t d", p=128), [], [GBr])
            for d_ in range(2):
                kb.memset("pool", S[d_][:], 0.0, [Sr[d_]])
                kb.memset("pool", Sb[d_][:], 0.0, [Sbr[d_]])
            yield
            for c0 in range(0, 18, 2):
                cs_ = [c0, c0 + 1]
                for j, c in enumerate(cs_):
                    cs = slice(c * 128, (c + 1) * 128)
                    kb.mm(P[B0][:, j * 128:(j + 1) * 128], [(Kt[:, cs], Qt[:, cs])], [Kr, Qr], [PR[B0]])
                    psk = P[B0 + 1][:].bitcast(BF16)[:, j * 128:(j + 1) * 128]
                    kb.tr([(psk, Kt[:, cs])], g.ident_b[:], [Kr, g.const_r], [PR[B0 + 1]])
                    kb.tt("pool", pre[2][:, c, :], Qt[:, cs], htab[:, h, 2, :], ALU.mult, [Qr, cr], [prer[2][c]])
                    kb.tt("pool", pre[3][:, c, :], Qt[:, cs], htab[:, h, 3, :], ALU.mult, [Qr, cr], [prer[3][c]])
                yield
                for j, c in enumerate(cs_):
                    pss = P[B0][:, j * 128:(j + 1) * 128]
                    psk = P[B0 + 1][:].bitcast(BF16)[:, j * 128:(j + 1) * 128]
                    kb.tt("dve", pre[0][:, c, :], pss, htab[:, h, 0, :], ALU.mult, [PR[B0], cr], [prer[0][c]])
                    kb.tt("dve", pre[1][:, c, :], pss, htab[:, h, 1, :], ALU.mult, [PR[B0], cr], [prer[1][c]])
                    kb.act(pre[4][:, c, :], psk, AF.Copy, [PR[B0 + 1], cr], [prer[4][c]], scale=hcol[:, h, 0:1])
                    kb.act(pre[5][:, c, :], psk, AF.Copy, [PR[B0 + 1], cr], [prer[5][c]], scale=hcol[:, h, 1:2])
                yield
            orders = [list(range(18)), [1, 0] + list(range(17, 1, -1))]
            for step in range(18):
                sts = [None, None]
                for d_ in range(2):
                    c = orders[d_][step]
                    po = P[B0 + 2 + d_][:, 0:256]
                    pS = P[B0 + d_][:, 0:256]
                    kb.mm(po, [(pre[d_][:, c, :], Vt[:, c, :]), (pre[2 + d_][:, c, :], Sb[d_][:])],
                          [prer[d_][c], prer[2 + d_][c], Vr, Sbr[d_]], [PR[B0 + 2 + d_]])
                    kb.mm(pS, [(pre[4 + d_][:, c, :], Vt[:, c, :])], [prer[4 + d_][c], Vr], [PR[B0 + d_]])
                yield
                outs = [[], []]
                gens = []
                for d_ in range(2):
                    pS = P[B0 + d_][:, 0:256]
                    po = P[B0 + 2 + d_][:, 0:256]
                    kb.stt("dve", S[d_][:], S[d_][:], hcol[:, h, 2 + d_:3 + d_], pS, ALU.mult, ALU.add,
                           [Sr[d_], PR[B0 + d_], cr], [Sr[d_]])
                    gens.append(rms_stats_gen(g, ws, [po], [PR[B0 + 2 + d_]], outs[d_], 1.0 / 16.0))
                yield
                for d_ in range(2):
                    kb.cp("act", Sb[d_][:], S[d_][:], [Sr[d_]], [Sbr[d_]])
                alive = True
                while alive:
                    alive = False
                    for gn_ in gens:
                        try:
                            next(gn_)
                            alive = True
                        except StopIteration:
                            pass
                    if alive:
                        yield
                for d_ in range(2):
                    c = orders[d_][step]
                    po = P[B0 + 2 + d_][:, 0:256]
                    st, sr = outs[d_][0]
                    gate = GF if d_ == 0 else GB
                    a_, a_r = arot.next()
                    kb.stt("dve", a_[:], po, st[:, 2:3], gate[:, c, :], ALU.mult, ALU.mult,
                           [PR[B0 + 2 + d_], sr, GFr if d_ == 0 else GBr], [a_r])
                    tt = b * NT + c
                    kb.dma("sp", reta[d_][tt * 128:(tt + 1) * 128, h * 256:(h + 1) * 256], a_[:], [a_r], [Reg()])
                yield
        interleave((head_unit(b, h, i % NU) for i, (b, h) in enumerate((b, h) for b in range(NB) for h in range(8))), NU)
    kb.barrier()
    phase_outproj(g, l, reta, ar_, 2048, I["ret_w_out"], src, dst, need_ctx)


def interleave(gens, width):
    gens = iter(gens)
    active = []
    done = False
    while True:
        while not done and len(active) < width:
            try:
                active.append(next(gens))
            except StopIteration:
                done = True
        if not active:
            break
        nxt = []
        for gen in active:
            try:
                next(gen)
                nxt.append(gen)
            except StopIteration:
                pass
        active = nxt


def mixer_dn(g, l, src, dst, need_ctx, hook=None):
    kb, nc, I = g.kb, g.nc, g.I
    NTOK = TT * 128
    NCH = NB * 2 * 36

    def dt_(name, shape, dt):
        if g.dbg and name in ("dnq", "dnk", "dnvt", "dnba", "dnkt"):
            return nc.dram_tensor("dbg_" + name, list(shape), dt, kind="ExternalOutput").ap()
        return nc.dram_tensor(kb.name(name), list(shape), dt, kind="Internal").ap()
    dnq = dt_("dnq", [8, 128, NTOK], BF16)
    dnk = dt_("dnk", [8, 128, NTOK], BF16)
    dnkt = dt_("dnkt", [NTOK, D], BF16)
    dnvt = dt_("dnvt", [NTOK, D], BF16)
    dnba = dt_("dnba", [NTOK, 32], F32)
    dng = dt_("dng", [NTOK, 2048], BF16)
    dna = [dt_("dna", [NTOK, D], BF16) for _ in range(2)]
    pu = dt_("dpu", [NCH, 64, 1024], F32)
    pw = dt_("dpw", [NCH, 128, 512], BF16)
    pat = dt_("dpat", [NCH, 64, 512], BF16)
    pqg = dt_("dpqg", [NCH, 128, 512], BF16)
    pkg = dt_("dpkg", [NCH, 64, 1024], BF16)
    pgl = dt_("dpgl", [NCH, 128, 8], F32)
    ar_ = [Reg() for _ in range(TT)]

    class Fresh(list):
        def __getitem__(self, i):
            return Reg()
    NX = 2308
    BLK = ((0, 512), (512, 512), (1024, 512), (1536, 512), (2048, 260))
    with ExitStack() as es:
        uT, uTr = phase_uT(g, es, l, src)
        cr = Reg()
        cw = kb.sb(es, "dcw", [128, 24, 5], F32)
        for k in range(5):
            kb.dma("sp", cw[:, :, k], I["dn_conv_w"][k, :].rearrange("(c p) -> p c", p=128), [], [cr],
                   allow_slow_non_contiguous=True)
        ones_b = kb.sb(es, "donesb", [128, 128], BF16)
        kb.memset("dve", ones_b[:], 1.0, [cr])
        wrot = kb.rot(es, "dw", [128, 8, 512], BF16, 3)
        XDT = F32 if os.environ.get("DN_XF32", "1") == "1" else BF16
        dgrot = kb.rot(es, "ddg", [128, 5, 128], XDT, 3)
        Xrot = kb.rot(es, "dX", [128, 2320], XDT, 3 if XDT == BF16 else 2)
        for xt_, xr_ in Xrot.items:
            kb.memset("pool", xt_[:, 0:2], 0.0, [xr_])
            kb.memset("pool", xt_[:, 258:262], 0.0, [xr_])
            kb.memset("pool", xt_[:, 2310:2320], 0.0, [xr_])
        Srot = kb.rot(es, "dsil", [128, NX], F32, 3)
        sqrot = kb.rot(es, "dsq", [128, 512], BF16, 4)
        sdrot = kb.rot(es, "dsd", [128, 512], F32, 4)
        QNrot = kb.rot(es, "dQN", [128, NX], BF16, 3)
        tstg = kb.rot(es, "dts", [128, 8, 128], BF16, 3)

        def inproj_unit(cb, b, slot, wt, wr, dg, dgr):
            kind, h = cb // 8, cb % 8
            base = b * SEQ
            pbs = [slot * 3, slot * 3 + 1]
            ptb = slot * 3 + 2
            X, Xr = Xrot.next()
            for bi, (o, n) in enumerate(((0, 512), (512, 512), (1024, 512), (1536, 512), (2048, 256))):
                t0 = base + o
                tiles = list(range(t0 // 128, (t0 + n) // 128))
                pb = pbs[bi % 2]
                kb.mm(g.ps[pb][:, 0:n], [(wt[:, kc, 0:128], uT[:, kc, t0:t0 + n]) for kc in range(8)],
                      [uTr[t] for t in tiles] + [wr], [g.psr[pb]])
                if o == 0:
                    kb.cp("act", X[:, 2:258], g.ps[pb][:, 0:256], [g.psr[pb]], [Xr])
                    kb.cp("act", X[:, 262:518], g.ps[pb][:, 256:512], [g.psr[pb]], [Xr])
                else:
                    kb.cp("act", X[:, 6 + o:6 + o + n], g.ps[pb][:, 0:n], [g.psr[pb]], [Xr])
                yield
            Ssil, Ssr = Srot.next()
            for bi, (o, n) in enumerate(BLK):
                pb = pbs[(bi + 1) % 2]
                kb.mm(g.ps[pb][:, 0:n], [(dg[:, k, :], X[:, o + k:o + k + n]) for k in range(5)], [Xr, dgr],
                      [g.psr[pb]])
                kb.act(Ssil[:, o:o + n], g.ps[pb][:, 0:n], AF.Silu, [g.psr[pb]], [Ssr])
                yield
            QN, QNr = QNrot.next()
            if kind < 2:
                for bi, (o, n) in enumerate(BLK):
                    pb = pbs[bi % 2]
                    sq, sqr = sqrot.next()
                    kb.act(sq[:, 0:n], Ssil[:, o:o + n], AF.Square, [Ssr], [sqr])
                    kb.mm(g.ps[pb][:, 0:n], [(ones_b[:], sq[:, 0:n])], [sqr, cr], [g.psr[pb]])
                    sd, sdr = sdrot.next()
                    kb.act(sd[:, 0:n], g.ps[pb][:, 0:n], AF.Sqrt, [g.psr[pb], g.const_r], [sdr],
                           bias=(g.eps128 if kind == 0 else g.eps)[:, 0:1], scale=128.0 if kind == 0 else 1.0)
                    kb.recip(sd[:, 0:n], sd[:, 0:n], [sdr], [sdr])
                    kb.tt("pool", QN[:, o:o + n], Ssil[:, o:o + n], sd[:, 0:n], ALU.mult, [Ssr, sdr], [QNr])
                    yield
                dstq = dnq if kind == 0 else dnk
                kb.dma("sp", dstq[h][:, base:base + TC], QN[:, 0:TC], [QNr], [Reg()])
                kb.dma("sp", dstq[h][:, base + TC:base + SEQ], QN[:, 260:NX], [QNr], [Reg()])
            else:
                kb.cp("pool", QN[:], Ssil[:], [Ssr], [QNr])
                yield
            if kind >= 1:
                dstt = dnkt if kind == 1 else dnvt
                for t0 in range(0, NT, 8):
                    ts_ = list(range(t0, min(NT, t0 + 8)))
                    psT = g.ps[ptb][:].bitcast(BF16).rearrange("p (c t) -> p c t", c=8)
                    kb.tr([(psT[:, j, :], QN[:, (t * 128 if t < 2 else t * 128 + 4):(t * 128 if t < 2 else t * 128 + 4) + 128])
                           for j, t in enumerate(ts_)], g.ident_b[:], [QNr, g.const_r], [g.psr[ptb]])
                    stt_, str_ = tstg.next()
                    kb.cp("act", stt_[:, 0:len(ts_), :], psT[:, 0:len(ts_), :], [g.psr[ptb]], [str_])
                    r0 = base + t0 * 128
                    kb.dma("sp", dstt[r0:r0 + len(ts_) * 128, h * 128:(h + 1) * 128]
                           .rearrange("(t p) d -> p t d", p=128), stt_[:, 0:len(ts_), :], [str_], [Reg()])
                    yield

        def inproj_units():
            i = 0
            for cb in range(24):
                wt, wr = load_w_bf16(g, wrot, I["dn_w_in"], cb * 128, 128)
                dg, dgr = dgrot.next()
                for k in range(5):
                    kb.ts("pool", dg[:, k, :], g.ident_f[:], cw[:, cb, k:k + 1], None, ALU.mult, None,
                          [g.const_r, cr], [dgr])
                for b in range(NB):
                    yield inproj_unit(cb, b, i % 2, wt, wr, dg, dgr)
                    i += 1
        interleave(inproj_units(), 2)
        abr = Reg()
        dtb = kb.sb(es, "ddtb", [128, 16], F32)
        nea = kb.sb(es, "dnea", [128, 16], F32)
        kb.dma("sp", dtb[:], I["dn_dt_bias"][0, :].partition_broadcast(128), [], [abr])
        kb.dma("sp", nea[:], I["dn_a_log"][0, :].partition_broadcast(128), [], [abr])
        kb.act(nea[:], nea[:], AF.Exp, [abr], [abr])
        kb.ts("dve", nea[:], nea[:], -1.0, None, ALU.mult, None, [abr], [abr])
        wt, wr = load_w_bf16(g, wrot, I["dn_w_in"], 3072, 32)
        bstg = kb.rot(es, "dbs", [128, 32], F32, 4)
        btmp = kb.rot(es, "dbt", [128, 16], F32, 4)

        def ba_unit(tt):
            pb = 6 + tt % 2
            off = (tt // 2) % 8 * 32
            pp = g.ps[pb][:, off:off + 32]
            kb.mm(pp, [(uT[:, kc, tt * 128:(tt + 1) * 128], wt[:, kc, 0:32]) for kc in range(8)],
                  [uTr[tt], wr], [g.psr[pb]])
            bs, bsr = bstg.next()
            bt, btr = btmp.next()
            yield
            kb.act(bs[:, 0:16], pp[:, 0:16], AF.Sigmoid, [g.psr[pb]], [bsr])
            kb.tt("dve", bt[:], pp[:, 16:32], dtb[:], ALU.add, [g.psr[pb], abr], [btr])
            yield
            kb.act(bt[:], bt[:], AF.Exp, [btr], [btr])
            yield
            kb.ts("dve", bt[:], bt[:], 1.0, None, ALU.add, None, [btr], [btr])
            yield
            kb.act(bt[:], bt[:], AF.Ln, [btr], [btr])
            yield
            kb.tt("dve", bs[:, 16:32], bt[:], nea[:], ALU.mult, [btr, abr], [bsr])
            kb.dma("sp", dnba[tt * 128:(tt + 1) * 128, :], bs[:], [bsr], [Reg()])
        interleave((ba_unit(tt) for tt in range(TT)), 4)
        stgb = kb.rot(es, "dgs", [128, 512], BF16, 4)
        tiles = [tt for tt in range(TT) if need_ctx or (tt % NT) >= 2]
        for ch in range(4):
            proj_tm(g, uT, uTr, I["dn_w_in"], 3104 + ch * 512, 512, dng, Fresh(), ch * 512, wrot, stgb, 0,
                    func=AF.Silu, tiles=tiles)
    kb.barrier()
    bc64 = lambda ap: ap.unsqueeze(2).broadcast_to([64, 8, 64])
    bc128 = lambda ap: ap.unsqueeze(2).broadcast_to([64, 8, 128])

    def chunk_id(b, d_, c):
        return (b * 2 + d_) * 36 + c
    with ExitStack() as es:
        cr = Reg()
        ktab = kb.sb(es, "dktab", [64, 4, 64], F32)
        kb.dma("sp", ktab[:], I["k_dn"].rearrange("t p n -> p t n"), [], [cr])
        ones3 = kb.sb(es, "dones3", [64, 128], F32)
        kb.memset("dve", ones3[:], 1.0, [cr])
        identf = g.ident_f
        identb = g.ident_b
        NW = 2

        def R1(shape, dt, name, n=NW + 1):
            return kb.rot(es, name, shape, dt, n)
        kTr_, qTr_ = R1([128, 8, 64], BF16, "dkT"), R1([128, 8, 64], BF16, "dqT")
        ktr_, vtr_ = R1([64, 8, 128], BF16, "dkt"), R1([64, 8, 128], BF16, "dvt")
        bar_ = R1([64, 32], F32, "dba")
        LBn_ = R1([64, 8, 128], F32, "dLBn")
        sm_ = R1([128, 32], F32, "dsm")
        gdm_ = R1([64, 8, 64], F32, "dgdm")
        t12_ = R1([64, 2, 8, 64], F32, "dt12")
        Dms_, DTi_ = R1([64, 8, 64], F32, "dDms"), R1([64, 8, 64], F32, "dDTi")
        Egr_ = R1([128, 8, 64], F32, "dEgr")
        qgT_ = R1([128, 8, 64], BF16, "dqgT")
        INV_F32 = os.environ.get("DN_INV_F32", "1") == "1"
        IDT = F32 if INV_F32 else BF16
        A_ = R1([64, 8, 64], IDT, "dA", 2 * NW + 2)
        B_ = R1([64, 8, 64], IDT, "dB", 2 * NW + 2)
        Af_ = R1([64, 8, 64], F32, "dAf")
        Tt_ = R1([64, 8, 64], F32, "dTt")
        Ttb_ = R1([64, 8, 64], IDT, "dTtb", 2 * NW + 2)
        Tt16_ = R1([64, 8, 64], BF16, "dTt16")
        attnT_ = R1([64, 8, 64], BF16, "dattn")
        rv_, kbg_, kg_ = R1([64, 8, 128], BF16, "drv"), R1([64, 8, 128], BF16, "dkbg"), R1([64, 8, 128], BF16, "dkg")
        u_ = R1([64, 8, 128], F32, "du")
        wT_ = R1([128, 8, 64], BF16, "dwT")
        gls_ = R1([128, 8], F32, "dgls")
        P, PR = g.ps, g.psr

        def pre_unit(b, d_, c, slot):
            cid = chunk_id(b, d_, c)
            tok0 = b * SEQ + c * 64
            Pa, Pb, Pc, Pd = (slot * 4 + i for i in range(4))
            Tri = ktab[:, d_, :]
            strict = ktab[:, 2 + d_, :]
            kT, kTr = kTr_.next()
            qT, qTr = qTr_.next()
            kt, ktr = ktr_.next()
            vt, vtr = vtr_.next()
            ba, bar = bar_.next()
            kb.dma("sp", kT[:], dnk[:, :, tok0:tok0 + 64].rearrange("h p n -> p h n"), [], [kTr])
            kb.dma("sp", qT[:], dnq[:, :, tok0:tok0 + 64].rearrange("h p n -> p h n"), [], [qTr])
            kb.dma("sp", kt[:], dnkt[tok0:tok0 + 64, :].rearrange("p (h d) -> p h d", h=8), [], [ktr])
            kb.dma("sp", vt[:], dnvt[tok0:tok0 + 64, :].rearrange("p (h d) -> p h d", h=8), [], [vtr])
            kb.dma("sp", ba[:], dnba[tok0:tok0 + 64, :], [], [bar])
            beta = ba[:, 8 * d_:8 * d_ + 8]
            la = ba[:, 16 + 8 * d_:24 + 8 * d_]
            yield
            LBn, LBr = LBn_.next()
            kb.ts("dve", LBn[:], bc128(la), -1.0, None, ALU.mult, None, [bar], [LBr])
            kb.mm(P[Pd][0:64, 0:8], [(Tri, la)], [cr, bar], [PR[Pd]])
            kb.mm(P[Pd][:, 8:16], [(ones3[:, :], la)], [cr, bar], [PR[Pd]])
            for h in range(8):
                kb.mm(P[Pa][0:64, h * 64:(h + 1) * 64], [(kT[:, h, :], kT[:, h, :])], [kTr], [PR[Pa]])
            for h in range(8):
                kb.mm(P[Pb][0:64, h * 64:(h + 1) * 64], [(kT[:, h, :], qT[:, h, :])], [kTr, qTr], [PR[Pb]])
            yield
            for h in range(8):
                kb.mm(P[Pc][:, h * 64:(h + 1) * 64], [(LBn[:, h, :], Tri)], [LBr, cr], [PR[Pc]])
            sm, smr = sm_.next()
            kb.cp("dve", sm[0:64, 0:8], P[Pd][0:64, 0:8], [PR[Pd]], [smr])
            gc = sm[0:64, 0:8]
            gls, glsr = gls_.next()
            kb.act(gls[:], P[Pd][:, 8:16], AF.Exp, [PR[Pd]], [glsr])
            kb.dma("sp", pgl[cid], gls[:], [glsr], [Reg()])
            yield
            gdm, gdmr = gdm_.next()
            GR = P[Pc][:].rearrange("p (h n) -> p h n", h=8)
            kb.tt("dve", gdm[:], GR[0:64], bc64(gc), ALU.add, [PR[Pc], smr], [gdmr])
            Egr, Egrr = Egr_.next()
            kb.act(Egr[:], GR, AF.Exp, [PR[Pc]], [Egrr], scale=-1.0)
            kb.act(sm[0:64, 8:16], gc, AF.Exp, [smr], [smr])
            kb.tt("dve", sm[0:64, 16:24], P[Pd][0:64, 8:16], gc, ALU.subtract, [PR[Pd], smr], [smr])
            yield
            t12, t12r = t12_.next()
            kb.ts("dve", t12[:, 0], gdm[:], 0.0, None, ALU.min, None, [gdmr], [t12r])
            kb.ts("dve", t12[:, 1], gdm[:], -1.0, 0.0, ALU.mult, ALU.min, [gdmr], [t12r])
            qgT, qgTr = qgT_.next()
            kb.tt("pool", qgT[:], qT[:], Egr[:], ALU.mult, [qTr, Egrr], [qgTr])
            kb.dma("sp", pqg[cid].rearrange("p (h n) -> p h n", h=8), qgT[:], [qgTr], [Reg()])
            kb.tt("dve", sm[0:64, 8:16], sm[0:64, 8:16], beta, ALU.mult, [smr, bar], [smr])
            kb.act(sm[0:64, 16:24], sm[0:64, 16:24], AF.Exp, [smr], [smr])
            yield
            kb.act(t12[:], t12[:], AF.Exp, [t12r], [t12r])
            rv, rvr = rv_.next()
            kbg, kbgr = kbg_.next()
            kg, kgr = kg_.next()
            kb.tt("pool", rv[:], vt[:], bc128(beta), ALU.mult, [vtr, bar], [rvr])
            kb.tt("pool", kbg[:], kt[:], bc128(sm[0:64, 8:16]), ALU.mult, [ktr, smr], [kbgr])
            kb.tt("pool", kg[:], kt[:], bc128(sm[0:64, 16:24]), ALU.mult, [ktr, smr], [kgr])
            kb.dma("sp", pkg[cid].rearrange("p (h n) -> p h n", h=8), kg[:], [kgr], [Reg()])
            yield
            Dms, Dmsr = Dms_.next()
            DTi, DTir = DTi_.next()
            kb.tt("pool", Dms[:], t12[:, 0], strict.unsqueeze(1).broadcast_to([64, 8, 64]), ALU.mult,
                  [t12r, cr], [Dmsr])
            kb.tt("pool", DTi[:], t12[:, 1], Tri.unsqueeze(1).broadcast_to([64, 8, 64]), ALU.mult,
                  [t12r, cr], [DTir])
            KK = P[Pa][0:64, :].rearrange("p (h n) -> p h n", h=8)
            QK = P[Pb][0:64, :].rearrange("p (h n) -> p h n", h=8)
            Af, Afr = Af_.next()
            kb.tt("dve", Af[:], KK, bc64(beta), ALU.mult, [PR[Pa], bar], [Afr])
            yield
            A0, A0r = A_.next()
            kb.tt("dve", A0[:], Af[:], Dms[:], ALU.mult, [Afr, Dmsr], [A0r])
            attnT, attnr = attnT_.next()
            kb.tt("dve", attnT[:], QK, DTi[:], ALU.mult, [PR[Pb], DTir], [attnr])
            kb.dma("sp", pat[cid].rearrange("p (h n) -> p h n", h=8), attnT[:], [attnr], [Reg()])
            yield
            if INV_F32:
                KKb = P[Pa][0:64, :].rearrange("p (h n) -> p h n", h=8)
            else:
                KKb = P[Pa][0:64, :].bitcast(BF16)[:, 0:512].rearrange("p (h n) -> p h n", h=8)
            kb.tr([(KKb[:, h, :], A0[:, h, :]) for h in range(8)], (identf if INV_F32 else identb)[0:64, 0:64],
                  [A0r, g.const_r], [PR[Pa]])
            yield
            B0, B0r = B_.next()
            kb.cp("act", B0[:], KKb, [PR[Pa]], [B0r])
            Tt, Ttr = Tt_.next()
            kb.tt("dve", Tt[:], identf[0:64, 0:64].unsqueeze(1).broadcast_to([64, 8, 64]), KKb,
                  ALU.subtract, [g.const_r, PR[Pa]], [Ttr])
            Ttb, Ttbr = Ttb_.next()
            kb.cp("pool", Ttb[:], Tt[:], [Ttr], [Ttbr])
            yield
            Ak, Akr, Bk, Bkr = A0, A0r, B0, B0r

            def sq_mm(Ak, Akr, Bk, Bkr, need_b):
                for h in range(8):
                    kb.mm(P[Pa][0:64, h * 64:(h + 1) * 64], [(Bk[:, h, :], Ak[:, h, :])], [Akr, Bkr], [PR[Pa]])
                if need_b:
                    for h in range(8):
                        kb.mm(P[Pb][0:64, h * 64:(h + 1) * 64], [(Ak[:, h, :], Bk[:, h, :])], [Akr, Bkr], [PR[Pb]])
            sq_mm(Ak, Akr, Bk, Bkr, True)
            yield
            An, Anr = A_.next()
            Bn, Bnr = B_.next()
            kb.cp("act", An[:], KK, [PR[Pa]], [Anr])
            kb.cp("pool" if False else "dve", Bn[:], QK, [PR[Pb]], [Bnr])
            yield
            for lev in range(5):
                for h in range(8):
                    kb.mm(P[Pd][0:64, h * 64:(h + 1) * 64], [(An[:, h, :], Ttb[:, h, :])], [Anr, Ttbr], [PR[Pd]])
                if lev < 4:
                    sq_mm(An, Anr, Bn, Bnr, lev < 3)
                yield
                kb.tt("dve", Tt[:], Tt[:], P[Pd][0:64, :].rearrange("p (h n) -> p h n", h=8), ALU.add,
                      [Ttr, PR[Pd]], [Ttr])
                if lev < 4:
                    An2, An2r = A_.next()
                    kb.cp("act", An2[:], KK, [PR[Pa]], [An2r])
                    if lev < 3:
                        Bn2, Bn2r = B_.next()
                        kb.cp("pool" if False else "dve", Bn2[:], QK, [PR[Pb]], [Bn2r])
                        Bn, Bnr = Bn2, Bn2r
                    An, Anr = An2, An2r
                yield
                if lev < 4:
                    Ttb, Ttbr = Ttb_.next()
                    kb.cp("act", Ttb[:], Tt[:], [Ttr], [Ttbr])
                    yield
            u, ur = u_.next()
            wT, wTr = wT_.next()
            if INV_F32:
                Ttb, Ttbr = Tt16_.next()
                kb.cp("act", Ttb[:], Tt[:], [Ttr], [Ttbr])
                yield
            for hb in range(2):
                pp = (Pa, Pb)[hb]
                for hh in range(4):
                    h = hb * 4 + hh
                    kb.mm(P[pp][0:64, hh * 128:(hh + 1) * 128], [(Ttb[:, h, :], rv[:, h, :])],
                          [Ttbr, rvr], [PR[pp]])
            for h in range(8):
                kb.mm(P[Pc][:, h * 64:(h + 1) * 64], [(kbg[:, h, :], Ttb[:, h, :])], [kbgr, Ttbr], [PR[Pc]])
            yield
            for hb in range(2):
                pp = (Pa, Pb)[hb]
                kb.cp("act" if hb == 0 else "dve", u[:, hb * 4:(hb + 1) * 4, :],
                      P[pp][0:64, :].rearrange("p (h n) -> p h n", h=4), [PR[pp]], [ur])
            kb.cp("act", wT[:], GR, [PR[Pc]], [wTr])
            yield
            kb.dma("sp", pu[cid].rearrange("p (h n) -> p h n", h=8), u[:], [ur], [Reg()])
            kb.dma("sp", pw[cid].rearrange("p (h n) -> p h n", h=8), wT[:], [wTr], [Reg()])

        units = []
        for b in range(NB):
            for d_ in range(2):
                for c in range(36):
                    units.append((b, d_, c))
        if int(os.environ.get("DNSTOP", "9")) >= 2:
            interleave((pre_unit(b, d_, c, i % NW) for i, (b, d_, c) in enumerate(units)), int(os.environ.get("DN_NW", NW)))
    kb.barrier()
    with ExitStack() as es:
        cr = Reg()
        ng = kb.sb(es, "dng_", [64, 128], F32)
        kb.dma("sp", ng[:], I["dn_norm_g"][0, :].partition_broadcast(64), [], [cr])
        P, PR = g.ps, g.psr
        NC_ = 4

        def R2(shape, dt, name, n=2):
            return [kb.rot(es, name, shape, dt, n) for _ in range(NC_)]
        u_, wT_ = R2([64, 8, 128], F32, "eu"), R2([128, 8, 64], BF16, "ewT")
        at_, qg_ = R2([64, 8, 64], BF16, "eat"), R2([128, 8, 64], BF16, "eqg")
        kg_, gl_ = R2([64, 8, 128], BF16, "ekg"), R2([128, 8], F32, "egl")
        gt_ = R2([64, 8, 128], BF16, "egt")
        vn_ = R2([64, 8, 128], BF16, "evn", 1)
        osb_ = R2([64, 8, 128], F32, "eosb", 1)
        sq_ = R2([64, 8, 128], F32, "esq", 1)
        gn_ = R2([64, 8, 128], F32, "egn", 1)
        oa_ = R2([64, 8, 128], BF16, "eoa", 2)
        sm_ = R2([64, 32], F32, "esm", 2)
        print("sbuf remaining (dn rec)", nc.sbuf_bytes_remaining)
        S = [kb.sb(es, "dS", [128, 8, 128], F32) for _ in range(NC_)]
        Sb = [kb.sb(es, "dSb", [128, 8, 128], BF16) for _ in range(NC_)]
        Sr = [Reg() for _ in range(NC_)]
        Sbr = [Reg() for _ in range(NC_)]

        def rec_chain(b, d_, ch):
            order = list(range(36)) if d_ == 0 else [3, 2, 1, 0] + list(range(35, 3, -1))
            Ra, Rb = ch * 2, ch * 2 + 1
            kb.memset("pool", S[ch][:], 0.0, [Sr[ch]])
            kb.memset("pool", Sb[ch][:], 0.0, [Sbr[ch]])
            loaded = {}

            def load(c):
                cid = chunk_id(b, d_, c)
                tok0 = b * SEQ + c * 64
                emit = need_ctx or c >= 4
                r = {}
                for nm, rot_, srcap in (("u", u_[ch], pu[cid]), ("wT", wT_[ch], pw[cid]), ("at", at_[ch], pat[cid]),
                                        ("qg", qg_[ch], pqg[cid]), ("kg", kg_[ch], pkg[cid])):
                    if nm in ("at", "qg") and not emit:
                        continue
                    t_, tr_ = rot_.next()
                    kb.dma("sp", t_[:], srcap.rearrange("p (h n) -> p h n", h=8), [], [tr_])
                    r[nm] = (t_, tr_)
                t_, tr_ = gl_[ch].next()
                kb.dma("sp", t_[:], pgl[cid], [], [tr_])
                r["gl"] = (t_, tr_)
                if emit:
                    t_, tr_ = gt_[ch].next()
                    kb.dma("sp", t_[:], dng[tok0:tok0 + 64, d_ * 1024:(d_ + 1) * 1024]
                           .rearrange("p (h d) -> p h d", h=8), [], [tr_])
                    r["gt"] = (t_, tr_)
                return r
            loaded[0] = load(order[0])
            for step, c in enumerate(order):
                if step + 1 < 36:
                    loaded[step + 1] = load(order[step + 1])
                L_ = loaded.pop(step)
                tok0 = b * SEQ + c * 64
                emit = need_ctx or c >= 4
                (u, ur), (wT, wTr), (kg, kgr), (gl, glr) = L_["u"], L_["wT"], L_["kg"], L_["gl"]
                for hb in range(2):
                    pp = (Ra, Rb)[hb]
                    for hh in range(4):
                        h = hb * 4 + hh
                        kb.mm(P[pp][0:64, hh * 128:(hh + 1) * 128], [(wT[:, h, :], Sb[ch][:, h, :])],
                              [wTr, Sbr[ch]], [PR[pp]])
                yield
                vn, vnr = vn_[ch].next()
                for hb in range(2):
                    pp = (Ra, Rb)[hb]
                    kb.tt("dve", vn[:, hb * 4:(hb + 1) * 4, :], u[:, hb * 4:(hb + 1) * 4, :],
                          P[pp][0:64, :].rearrange("p (h n) -> p h n", h=4), ALU.subtract, [ur, PR[pp]], [vnr])
                if emit:
                    gn, gnr = gn_[ch].next()
                    kb.tt("pool", gn[:], L_["gt"][0][:], ng[:, :].unsqueeze(1).broadcast_to([64, 8, 128]), ALU.mult,
                          [L_["gt"][1], cr], [gnr])
                yield
                if emit:
                    (at, atr), (qg, qgr) = L_["at"], L_["qg"]
                    for hb in range(2):
                        pp = (Ra, Rb)[hb]
                        for hh in range(4):
                            h = hb * 4 + hh
                            kb.mm(P[pp][0:64, hh * 128:(hh + 1) * 128],
                                  [(qg[:, h, :], Sb[ch][:, h, :]), (at[:, h, :], vn[:, h, :])],
                                  [qgr, Sbr[ch], atr, vnr], [PR[pp]])
                    yield
                    osb, osr = osb_[ch].next()
                    kb.cp("act", osb[:, 0:4, :], P[Ra][0:64, :].rearrange("p (h n) -> p h n", h=4), [PR[Ra]], [osr])
                    kb.cp("dve", osb[:, 4:8, :], P[Rb][0:64, :].rearrange("p (h n) -> p h n", h=4), [PR[Rb]], [osr])
                    yield
                for hb in range(2):
                    pp = (Ra, Rb)[hb]
                    for hh in range(4):
                        h = hb * 4 + hh
                        kb.mm(P[pp][:, hh * 128:(hh + 1) * 128], [(kg[:, h, :], vn[:, h, :])], [kgr, vnr], [PR[pp]])
                kb.tt("pool", S[ch][:], S[ch][:], gl[:, :].unsqueeze(2).broadcast_to([128, 8, 128]), ALU.mult,
                      [Sr[ch], glr], [Sr[ch]])
                yield
                for hb in range(2):
                    pp = (Ra, Rb)[hb]
                    kb.tt("dve", S[ch][:, hb * 4:(hb + 1) * 4, :], S[ch][:, hb * 4:(hb + 1) * 4, :],
                          P[pp][:, :].rearrange("p (h n) -> p h n", h=4), ALU.add, [Sr[ch], PR[pp]], [Sr[ch]])
                yield
                kb.cp("act", Sb[ch][:], S[ch][:], [Sr[ch]], [Sbr[ch]])
                if emit:
                    sq, sqr = sq_[ch].next()
                    kb.act(sq[:], osb[:], AF.Square, [osr], [sqr])
                    yield
                    sm, smr = sm_[ch].next()
                    kb.op("dve", lambda hd, o=sm[:, 0:8], i=sq[:]: hd.reduce_sum(out=o, in_=i, axis=AX.X),
                          [sqr], [smr])
                    yield
                    kb.act(sm[:, 8:16], sm[:, 0:8], AF.Sqrt, [smr, g.const_r], [smr], bias=g.eps[0:64, 0:1],
                           scale=1.0 / 128.0)
                    yield
                    kb.recip(sm[:, 16:24], sm[:, 8:16], [smr], [smr])
                    yield
                    kb.tt("dve", osb[:], osb[:], sm[:, 16:24].unsqueeze(2).broadcast_to([64, 8, 128]), ALU.mult,
                          [osr, smr], [osr])
                    yield
                    oa, oar = oa_[ch].next()
                    kb.tt("pool", oa[:], osb[:], gn[:], ALU.mult, [osr, gnr], [oar])
                    kb.dma("sp", dna[d_][tok0:tok0 + 64, :].rearrange("p (h d) -> p h d", h=8), oa[:], [oar], [Reg()])
                yield
        if int(os.environ.get("DNSTOP", "9")) >= 3:
            interleave((rec_chain(b, d_, b * 2 + d_) for b in range(NB) for d_ in range(2)), int(os.environ.get("DN_NC", NC_)))
    kb.barrier()
    if hook is not None:
        hook()
    phase_outproj(g, l, dna, ar_, D, I["dn_w_out"], src, dst, need_ctx)


def host_consts():
    c = {}
    c["k_ident"] = np.eye(128, dtype=np.float32)
    mats = np.zeros((20, 128, 128), np.float32)
    for wi, w in enumerate((2, 4, 8, 16)):
        lo = w // 2
        hi = w - 1 - lo
        n = 384
        for var, (t0, nseq_lo, nseq_hi) in enumerate(((128, 0, 384), (0, 0, 384), (256, 0, 384))):
            pass
        A = np.zeros((n, n), np.float32)
        for t in range(n):
            a, b_ = max(0, t - lo), min(n, t + hi + 1)
            A[t, a:b_] = 1.0 / (b_ - a)
        M = A - np.eye(n, dtype=np.float32)
        MT = M.T
        mats[wi * 5 + 0] = MT[128:256, 128:256]
        mats[wi * 5 + 1] = MT[0:128, 0:128]
        mats[wi * 5 + 2] = MT[256:384, 256:384]
        mats[wi * 5 + 3] = MT[0:128, 128:256]
        mats[wi * 5 + 4] = MT[256:384, 128:256]
    c["k_pool"] = mats
    pos = np.arange(T)
    row, col = pos // 64, pos % 64

    def rope_tab(dh, nrep):
        q = dh // 4
        tab = np.zeros((2, dh, SEQ), np.float64)
        tab[0, :, :TC] = 1.0
        for d in range(dh):
            blk, i = d // q, d % q
            inv = 10000.0 ** (-i / q)
            p_ = row if blk < 2 else col
            ang = (p_.astype(np.float32) * np.float32(inv)).astype(np.float64)
            tab[0, d, TC:] = np.cos(ang)
            tab[1, d, TC:] = np.sin(ang) * (-1.0 if blk % 2 == 0 else 1.0)
        return np.tile(tab, (1, nrep, 1)).astype(np.float32)
    c["k_rope_att"] = rope_tab(64, 2)
    c["k_rope_ret"] = rope_tab(128, 1)
    qi = np.arange(128)[:, None]
    kj = np.arange(384)[None, :]
    keep = np.abs(kj - 128 - qi) <= 128
    m = np.zeros((3, 128, 384), np.float32)
    m[0] = np.where(keep, 0.0, -30000.0)
    m[1] = np.where(keep & (kj >= 128), 0.0, -30000.0)
    m[2] = np.where(keep & (kj < 256), 0.0, -30000.0)
    c["k_att_mask"] = m
    jj = np.arange(128)[:, None].astype(np.float64)
    ii = np.arange(128)[None, :].astype(np.float64)
    rt = np.zeros((5, 128, 128), np.float64)
    rt[0] = np.where(ii >= jj, ii - jj, 1e9)
    rt[1] = np.where(jj >= ii, jj - ii, 1e9)
    rt[2] = np.broadcast_to(ii + 1.0, (128, 128))
    rt[3] = np.broadcast_to(128.0 - ii, (128, 128))
    rt[4, :, 0] = 127.0 - np.arange(128)
    rt[4, :, 1] = np.arange(128)
    rt[4, :, 2] = 128.0
    c["k_ret_tab"] = rt.astype(np.float32)
    p_ = np.arange(64)[:, None]
    f_ = np.arange(64)[None, :]
    c["k_dn"] = np.stack([p_ <= f_, p_ >= f_, p_ > f_, p_ < f_]).astype(np.float32)
    return c


LAYERS = [0, 1, 2, 3]
_cache = {}


def _prep_inputs(inputs):
    f = lambda a: np.ascontiguousarray(np.asarray(a, dtype=np.float32))
    shared = {}
    for k in ("ada_w", "ada_b", "mix_pre_g", "mix_post_g", "mlp_pre_g", "mlp_post_g", "mlp_w1", "mlp_w2"):
        shared[k] = f(inputs[k])
    shared["c_ctx"] = f(inputs["c_ctx"]).reshape(1, D)
    shared["ret_w_in"] = f(inputs["ret_w_in"][0])
    shared["ret_decay_logit"] = f(inputs["ret_decay_logit"][0]).reshape(1, 16)
    shared["ret_w_out"] = f(inputs["ret_w_out"][0])
    shared["att_w_in"] = f(inputs["att_w_in"][0])
    shared["att_sink"] = f(inputs["att_sink"][0]).reshape(1, 16)
    shared["att_w_out"] = f(inputs["att_w_out"][0])
    shared["pool_w"] = f(inputs["pool_w"][0])
    shared["pool_b"] = f(inputs["pool_b"][0]).reshape(1, D)
    shared["pool_scale"] = f(inputs["pool_scale"][0]).reshape(1, D)
    shared["dn_w_in"] = f(inputs["dn_w_in"][0])
    shared["dn_conv_w"] = f(inputs["dn_conv_w"][0])
    shared["dn_a_log"] = f(inputs["dn_a_log"][0]).reshape(1, 16)
    shared["dn_dt_bias"] = f(inputs["dn_dt_bias"][0]).reshape(1, 16)
    shared["dn_norm_g"] = f(inputs["dn_norm_g"][0]).reshape(1, 128)
    shared["dn_w_out"] = f(inputs["dn_w_out"][0])
    shared.update(host_consts())
    return shared


def run(inputs, layers, n_cores, dbg=False):
    key = (tuple(layers), dbg)
    if key not in _cache:
        _cache[key] = build_program(layers, dbg)
    nc = _cache[key]
    shared = _prep_inputs(inputs)
    x = np.asarray(inputs["x"], dtype=np.float32)
    c = np.asarray(inputs["c"], dtype=np.float32)
    ctx = np.asarray(inputs["ctx"], dtype=np.float32)
    in_maps = []
    for i in range(n_cores):
        m = dict(shared)
        m["x"] = np.ascontiguousarray(x[i * NB:(i + 1) * NB])
        m["c"] = np.ascontiguousarray(c[i * NB:(i + 1) * NB])
        m["ctx"] = np.ascontiguousarray(ctx[i * NB:(i + 1) * NB])
        in_maps.append(m)
    res = run_bass_kernel_spmd(nc, in_maps, core_ids=list(range(n_cores)))
    y = np.concatenate([r["y"] for r in res.results], axis=0)
    if dbg:
        global DBG_OUT
        DBG_OUT = {k: np.asarray(v) for k, v in res.results[0].items() if k.startswith("dbg_")}
        return y, np.concatenate([r["ctx_out"] for r in res.results], axis=0)
    return y


def kernel(**inputs):
    return run(inputs, LAYERS, 8).astype(np.float32)
```

```python
import os
import numpy as np
from contextlib import ExitStack
import concourse.bass as bass
import concourse.mybir as mybir
from concourse.bass_utils import run_bass_kernel_spmd

F32 = mybir.dt.float32
BF16 = mybir.dt.bfloat16
ALU = mybir.AluOpType
AF = mybir.ActivationFunctionType
AX = mybir.AxisListType

D = 1024
T = 2048
TC = 256
NB = 2
DFF = 4096
EPS = 1e-6
NT = 18
TT = NB * NT
SEQ = TC + T


class Reg:
    __slots__ = ("w", "r", "excl")

    def __init__(self, excl=False):
        self.w = {}
        self.r = {}
        self.excl = excl


class Rot:
    def __init__(self, items):
        self.items = items
        self.i = 0

    def next(self):
        it = self.items[self.i % len(self.items)]
        self.i += 1
        return it


class KB:
    def __init__(self, nc):
        self.nc = nc
        self.eh = {"pe": nc.tensor, "dve": nc.vector, "act": nc.scalar, "pool": nc.gpsimd, "sp": nc.sync}
        self.esem = {k: nc.alloc_semaphore("es_" + k) for k in self.eh}
        self.ecnt = {k: 0 for k in self.eh}
        self.waited = {k: {} for k in self.eh}
        self.dpool = {q: [[nc.alloc_semaphore("ds_%s%d" % (q, i)), 0] for i in range(n)]
                      for q, n in (("sp", 32), ("pool", 16), ("act", 8))}
        self.dnext = {q: 0 for q in self.dpool}
        self.uid = 0

    def name(self, base):
        self.uid += 1
        return "%s_%d" % (base, self.uid)

    def sb(self, es, base, shape, dt):
        return es.enter_context(self.nc.sbuf_tensor(self.name(base), list(shape), dt))

    def rot(self, es, base, shape, dt, n):
        return Rot([(self.sb(es, base, shape, dt), Reg()) for _ in range(n)])

    def _wait(self, e, sem, val):
        w = self.waited[e]
        if w.get(sem.num, 0) < val:
            self.eh[e].wait_ge(sem, val)
            w[sem.num] = val

    def _need(self, e, reads, writes, is_dma):
        own = None if is_dma else self.esem[e].num
        need = {}

        def add(d, skip_same):
            for num, (sem, val) in d.items():
                if num == own and (skip_same or e == "pe"):
                    continue
                if need.get(num, (None, 0))[1] < val:
                    need[num] = (sem, val)
        for r in reads:
            add(r.w, False)
            if r.excl:
                add(r.r, True)
        for r in writes:
            add(r.w, True)
            add(r.r, True)
        return need

    def _mark(self, tok, reads, writes):
        num = tok[0].num
        for r in reads:
            r.r[num] = tok
        for r in writes:
            r.w = {num: tok}
            r.r = {}

    def op(self, e, fn, reads, writes):
        need = self._need(e, reads, writes, False)
        for sem, val in need.values():
            self._wait(e, sem, val)
        inst = fn(self.eh[e])
        self.ecnt[e] += 1
        inst.then_inc(self.esem[e], 1)
        self._mark((self.esem[e], self.ecnt[e]), reads, writes)

    def dma(self, q, out, in_, reads, writes, **kw):
        pool = self.dpool[q]
        i = self.dnext[q]
        self.dnext[q] = (i + 1) % len(pool)
        sem, val = pool[i]
        need = self._need(q, reads, writes, True)
        if val > 0:
            need[sem.num] = (sem, val)
        for s, v in need.values():
            self._wait(q, s, v)
        inst = self.eh[q].dma_start(out=out, in_=in_, **kw)
        inst.then_inc(sem, 16)
        pool[i][1] = val + 16
        self._mark((sem, val + 16), reads, writes)

    def barrier(self):
        toks = [(self.esem[e], self.ecnt[e]) for e in self.eh if self.ecnt[e] > 0]
        toks += [(s, v) for p in self.dpool.values() for (s, v) in p if v > 0]
        for e in self.eh:
            for s, v in toks:
                if s.num != self.esem[e].num:
                    self._wait(e, s, v)

    def mm(self, out, pairs, reads, writes):
        n = len(pairs)

        def fn(h):
            inst = None
            for i, (l, r) in enumerate(pairs):
                inst = h.matmul(out, l, r, start=(i == 0), stop=(i == n - 1))
            return inst
        self.op("pe", fn, reads, writes)

    def mm1(self, out, l, r, start, stop, reads, writes):
        self.op("pe", lambda h: h.matmul(out, l, r, start=start, stop=stop), reads, writes)

    def tr(self, outs_ins, ident, reads, writes):
        def fn(h):
            inst = None
            for o, i in outs_ins:
                inst = h.transpose(o, i, ident)
            return inst
        self.op("pe", fn, reads, writes)

    def act(self, out, in_, func, reads, writes, **kw):
        self.op("act", lambda h: h.activation(out=out, in_=in_, func=func, **kw), reads, writes)

    def ts(self, e, out, in0, s1, s2, op0, op1, reads, writes):
        if s2 is None:
            self.op(e, lambda h: h.tensor_scalar(out=out, in0=in0, scalar1=s1, scalar2=None, op0=op0), reads, writes)
        else:
            self.op(e, lambda h: h.tensor_scalar(out=out, in0=in0, scalar1=s1, scalar2=s2, op0=op0, op1=op1),
                    reads, writes)

    def tt(self, e, out, in0, in1, op, reads, writes):
        self.op(e, lambda h: h.tensor_tensor(out=out, in0=in0, in1=in1, op=op), reads, writes)

    def stt(self, e, out, in0, scalar, in1, op0, op1, reads, writes):
        self.op(e, lambda h: h.scalar_tensor_tensor(out=out, in0=in0, scalar=scalar, in1=in1, op0=op0, op1=op1),
                reads, writes)

    def cp(self, e, out, in_, reads, writes):
        if e == "act":
            self.op(e, lambda h: h.copy(out=out, in_=in_), reads, writes)
        else:
            self.op(e, lambda h: h.tensor_copy(out=out, in_=in_), reads, writes)

    def memset(self, e, ap, val, writes):
        self.op(e, lambda h: h.memset(ap, val), [], writes)

    def recip(self, out, in_, reads, writes):
        self.op("dve", lambda h: h.reciprocal(out=out, in_=in_), reads, writes)


class Stream:
    def __init__(self, xap, cap):
        self.x = xap
        self.c = cap
        self.regs = [Reg() for _ in range(TT)]

    def tile(self, tt):
        b, r = divmod(tt, NT)
        if r < 2:
            return self.c[b, r * 128:(r + 1) * 128, :]
        return self.x[b, (r - 2) * 128:(r - 1) * 128, :]


def tile_slot(tt):
    b, r = divmod(tt, NT)
    return 2 if r < 2 else b


class G:
    pass


def build_program(layers, dbg=False):
    nc = bass.Bass("TRN2", target_bir_lowering=False)
    kb = KB(nc)
    g = G()
    g.nc, g.kb = nc, kb
    g.dbg = dbg
    L = 4

    def din(name, shape, dt=F32):
        return nc.dram_tensor(name, list(shape), dt, kind="ExternalInput").ap()

    def dscr(name, shape, dt=F32):
        return nc.dram_tensor(name, list(shape), dt, kind="Internal").ap()

    I = {}
    I["x"] = din("x", [NB, T, D])
    I["c"] = din("c", [NB, D])
    I["ctx"] = din("ctx", [NB, TC, D])
    I["c_ctx"] = din("c_ctx", [1, D])
    I["ada_w"] = din("ada_w", [L, D, 6 * D])
    I["ada_b"] = din("ada_b", [L, 6 * D])
    for nm in ("mix_pre_g", "mix_post_g", "mlp_pre_g", "mlp_post_g"):
        I[nm] = din(nm, [L, D])
    I["mlp_w1"] = din("mlp_w1", [L, D, DFF])
    I["mlp_w2"] = din("mlp_w2", [L, DFF, D])
    I["ret_w_in"] = din("ret_w_in", [D, 8192])
    I["ret_decay_logit"] = din("ret_decay_logit", [1, 16])
    I["ret_w_out"] = din("ret_w_out", [2048, D])
    I["att_w_in"] = din("att_w_in", [D, 1536])
    I["att_sink"] = din("att_sink", [1, 16])
    I["att_w_out"] = din("att_w_out", [D, D])
    I["pool_w"] = din("pool_w", [4, 256, 256])
    I["pool_b"] = din("pool_b", [1, D])
    I["pool_scale"] = din("pool_scale", [1, D])
    I["dn_w_in"] = din("dn_w_in", [D, 5152])
    I["dn_conv_w"] = din("dn_conv_w", [5, 3072])
    I["dn_a_log"] = din("dn_a_log", [1, 16])
    I["dn_dt_bias"] = din("dn_dt_bias", [1, 16])
    I["dn_norm_g"] = din("dn_norm_g", [1, 128])
    I["dn_w_out"] = din("dn_w_out", [D, D])
    I["k_ident"] = din("k_ident", [128, 128])
    I["k_pool"] = din("k_pool", [20, 128, 128])
    I["k_rope_att"] = din("k_rope_att", [2, 128, SEQ])
    I["k_rope_ret"] = din("k_rope_ret", [2, 128, SEQ])
    I["k_att_mask"] = din("k_att_mask", [3, 128, 384])
    I["k_ret_tab"] = din("k_ret_tab", [5, 128, 128])
    I["k_dn"] = din("k_dn", [4, 64, 64])
    g.I = I
    yout = nc.dram_tensor("y", [NB, T, D], F32, kind="ExternalOutput").ap()

    g.modD = dscr("modD", [L, 3, 6 * D])
    s_in = Stream(I["x"], I["ctx"])
    s1 = Stream(dscr("s1x", [NB, T, D]), dscr("s1c", [NB, TC, D]))
    s2 = Stream(dscr("s2x", [NB, T, D]), dscr("s2c", [NB, TC, D]))
    s_out = Stream(yout, s2.c)
    g.modD_r = Reg()

    g.ps = [nc.alloc_psum_tensor("psb%d" % i, [128, 512], F32) for i in range(8)]
    g.psr = [Reg(excl=True) for _ in range(8)]

    with ExitStack() as ges:
        g.ident_f = kb.sb(ges, "identf", [128, 128], F32)
        g.ident_b = kb.sb(ges, "identb", [128, 128], BF16)
        g.eps = kb.sb(ges, "eps", [128, 1], F32)
        g.const_r = Reg()
        kb.dma("sp", g.ident_f[:], I["k_ident"][:, :], [], [g.const_r])
        kb.dma("pool", g.ident_b[:], I["k_ident"][:, :], [], [g.const_r])
        kb.memset("dve", g.eps[:], EPS, [g.const_r])
        g.eps128 = kb.sb(ges, "eps128", [128, 1], F32)
        kb.memset("dve", g.eps128[:], EPS * 128.0, [g.const_r])
        kb.barrier()

        prologue_ada(g, layers)
        kb.barrier()

        cur = s_in
        for li, l in enumerate(layers):
            last = (li == len(layers) - 1)
            need_ctx = (l < 3) or dbg
            kind = l % 4
            if kind == 2:
                mixer_pool(g, l, cur, s1, need_ctx)
            elif kind == 1:
                mixer_att(g, l, cur, s1, need_ctx)
            elif kind == 0:
                mixer_ret(g, l, cur, s1, need_ctx)
            elif kind == 3:
                mixer_dn(g, l, cur, s1, need_ctx)
            else:
                raise NotImplementedError
            kb.barrier()
            dst = s_out if last else s2
            ffn(g, l, s1, dst, need_ctx)
            kb.barrier()
            cur = s2
        if dbg:
            cout = nc.dram_tensor("ctx_out", [NB, TC, D], F32, kind="ExternalOutput").ap()
            for b in range(NB):
                kb.dma("sp", cout[b], s2.c[b], [s2.regs[b * NT], s2.regs[b * NT + 1], s_out.regs[b * NT],
                                                s_out.regs[b * NT + 1]], [Reg()])
    kb.barrier()
    return nc


def prologue_ada(g, layers):
    kb, nc, I = g.kb, g.nc, g.I
    with ExitStack() as es:
        condT = kb.sb(es, "condT", [128, 8, 4], F32)
        cr = Reg()
        for s in range(3):
            src = I["c"][s, :] if s < 2 else I["c_ctx"][0, :]
            kb.dma("sp", condT[:, :, s], src.rearrange("(c p) -> p c", p=128), [], [cr],
                   allow_slow_non_contiguous=True)
        kb.memset("dve", condT[:, :, 3], 0.0, [cr])
        kb.act(condT[:, :, 0:3], condT[:, :, 0:3], AF.Silu, [cr], [cr])
        wrot = kb.rot(es, "adaw", [128, 8, 512], F32, 3)
        modrow = kb.sb(es, "modrow", [3, 6 * D], F32)
        mr = Reg()
        bias = kb.sb(es, "adab", [3, 6 * D], F32)
        gains = kb.sb(es, "gains", [3, 4, D], F32)
        br = Reg()
        for l in layers:
            kb.dma("sp", bias[:], I["ada_b"][l, :].partition_broadcast(3), [], [br])
            for gi, nm in enumerate(("mix_pre_g", "mix_post_g", "mlp_pre_g", "mlp_post_g")):
                kb.dma("sp", gains[:, gi, :], I[nm][l, :].partition_broadcast(3), [], [br])
            for j in range(12):
                wt, wr = wrot.next()
                kb.dma("sp", wt[:], I["ada_w"][l, :, j * 512:(j + 1) * 512].rearrange("(c p) n -> p c n", p=128),
                       [], [wr])
                pb = j % 2
                kb.mm(g.ps[pb][0:3, :], [(condT[:, kc, 0:3], wt[:, kc, :]) for kc in range(8)],
                      [cr, wr], [g.psr[pb]])
                kb.tt("dve", modrow[:, j * 512:(j + 1) * 512], g.ps[pb][0:3, :], bias[:, j * 512:(j + 1) * 512],
                      ALU.add, [g.psr[pb], br], [mr])
            for seg, gi, plus1 in ((1, 0, True), (2, 1, False), (4, 2, True), (5, 3, False)):
                sl = modrow[:, seg * D:(seg + 1) * D]
                if plus1:
                    kb.stt("dve", sl, sl, 1.0, gains[:, gi, :], ALU.add, ALU.mult, [mr, br], [mr])
                else:
                    kb.tt("dve", sl, sl, gains[:, gi, :], ALU.mult, [mr, br], [mr])
            kb.dma("sp", g.modD[l], modrow[:], [mr], [g.modD_r])


def load_cols(g, es, l, segA, segB):
    kb = g.kb
    r = Reg()
    outs = []
    for seg in (segA, segB):
        t = kb.sb(es, "mcol", [128, 3, 8], F32)
        for s in range(3):
            kb.dma("sp", t[:, s, :], g.modD[l, s, seg * D:(seg + 1) * D].rearrange("(c p) -> p c", p=128),
                   [g.modD_r], [r], allow_slow_non_contiguous=True)
        outs.append(t)
    return outs[0], outs[1], r


def load_bc(g, es, l, seg):
    kb = g.kb
    out = []
    r = Reg()
    for s in range(3):
        t = kb.sb(es, "mbc", [128, D], F32)
        kb.dma("sp", t[:], g.modD[l, s, seg * D:(seg + 1) * D].partition_broadcast(128), [g.modD_r], [r])
        out.append(t)
    return out, r


class NormWS:
    def __init__(self, g, es, nbuf=2):
        kb = g.kb
        self.st = kb.rot(es, "nst", [128, 4], F32, 8)
        self.junk = kb.sb(es, "njunk", [128, D], BF16)
        self.junk_r = Reg()
        self.xn = kb.rot(es, "nxn", [128, D], BF16, nbuf)


def rms_stats_gen(g, ws, y_aps, y_regs, out, isn=1.0 / 32.0):
    kb = g.kb
    st, sr = ws.st.next()
    out.append((st, sr))
    kb.memset("pool", st[:], 0.0, [sr])
    yield
    off = 0
    for i, ya in enumerate(y_aps):
        n = ya.shape[-1]
        kb.act(ws.junk[:, off:off + n], ya, AF.Square, y_regs + [sr], [ws.junk_r, sr], scale=isn,
               accum_out=st[:, i:i + 1])
        off += n
    yield
    if len(y_aps) == 2:
        kb.tt("dve", st[:, 0:1], st[:, 0:1], st[:, 1:2], ALU.add, [sr], [sr])
        yield
    kb.act(st[:, 1:2], st[:, 0:1], AF.Sqrt, [sr, g.const_r], [sr], bias=g.eps[:, 0:1], scale=1.0)
    yield
    kb.recip(st[:, 2:3], st[:, 1:2], [sr], [sr])
    yield


def rms_stats(g, ws, y_aps, y_regs, isn=1.0 / 32.0):
    out = []
    for _ in rms_stats_gen(g, ws, y_aps, y_regs, out, isn):
        pass
    return out[0]


def norm_T_gen(g, ws, xt, xr, Acol, Bcol, colr, slot, dst, dst_r, pbank):
    kb = g.kb
    out = []
    yield from rms_stats_gen(g, ws, [xt], [xr], out)
    st, sr = out[0]
    xn, xnr = ws.xn.next()
    kb.act(xn[:], xt, AF.Copy, [xr, sr], [xnr], scale=st[:, 2:3])
    yield
    psT = g.ps[pbank][:].bitcast(BF16).rearrange("p (c t) -> p c t", c=8)
    kb.tr([(psT[:, c, :], xn[:, c * 128:(c + 1) * 128]) for c in range(8)], g.ident_b[:],
          [xnr, g.const_r], [g.psr[pbank]])
    yield
    for c in range(8):
        kb.ts("dve", dst[:, c, :], psT[:, c, :], Acol[:, slot, c:c + 1], Bcol[:, slot, c:c + 1], ALU.mult, ALU.add,
              [g.psr[pbank], colr], [dst_r])
    yield


def norm_T(*args):
    for _ in norm_T_gen(*args):
        pass


def post_res_gen(g, ws, y_aps, y_regs, xt, xr, Gbc, gr, tmp, tmpr):
    kb = g.kb
    out = []
    yield from rms_stats_gen(g, ws, y_aps, y_regs, out)
    st, sr = out[0]
    off = 0
    for ya in y_aps:
        n = ya.shape[-1]
        kb.stt("dve", tmp[:, off:off + n], ya, st[:, 2:3], Gbc[:, off:off + n], ALU.mult, ALU.mult,
               y_regs + [sr, gr], [tmpr])
        off += n
    yield
    kb.tt("pool", xt, xt, tmp[:, :], ALU.add, [xr, tmpr], [xr])
    yield


def post_res(*args):
    for _ in post_res_gen(*args):
        pass


def ffn_weights(g, es, l):
    kb, I = g.kb, g.I
    w1 = kb.sb(es, "w1", [128, 8, DFF], BF16)
    w2 = kb.sb(es, "w2", [128, 32, D], BF16)
    w1r = [Reg() for _ in range(8)]
    w2r = [Reg() for _ in range(8)]
    for kc in range(8):
        for hf in range(2):
            kb.dma("pool", w1[:, kc, hf * 2048:(hf + 1) * 2048],
                   I["mlp_w1"][l, kc * 128:(kc + 1) * 128, hf * 2048:(hf + 1) * 2048], [], [w1r[kc]])
    for q in range(8):
        kb.dma("pool", w2[:, q * 4:(q + 1) * 4, :],
               I["mlp_w2"][l, q * 512:(q + 1) * 512, :].rearrange("(c p) n -> p c n", p=128), [], [w2r[q]])
    return w1, w2, w1r, w2r


def ffn(g, l, src, dst, need_ctx, weights=None):
    kb, nc, I = g.kb, g.nc, g.I
    with ExitStack() as es:
        w1, w2, w1r, w2r = weights if weights is not None else ffn_weights(g, es, l)
        Acol, Bcol, acr = load_cols(g, es, l, 4, 3)
        Gbc, gbr = load_bc(g, es, l, 5)
        ws = NormWS(g, es)
        xrot = kb.rot(es, "fx", [128, D], F32, 6)
        uT = kb.rot(es, "fuT", [128, 8, 256], BF16, 2)
        hT = kb.sb(es, "fhT", [128, 32, 256], BF16)
        hTr = [Reg() for _ in range(32)]
        rl = kb.rot(es, "frl", [128, 256], F32, 4)
        tmp = kb.rot(es, "ftmp", [128, D], F32, 2)
        tiles = [tt for tt in range(TT) if need_ctx or (tt % NT) >= 2]
        groups = [tiles[i:i + 2] for i in range(0, len(tiles), 2)]
        hslot = 0

        def prep(grp):
            u, ur = uT.next()
            xs = []
            for j, tt in enumerate(grp):
                xt, xr = xrot.next()
                kb.dma("sp", xt[:], src.tile(tt), [src.regs[tt]], [xr])
                norm_T(g, ws, xt[:], xr, Acol, Bcol, acr, tile_slot(tt), u[:, :, j * 128:(j + 1) * 128], ur, 4)
                xs.append((xt, xr))
            return u, ur, xs
        nxt = prep(groups[0])
        for gi_, grp in enumerate(groups):
            u, ur, xs = nxt
            for fc in range(32):
                hb = 5 + (hslot // 2) % 2
                hh = hslot % 2
                hslot += 1
                hp = g.ps[hb][:, hh * 256:(hh + 1) * 256]
                kb.mm(hp, [(w1[:, kc, fc * 128:(fc + 1) * 128], u[:, kc, :]) for kc in range(8)],
                      [ur] + w1r, [g.psr[hb]])
                r_, rr = rl.next()
                kb.act(r_[:], hp, AF.Relu, [g.psr[hb]], [rr])
                kb.tt("dve", hT[:, fc, :], r_[:], r_[:], ALU.mult, [rr], [hTr[fc]])
            if gi_ + 1 < len(groups):
                nxt = prep(groups[gi_ + 1])
            for j, tt in enumerate(grp):
                xt, xr = xs[j]
                for half in range(2):
                    pb = j * 2 + half
                    kb.mm(g.ps[pb][:, :], [(hT[:, fc, j * 128:(j + 1) * 128], w2[:, fc, half * 512:(half + 1) * 512])
                                           for fc in range(32)], hTr + w2r, [g.psr[pb]])
                t_, tr_ = tmp.next()
                post_res(g, ws, [g.ps[j * 2][:, :], g.ps[j * 2 + 1][:, :]], [g.psr[j * 2], g.psr[j * 2 + 1]],
                         xt[:], xr, Gbc[tile_slot(tt)], gbr, t_, tr_)
                kb.dma("sp", dst.tile(tt), xt[:], [xr], [dst.regs[tt]])


def mixer_pool(g, l, src, dst, need_ctx):
    kb, nc, I = g.kb, g.nc, g.I
    with ExitStack() as es:
        Acol, Bcol, acr = load_cols(g, es, l, 1, 0)
        Gbc, gbr = load_bc(g, es, l, 2)
        ws = NormWS(g, es, 4)
        wg = kb.sb(es, "pw", [128, 4, 2, 256], BF16)
        wgr = Reg()
        for gi in range(4):
            kb.dma("pool", wg[:, gi, :, :], I["pool_w"][gi].rearrange("(c p) n -> p c n", p=128), [], [wgr])
        pm = kb.sb(es, "pm", [128, 20, 128], BF16)
        kb.dma("pool", pm[:], I["k_pool"].rearrange("m p n -> p m n"), [], [wgr])
        pbb = kb.sb(es, "pbb", [128, D], F32)
        psb = kb.sb(es, "psb", [128, D], F32)
        kb.dma("sp", pbb[:], I["pool_b"][0, :].partition_broadcast(128), [], [wgr])
        kb.dma("sp", psb[:], I["pool_scale"][0, :].partition_broadcast(128), [], [wgr])
        xrot = kb.rot(es, "px", [128, D], F32, 4)
        uT = kb.rot(es, "puT", [128, 8, 128], BF16, 3)
        vall = kb.sb(es, "pvall", [128, TT, D], BF16)
        vreg = [Reg() for _ in range(TT)]
        yb = kb.rot(es, "pyb", [128, D], F32, 4)
        tmp = kb.rot(es, "ptmp", [128, D], F32, 4)
        tiles = [tt for tt in range(TT) if need_ctx or (tt % NT) >= 2]

        def unit1(tt, slot):
            xt, xr = xrot.next()
            kb.dma("sp", xt[:], src.tile(tt), [src.regs[tt]], [xr])
            u, ur = uT.next()
            yield
            yield from norm_T_gen(g, ws, xt[:], xr, Acol, Bcol, acr, tile_slot(tt), u, ur, 4 + slot)
            for gi in range(4):
                pb = slot * 2 + gi // 2
                kb.mm(g.ps[pb][:, (gi % 2) * 256:(gi % 2 + 1) * 256],
                      [(u[:, gi * 2 + kc, :], wg[:, gi, kc, :]) for kc in range(2)], [ur, wgr], [g.psr[pb]])
            yield
            kb.cp("act", vall[:, tt, 0:512], g.ps[slot * 2][:, :], [g.psr[slot * 2]], [vreg[tt]])
            kb.cp("act", vall[:, tt, 512:1024], g.ps[slot * 2 + 1][:, :], [g.psr[slot * 2 + 1]], [vreg[tt]])
            yield
        interleave((unit1(tt, i % 2) for i, tt in enumerate(tiles)), 2, 5)

        def unit2(tt, slot):
            r = tt % NT
            seg_lo, seg_n = (0, 2) if r < 2 else (2, 16)
            tq = r - seg_lo
            xt, xr = xrot.next()
            kb.dma("sp", xt[:], src.tile(tt), [src.regs[tt]], [xr])
            for gi in range(4):
                pb = slot * 2 + gi // 2
                cidx = 0 if (0 < tq < seg_n - 1) else (1 if tq == 0 else 2)
                pairs = [(pm[:, gi * 5 + cidx, :], vall[:, tt, gi * 256:(gi + 1) * 256])]
                regs = [wgr, vreg[tt]]
                if tq > 0:
                    pairs.append((pm[:, gi * 5 + 3, :], vall[:, tt - 1, gi * 256:(gi + 1) * 256]))
                    regs.append(vreg[tt - 1])
                if tq < seg_n - 1:
                    pairs.append((pm[:, gi * 5 + 4, :], vall[:, tt + 1, gi * 256:(gi + 1) * 256]))
                    regs.append(vreg[tt + 1])
                kb.mm(g.ps[pb][:, (gi % 2) * 256:(gi % 2 + 1) * 256], pairs, regs, [g.psr[pb]])
            yield
            y_, yr = yb.next()
            for h in range(2):
                kb.tt("dve", y_[:, h * 512:(h + 1) * 512], g.ps[slot * 2 + h][:, :], pbb[:, h * 512:(h + 1) * 512],
                      ALU.add, [g.psr[slot * 2 + h], wgr], [yr])
            yield
            kb.tt("pool", y_[:], y_[:], psb[:], ALU.mult, [yr, wgr], [yr])
            yield
            t_, tr_ = tmp.next()
            yield from post_res_gen(g, ws, [y_[:, :]], [yr], xt[:], xr, Gbc[tile_slot(tt)], gbr, t_, tr_)
            kb.dma("sp", dst.tile(tt), xt[:], [xr], [dst.regs[tt]])
        interleave((unit2(tt, i % 3) for i, tt in enumerate(tiles)), 3, 3)


def phase_uT(g, es, l, src):
    kb = g.kb
    uT = kb.sb(es, "uTall", [128, 8, TT * 128], BF16)
    regs = [Reg() for _ in range(TT)]
    with ExitStack() as es2:
        Acol, Bcol, acr = load_cols(g, es2, l, 1, 0)
        ws = NormWS(g, es2, 4)
        xrot = kb.rot(es2, "ux", [128, D], F32, 4)

        def unit(tt, slot):
            xt, xr = xrot.next()
            kb.dma("sp", xt[:], src.tile(tt), [src.regs[tt]], [xr])
            yield
            yield from norm_T_gen(g, ws, xt[:], xr, Acol, Bcol, acr, tile_slot(tt), uT[:, :, tt * 128:(tt + 1) * 128],
                                  regs[tt], 4 + slot)
        interleave((unit(tt, tt % 3) for tt in range(TT)), 3, 3)
        kb.barrier()
    return uT, regs


def load_w_bf16(g, rot, wd, c0, ncols):
    kb = g.kb
    wt, wr = rot.next()
    kb.dma("pool", wt[:, :, 0:ncols], wd[:, c0:c0 + ncols].rearrange("(c p) n -> p c n", p=128), [], [wr])
    return wt, wr


def phase_outproj(g, l, a_d, a_regs, Kd, wout_d, src, dst, need_ctx):
    kb, nc = g.kb, g.nc
    nk = Kd // 128
    with ExitStack() as es:
        Gbc, gbr = load_bc(g, es, l, 2)
        ws = NormWS(g, es)
        wo = kb.sb(es, "wo", [128, nk, D], BF16)
        wor = Reg()
        for q in range(nk // 4):
            kb.dma("pool", wo[:, q * 4:(q + 1) * 4, :],
                   wout_d[q * 512:(q + 1) * 512, :].rearrange("(c p) n -> p c n", p=128), [], [wor])
        a_list = a_d if isinstance(a_d, list) else [a_d]
        arot = kb.rot(es, "oa", [128, Kd], BF16, 3 * len(a_list))
        aTrot = kb.rot(es, "oaT", [128, nk, 128], BF16, 3)
        xrot = kb.rot(es, "ox", [128, D], F32, 3)
        tmp = kb.rot(es, "otmp", [128, D], F32, 3)
        def unit(tt, slot):
            at, ar = arot.next()
            kb.dma("sp", at[:], a_list[0][tt * 128:(tt + 1) * 128, :], [a_regs[tt]], [ar])
            xt, xr = xrot.next()
            kb.dma("sp", xt[:], src.tile(tt), [src.regs[tt]], [xr])
            for extra in a_list[1:]:
                at2, ar2 = arot.next()
                kb.dma("sp", at2[:], extra[tt * 128:(tt + 1) * 128, :], [a_regs[tt]], [ar2])
                yield
                kb.tt("pool", at[:], at[:], at2[:], ALU.add, [ar, ar2], [ar])
            yield
            aT, aTr = aTrot.next()
            pb = slot * 3 + 2
            for q in range(nk // 8):
                psT = g.ps[pb][:].bitcast(BF16).rearrange("p (c t) -> p c t", c=8)
                kb.tr([(psT[:, c, :], at[:, (q * 8 + c) * 128:(q * 8 + c + 1) * 128]) for c in range(8)],
                      g.ident_b[:], [ar, g.const_r], [g.psr[pb]])
                yield
                kb.cp("act", aT[:, q * 8:(q + 1) * 8, :], psT, [g.psr[pb]], [aTr])
                yield
            yb = slot * 3
            for half in range(2):
                kb.mm(g.ps[yb + half][:, :], [(aT[:, kc, :], wo[:, kc, half * 512:(half + 1) * 512])
                                              for kc in range(nk)], [aTr, wor], [g.psr[yb + half]])
            yield
            t_, tr_ = tmp.next()
            yield from post_res_gen(g, ws, [g.ps[yb][:, :], g.ps[yb + 1][:, :]], [g.psr[yb], g.psr[yb + 1]], xt[:], xr,
                                    Gbc[tile_slot(tt)], gbr, t_, tr_)
            kb.dma("sp", dst.tile(tt), xt[:], [xr], [dst.regs[tt]])
        tts = [tt for tt in range(TT) if need_ctx or (tt % NT) >= 2]
        interleave((unit(tt, i % 2) for i, tt in enumerate(tts)), 2, 6)


def make_perm(g, wt, wr, wp, wpr, ncols, blk):
    kb = g.kb
    v_in = wt[:, :, 0:ncols].rearrange("p c (q two i) -> p c q two i", two=2, i=blk)
    v_out = wp[:, :, 0:ncols].rearrange("p c (q two i) -> p c q two i", two=2, i=blk)
    for kc in range(8):
        kb.cp("pool", v_out[:, kc, :, 0, :], v_in[:, kc, :, 1, :], [wr], [wpr])
        kb.cp("pool", v_out[:, kc, :, 1, :], v_in[:, kc, :, 0, :], [wr], [wpr])


SEQ_BLOCKS = [(b, o, min(512, SEQ - o)) for b in range(NB) for o in range(0, SEQ, 512)]


def proj_fm_rope(g, uT, uTr, wd, c0, rope, rope_r, scale, blk, out_d, out_r, wrot, wprot, stg, pbase):
    kb = g.kb
    wt, wr = load_w_bf16(g, wrot, wd, c0, 128)
    wp, wpr = wprot.next()
    make_perm(g, wt, wr, wp, wpr, 128, blk)
    for bi, (b, o, n) in enumerate(SEQ_BLOCKS):
        t0 = b * SEQ + o
        tiles = list(range(t0 // 128, (t0 + n) // 128))
        rr = [uTr[t] for t in tiles]
        pa, pb = pbase + (bi % 2) * 2, pbase + (bi % 2) * 2 + 1
        kb.mm(g.ps[pa][:, 0:n], [(wt[:, kc, 0:128], uT[:, kc, t0:t0 + n]) for kc in range(8)], rr + [wr], [g.psr[pa]])
        kb.mm(g.ps[pb][:, 0:n], [(wp[:, kc, 0:128], uT[:, kc, t0:t0 + n]) for kc in range(8)], rr + [wpr], [g.psr[pb]])
        (t1, t1r), (t2, t2r), (t3, t3r) = stg[0].next(), stg[1].next(), stg[2].next()
        kb.stt("dve", t1[:, 0:n], g.ps[pa][:, 0:n], scale, rope[:, 0, o:o + n], ALU.mult, ALU.mult,
               [g.psr[pa], rope_r], [t1r])
        kb.stt("dve", t2[:, 0:n], g.ps[pb][:, 0:n], scale, rope[:, 1, o:o + n], ALU.mult, ALU.mult,
               [g.psr[pb], rope_r], [t2r])
        kb.tt("pool", t3[:, 0:n], t1[:, 0:n], t2[:, 0:n], ALU.add, [t1r, t2r], [t3r])
        kb.dma("sp", out_d[:, t0:t0 + n], t3[:, 0:n], [t3r], [out_r])


def proj_tm(g, uT, uTr, wd, c0, ncols, out_d, out_regs, col0, wrot, stg, pbase, func=None, tiles=None):
    kb = g.kb
    wt, wr = load_w_bf16(g, wrot, wd, c0, ncols)
    for i, tt in enumerate(tiles if tiles is not None else range(TT)):
        pb = pbase + i % 2
        kb.mm(g.ps[pb][:, 0:ncols], [(uT[:, kc, tt * 128:(tt + 1) * 128], wt[:, kc, 0:ncols]) for kc in range(8)],
              [uTr[tt], wr], [g.psr[pb]])
        st_, sr_ = stg.next()
        if func is None:
            kb.cp("act", st_[:, 0:ncols], g.ps[pb][:, 0:ncols], [g.psr[pb]], [sr_])
        else:
            kb.act(st_[:, 0:ncols], g.ps[pb][:, 0:ncols], func, [g.psr[pb]], [sr_])
        kb.dma("sp", out_d[tt * 128:(tt + 1) * 128, col0:col0 + ncols], st_[:, 0:ncols], [sr_], [out_regs[tt]])


def mixer_att(g, l, src, dst, need_ctx, hook=None):
    kb, nc, I = g.kb, g.nc, g.I
    NTOK = TT * 128
    attq = nc.dram_tensor(kb.name("attq"), [8, 128, NTOK], BF16, kind="Internal").ap()
    attk = nc.dram_tensor(kb.name("attk"), [2, 128, NTOK], BF16, kind="Internal").ap()
    attv = nc.dram_tensor(kb.name("attv"), [NTOK, 256], BF16, kind="Internal").ap()
    atta = nc.dram_tensor(kb.name("atta"), [NTOK, D], BF16, kind="Internal").ap()
    qr_, kr_ = [Reg() for _ in range(8)], [Reg() for _ in range(2)]
    vr_ = [Reg() for _ in range(TT)]
    ar_ = [Reg() for _ in range(TT)]
    with ExitStack() as es:
        uT, uTr = phase_uT(g, es, l, src)
        rope = kb.sb(es, "ropeA", [128, 2, SEQ], F32)
        rope_r = Reg()
        kb.dma("sp", rope[:], I["k_rope_att"].rearrange("t p n -> p t n"), [], [rope_r])
        wrot = kb.rot(es, "aw", [128, 8, 256], BF16, 2)
        wprot = kb.rot(es, "awp", [128, 8, 128], BF16, 2)
        stg = [kb.rot(es, "astg", [128, 512], F32, 2), kb.rot(es, "astg", [128, 512], F32, 2),
               kb.rot(es, "astgb", [128, 512], BF16, 3)]
        for cb in range(8):
            proj_fm_rope(g, uT, uTr, I["att_w_in"], cb * 128, rope, rope_r, 0.125, 16, attq[cb], qr_[cb],
                         wrot, wprot, stg, 0)
        for cb in range(2):
            proj_fm_rope(g, uT, uTr, I["att_w_in"], 1024 + cb * 128, rope, rope_r, 1.0, 16, attk[cb], kr_[cb],
                         wrot, wprot, stg, 0)
        proj_tm(g, uT, uTr, I["att_w_in"], 1280, 256, attv, vr_, 0, wrot, stg[2], 0)
    kb.barrier()
    with ExitStack() as es:
        mb = kb.sb(es, "amask", [128, 3, 384], BF16)
        cr = Reg()
        kb.dma("pool", mb[:], I["k_att_mask"].rearrange("v p n -> p v n"), [], [cr])
        sink = kb.sb(es, "asink", [128, 16], F32)
        kb.dma("sp", sink[:], I["att_sink"][0, :].partition_broadcast(128), [], [cr])
        Krot = kb.rot(es, "aK", [64, 2560], BF16, 3)
        Vrot = kb.rot(es, "aV", [128, 20, 64], BF16, 3)
        for kt, kr in Krot.items:
            kb.memset("pool", kt[:, 0:128], 0.0, [kr])
            kb.memset("pool", kt[:, 2176:2304], 0.0, [kr])
        for vt, vr in Vrot.items:
            kb.memset("pool", vt[:, 0, :], 0.0, [vr])
            kb.memset("pool", vt[:, 17, :], 0.0, [vr])
        Qrot = kb.rot(es, "aQ", [64, SEQ], BF16, 4)
        prot = kb.rot(es, "ap", [128, 640], BF16, 4)
        pTrot = kb.rot(es, "apT", [128, 5, 128], BF16, 4)
        strot = kb.rot(es, "ast", [128, 8], F32, 8)
        ostg = kb.rot(es, "aos", [128, 64], BF16, 6)

        def att_unit(b, h, blk, slot, cnt, Qt, Qr, Kt, Kr, Vt, Vr):
            lat = blk >= 2
            bi = blk - 2
            qs = Qt[:, blk * 128:(blk + 1) * 128]
            pa, pbk = slot * 2, slot * 2 + 1
            ptb = 4 + slot
            po = g.ps[6 + slot][:, (cnt % 8) * 64:(cnt % 8) * 64 + 64]
            por = g.psr[6 + slot]
            st, sr = strot.next()
            if lat:
                var = 1 if bi == 0 else (2 if bi == 15 else 0)
                kb.mm(g.ps[pa][:, 0:384], [(qs, Kt[:, bi * 128:bi * 128 + 384]),
                                           (g.ident_b[:], mb[:, var, :])], [Qr, Kr, cr, g.const_r], [g.psr[pa]])
            kb.mm(g.ps[pbk][:, 0:256], [(qs, Kt[:, 2304:2560])], [Qr, Kr], [g.psr[pbk]])
            kb.memset("pool", st[:, 4:7], 0.0, [sr])
            yield
            if lat:
                kb.op("dve", lambda hd, o=st[:, 0:1], i=g.ps[pa][:, 0:384]: hd.reduce_max(out=o, in_=i, axis=AX.X),
                      [g.psr[pa]], [sr])
            kb.op("dve", lambda hd, o=st[:, 1:2], i=g.ps[pbk][:, 0:256]: hd.reduce_max(out=o, in_=i, axis=AX.X),
                  [g.psr[pbk]], [sr])
            if lat:
                kb.tt("dve", st[:, 1:2], st[:, 0:1], st[:, 1:2], ALU.max, [sr], [sr])
            kb.ts("dve", st[:, 2:3], st[:, 1:2], sink[:, h:h + 1], -1.0, ALU.max, ALU.mult, [sr, cr], [sr])
            yield
            p_, pr = prot.next()
            if lat:
                kb.act(p_[:, 0:384], g.ps[pa][:, 0:384], AF.Exp, [g.psr[pa], sr], [pr, sr],
                       bias=st[:, 2:3], scale=1.0, accum_out=st[:, 4:5])
            kb.act(p_[:, 384:640], g.ps[pbk][:, 0:256], AF.Exp, [g.psr[pbk], sr], [pr, sr],
                   bias=st[:, 2:3], scale=1.0, accum_out=st[:, 5:6])
            kb.act(st[:, 6:7], st[:, 2:3], AF.Exp, [sr, cr], [sr], bias=sink[:, h:h + 1], scale=1.0)
            yield
            js = list(range(5)) if lat else [3, 4]
            psT = g.ps[ptb][:].bitcast(BF16).rearrange("p (c t) -> p c t", c=8)
            kb.tr([(psT[:, j, :], p_[:, j * 128:(j + 1) * 128]) for j in js], g.ident_b[:],
                  [pr, g.const_r], [g.psr[ptb]])
            kb.stt("dve", st[:, 7:8], st[:, 4:5], st[:, 5:6], st[:, 6:7], ALU.add, ALU.add, [sr], [sr])
            yield
            pT, pTr = pTrot.next()
            kb.cp("dve", pT[:, js[0]:5, :], psT[:, js[0]:5, :], [g.psr[ptb]], [pTr])
            kb.recip(st[:, 3:4], st[:, 7:8], [sr], [sr])
            yield
            pairs = []
            if lat:
                pairs += [(pT[:, j, :], Vt[:, bi + j, :]) for j in range(3)]
            pairs += [(pT[:, 3, :], Vt[:, 18, :]), (pT[:, 4, :], Vt[:, 19, :])]
            kb.mm(po, pairs, [pTr, Vr], [por])
            yield
            os_, osr = ostg.next()
            kb.ts("dve", os_[:], po, st[:, 3:4], None, ALU.mult, None, [por, sr], [osr])
            tt = b * NT + blk
            kb.dma("sp", atta[tt * 128:(tt + 1) * 128, h * 64:(h + 1) * 64], os_[:], [osr], [Reg()])

        def att_units():
            cnt = 0
            for b in range(NB):
                for kv in range(4):
                    Kt, Kr = Krot.next()
                    Vt, Vr = Vrot.next()
                    base = b * SEQ
                    ksrc = attk[kv // 2, (kv % 2) * 64:(kv % 2) * 64 + 64, :]
                    kb.dma("sp", Kt[:, 128:2176], ksrc[:, base + TC:base + SEQ], [kr_[kv // 2]], [Kr])
                    kb.dma("sp", Kt[:, 2304:2560], ksrc[:, base:base + TC], [kr_[kv // 2]], [Kr])
                    vsrc = attv[:, kv * 64:(kv + 1) * 64]
                    kb.dma("sp", Vt[:, 1:17, :], vsrc[base + TC:base + SEQ, :].rearrange("(t p) d -> p t d", p=128),
                           [vr_[b * NT + t] for t in range(2, 18)], [Vr])
                    kb.dma("sp", Vt[:, 18:20, :], vsrc[base:base + TC, :].rearrange("(t p) d -> p t d", p=128),
                           [vr_[b * NT], vr_[b * NT + 1]], [Vr])
                    for hh in range(4):
                        h = kv * 4 + hh
                        Qt, Qr = Qrot.next()
                        kb.dma("sp", Qt[:], attq[h // 2, (h % 2) * 64:(h % 2) * 64 + 64, base:base + SEQ],
                               [qr_[h // 2]], [Qr])
                        for blk in range(18):
                            if blk < 2 and not need_ctx:
                                continue
                            yield att_unit(b, h, blk, cnt % 2, cnt // 2, Qt, Qr, Kt, Kr, Vt, Vr)
                            cnt += 1
        interleave(att_units(), 2, 3)
    kb.barrier()
    if hook is not None:
        hook()
    phase_outproj(g, l, atta, ar_, D, I["att_w_out"], src, dst, need_ctx)


def mixer_ret(g, l, src, dst, need_ctx):
    kb, nc, I = g.kb, g.nc, g.I
    NTOK = TT * 128
    retq = nc.dram_tensor(kb.name("retq"), [8, 128, NTOK], BF16, kind="Internal").ap()
    retk = nc.dram_tensor(kb.name("retk"), [8, 128, NTOK], BF16, kind="Internal").ap()
    retv = nc.dram_tensor(kb.name("retv"), [NTOK, 2048], BF16, kind="Internal").ap()
    retg = nc.dram_tensor(kb.name("retg"), [NTOK, 4096], BF16, kind="Internal").ap()
    reta = [nc.dram_tensor(kb.name("reta"), [NTOK, 2048], BF16, kind="Internal").ap() for _ in range(2)]
    ar_ = [Reg() for _ in range(TT)]

    class Fresh(list):
        def __getitem__(self, i):
            return Reg()
    with ExitStack() as es:
        uT, uTr = phase_uT(g, es, l, src)
        rope = kb.sb(es, "ropeR", [128, 2, SEQ], F32)
        rope_r = Reg()
        kb.dma("sp", rope[:], I["k_rope_ret"].rearrange("t p n -> p t n"), [], [rope_r])
        wrot = kb.rot(es, "rw", [128, 8, 512], BF16, 2)
        wprot = kb.rot(es, "rwp", [128, 8, 128], BF16, 2)
        stg = [kb.rot(es, "rstg", [128, 512], F32, 2), kb.rot(es, "rstg", [128, 512], F32, 2),
               kb.rot(es, "rstgb", [128, 512], BF16, 4)]
        for h in range(8):
            proj_fm_rope(g, uT, uTr, I["ret_w_in"], h * 128, rope, rope_r, 128.0 ** -0.5, 32, retq[h], Reg(),
                         wrot, wprot, stg, 0)
            proj_fm_rope(g, uT, uTr, I["ret_w_in"], 1024 + h * 128, rope, rope_r, 1.0, 32, retk[h], Reg(),
                         wrot, wprot, stg, 0)
        for ch in range(4):
            proj_tm(g, uT, uTr, I["ret_w_in"], 2048 + ch * 512, 512, retv, Fresh(), ch * 512, wrot, stg[2], 4)
        for ch in range(8):
            proj_tm(g, uT, uTr, I["ret_w_in"], 4096 + ch * 512, 512, retg, Fresh(), ch * 512, wrot, stg[2], 4,
                    func=AF.Silu)
    kb.barrier()
    with ExitStack() as es:
        cr = Reg()
        tab = kb.sb(es, "rtab", [128, 5, 128], F32)
        kb.dma("sp", tab[:], I["k_ret_tab"].rearrange("t p n -> p t n"), [], [cr])
        lg = kb.sb(es, "rlg", [128, 16], F32)
        kb.dma("sp", lg[:], I["ret_decay_logit"][0, :].partition_broadcast(128), [], [cr])
        kb.act(lg[:], lg[:], AF.Exp, [cr], [cr], scale=-1.0)
        kb.ts("dve", lg[:], lg[:], 1.0, None, ALU.add, None, [cr], [cr])
        kb.act(lg[:], lg[:], AF.Ln, [cr], [cr])
        kb.ts("dve", lg[:], lg[:], -1.0, None, ALU.mult, None, [cr], [cr])
        htab = kb.sb(es, "rhtab", [128, 8, 4, 128], F32)
        hcol = kb.sb(es, "rhcol", [128, 8, 4], F32)
        for h in range(8):
            for ti, (src_i, lcol) in enumerate(((0, h), (1, 8 + h), (2, h), (3, 8 + h))):
                kb.act(htab[:, h, ti, :], tab[:, src_i, :], AF.Exp, [cr], [cr], scale=lg[:, lcol:lcol + 1])
            for ci, (src_c, lcol) in enumerate(((0, h), (1, 8 + h), (2, h), (2, 8 + h))):
                kb.act(hcol[:, h, ci:ci + 1], tab[:, 4, src_c:src_c + 1], AF.Exp, [cr], [cr],
                       scale=lg[:, lcol:lcol + 1])
        ws = NormWS(g, es)
        NU = 2
        Qrot = kb.rot(es, "rQ", [128, SEQ], BF16, NU)
        Krot = kb.rot(es, "rK", [128, SEQ], BF16, NU)
        Vrot = kb.rot(es, "rV", [128, 18, 256], BF16, NU)
        GFrot = kb.rot(es, "rGF", [128, 18, 256], BF16, NU)
        GBrot = kb.rot(es, "rGB", [128, 18, 256], BF16, NU)
        pre_all = [[kb.sb(es, "rpre", [128, 18, 128], BF16) for _ in range(6)] for _ in range(NU)]
        prer_all = [[[Reg() for _ in range(18)] for _ in range(6)] for _ in range(NU)]
        S_all = [[kb.sb(es, "rS", [128, 256], F32) for _ in range(2)] for _ in range(NU)]
        Sb_all = [[kb.sb(es, "rSb", [128, 256], BF16) for _ in range(2)] for _ in range(NU)]
        Sr_all = [[Reg(), Reg()] for _ in range(NU)]
        Sbr_all = [[Reg(), Reg()] for _ in range(NU)]
        arot = kb.rot(es, "ra", [128, 256], BF16, 8)
        P, PR = g.ps, g.psr

        def head_unit(b, h, slot):
            base = b * SEQ
            pre, prer = pre_all[slot], prer_all[slot]
            S, Sb, Sr, Sbr = S_all[slot], Sb_all[slot], Sr_all[slot], Sbr_all[slot]
            B0 = slot * 4
            Qt, Qr = Qrot.next()
            Kt, Kr = Krot.next()
            Vt, Vr = Vrot.next()
            GF, GFr = GFrot.next()
            GB, GBr = GBrot.next()
            kb.dma("sp", Qt[:], retq[h][:, base:base + SEQ], [], [Qr])
            kb.dma("sp", Kt[:], retk[h][:, base:base + SEQ], [], [Kr])
            kb.dma("sp", Vt[:], retv[base:base + SEQ, h * 256:(h + 1) * 256].rearrange("(t p) d -> p t d", p=128),
                   [], [Vr])
            kb.dma("sp", GF[:], retg[base:base + SEQ, h * 256:(h + 1) * 256].rearrange("(t p) d -> p t d", p=128),
                   [], [GFr])
            kb.dma("sp", GB[:], retg[base:base + SEQ, 2048 + h * 256:2048 + (h + 1) * 256]
                   .rearrange("(t p) d -> p t d", p=128), [], [GBr])
            for d_ in range(2):
                kb.memset("pool", S[d_][:], 0.0, [Sr[d_]])
                kb.memset("pool", Sb[d_][:], 0.0, [Sbr[d_]])
            yield
            for c0 in range(0, 18, 2):
                cs_ = [c0, c0 + 1]
                for j, c in enumerate(cs_):
                    cs = slice(c * 128, (c + 1) * 128)
                    kb.mm(P[B0][:, j * 128:(j + 1) * 128], [(Kt[:, cs], Qt[:, cs])], [Kr, Qr], [PR[B0]])
                    psk = P[B0 + 1][:].bitcast(BF16)[:, j * 128:(j + 1) * 128]
                    kb.tr([(psk, Kt[:, cs])], g.ident_b[:], [Kr, g.const_r], [PR[B0 + 1]])
                    kb.tt("pool", pre[2][:, c, :], Qt[:, cs], htab[:, h, 2, :], ALU.mult, [Qr, cr], [prer[2][c]])
                    kb.tt("pool", pre[3][:, c, :], Qt[:, cs], htab[:, h, 3, :], ALU.mult, [Qr, cr], [prer[3][c]])
                yield
                for j, c in enumerate(cs_):
                    pss = P[B0][:, j * 128:(j + 1) * 128]
                    psk = P[B0 + 1][:].bitcast(BF16)[:, j * 128:(j + 1) * 128]
                    kb.tt("dve", pre[0][:, c, :], pss, htab[:, h, 0, :], ALU.mult, [PR[B0], cr], [prer[0][c]])
                    kb.tt("dve", pre[1][:, c, :], pss, htab[:, h, 1, :], ALU.mult, [PR[B0], cr], [prer[1][c]])
                    kb.act(pre[4][:, c, :], psk, AF.Copy, [PR[B0 + 1], cr], [prer[4][c]], scale=hcol[:, h, 0:1])
                    kb.act(pre[5][:, c, :], psk, AF.Copy, [PR[B0 + 1], cr], [prer[5][c]], scale=hcol[:, h, 1:2])
                yield
            orders = [list(range(18)), [1, 0] + list(range(17, 1, -1))]
            for step in range(18):
                sts = [None, None]
                for d_ in range(2):
                    c = orders[d_][step]
                    po = P[B0 + 2 + d_][:, 0:256]
                    pS = P[B0 + d_][:, 0:256]
                    kb.mm(po, [(pre[d_][:, c, :], Vt[:, c, :]), (pre[2 + d_][:, c, :], Sb[d_][:])],
                          [prer[d_][c], prer[2 + d_][c], Vr, Sbr[d_]], [PR[B0 + 2 + d_]])
                    kb.mm(pS, [(pre[4 + d_][:, c, :], Vt[:, c, :])], [prer[4 + d_][c], Vr], [PR[B0 + d_]])
                yield
                outs = [[], []]
                gens = []
                for d_ in range(2):
                    pS = P[B0 + d_][:, 0:256]
                    po = P[B0 + 2 + d_][:, 0:256]
                    kb.stt("dve", S[d_][:], S[d_][:], hcol[:, h, 2 + d_:3 + d_], pS, ALU.mult, ALU.add,
                           [Sr[d_], PR[B0 + d_], cr], [Sr[d_]])
                    gens.append(rms_stats_gen(g, ws, [po], [PR[B0 + 2 + d_]], outs[d_], 1.0 / 16.0))
                yield
                for d_ in range(2):
                    kb.cp("act", Sb[d_][:], S[d_][:], [Sr[d_]], [Sbr[d_]])
                alive = True
                while alive:
                    alive = False
                    for gn_ in gens:
                        try:
                            next(gn_)
                            alive = True
                        except StopIteration:
                            pass
                    if alive:
                        yield
                for d_ in range(2):
                    c = orders[d_][step]
                    po = P[B0 + 2 + d_][:, 0:256]
                    st, sr = outs[d_][0]
                    gate = GF if d_ == 0 else GB
                    a_, a_r = arot.next()
                    kb.stt("dve", a_[:], po, st[:, 2:3], gate[:, c, :], ALU.mult, ALU.mult,
                           [PR[B0 + 2 + d_], sr, GFr if d_ == 0 else GBr], [a_r])
                    tt = b * NT + c
                    kb.dma("sp", reta[d_][tt * 128:(tt + 1) * 128, h * 256:(h + 1) * 256], a_[:], [a_r], [Reg()])
                yield
        interleave((head_unit(b, h, i % NU) for i, (b, h) in enumerate((b, h) for b in range(NB) for h in range(8))), NU, 3)
    kb.barrier()
    phase_outproj(g, l, reta, ar_, 2048, I["ret_w_out"], src, dst, need_ctx)


def interleave(gens, width, stagger=0):
    gens = iter(gens)
    active = []
    done = False
    rounds = 0
    next_admit = 0
    while True:
        while not done and len(active) < width and rounds >= next_admit:
            try:
                active.append(next(gens))
            except StopIteration:
                done = True
                break
            if stagger:
                next_admit = rounds + stagger
        if not active and done:
            break
        nxt = []
        for gen in active:
            try:
                next(gen)
                nxt.append(gen)
            except StopIteration:
                pass
        active = nxt
        rounds += 1


def mixer_dn(g, l, src, dst, need_ctx, hook=None):
    kb, nc, I = g.kb, g.nc, g.I
    NTOK = TT * 128
    NCH = NB * 2 * 36

    def dt_(name, shape, dt):
        if g.dbg and name in ("dnq", "dnk", "dnvt", "dnba", "dnkt"):
            return nc.dram_tensor("dbg_" + name, list(shape), dt, kind="ExternalOutput").ap()
        return nc.dram_tensor(kb.name(name), list(shape), dt, kind="Internal").ap()
    dnq = dt_("dnq", [8, 128, NTOK], BF16)
    dnk = dt_("dnk", [8, 128, NTOK], BF16)
    dnkt = dt_("dnkt", [NTOK, D], BF16)
    dnvt = dt_("dnvt", [NTOK, D], BF16)
    dnba = dt_("dnba", [NTOK, 32], F32)
    dng = dt_("dng", [NTOK, 2048], BF16)
    dna = [dt_("dna", [NTOK, D], BF16) for _ in range(2)]
    pu = dt_("dpu", [NCH, 64, 1024], F32)
    pw = dt_("dpw", [NCH, 128, 512], BF16)
    pat = dt_("dpat", [NCH, 64, 512], BF16)
    pqg = dt_("dpqg", [NCH, 128, 512], BF16)
    pkg = dt_("dpkg", [NCH, 64, 1024], BF16)
    pgl = dt_("dpgl", [NCH, 128, 8], F32)
    ar_ = [Reg() for _ in range(TT)]

    class Fresh(list):
        def __getitem__(self, i):
            return Reg()
    NX = 2308
    BLK = ((0, 512), (512, 512), (1024, 512), (1536, 512), (2048, 260))
    with ExitStack() as es:
        uT, uTr = phase_uT(g, es, l, src)
        cr = Reg()
        cw = kb.sb(es, "dcw", [128, 24, 5], F32)
        for k in range(5):
            kb.dma("sp", cw[:, :, k], I["dn_conv_w"][k, :].rearrange("(c p) -> p c", p=128), [], [cr],
                   allow_slow_non_contiguous=True)
        ones_b = kb.sb(es, "donesb", [128, 128], BF16)
        kb.memset("dve", ones_b[:], 1.0, [cr])
        wrot = kb.rot(es, "dw", [128, 8, 512], BF16, 3)
        XDT = F32 if os.environ.get("DN_XF32", "1") == "1" else BF16
        dgrot = kb.rot(es, "ddg", [128, 5, 128], XDT, 3)
        Xrot = kb.rot(es, "dX", [128, 2320], XDT, 3 if XDT == BF16 else 2)
        for xt_, xr_ in Xrot.items:
            kb.memset("pool", xt_[:, 0:2], 0.0, [xr_])
            kb.memset("pool", xt_[:, 258:262], 0.0, [xr_])
            kb.memset("pool", xt_[:, 2310:2320], 0.0, [xr_])
        Srot = kb.rot(es, "dsil", [128, NX], F32, 3)
        sqrot = kb.rot(es, "dsq", [128, 512], BF16, 4)
        sdrot = kb.rot(es, "dsd", [128, 512], F32, 4)
        QNrot = kb.rot(es, "dQN", [128, NX], BF16, 3)
        tstg = kb.rot(es, "dts", [128, 8, 128], BF16, 3)

        def inproj_unit(cb, b, slot, wt, wr, dg, dgr):
            kind, h = cb // 8, cb % 8
            base = b * SEQ
            pbs = [slot * 3, slot * 3 + 1]
            ptb = slot * 3 + 2
            X, Xr = Xrot.next()
            for bi, (o, n) in enumerate(((0, 512), (512, 512), (1024, 512), (1536, 512), (2048, 256))):
                t0 = base + o
                tiles = list(range(t0 // 128, (t0 + n) // 128))
                pb = pbs[bi % 2]
                kb.mm(g.ps[pb][:, 0:n], [(wt[:, kc, 0:128], uT[:, kc, t0:t0 + n]) for kc in range(8)],
                      [uTr[t] for t in tiles] + [wr], [g.psr[pb]])
                if o == 0:
                    kb.cp("act", X[:, 2:258], g.ps[pb][:, 0:256], [g.psr[pb]], [Xr])
                    kb.cp("act", X[:, 262:518], g.ps[pb][:, 256:512], [g.psr[pb]], [Xr])
                else:
                    kb.cp("act", X[:, 6 + o:6 + o + n], g.ps[pb][:, 0:n], [g.psr[pb]], [Xr])
                yield
            Ssil, Ssr = Srot.next()
            for bi, (o, n) in enumerate(BLK):
                pb = pbs[(bi + 1) % 2]
                kb.mm(g.ps[pb][:, 0:n], [(dg[:, k, :], X[:, o + k:o + k + n]) for k in range(5)], [Xr, dgr],
                      [g.psr[pb]])
                kb.act(Ssil[:, o:o + n], g.ps[pb][:, 0:n], AF.Silu, [g.psr[pb]], [Ssr])
                yield
            QN, QNr = QNrot.next()
            if kind < 2:
                for bi, (o, n) in enumerate(BLK):
                    pb = pbs[bi % 2]
                    sq, sqr = sqrot.next()
                    kb.act(sq[:, 0:n], Ssil[:, o:o + n], AF.Square, [Ssr], [sqr])
                    kb.mm(g.ps[pb][:, 0:n], [(ones_b[:], sq[:, 0:n])], [sqr, cr], [g.psr[pb]])
                    sd, sdr = sdrot.next()
                    kb.act(sd[:, 0:n], g.ps[pb][:, 0:n], AF.Sqrt, [g.psr[pb], g.const_r], [sdr],
                           bias=(g.eps128 if kind == 0 else g.eps)[:, 0:1], scale=128.0 if kind == 0 else 1.0)
                    kb.recip(sd[:, 0:n], sd[:, 0:n], [sdr], [sdr])
                    kb.tt("pool", QN[:, o:o + n], Ssil[:, o:o + n], sd[:, 0:n], ALU.mult, [Ssr, sdr], [QNr])
                    yield
                dstq = dnq if kind == 0 else dnk
                kb.dma("sp", dstq[h][:, base:base + TC], QN[:, 0:TC], [QNr], [Reg()])
                kb.dma("sp", dstq[h][:, base + TC:base + SEQ], QN[:, 260:NX], [QNr], [Reg()])
            else:
                kb.cp("pool", QN[:], Ssil[:], [Ssr], [QNr])
                yield
            if kind >= 1:
                dstt = dnkt if kind == 1 else dnvt
                for t0 in range(0, NT, 8):
                    ts_ = list(range(t0, min(NT, t0 + 8)))
                    psT = g.ps[ptb][:].bitcast(BF16).rearrange("p (c t) -> p c t", c=8)
                    kb.tr([(psT[:, j, :], QN[:, (t * 128 if t < 2 else t * 128 + 4):(t * 128 if t < 2 else t * 128 + 4) + 128])
                           for j, t in enumerate(ts_)], g.ident_b[:], [QNr, g.const_r], [g.psr[ptb]])
                    stt_, str_ = tstg.next()
                    kb.cp("act", stt_[:, 0:len(ts_), :], psT[:, 0:len(ts_), :], [g.psr[ptb]], [str_])
                    r0 = base + t0 * 128
                    kb.dma("sp", dstt[r0:r0 + len(ts_) * 128, h * 128:(h + 1) * 128]
                           .rearrange("(t p) d -> p t d", p=128), stt_[:, 0:len(ts_), :], [str_], [Reg()])
                    yield

        def inproj_units():
            i = 0
            for cb in range(24):
                wt, wr = load_w_bf16(g, wrot, I["dn_w_in"], cb * 128, 128)
                dg, dgr = dgrot.next()
                for k in range(5):
                    kb.ts("pool", dg[:, k, :], g.ident_f[:], cw[:, cb, k:k + 1], None, ALU.mult, None,
                          [g.const_r, cr], [dgr])
                for b in range(NB):
                    yield inproj_unit(cb, b, i % 2, wt, wr, dg, dgr)
                    i += 1
        interleave(inproj_units(), 2, 6)
        abr = Reg()
        dtb = kb.sb(es, "ddtb", [128, 16], F32)
        nea = kb.sb(es, "dnea", [128, 16], F32)
        kb.dma("sp", dtb[:], I["dn_dt_bias"][0, :].partition_broadcast(128), [], [abr])
        kb.dma("sp", nea[:], I["dn_a_log"][0, :].partition_broadcast(128), [], [abr])
        kb.act(nea[:], nea[:], AF.Exp, [abr], [abr])
        kb.ts("dve", nea[:], nea[:], -1.0, None, ALU.mult, None, [abr], [abr])
        wt, wr = load_w_bf16(g, wrot, I["dn_w_in"], 3072, 32)
        bstg = kb.rot(es, "dbs", [128, 32], F32, 4)
        btmp = kb.rot(es, "dbt", [128, 16], F32, 4)

        def ba_unit(tt):
            pb = 6 + tt % 2
            off = (tt // 2) % 8 * 32
            pp = g.ps[pb][:, off:off + 32]
            kb.mm(pp, [(uT[:, kc, tt * 128:(tt + 1) * 128], wt[:, kc, 0:32]) for kc in range(8)],
                  [uTr[tt], wr], [g.psr[pb]])
            bs, bsr = bstg.next()
            bt, btr = btmp.next()
            yield
            kb.act(bs[:, 0:16], pp[:, 0:16], AF.Sigmoid, [g.psr[pb]], [bsr])
            kb.tt("dve", bt[:], pp[:, 16:32], dtb[:], ALU.add, [g.psr[pb], abr], [btr])
            yield
            kb.act(bt[:], bt[:], AF.Exp, [btr], [btr])
            yield
            kb.ts("dve", bt[:], bt[:], 1.0, None, ALU.add, None, [btr], [btr])
            yield
            kb.act(bt[:], bt[:], AF.Ln, [btr], [btr])
            yield
            kb.tt("dve", bs[:, 16:32], bt[:], nea[:], ALU.mult, [btr, abr], [bsr])
            kb.dma("sp", dnba[tt * 128:(tt + 1) * 128, :], bs[:], [bsr], [Reg()])
        interleave((ba_unit(tt) for tt in range(TT)), 4, 1)
        stgb = kb.rot(es, "dgs", [128, 512], BF16, 4)
        tiles = [tt for tt in range(TT) if need_ctx or (tt % NT) >= 2]
        for ch in range(4):
            proj_tm(g, uT, uTr, I["dn_w_in"], 3104 + ch * 512, 512, dng, Fresh(), ch * 512, wrot, stgb, 0,
                    func=AF.Silu, tiles=tiles)
    kb.barrier()
    bc64 = lambda ap: ap.unsqueeze(2).broadcast_to([64, 8, 64])
    bc128 = lambda ap: ap.unsqueeze(2).broadcast_to([64, 8, 128])

    def chunk_id(b, d_, c):
        return (b * 2 + d_) * 36 + c
    with ExitStack() as es:
        cr = Reg()
        ktab = kb.sb(es, "dktab", [64, 4, 64], F32)
        kb.dma("sp", ktab[:], I["k_dn"].rearrange("t p n -> p t n"), [], [cr])
        ones3 = kb.sb(es, "dones3", [64, 128], F32)
        kb.memset("dve", ones3[:], 1.0, [cr])
        identf = g.ident_f
        identb = g.ident_b
        NW = 2

        def R1(shape, dt, name, n=NW + 1):
            return kb.rot(es, name, shape, dt, n)
        kTr_, qTr_ = R1([128, 8, 64], BF16, "dkT"), R1([128, 8, 64], BF16, "dqT")
        ktr_, vtr_ = R1([64, 8, 128], BF16, "dkt"), R1([64, 8, 128], BF16, "dvt")
        bar_ = R1([64, 32], F32, "dba")
        LBn_ = R1([64, 8, 128], F32, "dLBn")
        sm_ = R1([128, 32], F32, "dsm")
        gdm_ = R1([64, 8, 64], F32, "dgdm")
        t12_ = R1([64, 2, 8, 64], F32, "dt12")
        Dms_, DTi_ = R1([64, 8, 64], F32, "dDms"), R1([64, 8, 64], F32, "dDTi")
        Egr_ = R1([128, 8, 64], F32, "dEgr")
        qgT_ = R1([128, 8, 64], BF16, "dqgT")
        INV_F32 = os.environ.get("DN_INV_F32", "1") == "1"
        IDT = F32 if INV_F32 else BF16
        A_ = R1([64, 8, 64], IDT, "dA", 2 * NW + 2)
        B_ = R1([64, 8, 64], IDT, "dB", 2 * NW + 2)
        Af_ = R1([64, 8, 64], F32, "dAf")
        Tt_ = R1([64, 8, 64], F32, "dTt")
        Ttb_ = R1([64, 8, 64], IDT, "dTtb", 2 * NW + 2)
        Tt16_ = R1([64, 8, 64], BF16, "dTt16")
        attnT_ = R1([64, 8, 64], BF16, "dattn")
        rv_, kbg_, kg_ = R1([64, 8, 128], BF16, "drv"), R1([64, 8, 128], BF16, "dkbg"), R1([64, 8, 128], BF16, "dkg")
        u_ = R1([64, 8, 128], F32, "du")
        wT_ = R1([128, 8, 64], BF16, "dwT")
        gls_ = R1([128, 8], F32, "dgls")
        P, PR = g.ps, g.psr

        def pre_unit(b, d_, c, slot):
            cid = chunk_id(b, d_, c)
            tok0 = b * SEQ + c * 64
            Pa, Pb, Pc, Pd = (slot * 4 + i for i in range(4))
            Tri = ktab[:, d_, :]
            strict = ktab[:, 2 + d_, :]
            kT, kTr = kTr_.next()
            qT, qTr = qTr_.next()
            kt, ktr = ktr_.next()
            vt, vtr = vtr_.next()
            ba, bar = bar_.next()
            kb.dma("sp", kT[:], dnk[:, :, tok0:tok0 + 64].rearrange("h p n -> p h n"), [], [kTr])
            kb.dma("sp", qT[:], dnq[:, :, tok0:tok0 + 64].rearrange("h p n -> p h n"), [], [qTr])
            kb.dma("sp", kt[:], dnkt[tok0:tok0 + 64, :].rearrange("p (h d) -> p h d", h=8), [], [ktr])
            kb.dma("sp", vt[:], dnvt[tok0:tok0 + 64, :].rearrange("p (h d) -> p h d", h=8), [], [vtr])
            kb.dma("sp", ba[:], dnba[tok0:tok0 + 64, :], [], [bar])
            beta = ba[:, 8 * d_:8 * d_ + 8]
            la = ba[:, 16 + 8 * d_:24 + 8 * d_]
            yield
            LBn, LBr = LBn_.next()
            kb.ts("dve", LBn[:], bc128(la), -1.0, None, ALU.mult, None, [bar], [LBr])
            kb.mm(P[Pd][0:64, 0:8], [(Tri, la)], [cr, bar], [PR[Pd]])
            kb.mm(P[Pd][:, 8:16], [(ones3[:, :], la)], [cr, bar], [PR[Pd]])
            for h in range(8):
                kb.mm(P[Pa][0:64, h * 64:(h + 1) * 64], [(kT[:, h, :], kT[:, h, :])], [kTr], [PR[Pa]])
            for h in range(8):
                kb.mm(P[Pb][0:64, h * 64:(h + 1) * 64], [(kT[:, h, :], qT[:, h, :])], [kTr, qTr], [PR[Pb]])
            yield
            for h in range(8):
                kb.mm(P[Pc][:, h * 64:(h + 1) * 64], [(LBn[:, h, :], Tri)], [LBr, cr], [PR[Pc]])
            sm, smr = sm_.next()
            kb.cp("dve", sm[0:64, 0:8], P[Pd][0:64, 0:8], [PR[Pd]], [smr])
            gc = sm[0:64, 0:8]
            gls, glsr = gls_.next()
            kb.act(gls[:], P[Pd][:, 8:16], AF.Exp, [PR[Pd]], [glsr])
            kb.dma("sp", pgl[cid], gls[:], [glsr], [Reg()])
            yield
            gdm, gdmr = gdm_.next()
            GR = P[Pc][:].rearrange("p (h n) -> p h n", h=8)
            kb.tt("dve", gdm[:], GR[0:64], bc64(gc), ALU.add, [PR[Pc], smr], [gdmr])
            Egr, Egrr = Egr_.next()
            kb.act(Egr[:], GR, AF.Exp, [PR[Pc]], [Egrr], scale=-1.0)
            kb.act(sm[0:64, 8:16], gc, AF.Exp, [smr], [smr])
            kb.tt("dve", sm[0:64, 16:24], P[Pd][0:64, 8:16], gc, ALU.subtract, [PR[Pd], smr], [smr])
            yield
            t12, t12r = t12_.next()
            kb.ts("dve", t12[:, 0], gdm[:], 0.0, None, ALU.min, None, [gdmr], [t12r])
            kb.ts("dve", t12[:, 1], gdm[:], -1.0, 0.0, ALU.mult, ALU.min, [gdmr], [t12r])
            qgT, qgTr = qgT_.next()
            kb.tt("pool", qgT[:], qT[:], Egr[:], ALU.mult, [qTr, Egrr], [qgTr])
            kb.dma("sp", pqg[cid].rearrange("p (h n) -> p h n", h=8), qgT[:], [qgTr], [Reg()])
            kb.tt("dve", sm[0:64, 8:16], sm[0:64, 8:16], beta, ALU.mult, [smr, bar], [smr])
            kb.act(sm[0:64, 16:24], sm[0:64, 16:24], AF.Exp, [smr], [smr])
            yield
            kb.act(t12[:], t12[:], AF.Exp, [t12r], [t12r])
            rv, rvr = rv_.next()
            kbg, kbgr = kbg_.next()
            kg, kgr = kg_.next()
            kb.tt("pool", rv[:], vt[:], bc128(beta), ALU.mult, [vtr, bar], [rvr])
            kb.tt("pool", kbg[:], kt[:], bc128(sm[0:64, 8:16]), ALU.mult, [ktr, smr], [kbgr])
            kb.tt("pool", kg[:], kt[:], bc128(sm[0:64, 16:24]), ALU.mult, [ktr, smr], [kgr])
            kb.dma("sp", pkg[cid].rearrange("p (h n) -> p h n", h=8), kg[:], [kgr], [Reg()])
            yield
            Dms, Dmsr = Dms_.next()
            DTi, DTir = DTi_.next()
            kb.tt("pool", Dms[:], t12[:, 0], strict.unsqueeze(1).broadcast_to([64, 8, 64]), ALU.mult,
                  [t12r, cr], [Dmsr])
            kb.tt("pool", DTi[:], t12[:, 1], Tri.unsqueeze(1).broadcast_to([64, 8, 64]), ALU.mult,
                  [t12r, cr], [DTir])
            KK = P[Pa][0:64, :].rearrange("p (h n) -> p h n", h=8)
            QK = P[Pb][0:64, :].rearrange("p (h n) -> p h n", h=8)
            Af, Afr = Af_.next()
            kb.tt("dve", Af[:], KK, bc64(beta), ALU.mult, [PR[Pa], bar], [Afr])
            yield
            A0, A0r = A_.next()
            kb.tt("dve", A0[:], Af[:], Dms[:], ALU.mult, [Afr, Dmsr], [A0r])
            attnT, attnr = attnT_.next()
            kb.tt("dve", attnT[:], QK, DTi[:], ALU.mult, [PR[Pb], DTir], [attnr])
            kb.dma("sp", pat[cid].rearrange("p (h n) -> p h n", h=8), attnT[:], [attnr], [Reg()])
            yield
            if INV_F32:
                KKb = P[Pa][0:64, :].rearrange("p (h n) -> p h n", h=8)
            else:
                KKb = P[Pa][0:64, :].bitcast(BF16)[:, 0:512].rearrange("p (h n) -> p h n", h=8)
            kb.tr([(KKb[:, h, :], A0[:, h, :]) for h in range(8)], (identf if INV_F32 else identb)[0:64, 0:64],
                  [A0r, g.const_r], [PR[Pa]])
            yield
            B0, B0r = B_.next()
            kb.cp("act", B0[:], KKb, [PR[Pa]], [B0r])
            Tt, Ttr = Tt_.next()
            kb.tt("dve", Tt[:], identf[0:64, 0:64].unsqueeze(1).broadcast_to([64, 8, 64]), KKb,
                  ALU.subtract, [g.const_r, PR[Pa]], [Ttr])
            Ttb, Ttbr = Ttb_.next()
            kb.cp("pool", Ttb[:], Tt[:], [Ttr], [Ttbr])
            yield
            Ak, Akr, Bk, Bkr = A0, A0r, B0, B0r

            def sq_mm(Ak, Akr, Bk, Bkr, need_b):
                for h in range(8):
                    kb.mm(P[Pa][0:64, h * 64:(h + 1) * 64], [(Bk[:, h, :], Ak[:, h, :])], [Akr, Bkr], [PR[Pa]])
                if need_b:
                    for h in range(8):
                        kb.mm(P[Pb][0:64, h * 64:(h + 1) * 64], [(Ak[:, h, :], Bk[:, h, :])], [Akr, Bkr], [PR[Pb]])
            sq_mm(Ak, Akr, Bk, Bkr, True)
            yield
            An, Anr = A_.next()
            Bn, Bnr = B_.next()
            kb.cp("act", An[:], KK, [PR[Pa]], [Anr])
            kb.cp("pool" if False else "dve", Bn[:], QK, [PR[Pb]], [Bnr])
            yield
            for lev in range(5):
                for h in range(8):
                    kb.mm(P[Pd][0:64, h * 64:(h + 1) * 64], [(An[:, h, :], Ttb[:, h, :])], [Anr, Ttbr], [PR[Pd]])
                if lev < 4:
                    sq_mm(An, Anr, Bn, Bnr, lev < 3)
                yield
                kb.tt("dve", Tt[:], Tt[:], P[Pd][0:64, :].rearrange("p (h n) -> p h n", h=8), ALU.add,
                      [Ttr, PR[Pd]], [Ttr])
                if lev < 4:
                    An2, An2r = A_.next()
                    kb.cp("act", An2[:], KK, [PR[Pa]], [An2r])
                    if lev < 3:
                        Bn2, Bn2r = B_.next()
                        kb.cp("pool" if False else "dve", Bn2[:], QK, [PR[Pb]], [Bn2r])
                        Bn, Bnr = Bn2, Bn2r
                    An, Anr = An2, An2r
                yield
                if lev < 4:
                    Ttb, Ttbr = Ttb_.next()
                    kb.cp("act", Ttb[:], Tt[:], [Ttr], [Ttbr])
                    yield
            u, ur = u_.next()
            wT, wTr = wT_.next()
            if INV_F32:
                Ttb, Ttbr = Tt16_.next()
                kb.cp("act", Ttb[:], Tt[:], [Ttr], [Ttbr])
                yield
            for hb in range(2):
                pp = (Pa, Pb)[hb]
                for hh in range(4):
                    h = hb * 4 + hh
                    kb.mm(P[pp][0:64, hh * 128:(hh + 1) * 128], [(Ttb[:, h, :], rv[:, h, :])],
                          [Ttbr, rvr], [PR[pp]])
            for h in range(8):
                kb.mm(P[Pc][:, h * 64:(h + 1) * 64], [(kbg[:, h, :], Ttb[:, h, :])], [kbgr, Ttbr], [PR[Pc]])
            yield
            for hb in range(2):
                pp = (Pa, Pb)[hb]
                kb.cp("act" if hb == 0 else "dve", u[:, hb * 4:(hb + 1) * 4, :],
                      P[pp][0:64, :].rearrange("p (h n) -> p h n", h=4), [PR[pp]], [ur])
            kb.cp("act", wT[:], GR, [PR[Pc]], [wTr])
            yield
            kb.dma("sp", pu[cid].rearrange("p (h n) -> p h n", h=8), u[:], [ur], [Reg()])
            kb.dma("sp", pw[cid].rearrange("p (h n) -> p h n", h=8), wT[:], [wTr], [Reg()])

        units = []
        for b in range(NB):
            for d_ in range(2):
                for c in range(36):
                    units.append((b, d_, c))
        if int(os.environ.get("DNSTOP", "9")) >= 2:
            interleave((pre_unit(b, d_, c, i % NW) for i, (b, d_, c) in enumerate(units)), NW, 14)
    kb.barrier()
    with ExitStack() as es:
        cr = Reg()
        ng = kb.sb(es, "dng_", [64, 128], F32)
        kb.dma("sp", ng[:], I["dn_norm_g"][0, :].partition_broadcast(64), [], [cr])
        P, PR = g.ps, g.psr
        NC_ = 4

        def R2(shape, dt, name, n=2):
            return [kb.rot(es, name, shape, dt, n) for _ in range(NC_)]
        u_, wT_ = R2([64, 8, 128], F32, "eu"), R2([128, 8, 64], BF16, "ewT")
        at_, qg_ = R2([64, 8, 64], BF16, "eat"), R2([128, 8, 64], BF16, "eqg")
        kg_, gl_ = R2([64, 8, 128], BF16, "ekg"), R2([128, 8], F32, "egl")
        gt_ = R2([64, 8, 128], BF16, "egt")
        vn_ = R2([64, 8, 128], BF16, "evn", 1)
        osb_ = R2([64, 8, 128], F32, "eosb", 1)
        sq_ = R2([64, 8, 128], F32, "esq", 1)
        gn_ = R2([64, 8, 128], F32, "egn", 1)
        oa_ = R2([64, 8, 128], BF16, "eoa", 2)
        sm_ = R2([64, 32], F32, "esm", 2)
        print("sbuf remaining (dn rec)", nc.sbuf_bytes_remaining)
        S = [kb.sb(es, "dS", [128, 8, 128], F32) for _ in range(NC_)]
        Sb = [kb.sb(es, "dSb", [128, 8, 128], BF16) for _ in range(NC_)]
        Sr = [Reg() for _ in range(NC_)]
        Sbr = [Reg() for _ in range(NC_)]

        def rec_chain(b, d_, ch):
            order = list(range(36)) if d_ == 0 else [3, 2, 1, 0] + list(range(35, 3, -1))
            Ra, Rb = ch * 2, ch * 2 + 1
            kb.memset("pool", S[ch][:], 0.0, [Sr[ch]])
            kb.memset("pool", Sb[ch][:], 0.0, [Sbr[ch]])
            loaded = {}

            def load(c):
                cid = chunk_id(b, d_, c)
                tok0 = b * SEQ + c * 64
                emit = need_ctx or c >= 4
                r = {}
                for nm, rot_, srcap in (("u", u_[ch], pu[cid]), ("wT", wT_[ch], pw[cid]), ("at", at_[ch], pat[cid]),
                                        ("qg", qg_[ch], pqg[cid]), ("kg", kg_[ch], pkg[cid])):
                    if nm in ("at", "qg") and not emit:
                        continue
                    t_, tr_ = rot_.next()
                    kb.dma("sp", t_[:], srcap.rearrange("p (h n) -> p h n", h=8), [], [tr_])
                    r[nm] = (t_, tr_)
                t_, tr_ = gl_[ch].next()
                kb.dma("sp", t_[:], pgl[cid], [], [tr_])
                r["gl"] = (t_, tr_)
                if emit:
                    t_, tr_ = gt_[ch].next()
                    kb.dma("sp", t_[:], dng[tok0:tok0 + 64, d_ * 1024:(d_ + 1) * 1024]
                           .rearrange("p (h d) -> p h d", h=8), [], [tr_])
                    r["gt"] = (t_, tr_)
                return r
            loaded[0] = load(order[0])
            for step, c in enumerate(order):
                if step + 1 < 36:
                    loaded[step + 1] = load(order[step + 1])
                L_ = loaded.pop(step)
                tok0 = b * SEQ + c * 64
                emit = need_ctx or c >= 4
                (u, ur), (wT, wTr), (kg, kgr), (gl, glr) = L_["u"], L_["wT"], L_["kg"], L_["gl"]
                for hb in range(2):
                    pp = (Ra, Rb)[hb]
                    for hh in range(4):
                        h = hb * 4 + hh
                        kb.mm(P[pp][0:64, hh * 128:(hh + 1) * 128], [(wT[:, h, :], Sb[ch][:, h, :])],
                              [wTr, Sbr[ch]], [PR[pp]])
                yield
                vn, vnr = vn_[ch].next()
                for hb in range(2):
                    pp = (Ra, Rb)[hb]
                    kb.tt("dve", vn[:, hb * 4:(hb + 1) * 4, :], u[:, hb * 4:(hb + 1) * 4, :],
                          P[pp][0:64, :].rearrange("p (h n) -> p h n", h=4), ALU.subtract, [ur, PR[pp]], [vnr])
                if emit:
                    gn, gnr = gn_[ch].next()
                    kb.tt("pool", gn[:], L_["gt"][0][:], ng[:, :].unsqueeze(1).broadcast_to([64, 8, 128]), ALU.mult,
                          [L_["gt"][1], cr], [gnr])
                yield
                if emit:
                    (at, atr), (qg, qgr) = L_["at"], L_["qg"]
                    for hb in range(2):
                        pp = (Ra, Rb)[hb]
                        for hh in range(4):
                            h = hb * 4 + hh
                            kb.mm(P[pp][0:64, hh * 128:(hh + 1) * 128],
                                  [(qg[:, h, :], Sb[ch][:, h, :]), (at[:, h, :], vn[:, h, :])],
                                  [qgr, Sbr[ch], atr, vnr], [PR[pp]])
                    yield
                    osb, osr = osb_[ch].next()
                    kb.cp("act", osb[:, 0:4, :], P[Ra][0:64, :].rearrange("p (h n) -> p h n", h=4), [PR[Ra]], [osr])
                    kb.cp("dve", osb[:, 4:8, :], P[Rb][0:64, :].rearrange("p (h n) -> p h n", h=4), [PR[Rb]], [osr])
                    yield
                for hb in range(2):
                    pp = (Ra, Rb)[hb]
                    for hh in range(4):
                        h = hb * 4 + hh
                        kb.mm(P[pp][:, hh * 128:(hh + 1) * 128], [(kg[:, h, :], vn[:, h, :])], [kgr, vnr], [PR[pp]])
                kb.tt("pool", S[ch][:], S[ch][:], gl[:, :].unsqueeze(2).broadcast_to([128, 8, 128]), ALU.mult,
                      [Sr[ch], glr], [Sr[ch]])
                yield
                for hb in range(2):
                    pp = (Ra, Rb)[hb]
                    kb.tt("dve", S[ch][:, hb * 4:(hb + 1) * 4, :], S[ch][:, hb * 4:(hb + 1) * 4, :],
                          P[pp][:, :].rearrange("p (h n) -> p h n", h=4), ALU.add, [Sr[ch], PR[pp]], [Sr[ch]])
                yield
                kb.cp("act", Sb[ch][:], S[ch][:], [Sr[ch]], [Sbr[ch]])
                if emit:
                    sq, sqr = sq_[ch].next()
                    kb.act(sq[:], osb[:], AF.Square, [osr], [sqr])
                    yield
                    sm, smr = sm_[ch].next()
                    kb.op("dve", lambda hd, o=sm[:, 0:8], i=sq[:]: hd.reduce_sum(out=o, in_=i, axis=AX.X),
                          [sqr], [smr])
                    yield
                    kb.act(sm[:, 8:16], sm[:, 0:8], AF.Sqrt, [smr, g.const_r], [smr], bias=g.eps[0:64, 0:1],
                           scale=1.0 / 128.0)
                    yield
                    kb.recip(sm[:, 16:24], sm[:, 8:16], [smr], [smr])
                    yield
                    kb.tt("dve", osb[:], osb[:], sm[:, 16:24].unsqueeze(2).broadcast_to([64, 8, 128]), ALU.mult,
                          [osr, smr], [osr])
                    yield
                    oa, oar = oa_[ch].next()
                    kb.tt("pool", oa[:], osb[:], gn[:], ALU.mult, [osr, gnr], [oar])
                    kb.dma("sp", dna[d_][tok0:tok0 + 64, :].rearrange("p (h d) -> p h d", h=8), oa[:], [oar], [Reg()])
                yield
        if int(os.environ.get("DNSTOP", "9")) >= 3:
            interleave((rec_chain(b, d_, b * 2 + d_) for b in range(NB) for d_ in range(2)), NC_, 2)
    kb.barrier()
    if hook is not None:
        hook()
    phase_outproj(g, l, dna, ar_, D, I["dn_w_out"], src, dst, need_ctx)


def host_consts():
    c = {}
    c["k_ident"] = np.eye(128, dtype=np.float32)
    mats = np.zeros((20, 128, 128), np.float32)
    for wi, w in enumerate((2, 4, 8, 16)):
        lo = w // 2
        hi = w - 1 - lo
        n = 384
        for var, (t0, nseq_lo, nseq_hi) in enumerate(((128, 0, 384), (0, 0, 384), (256, 0, 384))):
            pass
        A = np.zeros((n, n), np.float32)
        for t in range(n):
            a, b_ = max(0, t - lo), min(n, t + hi + 1)
            A[t, a:b_] = 1.0 / (b_ - a)
        M = A - np.eye(n, dtype=np.float32)
        MT = M.T
        mats[wi * 5 + 0] = MT[128:256, 128:256]
        mats[wi * 5 + 1] = MT[0:128, 0:128]
        mats[wi * 5 + 2] = MT[256:384, 256:384]
        mats[wi * 5 + 3] = MT[0:128, 128:256]
        mats[wi * 5 + 4] = MT[256:384, 128:256]
    c["k_pool"] = mats
    pos = np.arange(T)
    row, col = pos // 64, pos % 64

    def rope_tab(dh, nrep):
        q = dh // 4
        tab = np.zeros((2, dh, SEQ), np.float64)
        tab[0, :, :TC] = 1.0
        for d in range(dh):
            blk, i = d // q, d % q
            inv = 10000.0 ** (-i / q)
            p_ = row if blk < 2 else col
            ang = (p_.astype(np.float32) * np.float32(inv)).astype(np.float64)
            tab[0, d, TC:] = np.cos(ang)
            tab[1, d, TC:] = np.sin(ang) * (-1.0 if blk % 2 == 0 else 1.0)
        return np.tile(tab, (1, nrep, 1)).astype(np.float32)
    c["k_rope_att"] = rope_tab(64, 2)
    c["k_rope_ret"] = rope_tab(128, 1)
    qi = np.arange(128)[:, None]
    kj = np.arange(384)[None, :]
    keep = np.abs(kj - 128 - qi) <= 128
    m = np.zeros((3, 128, 384), np.float32)
    m[0] = np.where(keep, 0.0, -30000.0)
    m[1] = np.where(keep & (kj >= 128), 0.0, -30000.0)
    m[2] = np.where(keep & (kj < 256), 0.0, -30000.0)
    c["k_att_mask"] = m
    jj = np.arange(128)[:, None].astype(np.float64)
    ii = np.arange(128)[None, :].astype(np.float64)
    rt = np.zeros((5, 128, 128), np.float64)
    rt[0] = np.where(ii >= jj, ii - jj, 1e9)
    rt[1] = np.where(jj >= ii, jj - ii, 1e9)
    rt[2] = np.broadcast_to(ii + 1.0, (128, 128))
    rt[3] = np.broadcast_to(128.0 - ii, (128, 128))
    rt[4, :, 0] = 127.0 - np.arange(128)
    rt[4, :, 1] = np.arange(128)
    rt[4, :, 2] = 128.0
    c["k_ret_tab"] = rt.astype(np.float32)
    p_ = np.arange(64)[:, None]
    f_ = np.arange(64)[None, :]
    c["k_dn"] = np.stack([p_ <= f_, p_ >= f_, p_ > f_, p_ < f_]).astype(np.float32)
    return c


LAYERS = [0, 1, 2, 3]
_cache = {}


def _prep_inputs(inputs):
    f = lambda a: np.ascontiguousarray(np.asarray(a, dtype=np.float32))
    shared = {}
    for k in ("ada_w", "ada_b", "mix_pre_g", "mix_post_g", "mlp_pre_g", "mlp_post_g", "mlp_w1", "mlp_w2"):
        shared[k] = f(inputs[k])
    shared["c_ctx"] = f(inputs["c_ctx"]).reshape(1, D)
    shared["ret_w_in"] = f(inputs["ret_w_in"][0])
    shared["ret_decay_logit"] = f(inputs["ret_decay_logit"][0]).reshape(1, 16)
    shared["ret_w_out"] = f(inputs["ret_w_out"][0])
    shared["att_w_in"] = f(inputs["att_w_in"][0])
    shared["att_sink"] = f(inputs["att_sink"][0]).reshape(1, 16)
    shared["att_w_out"] = f(inputs["att_w_out"][0])
    shared["pool_w"] = f(inputs["pool_w"][0])
    shared["pool_b"] = f(inputs["pool_b"][0]).reshape(1, D)
    shared["pool_scale"] = f(inputs["pool_scale"][0]).reshape(1, D)
    shared["dn_w_in"] = f(inputs["dn_w_in"][0])
    shared["dn_conv_w"] = f(inputs["dn_conv_w"][0])
    shared["dn_a_log"] = f(inputs["dn_a_log"][0]).reshape(1, 16)
    shared["dn_dt_bias"] = f(inputs["dn_dt_bias"][0]).reshape(1, 16)
    shared["dn_norm_g"] = f(inputs["dn_norm_g"][0]).reshape(1, 128)
    shared["dn_w_out"] = f(inputs["dn_w_out"][0])
    shared.update(host_consts())
    return shared


def run(inputs, layers, n_cores, dbg=False):
    key = (tuple(layers), dbg)
    if key not in _cache:
        _cache[key] = build_program(layers, dbg)
    nc = _cache[key]
    shared = _prep_inputs(inputs)
    x = np.asarray(inputs["x"], dtype=np.float32)
    c = np.asarray(inputs["c"], dtype=np.float32)
    ctx = np.asarray(inputs["ctx"], dtype=np.float32)
    in_maps = []
    for i in range(n_cores):
        m = dict(shared)
        m["x"] = np.ascontiguousarray(x[i * NB:(i + 1) * NB])
        m["c"] = np.ascontiguousarray(c[i * NB:(i + 1) * NB])
        m["ctx"] = np.ascontiguousarray(ctx[i * NB:(i + 1) * NB])
        in_maps.append(m)
    res = run_bass_kernel_spmd(nc, in_maps, core_ids=list(range(n_cores)))
    y = np.concatenate([r["y"] for r in res.results], axis=0)
    if dbg:
        global DBG_OUT
        DBG_OUT = {k: np.asarray(v) for k, v in res.results[0].items() if k.startswith("dbg_")}
        return y, np.concatenate([r["ctx_out"] for r in res.results], axis=0)
    return y


def kernel(**inputs):
    return run(inputs, LAYERS, 8).astype(np.float32)
```
